# Optimizing a Trainium2 kernel written in Bass

```python
import math
import jax
import jax.numpy as jnp
from jax import lax
import numpy as np

D_MODEL = 1024
BATCH = 8
SEQ = 4096
DEPTH = 4

HEAD_DIM = 64
N_EVEN = (DEPTH + 1) // 2
N_ODD = DEPTH // 2
NORM_EPS = 1e-6
D_FF = ((8 * D_MODEL // 3 + 255) // 256) * 256

POOL_WINDOWS = (2, 4, 8, 16)
POOL_GROUPS = len(POOL_WINDOWS)
POOL_WIDTH = D_MODEL // 4
POOL_GC = POOL_WIDTH // POOL_GROUPS

ATTN_Q_HEADS = (3 * D_MODEL // 4) // HEAD_DIM
ATTN_KV_HEADS = ATTN_Q_HEADS // 3
GQA_GROUP = ATTN_Q_HEADS // ATTN_KV_HEADS
WINDOW = 128
ATTN_BLOCK = 128
ROPE_THETA = 10000.0
EVEN_IN = POOL_WIDTH + (ATTN_Q_HEADS + 2 * ATTN_KV_HEADS) * HEAD_DIM
EVEN_MIX = POOL_WIDTH + ATTN_Q_HEADS * HEAD_DIM

CONV_CH = D_MODEL // 2
CONV_K = 31

SSM_INNER = D_MODEL
SSM_HEAD_DIM = 64
SSM_HEADS = SSM_INNER // SSM_HEAD_DIM
SSM_GROUPS = 4
SSM_STATE = 128
SSM_CONV = 4
SSM_CHUNK = 128
SSM_XBC = SSM_INNER + 2 * SSM_GROUPS * SSM_STATE
ODD_IN = 2 * CONV_CH + SSM_INNER + SSM_XBC + 2 * SSM_HEADS
ODD_MIX = CONV_CH + SSM_INNER

kernel_name = 'hybrid_pool_swa_conformer_ssd_encoder'


def rms_norm(x, w):
    xf = x.astype(jnp.float32)
    y = xf * lax.rsqrt(jnp.mean(xf * xf, axis=-1, keepdims=True) + NORM_EPS)
    return (y * w.astype(jnp.float32)).astype(x.dtype)


def layer_norm(x, g, b):
    xf = x.astype(jnp.float32)
    mu = jnp.mean(xf, axis=-1, keepdims=True)
    xc = xf - mu
    y = xc * lax.rsqrt(jnp.mean(xc * xc, axis=-1, keepdims=True) + NORM_EPS)
    return (y * g.astype(jnp.float32) + b.astype(jnp.float32)).astype(x.dtype)


def swiglu(h, w_gate, w_up, w_down):
    return (jax.nn.silu(h @ w_gate) * (h @ w_up)) @ w_down


def depthwise_conv(x, w, b, pad_left, pad_right):
    y = lax.conv_general_dilated(
        x, w[:, None, :].astype(x.dtype), window_strides=(1,),
        padding=[(pad_left, pad_right)],
        dimension_numbers=('NWC', 'WIO', 'NWC'),
        feature_group_count=x.shape[-1])
    return y + b.astype(x.dtype)


def rope_tables(seq_len):
    inv = 1.0 / (ROPE_THETA ** (jnp.arange(0, HEAD_DIM, 2, dtype=jnp.float32) / HEAD_DIM))
    ang = jnp.arange(seq_len, dtype=jnp.float32)[:, None] * inv[None, :]
    return jnp.cos(ang), jnp.sin(ang)


def apply_rope(t, cos, sin):
    tf = t.astype(jnp.float32)
    t1, t2 = tf[..., :HEAD_DIM // 2], tf[..., HEAD_DIM // 2:]
    c, s = cos[None, :, None, :], sin[None, :, None, :]
    return jnp.concatenate([t1 * c - t2 * s, t2 * c + t1 * s], axis=-1).astype(t.dtype)


def multiscale_pool(u, w_grp, scale):
    b, S, C = u.shape
    uf = u.astype(jnp.float32)
    cs = jnp.concatenate([jnp.zeros((b, 1, C), jnp.float32), jnp.cumsum(uf, axis=1)], axis=1)
    t = jnp.arange(S)
    outs = []
    for g, w in enumerate(POOL_WINDOWS):
        lo = jnp.clip(t - w // 2, 0, S)
        hi = jnp.clip(t + w - w // 2, 0, S)
        csg = cs[..., g * POOL_GC:(g + 1) * POOL_GC]
        cnt = (hi - lo).astype(jnp.float32)[None, :, None]
        mean = (jnp.take(csg, hi, axis=1) - jnp.take(csg, lo, axis=1)) / cnt
        outs.append(mean - uf[..., g * POOL_GC:(g + 1) * POOL_GC])
    p = jnp.stack(outs, axis=2).astype(u.dtype)
    y = jnp.einsum('bsgc,gcd->bsgd', p, w_grp).reshape(b, S, C)
    return y * scale


def windowed_gqa(q, k, v, sink):
    b, S = q.shape[0], q.shape[1]
    nb = S // ATTN_BLOCK
    qb = q.reshape(b, nb, ATTN_BLOCK, ATTN_KV_HEADS, GQA_GROUP, HEAD_DIM)

    def band(t):
        tp = jnp.pad(t, ((0, 0), (ATTN_BLOCK, ATTN_BLOCK), (0, 0), (0, 0)))
        tp = tp.reshape(b, nb + 2, ATTN_BLOCK, ATTN_KV_HEADS, HEAD_DIM)
        return jnp.concatenate([tp[:, :-2], tp[:, 1:-1], tp[:, 2:]], axis=2)

    kw, vw = band(k), band(v)
    s = jnp.einsum('bnqhgd,bnkhd->bnhgqk', qb, kw).astype(jnp.float32) * (HEAD_DIM ** -0.5)
    blk = jnp.arange(nb)[:, None, None] * ATTN_BLOCK
    qpos = blk + jnp.arange(ATTN_BLOCK)[None, :, None]
    kpos = blk - ATTN_BLOCK + jnp.arange(3 * ATTN_BLOCK)[None, None, :]
    valid = (jnp.abs(qpos - kpos) <= WINDOW) & (kpos >= 0) & (kpos < S)
    s = jnp.where(valid[None, :, None, None], s, -jnp.inf)
    sk = sink.astype(jnp.float32).reshape(ATTN_KV_HEADS, GQA_GROUP)[None, None, :, :, None, None]
    m = jnp.maximum(jnp.max(s, axis=-1, keepdims=True), sk)
    p = jnp.exp(s - m)
    denom = jnp.sum(p, axis=-1, keepdims=True) + jnp.exp(sk - m)
    o = jnp.einsum('bnhgqk,bnkhd->bnqhgd', (p / denom).astype(v.dtype), vw)
    return o.reshape(b, S, ATTN_Q_HEADS * HEAD_DIM)


def ssd_scan(x, dt, A, Bm, Cm):
    b, L, H, P = x.shape
    G, N = Bm.shape[2], Bm.shape[3]
    Hg = H // G
    c = L // SSM_CHUNK
    X = (x * dt[..., None]).reshape(b, c, SSM_CHUNK, G, Hg, P)
    a = jnp.moveaxis((dt * A).reshape(b, c, SSM_CHUNK, G, Hg), 2, -1)
    a_cs = jnp.cumsum(a, axis=-1)
    Bc = Bm.reshape(b, c, SSM_CHUNK, G, N)
    Cc = Cm.reshape(b, c, SSM_CHUNK, G, N)
    lower = jnp.tril(jnp.ones((SSM_CHUNK, SSM_CHUNK), bool))
    seg = a_cs[..., :, None] - a_cs[..., None, :]
    Lmat = jnp.exp(jnp.where(lower, seg, -jnp.inf))
    CB = jnp.einsum('bclgn,bcsgn->bcgls', Cc, Bc)
    y_diag = jnp.einsum('bcghls,bcsghp->bclghp', CB[:, :, :, None] * Lmat, X)
    decay_s = jnp.exp(a_cs[..., -1:] - a_cs)
    states = jnp.einsum('bclgn,bcghl,bclghp->bcghpn', Bc, decay_s, X)
    chunk_decay = jnp.exp(a_cs[..., -1])

    def step(h, inp):
        st, d = inp
        return h * d[..., None, None] + st, h

    h0 = jnp.zeros((b, G, Hg, P, N), jnp.float32)
    _, prev = lax.scan(step, h0, (jnp.moveaxis(states, 1, 0), jnp.moveaxis(chunk_decay, 1, 0)))
    prev = jnp.moveaxis(prev, 0, 1)
    y_off = jnp.einsum('bclgn,bcghpn,bcghl->bclghp', Cc, prev, jnp.exp(a_cs))
    return (y_diag + y_off).reshape(b, L, H, P)


def even_mixer(h, w_in, pool_w, pool_scale, sink, w_out, cos, sin):
    b, S, _ = h.shape
    proj = h @ w_in
    qw, kw = ATTN_Q_HEADS * HEAD_DIM, ATTN_KV_HEADS * HEAD_DIM
    o = POOL_WIDTH
    u = proj[..., :o]
    q = proj[..., o:o + qw].reshape(b, S, ATTN_Q_HEADS, HEAD_DIM)
    k = proj[..., o + qw:o + qw + kw].reshape(b, S, ATTN_KV_HEADS, HEAD_DIM)
    v = proj[..., o + qw + kw:o + qw + 2 * kw].reshape(b, S, ATTN_KV_HEADS, HEAD_DIM)
    q = apply_rope(q, cos, sin)
    k = apply_rope(k, cos, sin)
    pool_out = multiscale_pool(u, pool_w, pool_scale)
    attn_out = windowed_gqa(q, k, v, sink)
    return jnp.concatenate([pool_out, attn_out], axis=-1) @ w_out


def odd_mixer(h, w_in, dw_w, dw_b, ln_g, ln_b, conv_w, conv_b, A_log, dt_bias, D_skip, norm_w, w_out):
    b, S, _ = h.shape
    proj = h @ w_in
    glu_a = proj[..., :CONV_CH]
    glu_b = proj[..., CONV_CH:2 * CONV_CH]
    o = 2 * CONV_CH
    z = proj[..., o:o + SSM_INNER]
    o += SSM_INNER
    xbc = proj[..., o:o + SSM_XBC]
    o += SSM_XBC
    dt_raw = proj[..., o:o + 2 * SSM_HEADS].reshape(b, S, 2, SSM_HEADS)
    g = glu_a * jax.nn.sigmoid(glu_b)
    g = depthwise_conv(g, dw_w, dw_b, CONV_K // 2, CONV_K // 2)
    g = jax.nn.silu(layer_norm(g, ln_g, ln_b))
    xbc = jax.nn.silu(depthwise_conv(xbc, conv_w, conv_b, SSM_CONV // 2, SSM_CONV - 1 - SSM_CONV // 2))
    gn = SSM_GROUPS * SSM_STATE
    xs = xbc[..., :SSM_INNER].reshape(b, S, SSM_HEADS, SSM_HEAD_DIM).astype(jnp.float32)
    Bm = xbc[..., SSM_INNER:SSM_INNER + gn].reshape(b, S, SSM_GROUPS, SSM_STATE).astype(jnp.float32)
    Cm = xbc[..., SSM_INNER + gn:].reshape(b, S, SSM_GROUPS, SSM_STATE).astype(jnp.float32)
    dt = jax.nn.softplus(dt_raw.astype(jnp.float32) + dt_bias.astype(jnp.float32))
    A = -jnp.exp(A_log.astype(jnp.float32))
    flip = lambda t: jnp.flip(t, axis=1)
    y_f = ssd_scan(xs, dt[:, :, 0], A[0], Bm, Cm)
    y_b = flip(ssd_scan(flip(xs), flip(dt[:, :, 1]), A[1], flip(Bm), flip(Cm)))
    y = (y_f + y_b + xs * D_skip.astype(jnp.float32)[:, None]).reshape(b, S, SSM_INNER)
    y = rms_norm(y * jax.nn.silu(z.astype(jnp.float32)), norm_w).astype(h.dtype)
    return jnp.concatenate([g, y], axis=-1) @ w_out


def setup_inputs(seed: int = 0) -> dict:
    key = jax.random.key(seed)
    ks = jax.random.split(key, 24)
    f32 = jnp.float32

    def nrm(k, shape, fan_in):
        return jax.random.normal(k, shape, f32) * (fan_in ** -0.5)

    def near_one(k, shape, s):
        return 1.0 + s * jax.random.normal(k, shape, f32)

    dt0 = jnp.exp(jax.random.uniform(ks[20], (N_ODD, 2, SSM_HEADS), f32,
                                     minval=math.log(1e-3), maxval=math.log(1e-1)))
    return {
        'x': jax.random.normal(ks[0], (BATCH, SEQ, D_MODEL), f32),
        'norm_w': near_one(ks[1], (DEPTH, 6, D_MODEL), 0.05),
        'ffn_w_gate': nrm(ks[2], (DEPTH, 2, D_MODEL, D_FF), D_MODEL),
        'ffn_w_up': nrm(ks[3], (DEPTH, 2, D_MODEL, D_FF), D_MODEL),
        'ffn_w_down': nrm(ks[4], (DEPTH, 2, D_FF, D_MODEL), D_FF),
        'ev_w_in': nrm(ks[5], (N_EVEN, D_MODEL, EVEN_IN), D_MODEL),
        'ev_pool_w': nrm(ks[6], (N_EVEN, POOL_GROUPS, POOL_GC, POOL_GC), POOL_GC),
        'ev_pool_scale': near_one(ks[7], (N_EVEN, POOL_WIDTH), 0.1),
        'ev_sink': 0.5 * jax.random.normal(ks[8], (N_EVEN, ATTN_Q_HEADS), f32),
        'ev_w_out': nrm(ks[9], (N_EVEN, EVEN_MIX, D_MODEL), EVEN_MIX),
        'od_w_in': nrm(ks[10], (N_ODD, D_MODEL, ODD_IN), D_MODEL),
        'cv_dw_w': nrm(ks[11], (N_ODD, CONV_K, CONV_CH), CONV_K),
        'cv_dw_b': 0.02 * jax.random.normal(ks[12], (N_ODD, CONV_CH), f32),
        'cv_ln_g': near_one(ks[13], (N_ODD, CONV_CH), 0.05),
        'cv_ln_b': 0.02 * jax.random.normal(ks[14], (N_ODD, CONV_CH), f32),
        'ssm_conv_w': nrm(ks[15], (N_ODD, SSM_CONV, SSM_XBC), SSM_CONV),
        'ssm_conv_b': 0.02 * jax.random.normal(ks[16], (N_ODD, SSM_XBC), f32),
        'ssm_A_log': jnp.log(jax.random.uniform(ks[17], (N_ODD, 2, SSM_HEADS), f32, minval=1.0, maxval=16.0)),
        'ssm_dt_bias': dt0 + jnp.log(-jnp.expm1(-dt0)),
        'ssm_D': near_one(ks[18], (N_ODD, SSM_HEADS), 0.1),
        'ssm_norm_w': near_one(ks[19], (N_ODD, SSM_INNER), 0.05),
        'od_w_out': nrm(ks[21], (N_ODD, ODD_MIX, D_MODEL), ODD_MIX),
    }


def reference(x, norm_w, ffn_w_gate, ffn_w_up, ffn_w_down,
              ev_w_in, ev_pool_w, ev_pool_scale, ev_sink, ev_w_out,
              od_w_in, cv_dw_w, cv_dw_b, cv_ln_g, cv_ln_b,
              ssm_conv_w, ssm_conv_b, ssm_A_log, ssm_dt_bias, ssm_D, ssm_norm_w, od_w_out):
    cos, sin = rope_tables(x.shape[1])
    for l in range(DEPTH):
        nw = norm_w[l]
        x = x + 0.5 * rms_norm(swiglu(rms_norm(x, nw[0]), ffn_w_gate[l, 0], ffn_w_up[l, 0], ffn_w_down[l, 0]), nw[1])
        hn = rms_norm(x, nw[2])
        i = l // 2
        if l % 2 == 0:
            m = even_mixer(hn, ev_w_in[i], ev_pool_w[i], ev_pool_scale[i], ev_sink[i], ev_w_out[i], cos, sin)
        else:
            m = odd_mixer(hn, od_w_in[i], cv_dw_w[i], cv_dw_b[i], cv_ln_g[i], cv_ln_b[i],
                          ssm_conv_w[i], ssm_conv_b[i], ssm_A_log[i], ssm_dt_bias[i], ssm_D[i],
                          ssm_norm_w[i], od_w_out[i])
        x = x + rms_norm(m, nw[3])
        x = x + 0.5 * rms_norm(swiglu(rms_norm(x, nw[4]), ffn_w_gate[l, 1], ffn_w_up[l, 1], ffn_w_down[l, 1]), nw[5])
    return x
```

```python
import numpy as np
import concourse.bass as bass
import concourse.mybir as mybir
from concourse.bass_utils import run_bass_kernel_spmd

F32 = mybir.dt.float32
BF16 = mybir.dt.bfloat16
ALU = mybir.AluOpType
AF = mybir.ActivationFunctionType

D = 1024
S = 4096
DEPTH = 4
DFF = 2816
NCORES = 8
EPS = 1e-6
TG = 1024
NG = S // TG
NB = TG // 512


class Op:
    __slots__ = ("eng", "fn", "dma", "deps", "idx", "sem", "semval", "sig", "presem")


class Sched:
    ENG = ("pe", "act", "dve", "pool", "sp")

    def __init__(self, nc):
        self.nc = nc
        self.ops = []
        self.last_w = {}
        self.readers = {}
        self.dma_since = []
        self.last_on = {}

    def barrier(self):
        frontier = set(self.dma_since)
        for e, idx in self.last_on.items():
            frontier.add(idx)
        self.dma_since = []
        for e in self.ENG:
            self.add(e, lambda eng: eng.nop(), extra=frontier)

    def add(self, eng, fn, reads=(), writes=(), dma=False, extra=()):
        op = Op()
        op.eng = eng
        op.fn = fn
        op.dma = dma
        op.idx = len(self.ops)
        op.sig = False
        op.sem = None
        op.semval = 0
        op.presem = None
        deps = {}
        for k in reads:
            w = self.last_w.get(k)
            if w is not None:
                deps[w] = True
            if isinstance(k, tuple) and k[0] == "ps":
                for r in self.readers.get(k, ()):
                    if self.ops[r].eng != eng:
                        deps[r] = True
        for k in writes:
            w = self.last_w.get(k)
            if w is not None:
                deps.setdefault(w, False)
            for r in self.readers.get(k, ()):
                deps.setdefault(r, False)
        for k in reads:
            self.readers.setdefault(k, []).append(op.idx)
        for k in writes:
            self.last_w[k] = op.idx
            self.readers[k] = []
        for x in extra:
            deps[x] = True
        deps.pop(op.idx, None)
        op.deps = deps
        self.ops.append(op)
        if dma:
            self.dma_since.append(op.idx)
        else:
            self.last_on[eng] = op.idx
        return op

    def _needs_sync(self, c, p, raw):
        if p.dma:
            return True
        if c.eng != p.eng:
            return True
        if c.dma:
            return True
        if c.eng == "pe":
            return False
        return raw

    def emit(self, stack):
        nc = self.nc
        ops = self.ops
        for c in ops:
            for pi, raw in c.deps.items():
                p = ops[pi]
                if not p.dma and self._needs_sync(c, p, raw):
                    p.sig = True
        NDMA = 40
        dma_sems = [stack.enter_context(nc.semaphore("dq%d" % i)) for i in range(NDMA)]
        dma_cum = [0] * NDMA
        dma_rr = 0
        EPOCH = 20000
        eng_sems = {e: [] for e in self.ENG}
        eng_cnt = {e: 0 for e in self.ENG}
        for op in ops:
            if op.dma:
                i = dma_rr % NDMA
                dma_rr += 1
                op.sem = dma_sems[i]
                op.presem = (dma_sems[i], dma_cum[i]) if dma_cum[i] > 0 else None
                dma_cum[i] += 16
                op.semval = dma_cum[i]
            elif op.sig:
                e = op.eng
                if eng_cnt[e] % EPOCH == 0:
                    eng_sems[e].append(stack.enter_context(
                        nc.semaphore("e_%s_%d" % (e, len(eng_sems[e])))))
                eng_cnt[e] += 1
                op.sem = eng_sems[e][-1]
                op.semval = (eng_cnt[e] - 1) % EPOCH + 1
        per = {e: [] for e in self.ENG}
        for op in ops:
            per[op.eng].append(op)

        def run(engname, eng):
            waited = {}
            for op in per[engname]:
                ws = []
                for pi, raw in op.deps.items():
                    p = ops[pi]
                    if self._needs_sync(op, p, raw):
                        ws.append((p.sem, p.semval))
                if op.presem is not None:
                    ws.append(op.presem)
                for sem, val in ws:
                    key = id(sem)
                    if waited.get(key, 0) >= val:
                        continue
                    waited[key] = val
                    eng.wait_ge(sem, val)
                ins = op.fn(eng)
                if op.dma:
                    ins.then_inc(op.sem, 16)
                elif op.sig:
                    ins.then_inc(op.sem, 1)

        with nc.Block() as block:
            @block.tensor
            def _(e):
                run("pe", e)

            @block.scalar
            def _(e):
                run("act", e)

            @block.vector
            def _(e):
                run("dve", e)

            @block.gpsimd
            def _(e):
                run("pool", e)

            @block.sync
            def _(e):
                run("sp", e)


class Builder:
    def __init__(self, nc, stop=None):
        self.nc = nc
        self.sc = Sched(nc)
        self.stop = stop
        self.uid = 0
        nc_ = nc
        A = nc_.alloc_sbuf_tensor
        self.bX = A("bX", [128, 8 * TG], F32)
        self.bXN = A("bXN", [128, 8 * TG], BF16)
        self.bSQ = A("bSQ", [128, 8 * TG], BF16)
        self.bH = A("bH", [128, 22 * TG], BF16)
        self.NWS = 4
        self.wsF = [A("wsF%d" % i, [128, 2048], F32) for i in range(self.NWS)]
        self.wsB = [A("wsB%d" % i, [128, 2048], BF16) for i in range(self.NWS)]
        self.ws_i = 0
        self.bR = A("bR", [128, 2 * TG], F32)
        self.bSG = [A("bSG%d" % i, [128, TG], F32) for i in range(2)]
        self.sg_i = 0
        self.bT = [A("bT%d" % i, [128, TG], F32) for i in range(2)]
        self.t_i = 0
        self.ones_bf = A("ones_bf", [128, 128], BF16)
        self.ident_f = A("ident_f", [128, 128], F32)
        self.normw = A("normw_sb", [128, DEPTH * 6 * 8], F32)
        self.poolW = A("poolW", [128, 2, 128], BF16)
        self.poolS = A("poolS", [128, 2], F32)
        self.esink = A("esink", [128, 12], F32)
        self.amask = A("amask_sb", [128, 2, 384], F32)
        self.stg_i = 0
        self.ones_f = A("ones_f", [128, 128], F32)
        self.ident_b = A("ident_b", [128, 128], BF16)
        self.dww = A("dww", [128, 4, 31], F32)
        self.cvvec = A("cvvec", [128, 3, 4], F32)
        self.scw = A("scw", [128, 16, 5], F32)
        self.dexp = A("dexp", [128, 8], F32)
        self.snw = A("snw", [128, 8], F32)
        self.abcol = A("abcol", [48, 2], F32)
        self.abrow = A("abrow", [128, 2, 32], F32)
        self.tri = A("tri_sb", [128, 2, 128], F32)
        self.selm = A("selm_sb", [48, 4, 4, 128], F32)
        self.mbias = A("mbias_sb", [128, 2, 512], BF16)
        self.ps = nc_.alloc_psum_tensor("ps", [128, 8, 512], F32)
        self.ps_i = 0
        self.NPG = 8 // NB

    def op(self, eng, fn, reads=(), writes=(), dma=False):
        return self.sc.add(eng, fn, reads, writes, dma)

    def psum_group(self):
        g = self.ps_i % self.NPG
        self.ps_i += 1
        key = ("ps", g)
        view = self.ps[:, g * NB:(g + 1) * NB, :]
        return key, view

    def dma(self, out, in_, reads, writes, eng="sp"):
        return self.op(eng, lambda e, o=out, i=in_: e.dma_start(out=o, in_=i), reads, writes, dma=True)

    def load_w(self, W, r0, nk, c0, ncols):
        i = self.ws_i % self.NWS
        self.ws_i += 1
        f = self.wsF[i]
        b = self.wsB[i]
        fv = f[:, 0:nk * ncols].rearrange("p (k m) -> p k m", k=nk)
        bv = b[:, 0:nk * ncols].rearrange("p (k m) -> p k m", k=nk)
        src = W[r0:r0 + nk * 128, c0:c0 + ncols].rearrange("(k p) m -> p k m", p=128)
        self.dma(fv, src, reads=[], writes=[("wsF", i)])
        self.op("pool", lambda e, o=bv, s=fv: e.tensor_copy(out=o, in_=s),
                reads=[("wsF", i)], writes=[("wsB", i)])
        return ("wsB", i), bv


    def init_consts(self, aps):
        nc = self.nc
        self.op("pool", lambda e: e.memset(self.ones_bf[:], 1.0), [], ["ones_bf"])
        self.op("pool", lambda e: e.memset(self.ident_f[:], 0.0), [], ["ident_f"])
        self.op("pool", lambda e: e.affine_select(
            out=self.ident_f[:], in_=self.ident_f[:], pattern=[[-1, 128]],
            compare_op=ALU.not_equal, fill=1.0, base=0, channel_multiplier=1),
            ["ident_f"], ["ident_f"])
        self.dma(self.normw[:], aps["normw"], [], ["normw"])
        self.dma(self.amask[:], aps["amask"], [], ["amask"])
        self.op("pool", lambda e: e.memset(self.ones_f[:], 1.0), [], ["ones_f"])
        self.op("pool", lambda e: e.tensor_copy(out=self.ident_b[:], in_=self.ident_f[:]), ["ident_f"], ["ident_b"])
        self.dma(self.tri[:], aps["tri"], [], ["tri"])
        self.dma(self.selm[:], aps["selm"], [], ["selm"])
        st_ = self.bX[:, 0:1024].rearrange("p (d t) -> p d t", d=2)
        self.dma(st_, aps["mbias"], [], [("bX", 0)])
        self.op("pool", lambda e: e.tensor_copy(out=self.mbias[:], in_=st_), [("bX", 0)], ["mbias"])

    def flat(self, pv):
        return pv.rearrange("p b t -> p (b t)")

    def stage_load(self, x, X):
        hf = self.bH[:].bitcast(F32)
        xv = self.bX[:].rearrange("p (c t) -> p c t", c=8)
        for g in range(NG):
            tv = hf[:, 0:8 * D].rearrange("p (tb f) -> p tb f", tb=8)
            self.dma(tv, x[g * TG:(g + 1) * TG, :].rearrange("(tb p) f -> p tb f", p=128),
                     [], ["bH"])
            n = 0
            for tb in range(8):
                for c0 in range(0, 8, 4):
                    key, pv = self.psum_group()
                    pf = self.flat(pv)
                    for cc in range(4):
                        c = c0 + cc
                        self.op("pe", lambda e, o=pf[:, cc * 128:(cc + 1) * 128], i=tv[:, tb, c * 128:(c + 1) * 128]:
                                e.transpose(out=o, in_=i, identity=self.ident_f[:]),
                                ["bH", "ident_f"], [key])
                    src = pf[:, 0:512].rearrange("p (c t) -> p c t", c=4)
                    dst = xv[:, c0:c0 + 4, tb * 128:(tb + 1) * 128]
                    if n % 2 == 0:
                        self.op("act", lambda e, o=dst, i=src: e.copy(out=o, in_=i), [key],
                                [("bX", c0 + k) for k in range(4)])
                    else:
                        self.op("dve", lambda e, o=dst, i=src: e.tensor_copy(out=o, in_=i), [key],
                                [("bX", c0 + k) for k in range(4)])
                    n += 1
            self.dma(X[:, :, g * TG:(g + 1) * TG].rearrange("c p t -> p c t"), xv,
                     [("bX", c) for c in range(8)], [("X", g, c) for c in range(8)])

    def stage_store(self, X, out):
        hf = self.bH[:].bitcast(F32)
        xv = self.bX[:].rearrange("p (c t) -> p c t", c=8)
        for g in range(NG):
            tv = hf[:, 0:8 * D].rearrange("p (tb f) -> p tb f", tb=8)
            self.dma(xv, X[:, :, g * TG:(g + 1) * TG].rearrange("c p t -> p c t"),
                     [("X", g, c) for c in range(8)], [("bX", c) for c in range(8)])
            n = 0
            for tb in range(8):
                for c0 in range(0, 8, 4):
                    key, pv = self.psum_group()
                    pf = self.flat(pv)
                    for cc in range(4):
                        c = c0 + cc
                        self.op("pe", lambda e, o=pf[:, cc * 128:(cc + 1) * 128], i=xv[:, c, tb * 128:(tb + 1) * 128]:
                                e.transpose(out=o, in_=i, identity=self.ident_f[:]),
                                [("bX", c), "ident_f"], [key])
                    dst = tv[:, tb, c0 * 128:(c0 + 4) * 128]
                    src = pf[:, 0:512]
                    if n % 2 == 0:
                        self.op("act", lambda e, o=dst, i=src: e.copy(out=o, in_=i), [key], ["bH"])
                    else:
                        self.op("dve", lambda e, o=dst, i=src: e.tensor_copy(out=o, in_=i), [key], ["bH"])
                    n += 1
            self.dma(out[g * TG:(g + 1) * TG, :].rearrange("(tb p) f -> p tb f", p=128), tv,
                     ["bH"], [("out", g)])

    def rms_rinv(self, sqv, nchunks, off, scale, bias, sqkeys):
        key, pv = self.psum_group()
        for b in range(NB):
            for c in range(nchunks):
                self.op("pe", lambda e, o=pv[:, b, :], r=sqv[:, c, b * 512:(b + 1) * 512], st=(c == 0), sp=(c == nchunks - 1):
                        e.matmul(o, lhsT=self.ones_bf[:], rhs=r, start=st, stop=sp),
                        ["ones_bf", sqkeys[c]], [key])
        rv = self.bR[:, off:off + TG]
        rk = ("bR", off)
        self.op("act", lambda e, o=rv, i=self.flat(pv): e.activation(out=o, in_=i, func=AF.Sqrt, scale=scale, bias=bias),
                [key], [rk])
        self.op("dve", lambda e, o=rv: e.reciprocal(out=o, in_=o), [rk], [rk])
        return rk, rv

    def views(self):
        xv = self.bX[:].rearrange("p (c t) -> p c t", c=8)
        xn = self.bXN[:].rearrange("p (c t) -> p c t", c=8)
        sq = self.bSQ[:].rearrange("p (c t) -> p c t", c=8)
        kX = [("bX", c) for c in range(8)]
        kXN = [("bXN", c) for c in range(8)]
        kSQ = [("bSQ", c) for c in range(8)]
        return xv, xn, sq, kX, kXN, kSQ

    def load_norm(self, X, g, npre):
        xv, xn, sq, kX, kXN, kSQ = self.views()
        t0 = g * TG
        self.dma(xv, X[:, :, t0:t0 + TG].rearrange("c p t -> p c t"),
                 [("X", g, c) for c in range(8)], kX)
        for c in range(8):
            self.op("act", lambda e, o=sq[:, c, :], i=xv[:, c, :]: e.activation(out=o, in_=i, func=AF.Square),
                    [kX[c]], [kSQ[c]])
        rk, rv = self.rms_rinv(sq, 8, 0, 1.0 / D, EPS, kSQ)
        for c in range(8):
            self.op("dve", lambda e, o=xn[:, c, :], i=xv[:, c, :], s=self.normw[:, npre + c:npre + c + 1], r=rv:
                    e.scalar_tensor_tensor(out=o, in0=i, scalar=s, in1=r, op0=ALU.mult, op1=ALU.mult),
                    [kX[c], "normw", rk], [kXN[c]])
        return xn, kXN

    def proj_post_residual(self, X, g, rhs, rkeys, KC, W, npost, half):
        xv, xn, sq, kX, kXN, kSQ = self.views()
        nkt = (KC + 7) // 8
        for ct in range(4):
            grp = [self.psum_group(), self.psum_group()]
            for kt in range(nkt):
                nk = min(8, KC - kt * 8)
                kw, w = self.load_w(W, kt * 1024, nk, ct * 256, 256)
                for mm in range(2):
                    kp, pv = grp[mm]
                    for kk in range(nk):
                        kc = kt * 8 + kk
                        for b in range(NB):
                            self.op("pe", lambda e, o=pv[:, b, :], w_=w[:, kk, mm * 128:(mm + 1) * 128], r=rhs[:, kc, b * 512:(b + 1) * 512], st=(kc == 0), sp=(kc == KC - 1):
                                    e.matmul(o, lhsT=w_, rhs=r, start=st, stop=sp), [kw, rkeys[kc]], [kp])
            for mm in range(2):
                oc = ct * 2 + mm
                kp, pv = grp[mm]
                self.op("dve", lambda e, o=xv[:, oc, :], i=self.flat(pv): e.tensor_copy(out=o, in_=i),
                        [kp], [kX[oc]])
                self.op("act", lambda e, o=sq[:, oc, :], i=xv[:, oc, :]: e.activation(out=o, in_=i, func=AF.Square),
                        [kX[oc]], [kSQ[oc]])
        if half:
            rk2, rv2 = self.rms_rinv(sq, 8, TG, 4.0 / D, 4.0 * EPS, kSQ)
        else:
            rk2, rv2 = self.rms_rinv(sq, 8, TG, 1.0 / D, EPS, kSQ)
        self.residual_add(X, g, xv, kX, rk2, rv2, npost)

    def stage_ffn(self, X, Wg, Wu, Wd, npre, npost):
        h = self.bH[:].rearrange("p (c t) -> p c t", c=22)
        for g in range(NG):
            xn, kXN = self.load_norm(X, g, npre)
            for ct in range(11):
                kg, wg = self.load_w(Wg, 0, 8, ct * 256, 256)
                ku, wu = self.load_w(Wu, 0, 8, ct * 256, 256)
                for mm in range(2):
                    m = ct * 2 + mm
                    kpg, pg = self.psum_group()
                    for k in range(8):
                        for b in range(NB):
                            self.op("pe", lambda e, o=pg[:, b, :], w=wg[:, k, mm * 128:(mm + 1) * 128], r=xn[:, k, b * 512:(b + 1) * 512], st=(k == 0), sp=(k == 7):
                                    e.matmul(o, lhsT=w, rhs=r, start=st, stop=sp), [kg, kXN[k]], [kpg])
                    kpu, pu = self.psum_group()
                    for k in range(8):
                        for b in range(NB):
                            self.op("pe", lambda e, o=pu[:, b, :], w=wu[:, k, mm * 128:(mm + 1) * 128], r=xn[:, k, b * 512:(b + 1) * 512], st=(k == 0), sp=(k == 7):
                                    e.matmul(o, lhsT=w, rhs=r, start=st, stop=sp), [ku, kXN[k]], [kpu])
                    si = self.sg_i % 2
                    self.sg_i += 1
                    sg = self.bSG[si]
                    self.op("act", lambda e, o=sg[:], i=self.flat(pg): e.activation(out=o, in_=i, func=AF.Silu),
                            [kpg], [("bSG", si)])
                    self.op("dve", lambda e, o=h[:, m, :], a=sg[:], b_=self.flat(pu): e.tensor_tensor(out=o, in0=a, in1=b_, op=ALU.mult),
                            [("bSG", si), kpu], [("bH", m)])
            self.proj_post_residual(X, g, h, [("bH", m) for m in range(22)], 22, Wd, npost, True)

    def residual_add(self, X, g, yv, kY, rk, rv, npost):
        t0 = g * TG
        for oc in range(8):
            ti = self.t_i % 2
            self.t_i += 1
            xr = self.bT[ti]
            kt_ = ("bT", ti)
            self.dma(xr[:], X[oc, :, t0:t0 + TG], [("X", g, oc)], [kt_])
            self.op("dve", lambda e, o=yv[:, oc, :], s=self.normw[:, npost + oc:npost + oc + 1], r=rv:
                    e.scalar_tensor_tensor(out=o, in0=o, scalar=s, in1=r, op0=ALU.mult, op1=ALU.mult),
                    [kY[oc], "normw", rk], [kY[oc]])
            self.op("pool", lambda e, o=xr[:], b_=yv[:, oc, :]: e.tensor_tensor(out=o, in0=o, in1=b_, op=ALU.add),
                    [kt_, kY[oc]], [kt_])
            self.dma(X[oc, :, t0:t0 + TG], xr[:], [kt_], [("X", g, oc)])


    def even_consts(self, aps, li):
        stg = self.wsF[0][:, 0:256].rearrange("p (c m) -> p c m", c=2)
        stg = self.bT[0][:, 0:256].rearrange("p (c m) -> p c m", c=2)
        kt_ = ("bT", 0)
        self.op("pool", lambda e, o=stg: e.memset(o, 0.0), [], [kt_])
        for c in range(2):
            for gg in range(2):
                self.dma(stg[gg * 64:(gg + 1) * 64, c, gg * 64:(gg + 1) * 64], aps["ev_pool_w"][li, 2 * c + gg],
                         [kt_], [kt_])
        self.op("pool", lambda e, o=self.poolW[:], i=stg: e.tensor_copy(out=o, in_=i), [kt_], ["poolW"])
        self.dma(self.poolS[:], aps["ev_pool_scale"][li], ["poolS"], ["poolS"])
        self.dma(self.esink[:], aps["ev_sink"][li].partition_broadcast(128), ["esink"], ["esink"])
        self.op("act", lambda e, o=self.esink[:]: e.activation(out=o, in_=o, func=AF.Exp), ["esink"], ["esink"])

    def stage_even_in(self, X, aps, li, npre, QT, KT, V, U):
        xv, xn_, sq, kX, kXN_, kSQ = self.views()
        Win = aps["ev_win"][li]
        Wv = aps["ev_wv"][li]
        sqb = sq
        vst = self.bH[:, 0:8 * 256].rearrange("p (tb f) -> p tb f", tb=8)
        ust = self.bR[:].rearrange("p (c t) -> p c t", c=2)
        for g in range(NG):
            t0 = g * TG
            xn, kXN = self.load_norm(X, g, npre)
            cosT = self.bSG[0]
            sinT = self.bSG[1]
            self.dma(cosT[:], aps["ropec"][:, t0:t0 + TG], [], [("bSG", 0)])
            self.dma(sinT[:], aps["ropes"][:, t0:t0 + TG], [], [("bSG", 1)])
            for i in range(8):
                kw, w = self.load_w(Win, 0, 8, i * 256, 256)
                kq, pq = self.psum_group()
                for k in range(8):
                    for b in range(NB):
                        self.op("pe", lambda e, o=pq[:, b, :], w_=w[:, k, 0:128], r=xn[:, k, b * 512:(b + 1) * 512], st=(k == 0), sp=(k == 7):
                                e.matmul(o, lhsT=w_, rhs=r, start=st, stop=sp), [kw, kXN[k]], [kq])
                kp, pp = self.psum_group()
                for k in range(8):
                    for b in range(NB):
                        self.op("pe", lambda e, o=pp[:, b, :], w_=w[:, k, 128:256], r=xn[:, k, b * 512:(b + 1) * 512], st=(k == 0), sp=(k == 7):
                                e.matmul(o, lhsT=w_, rhs=r, start=st, stop=sp), [kw, kXN[k]], [kp])
                t1 = self.bT[0]
                t2 = self.bT[1]
                self.op("dve", lambda e, o=t1[:], a=self.flat(pq), c_=cosT[:]: e.tensor_tensor(out=o, in0=a, in1=c_, op=ALU.mult),
                        [kq, ("bSG", 0)], [("bT", 0)])
                self.op("dve", lambda e, o=t2[:], a=self.flat(pp), c_=sinT[:]: e.tensor_tensor(out=o, in0=a, in1=c_, op=ALU.mult),
                        [kp, ("bSG", 1)], [("bT", 1)])
                self.op("pool", lambda e, o=sqb[:, i, :], a=t1[:], b_=t2[:]: e.tensor_tensor(out=o, in0=a, in1=b_, op=ALU.add),
                        [("bT", 0), ("bT", 1)], [kSQ[i]])
                if i < 6:
                    self.dma(QT[i * 128:(i + 1) * 128, t0:t0 + TG], sqb[:, i, :], [kSQ[i]], [("QT", g)])
                else:
                    self.dma(KT[(i - 6) * 128:(i - 5) * 128, t0:t0 + TG], sqb[:, i, :], [kSQ[i]], [("KT", g)])
            kw, w = self.load_w(Win, 0, 8, 2048, 256)
            for mm in range(2):
                kp, pv = self.psum_group()
                for k in range(8):
                    for b in range(NB):
                        self.op("pe", lambda e, o=pv[:, b, :], w_=w[:, k, mm * 128:(mm + 1) * 128], r=xn[:, k, b * 512:(b + 1) * 512], st=(k == 0), sp=(k == 7):
                                e.matmul(o, lhsT=w_, rhs=r, start=st, stop=sp), [kw, kXN[k]], [kp])
                self.op("act", lambda e, o=ust[:, mm, :], i_=self.flat(pv): e.copy(out=o, in_=i_), [kp], [("bR", mm * TG)])
                self.dma(U[mm, :, t0:t0 + TG], ust[:, mm, :], [("bR", mm * TG)], [("U", g)])
            kw, w = self.load_w(Wv, 0, 8, 0, 256)
            for tb in range(8):
                kp, pv = self.psum_group()
                pf = self.flat(pv)
                for k in range(8):
                    self.op("pe", lambda e, o=pf[:, 0:256], l=xn[:, k, tb * 128:(tb + 1) * 128], r=w[:, k, :], st=(k == 0), sp=(k == 7):
                            e.matmul(o, lhsT=l, rhs=r, start=st, stop=sp), [kw, kXN[k]], [kp])
                if tb % 2 == 0:
                    self.op("act", lambda e, o=vst[:, tb, :], i_=pf[:, 0:256]: e.copy(out=o, in_=i_), [kp], [("bH", "v")])
                else:
                    self.op("dve", lambda e, o=vst[:, tb, :], i_=pf[:, 0:256]: e.tensor_copy(out=o, in_=i_), [kp], [("bH", "v")])
            self.dma(V[t0:t0 + TG, :].rearrange("(tb p) f -> p tb f", p=128), vst, [("bH", "v")], [("V", g)])

    def stage_even_pool(self, aps, U, AO):
        L = TG + 16
        bx = self.bX
        up = bx[:, 0:2 * L].rearrange("p (c t) -> p c t", c=2)
        a1 = bx[:, 2 * L:4 * L].rearrange("p (c t) -> p c t", c=2)
        a2 = bx[:, 4 * L:6 * L].rearrange("p (c t) -> p c t", c=2)
        xnf = self.bXN[:].bitcast(F32)
        a3 = xnf[:, 0:L]
        a4 = xnf[:, L:2 * L]
        icnt = self.bR[:].rearrange("p (c t) -> p c t", c=2)
        res = self.bSQ[:].bitcast(F32)[:, 0:2 * TG].rearrange("p (c t) -> p c t", c=2)
        pb = self.bH[:, 0:2 * TG].rearrange("p (c t) -> p c t", c=2)
        ob = self.bH[:, 2 * TG:4 * TG].rearrange("p (c t) -> p c t", c=2)
        for g in range(NG):
            t0 = g * TG
            lo = max(t0 - 8, 0)
            hi = min(t0 + TG + 8, S)
            j0 = lo - (t0 - 8)
            self.op("pool", lambda e, o=up: e.memset(o, 0.0), [], ["pl_u"])
            self.dma(up[:, :, j0:j0 + hi - lo], U[:, :, lo:hi].rearrange("c p t -> p c t"),
                     ["pl_u"] + [("U", gg) for gg in (g - 1, g, g + 1) if 0 <= gg < NG], ["pl_u"])
            self.dma(icnt, aps["icnt"][:, :, t0:t0 + TG].rearrange("c p t -> p c t"), [], [("bR", 0), ("bR", TG)])
            self.op("dve", lambda e: e.tensor_tensor(out=a1[:, :, 1:L], in0=up[:, :, 0:L - 1], in1=up[:, :, 1:L], op=ALU.add),
                    ["pl_u"], ["pl_a1"])
            self.op("dve", lambda e: e.tensor_tensor(out=a2[:, :, 2:L - 1], in0=a1[:, :, 1:L - 2], in1=a1[:, :, 3:L], op=ALU.add),
                    ["pl_a1"], ["pl_a2"])
            self.op("dve", lambda e: e.tensor_tensor(out=a3[:, 4:L - 3], in0=a2[:, 1, 2:L - 5], in1=a2[:, 1, 6:L - 1], op=ALU.add),
                    ["pl_a2"], ["pl_a3"])
            self.op("dve", lambda e: e.tensor_tensor(out=a4[:, 8:L - 8], in0=a3[:, 4:L - 12], in1=a3[:, 12:L - 4], op=ALU.add),
                    ["pl_a3"], ["pl_a4"])
            self.op("dve", lambda e: e.tensor_tensor(out=res[0:64, 0, :], in0=a1[0:64, 0, 8:TG + 8], in1=icnt[0:64, 0, :], op=ALU.mult),
                    ["pl_a1", ("bR", 0)], ["pl_res"])
            self.op("dve", lambda e: e.tensor_tensor(out=res[64:128, 0, :], in0=a2[64:128, 0, 8:TG + 8], in1=icnt[64:128, 0, :], op=ALU.mult),
                    ["pl_a2", ("bR", 0)], ["pl_res"])
            self.op("dve", lambda e: e.tensor_tensor(out=res[0:64, 1, :], in0=a3[0:64, 8:TG + 8], in1=icnt[0:64, 1, :], op=ALU.mult),
                    ["pl_a3", ("bR", TG)], ["pl_res"])
            self.op("dve", lambda e: e.tensor_tensor(out=res[64:128, 1, :], in0=a4[64:128, 8:TG + 8], in1=icnt[64:128, 1, :], op=ALU.mult),
                    ["pl_a4", ("bR", TG)], ["pl_res"])
            self.op("dve", lambda e: e.tensor_tensor(out=pb, in0=res, in1=up[:, :, 8:TG + 8], op=ALU.subtract),
                    ["pl_res", "pl_u"], ["pl_pb"])
            for c in range(2):
                kp, pv = self.psum_group()
                for b in range(NB):
                    self.op("pe", lambda e, o=pv[:, b, :], l=self.poolW[:, c, :], r=pb[:, c, b * 512:(b + 1) * 512]:
                            e.matmul(o, lhsT=l, rhs=r, start=True, stop=True), ["poolW", "pl_pb"], [kp])
                self.op("dve", lambda e, o=ob[:, c, :], i_=self.flat(pv), s=self.poolS[:, c:c + 1]:
                        e.tensor_scalar(out=o, in0=i_, scalar1=s, scalar2=None, op0=ALU.mult), [kp, "poolS"], [("pl_ob", c)])
                self.dma(AO[c, :, t0:t0 + TG], ob[:, c, :], [("pl_ob", c)], [("AOp", g)])

    def stage_even_attn(self, aps, li, QT, KT, V, AO):
        KW = TG + 256
        q_sb = self.bH[0:64, 0:12 * TG].rearrange("p (h t) -> p h t", h=12)
        k_sb = self.bXN[0:64, 0:4 * KW].rearrange("p (h t) -> p h t", h=4)
        v_sb = self.bSQ[:, 0:10 * 256].rearrange("p (kb f) -> p kb f", kb=10)
        o_sb = self.bX[:].bitcast(BF16)[0:64, 0:12 * TG].rearrange("p (h t) -> p h t", h=12)
        Pb = [self.bSG[0][:].bitcast(BF16)[:, 0:384], self.bSG[1][:].bitcast(BF16)[:, 0:384],
              self.bT[0][:].bitcast(BF16)[:, 0:384], self.bT[1][:].bitcast(BF16)[:, 0:384]]
        Pk = [("bSG", 0), ("bSG", 1), ("bT", 0), ("bT", 1)]
        recs = [self.bR[0:64, 0:384], self.bR[0:64, 512:896]]
        AOf = AO.rearrange("c p t -> (c p) t")
        pi = 0
        ri = 0
        for g in range(NG):
            t0 = g * TG
            lo = max(t0 - 128, 0)
            hi = min(t0 + TG + 128, S)
            j0 = lo - (t0 - 128)
            nbr = [gg for gg in (g - 1, g, g + 1) if 0 <= gg < NG]
            self.dma(q_sb, QT[:, t0:t0 + TG].rearrange("(h d) t -> d h t", d=64), [("QT", g)], ["at_q"])
            self.dma(k_sb[:, :, j0:j0 + hi - lo], KT[:, lo:hi].rearrange("(h d) t -> d h t", d=64),
                     [("KT", gg) for gg in nbr], ["at_k"])
            self.dma(v_sb[:, j0 // 128:(j0 + hi - lo) // 128, :], V[lo:hi, :].rearrange("(kb p) f -> p kb f", p=128),
                     [("V", gg) for gg in nbr], ["at_v"])
            for qi in range(8):
                n = g * 8 + qi
                for gk in range(4):
                    kbs = [kb for kb in (n - 1, n, n + 1) if 0 <= kb < S // 128]
                    Ps = []
                    for kb in kbs:
                        lk = kb - (g * 8 - 1)
                        ks, pv = self.psum_group()
                        pf = self.flat(pv)
                        self.op("pe", lambda e, o=pf[:, 0:384].rearrange("p (h q) -> p h q", h=3), l=k_sb[:, gk, lk * 128:(lk + 1) * 128],
                                r=q_sb[:, 3 * gk:3 * gk + 3, qi * 128:(qi + 1) * 128]:
                                e.matmul(o, lhsT=l, rhs=r, start=True, stop=True), ["at_k", "at_q"], [ks])
                        P = Pb[pi % 4]
                        kP = Pk[pi % 4]
                        pi += 1
                        self.op("act", lambda e, o=P, i_=pf[:, 0:384]: e.activation(out=o, in_=i_, func=AF.Exp, scale=0.125),
                                [ks], [kP])
                        if kb == n - 1:
                            self.op("dve", lambda e, o=P: e.tensor_tensor(out=o, in0=o, in1=self.amask[:, 0, :], op=ALU.mult),
                                    [kP, "amask"], [kP])
                        elif kb == n + 1:
                            self.op("dve", lambda e, o=P: e.tensor_tensor(out=o, in0=o, in1=self.amask[:, 1, :], op=ALU.mult),
                                    [kP, "amask"], [kP])
                        Ps.append((P, kP, lk))
                    kn, pn = self.psum_group()
                    num = pn[0:64, 0, 0:384]
                    den = pn[0:64, 1, 0:384]
                    for idx, (P, kP, lk) in enumerate(Ps):
                        st = (idx == 0)
                        sp = (idx == len(Ps) - 1)
                        self.op("pe", lambda e, o=num, l=v_sb[:, lk, gk * 64:(gk + 1) * 64], r=P, st=st, sp=sp:
                                e.matmul(o, lhsT=l, rhs=r, start=st, stop=sp), ["at_v", kP], [kn])
                        self.op("pe", lambda e, o=den, l=self.ones_bf[:, 0:64], r=P, st=st, sp=sp:
                                e.matmul(o, lhsT=l, rhs=r, start=st, stop=sp), ["ones_bf", kP], [kn])
                    rec = recs[ri % 2]
                    kr = ("rec", ri % 2)
                    ri += 1
                    for j in range(3):
                        hh = 3 * gk + j
                        self.op("dve", lambda e, o=rec[:, j * 128:(j + 1) * 128], i_=den[:, j * 128:(j + 1) * 128],
                                s=self.esink[0:64, hh:hh + 1]:
                                e.tensor_scalar(out=o, in0=i_, scalar1=s, scalar2=None, op0=ALU.add), [kn, "esink"], [kr])
                    self.op("dve", lambda e, o=rec: e.reciprocal(out=o, in_=o), [kr], [kr])
                    self.op("dve", lambda e, o=o_sb[:, 3 * gk:3 * gk + 3, qi * 128:(qi + 1) * 128],
                            a=num.rearrange("p (h q) -> p h q", h=3), b_=rec.rearrange("p (h q) -> p h q", h=3):
                            e.tensor_tensor(out=o, in0=a, in1=b_, op=ALU.mult), [kn, kr], ["at_o"])
            self.dma(AOf[256:1024, t0:t0 + TG].rearrange("(h d) t -> d h t", d=64), o_sb, ["at_o"], [("AOa", g)])

    def stage_even_out(self, X, aps, li, npost, AO):
        xv, xn, sq, kX, kXN, kSQ = self.views()
        for g in range(NG):
            t0 = g * TG
            self.dma(xn, AO[:, :, t0:t0 + TG].rearrange("c p t -> p c t"), [("AOp", g), ("AOa", g)], kXN)
            self.proj_post_residual(X, g, xn, kXN, 8, aps["ev_wout"][li], npost, False)

    def stg4(self):
        i = self.stg_i % 4
        self.stg_i += 1
        t = [self.bT[0], self.bT[1], self.bSG[0], self.bSG[1]][i]
        k = [("bT", 0), ("bT", 1), ("bSG", 0), ("bSG", 1)][i]
        return t, k

    def odd_consts(self, aps, li):
        self.dma(self.dww[:], aps["cv_dww"][li], ["dww"], ["dww"])
        self.dma(self.cvvec[:], aps["cv_vec"][li], ["cvvec"], ["cvvec"])
        self.dma(self.scw[:], aps["ssm_cw"][li], ["scw"], ["scw"])
        self.dma(self.dexp[:], aps["ssm_dexp"][li], ["dexp"], ["dexp"])
        self.dma(self.snw[:], aps["ssm_nw"][li], ["snw"], ["snw"])
        self.dma(self.abcol[:], aps["ssm_ab"][li], ["abcol"], ["abcol"])
        self.op("act", lambda e, o=self.abcol[:, 0:1]: e.activation(out=o, in_=o, func=AF.Exp), ["abcol"], ["abcol"])
        self.op("dve", lambda e, o=self.abcol[:, 0:1]: e.tensor_scalar(out=o, in0=o, scalar1=-1.0, scalar2=None, op0=ALU.mult),
                ["abcol"], ["abcol"])
        self.dma(self.abrow[:, 0, :], aps["ssm_A_log"][li].partition_broadcast(128), ["abrow"], ["abrow"])
        self.dma(self.abrow[:, 1, :], aps["ssm_dt_bias"][li].partition_broadcast(128), ["abrow"], ["abrow"])
        self.op("act", lambda e, o=self.abrow[:, 0, :]: e.activation(out=o, in_=o, func=AF.Exp), ["abrow"], ["abrow"])
        self.op("dve", lambda e, o=self.abrow[:, 0, :]: e.tensor_scalar(out=o, in0=o, scalar1=-1.0, scalar2=None, op0=ALU.mult),
                ["abrow"], ["abrow"])

    def stage_odd_in(self, X, aps, li, npre, G, ZS, XBCr, DTr, DTt):
        Win = aps["od_win"][li]
        for g in range(NG):
            t0 = g * TG
            xn, kXN = self.load_norm(X, g, npre)
            for i in range(16):
                kw, w = self.load_w(Win, 0, 8, i * 256, 256)
                pgs = []
                for mm in range(2):
                    kp, pv = self.psum_group()
                    for k in range(8):
                        for b in range(NB):
                            self.op("pe", lambda e, o=pv[:, b, :], w_=w[:, k, mm * 128:(mm + 1) * 128], r=xn[:, k, b * 512:(b + 1) * 512], st=(k == 0), sp=(k == 7):
                                    e.matmul(o, lhsT=w_, rhs=r, start=st, stop=sp), [kw, kXN[k]], [kp])
                    pgs.append((kp, pv))
                if i < 4:
                    s1, k1 = self.stg4()
                    s2, k2 = self.stg4()
                    self.op("act", lambda e, o=s1[:], i_=self.flat(pgs[1][1]): e.activation(out=o, in_=i_, func=AF.Sigmoid),
                            [pgs[1][0]], [k1])
                    self.op("dve", lambda e, o=s2[:], a=self.flat(pgs[0][1]), b_=s1[:]: e.tensor_tensor(out=o, in0=a, in1=b_, op=ALU.mult),
                            [pgs[0][0], k1], [k2])
                    self.dma(G[i, :, t0:t0 + TG], s2[:], [k2], [("G", g)])
                elif i < 8:
                    for mm in range(2):
                        s1, k1 = self.stg4()
                        self.op("act", lambda e, o=s1[:], i_=self.flat(pgs[mm][1]): e.activation(out=o, in_=i_, func=AF.Silu),
                                [pgs[mm][0]], [k1])
                        self.dma(ZS[(i - 4) * 2 + mm, :, t0:t0 + TG], s1[:], [k1], [("ZS", g)])
                else:
                    for mm in range(2):
                        s1, k1 = self.stg4()
                        if mm == 0:
                            self.op("act", lambda e, o=s1[:], i_=self.flat(pgs[mm][1]): e.copy(out=o, in_=i_), [pgs[mm][0]], [k1])
                        else:
                            self.op("dve", lambda e, o=s1[:], i_=self.flat(pgs[mm][1]): e.tensor_copy(out=o, in_=i_), [pgs[mm][0]], [k1])
                        self.dma(XBCr[(i - 8) * 2 + mm, :, t0:t0 + TG], s1[:], [k1], [("XBCr", g)])
            kw, w = self.load_w(Win, 0, 8, 4096, 32)
            kp, pv = self.psum_group()
            for k in range(8):
                for b in range(NB):
                    self.op("pe", lambda e, o=pv[0:32, b, :], w_=w[:, k, :], r=xn[:, k, b * 512:(b + 1) * 512], st=(k == 0), sp=(k == 7):
                            e.matmul(o, lhsT=w_, rhs=r, start=st, stop=sp), [kw, kXN[k]], [kp])
            s1, k1 = self.stg4()
            self.op("act", lambda e, o=s1[0:32, :], i_=self.flat(pv)[0:32, :]: e.copy(out=o, in_=i_), [kp], [k1])
            self.dma(DTr[:, t0:t0 + TG], s1[0:32, :], [k1], [("DTr", g)])
            kp, pv = self.psum_group()
            pf = self.flat(pv)
            for tb in range(8):
                for k in range(8):
                    self.op("pe", lambda e, o=pf[:, tb * 32:(tb + 1) * 32], l=xn[:, k, tb * 128:(tb + 1) * 128], r=w[:, k, :], st=(k == 0), sp=(k == 7):
                            e.matmul(o, lhsT=l, rhs=r, start=st, stop=sp), [kw, kXN[k]], [kp])
            s1, k1 = self.stg4()
            self.op("dve", lambda e, o=s1[:, 0:256], i_=pf[:, 0:256]: e.tensor_copy(out=o, in_=i_), [kp], [k1])
            self.dma(DTt[:, g * 8:(g + 1) * 8, :],
                     s1[:, 0:256].rearrange("p (tb j) -> p tb j", tb=8), [k1], [("DTt", g)])

    def stage_odd_conv(self, aps, li, G, XBCr, XBC, MO):
        hf = self.bH[:].bitcast(F32)
        acc = hf[:, 0:4 * TG].rearrange("p (c t) -> p c t", c=4)
        sqr = hf[:, 4 * TG:8 * TG].rearrange("p (c t) -> p c t", c=4)
        mo = self.bXN[:, 0:4 * TG].rearrange("p (c t) -> p c t", c=4)
        mean = self.bR[:, 0:TG]
        rstd = self.bR[:, TG:2 * TG]
        LG = TG + 30
        gp = self.bX[:, 0:4 * LG].rearrange("p (c t) -> p c t", c=4)
        for g in range(NG):
            t0 = g * TG
            lo = max(t0 - 15, 0)
            hi = min(t0 + TG + 15, S)
            j0 = lo - (t0 - 15)
            self.op("pool", lambda e, o=gp: e.memset(o, 0.0), [], ["cv_gp"])
            self.dma(gp[:, :, j0:j0 + hi - lo], G[:, :, lo:hi].rearrange("c p t -> p c t"),
                     ["cv_gp"] + [("G", gg) for gg in (g - 1, g, g + 1) if 0 <= gg < NG], ["cv_gp"])
            for c in range(4):
                self.op("dve", lambda e, o=acc[:, c, :], i_=gp[:, c, 0:TG], s1=self.dww[:, c, 0:1], s2=self.cvvec[:, 0, c:c + 1]:
                        e.tensor_scalar(out=o, in0=i_, scalar1=s1, scalar2=s2, op0=ALU.mult, op1=ALU.add),
                        ["cv_gp", "dww", "cvvec"], [("cv_acc", c)])
            for k in range(1, 31):
                for c in range(4):
                    self.op("dve", lambda e, o=acc[:, c, :], i_=gp[:, c, k:k + TG], s=self.dww[:, c, k:k + 1]:
                            e.scalar_tensor_tensor(out=o, in0=i_, scalar=s, in1=o, op0=ALU.mult, op1=ALU.add),
                            ["cv_gp", "dww", ("cv_acc", c)], [("cv_acc", c)])
            for c in range(4):
                self.op("act", lambda e, o=sqr[:, c, :], i_=acc[:, c, :]: e.activation(out=o, in_=i_, func=AF.Square),
                        [("cv_acc", c)], [("cv_sq", c)])
            k1, p1 = self.psum_group()
            k2, p2 = self.psum_group()
            for b in range(NB):
                for c in range(4):
                    self.op("pe", lambda e, o=p1[:, b, :], r=acc[:, c, b * 512:(b + 1) * 512], st=(c == 0), sp=(c == 3):
                            e.matmul(o, lhsT=self.ones_f[:], rhs=r, start=st, stop=sp), ["ones_f", ("cv_acc", c)], [k1])
            for b in range(NB):
                for c in range(4):
                    self.op("pe", lambda e, o=p2[:, b, :], r=sqr[:, c, b * 512:(b + 1) * 512], st=(c == 0), sp=(c == 3):
                            e.matmul(o, lhsT=self.ones_f[:], rhs=r, start=st, stop=sp), ["ones_f", ("cv_sq", c)], [k2])
            self.op("act", lambda e, o=mean, i_=self.flat(p1): e.mul(out=o, in_=i_, mul=1.0 / 512), [k1], [("bR", 0)])
            msq = sqr[:, 0, :]
            self.op("act", lambda e, o=msq, i_=mean: e.activation(out=o, in_=i_, func=AF.Square), [("bR", 0), k2], [("cv_sq", 0)])
            self.op("dve", lambda e, o=rstd, i_=self.flat(p2), m=msq:
                    e.scalar_tensor_tensor(out=o, in0=i_, scalar=1.0 / 512, in1=m, op0=ALU.mult, op1=ALU.subtract),
                    [k2, ("cv_sq", 0)], [("bR", TG)])
            self.op("act", lambda e, o=rstd: e.activation(out=o, in_=o, func=AF.Sqrt, scale=1.0, bias=EPS), [("bR", TG)], [("bR", TG)])
            self.op("dve", lambda e, o=rstd: e.reciprocal(out=o, in_=o), [("bR", TG)], [("bR", TG)])
            for c in range(4):
                self.op("dve", lambda e, o=acc[:, c, :], m=mean: e.tensor_tensor(out=o, in0=o, in1=m, op=ALU.subtract),
                        [("cv_acc", c), ("bR", 0)], [("cv_acc", c)])
                self.op("dve", lambda e, o=acc[:, c, :], s=self.cvvec[:, 1, c:c + 1], r=rstd:
                        e.scalar_tensor_tensor(out=o, in0=o, scalar=s, in1=r, op0=ALU.mult, op1=ALU.mult),
                        [("cv_acc", c), ("bR", TG), "cvvec"], [("cv_acc", c)])
                self.op("act", lambda e, o=mo[:, c, :], i_=acc[:, c, :], bb=self.cvvec[:, 2, c:c + 1]:
                        e.activation(out=o, in_=i_, func=AF.Silu, bias=bb), [("cv_acc", c), "cvvec"], [("bXN", c)])
            self.dma(MO[:, :, t0:t0 + TG].rearrange("c p t -> p c t"), mo, [("bXN", c) for c in range(4)], [("MO", g)])
        LX = TG + 3
        xp = self.bX[:, 0:4 * LX].rearrange("p (c t) -> p c t", c=4)
        for g in range(NG):
            t0 = g * TG
            lo = max(t0 - 2, 0)
            hi = min(t0 + TG + 1, S)
            j0 = lo - (t0 - 2)
            for cb in range(4):
                self.op("pool", lambda e, o=xp: e.memset(o, 0.0), [], ["cv_gp"])
                self.dma(xp[:, :, j0:j0 + hi - lo], XBCr[cb * 4:cb * 4 + 4, :, lo:hi].rearrange("c p t -> p c t"),
                         ["cv_gp"] + [("XBCr", gg) for gg in (g - 1, g, g + 1) if 0 <= gg < NG], ["cv_gp"])
                for cc in range(4):
                    c = cb * 4 + cc
                    self.op("dve", lambda e, o=acc[:, cc, :], i_=xp[:, cc, 0:TG], s1=self.scw[:, c, 0:1], s2=self.scw[:, c, 4:5]:
                            e.tensor_scalar(out=o, in0=i_, scalar1=s1, scalar2=s2, op0=ALU.mult, op1=ALU.add),
                            ["cv_gp", "scw"], [("cv_acc", cc)])
                for k in range(1, 4):
                    for cc in range(4):
                        c = cb * 4 + cc
                        self.op("dve", lambda e, o=acc[:, cc, :], i_=xp[:, cc, k:k + TG], s=self.scw[:, c, k:k + 1]:
                                e.scalar_tensor_tensor(out=o, in0=i_, scalar=s, in1=o, op0=ALU.mult, op1=ALU.add),
                                ["cv_gp", "scw", ("cv_acc", cc)], [("cv_acc", cc)])
                for cc in range(4):
                    self.op("act", lambda e, o=acc[:, cc, :]: e.activation(out=o, in_=o, func=AF.Silu),
                            [("cv_acc", cc)], [("cv_acc", cc)])
                self.dma(XBC[cb * 4:cb * 4 + 4, :, t0:t0 + TG].rearrange("c p t -> p c t"), acc,
                         [("cv_acc", cc) for cc in range(4)], [("XBC", g)])

    def stage_odd_dt(self, aps, li, DTr, DTt):
        xf = self.bXN[:].bitcast(F32)
        dtk = xf[:, 0:1024]
        dtd = xf[:, 1024:2048]
        cdk = xf[:, 2048:3072]
        a_tok = xf[:, 3072:4096]
        sf = self.bSQ[:].bitcast(F32)
        acs_tok = sf[:, 0:1024]
        raw = sf[:, 1024:2048]
        hf = self.bH[:].bitcast(F32)
        AT = hf[0:48, 0:S]
        c1 = hf[0:48, S:2 * S]
        acsT = self.bX[0:48, 0:S]
        nacsT = self.bX[0:48, S:2 * S]
        dm = lambda t: t.rearrange("p (d c h) -> p c d h", d=2, h=16)
        self.dma(raw, DTt.rearrange("p c j -> p (c j)"), [("DTt", g) for g in range(NG)], ["dtraw"])
        for d in range(2):
            self.op("dve", lambda e, d=d: e.tensor_tensor(
                out=dtk[:, d * 512:(d + 1) * 512].rearrange("p (c h) -> p c h", h=16),
                in0=raw.rearrange("p (c j) -> p c j", j=32)[:, :, d * 16:(d + 1) * 16],
                in1=self.abrow[:, 1:2, d * 16:(d + 1) * 16].broadcast_to([128, 32, 16]), op=ALU.add),
                ["dtraw", "abrow"], ["dtk"])
        self.op("act", lambda e: e.activation(out=dtk, in_=dtk, func=AF.Exp), ["dtk"], ["dtk"])
        self.op("act", lambda e: e.activation(out=dtk, in_=dtk, func=AF.Ln, bias=1.0), ["dtk"], ["dtk"])
        for d in range(2):
            self.op("dve", lambda e, d=d: e.tensor_tensor(
                out=a_tok[:, d * 512:(d + 1) * 512].rearrange("p (c h) -> p c h", h=16),
                in0=dtk[:, d * 512:(d + 1) * 512].rearrange("p (c h) -> p c h", h=16),
                in1=self.abrow[:, 0:1, d * 16:(d + 1) * 16].broadcast_to([128, 32, 16]), op=ALU.mult),
                ["dtk", "abrow"], ["a_tok"])
        import os
        dcut = int(os.environ.get("DT_CUT", "99"))
        if dcut <= 1:
            return
        k1, p1 = self.psum_group()
        for d in range(2):
            self.op("pe", lambda e, d=d: e.matmul(p1[:, d, :], lhsT=self.tri[:, d, :], rhs=a_tok[:, d * 512:(d + 1) * 512],
                                                   start=True, stop=True), ["a_tok", "tri"], [k1])
        self.op("dve", lambda e: e.tensor_copy(out=acs_tok, in_=self.flat(p1)), [k1], ["acs_tok"])
        k2, p2 = self.psum_group()
        for d in range(2):
            self.op("pe", lambda e, d=d: e.matmul(p2[:, d, :], lhsT=self.ones_f[:], rhs=a_tok[:, d * 512:(d + 1) * 512],
                                                   start=True, stop=True), ["a_tok", "ones_f"], [k2])
        self.op("act", lambda e: e.activation(out=cdk, in_=self.flat(p2), func=AF.Exp), [k2], ["cdk"])
        self.op("dve", lambda e: e.tensor_tensor(out=dtd, in0=self.flat(p2), in1=acs_tok, op=ALU.subtract), [k2, "acs_tok"], ["dtd"])
        self.op("act", lambda e: e.activation(out=dtd, in_=dtd, func=AF.Exp), ["dtd"], ["dtd"])
        self.op("dve", lambda e: e.tensor_tensor(out=dtd, in0=dtd, in1=dtk, op=ALU.mult), ["dtd", "dtk"], ["dtd"])
        if dcut <= 2:
            return
        self.op("pool", lambda e: e.memset(AT, 0.0), [], ["AT"])
        self.dma(AT[0:16, :], DTr[0:16, :], ["AT"] + [("DTr", g) for g in range(NG)], ["AT"])
        self.dma(AT[32:48, :], DTr[16:32, :], ["AT"], ["AT"])
        self.op("act", lambda e: e.activation(out=AT, in_=AT, func=AF.Exp, bias=self.abcol[:, 1:2]), ["AT", "abcol"], ["AT"])
        self.op("act", lambda e: e.activation(out=AT, in_=AT, func=AF.Ln, bias=1.0), ["AT"], ["AT"])
        self.op("dve", lambda e: e.tensor_scalar(out=AT, in0=AT, scalar1=self.abcol[:, 0:1], scalar2=None, op0=ALU.mult),
                ["AT", "abcol"], ["AT"])
        if dcut <= 3:
            return
        c3 = lambda t: t.rearrange("p (c l) -> p c l", l=128)
        srcb, dstb = AT, c1
        names = {id(AT): "AT", id(c1): "c1", id(acsT): "acsT"}
        bufs = [c1, acsT]
        cur, curk = AT, "AT"
        step = 1
        n = 0
        while step < 128:
            dst = bufs[n % 2]
            dk = ["c1", "acsT"][n % 2]
            self.op("dve", lambda e, dst=dst, cur=cur, step=step: e.tensor_copy(out=c3(dst)[:, :, 0:step], in_=c3(cur)[:, :, 0:step]),
                    [curk], [dk])
            self.op("dve", lambda e, dst=dst, cur=cur, step=step: e.tensor_tensor(
                out=c3(dst)[:, :, step:128], in0=c3(cur)[:, :, step:128], in1=c3(cur)[:, :, 0:128 - step], op=ALU.add),
                [curk], [dk])
            cur, curk = dst, dk
            step *= 2
            n += 1
        cs = cur
        csk = curk
        self.op("dve", lambda e: e.tensor_copy(out=acsT[0:16, :], in_=cs[0:16, :]), [csk], ["acsT"])
        self.op("dve", lambda e: e.tensor_tensor(out=c3(acsT)[32:48], in0=c3(cs)[32:48, :, 127:128].broadcast_to([16, 32, 128]),
                                                 in1=c3(cs)[32:48], op=ALU.subtract), [csk, "acsT"], ["acsT"])
        self.op("dve", lambda e: e.tensor_tensor(out=acsT[32:48, :], in0=acsT[32:48, :], in1=AT[32:48, :], op=ALU.add),
                ["acsT", "AT"], ["acsT"])
        self.op("dve", lambda e: e.tensor_scalar(out=nacsT[0:16, :], in0=acsT[0:16, :], scalar1=-1.0, scalar2=None, op0=ALU.mult),
                ["acsT"], ["nacsT"])
        self.op("dve", lambda e: e.tensor_scalar(out=nacsT[32:48, :], in0=acsT[32:48, :], scalar1=-1.0, scalar2=None, op0=ALU.mult),
                ["acsT"], ["nacsT"])

    def stage_odd_ssd(self, aps, XBC, Y, d):
        xf = self.bXN[:].bitcast(F32)
        dtk = xf[:, 0:1024].rearrange("p (d c h) -> p d c h", d=2, h=16)
        dtd = xf[:, 1024:2048].rearrange("p (d c h) -> p d c h", d=2, h=16)
        cdk = xf[:, 2048:3072].rearrange("p (d c h) -> p d c h", d=2, h=16)
        acsT = self.bX[0:48, 0:S]
        nacsT = self.bX[0:48, S:2 * S]
        r0 = 0 if d == 0 else 32
        state = self.bR[:, 0:1024]
        state_bf = self.bR[:, 1024:1536].bitcast(BF16)
        hb = self.bH
        tokb = hb[:, 0:1536].rearrange("p (c f) -> p c f", c=12)
        Xdt = hb[:, 1536:2560].rearrange("p (h q) -> p h q", h=16)
        Xdd = hb[:, 2560:3584].rearrange("p (h q) -> p h q", h=16)
        MT = [hb[:, 3584 + i * 512:3584 + (i + 1) * 512].rearrange("p (h l) -> p h l", h=4) for i in range(2)]
        Cs = [hb[:, 4608 + i * 512:4608 + (i + 1) * 512].rearrange("p (h l) -> p h l", h=4) for i in range(2)]
        hf = hb[:].bitcast(F32)
        Lf = [hf[:, 3072 + i * 512:3072 + (i + 1) * 512] for i in range(2)]
        Ef = [hf[:, 4096 + i * 512:4096 + (i + 1) * 512] for i in range(2)]
        rhs2 = [hf[0:48, 5120 + i * 512:5120 + (i + 1) * 512] for i in range(2)]
        yst = [self.bT[0], self.bT[1], self.bSG[0], self.bSG[1]]
        ysk = [("bT", 0), ("bT", 1), ("bSG", 0), ("bSG", 1)]
        self.op("pool", lambda e: e.memset(self.bR[:, 0:1536], 0.0), [], [("st", g) for g in range(4)] + [("stb", g) for g in range(4)])
        order = range(S // 128) if d == 0 else range(S // 128 - 1, -1, -1)
        it = 0
        for c in order:
            sl = slice(c * 128, (c + 1) * 128)
            fi = it % 2
            it += 1
            fx = self.wsF[fi][:].rearrange("p (c t) -> p c t", c=16)
            fb = self.wsB[fi][:].rearrange("p (c t) -> p c t", c=16)
            self.dma(fx, XBC[:, :, sl].rearrange("c p t -> p c t"), [("XBC", c // (TG // 128))], [("wsF", fi)])
            self.op("pool", lambda e, o=fb, i_=fx: e.tensor_copy(out=o, in_=i_), [("wsF", fi)], [("wsB", fi)])
            kp, pv = self.psum_group()
            pt = self.flat(pv).bitcast(BF16)
            for i in range(12):
                self.op("pe", lambda e, o=pt[:, i * 128:(i + 1) * 128], i_=fb[:, i, :]: e.transpose(out=o, in_=i_, identity=self.ident_b[:]),
                        [("wsB", fi), "ident_b"], [kp])
            self.op("act", lambda e, o=hb[:, 0:1536], i_=pt[:, 0:1536]: e.copy(out=o, in_=i_), [kp], ["tokb"])
            xs3 = hb[:, 0:1024].rearrange("p (h q) -> p h q", h=16)
            self.op("dve", lambda e, c=c: e.tensor_tensor(out=Xdt, in0=xs3, in1=dtk[:, d, c, :].unsqueeze(2).broadcast_to([128, 16, 64]), op=ALU.mult),
                    ["tokb", "dtk"], ["Xdt"])
            self.op("pool", lambda e, c=c: e.tensor_tensor(out=Xdd, in0=xs3, in1=dtd[:, d, c, :].unsqueeze(2).broadcast_to([128, 16, 64]), op=ALU.mult),
                    ["tokb", "dtd"], ["Xdd"])
            for g in range(4):
                bi = (it * 4 + g) % 2
                ka, pa = self.psum_group()
                pfa = self.flat(pa)
                self.op("pe", lambda e, o=pfa[:, 0:128], l=fb[:, 8 + g, :], r=fb[:, 12 + g, :]: e.matmul(o, lhsT=l, rhs=r, start=True, stop=True),
                        [("wsB", fi)], [ka])
                r2 = rhs2[bi][r0:r0 + 16, :]
                kr2 = ("rhs2", bi)
                self.op("dve", lambda e, o=r2.rearrange("p (h l) -> p h l", h=4), sl=sl, g=g:
                        e.tensor_tensor(out=o, in0=acsT[r0:r0 + 16, sl].unsqueeze(1).broadcast_to([16, 4, 128]),
                                        in1=self.selm[r0:r0 + 16, g, :, :], op=ALU.mult), ["acsT", "selm"], [kr2])
                kb_, pb_ = self.psum_group()
                pfb = self.flat(pb_)
                self.op("pe", lambda e, o=pfb[:, 0:512], sl=sl, g=g: e.matmul(o, lhsT=nacsT[r0:r0 + 16, sl], rhs=self.selm[r0:r0 + 16, g, :, :].rearrange("p h l -> p (h l)"),
                                                                 start=True, stop=False), ["nacsT", "selm"], [kb_])
                self.op("pe", lambda e, o=pfb[:, 0:512], r=r2: e.matmul(o, lhsT=self.ones_f[r0:r0 + 16, :], rhs=r, start=False, stop=False),
                        ["ones_f", kr2], [kb_])
                self.op("pe", lambda e, o=pfb[:, 0:512]: e.matmul(o, lhsT=self.ident_b[:], rhs=self.mbias[:, d, :], start=False, stop=True),
                        ["ident_b", "mbias"], [kb_])
                L = Lf[bi]
                kL = ("Lf", bi)
                self.op("act", lambda e, o=L, i_=pfb[:, 0:512]: e.activation(out=o, in_=i_, func=AF.Exp), [kb_], [kL])
                mt = MT[bi]
                kM = ("MT", bi)
                self.op("dve", lambda e, o=mt, a=L.rearrange("p (h l) -> p h l", h=4), b_=pfa[:, 0:128].unsqueeze(1).broadcast_to([128, 4, 128]):
                        e.tensor_tensor(out=o, in0=a, in1=b_, op=ALU.mult), [kL, ka], [kM])
                kc_, pc_ = self.psum_group()
                pfc = self.flat(pc_)
                self.op("pe", lambda e, o=pfc[:, 0:512], r=r2: e.matmul(o, lhsT=self.ones_f[r0:r0 + 16, :], rhs=r, start=True, stop=True),
                        ["ones_f", kr2], [kc_])
                E = Ef[bi]
                kE = ("Ef", bi)
                self.op("act", lambda e, o=E, i_=pfc[:, 0:512]: e.activation(out=o, in_=i_, func=AF.Exp), [kc_], [kE])
                cs_ = Cs[bi]
                kC = ("Cs", bi)
                self.op("pool", lambda e, o=cs_, a=E.rearrange("p (h l) -> p h l", h=4), b_=fb[:, 12 + g, :].unsqueeze(1).broadcast_to([128, 4, 128]):
                        e.tensor_tensor(out=o, in0=a, in1=b_, op=ALU.mult), [kE, ("wsB", fi)], [kC])
                kd, pd = self.psum_group()
                pfd = self.flat(pd)
                for hh in range(4):
                    head = 4 * g + hh
                    self.op("pe", lambda e, o=pfd[0:64, hh * 128:(hh + 1) * 128], l=Xdt[:, head, :], r=mt[:, hh, :]:
                            e.matmul(o, lhsT=l, rhs=r, start=True, stop=False), ["Xdt", kM], [kd])
                    self.op("pe", lambda e, o=pfd[0:64, hh * 128:(hh + 1) * 128], l=state_bf[:, head * 64:(head + 1) * 64], r=cs_[:, hh, :]:
                            e.matmul(o, lhsT=l, rhs=r, start=False, stop=True), [("stb", g), kC], [kd])
                ys = yst[g][0:64, 0:512]
                self.op("act", lambda e, o=ys, i_=pfd[0:64, 0:512]: e.copy(out=o, in_=i_), [kd], [ysk[g]])
                self.dma(Y[g * 256:(g + 1) * 256, sl].rearrange("(h p) l -> p h l", p=64), ys.rearrange("p (h l) -> p h l", h=4),
                         [ysk[g]], [("Y", d, c // (TG // 128))])
                ke, pe_ = self.psum_group()
                pfe = self.flat(pe_)
                self.op("pe", lambda e, o=pfe[:, 0:256], l=tokb[:, 8 + g, :], r=Xdd[:, 4 * g:4 * g + 4, :]:
                        e.matmul(o, lhsT=l, rhs=r, start=True, stop=True), ["tokb", "Xdd"], [ke])
                stg = state[:, g * 256:(g + 1) * 256]
                self.op("dve", lambda e, o=stg.rearrange("p (h q) -> p h q", h=4), c=c, g=g:
                        e.tensor_tensor(out=o, in0=o, in1=cdk[:, d, c, 4 * g:4 * g + 4].unsqueeze(2).broadcast_to([128, 4, 64]), op=ALU.mult),
                        [("st", g), "cdk"], [("st", g)])
                self.op("dve", lambda e, o=stg, i_=pfe[:, 0:256]: e.tensor_tensor(out=o, in0=o, in1=i_, op=ALU.add),
                        [("st", g), ke], [("st", g)])
                self.op("act", lambda e, o=state_bf[:, g * 256:(g + 1) * 256], i_=stg: e.copy(out=o, in_=i_), [("st", g)], [("stb", g)])

    def stage_odd_out(self, X, aps, li, npost, Yf, Yb, XBC, ZS, MO):
        xv, xn_, sq, kX, kXN_, kSQ = self.views()
        rhs = self.bH[:, 0:12 * TG].rearrange("p (c t) -> p c t", c=12)
        rkeys = [("bH", m) for m in range(12)]
        for g in range(NG):
            t0 = g * TG
            self.dma(rhs[:, 0:4, :], MO[:, :, t0:t0 + TG].rearrange("c p t -> p c t"), [("MO", g)], rkeys[0:4])
            for c in range(8):
                a, ka = self.stg4()
                b, kb = self.stg4()
                xs, kx = self.stg4()
                z, kz = self.stg4()
                self.dma(a[:], Yf[c * 128:(c + 1) * 128, t0:t0 + TG], [("Y", 0, g)], [ka])
                self.dma(b[:], Yb[c * 128:(c + 1) * 128, t0:t0 + TG], [("Y", 1, g)], [kb])
                self.dma(xs[:], XBC[c, :, t0:t0 + TG], [("XBC", g)], [kx])
                self.dma(z[:], ZS[c, :, t0:t0 + TG], [("ZS", g)], [kz])
                self.op("pool", lambda e, o=a[:], b_=b[:]: e.tensor_tensor(out=o, in0=o, in1=b_, op=ALU.add), [ka, kb], [ka])
                self.op("dve", lambda e, o=a[:], x_=xs[:], s=self.dexp[:, c:c + 1]:
                        e.scalar_tensor_tensor(out=o, in0=x_, scalar=s, in1=o, op0=ALU.mult, op1=ALU.add), [ka, kx, "dexp"], [ka])
                self.op("dve", lambda e, o=xv[:, c, :], a_=a[:], z_=z[:]: e.tensor_tensor(out=o, in0=a_, in1=z_, op=ALU.mult),
                        [ka, kz], [kX[c]])
                self.op("act", lambda e, o=sq[:, c, :], i_=xv[:, c, :]: e.activation(out=o, in_=i_, func=AF.Square),
                        [kX[c]], [kSQ[c]])
            rk, rv = self.rms_rinv(sq, 8, 0, 1.0 / D, EPS, kSQ)
            for c in range(8):
                self.op("dve", lambda e, o=rhs[:, 4 + c, :], i_=xv[:, c, :], s=self.snw[:, c:c + 1], r=rv:
                        e.scalar_tensor_tensor(out=o, in0=i_, scalar=s, in1=r, op0=ALU.mult, op1=ALU.mult),
                        [kX[c], "snw", rk], [rkeys[4 + c]])
            self.proj_post_residual(X, g, rhs, rkeys, 12, aps["od_wout"][li], npost, False)

def build_program(stop=None, only=None):
    from contextlib import ExitStack
    nc = bass.Bass("TRN2", target_bir_lowering=False)
    dt = nc.dram_tensor
    aps = {}

    def inp(name, shape):
        aps[name] = dt(name, shape, F32, kind="ExternalInput").ap()

    inp("x", [S, D])
    inp("normw", [128, DEPTH * 6 * 8])
    import os
    nff = 1 if os.environ.get("DEV_SMALL") else DEPTH * 2
    inp("wg", [nff, D, DFF])
    inp("wu", [nff, D, DFF])
    inp("wd", [nff, DFF, D])
    inp("ev_win", [2, D, 2304])
    inp("ev_wv", [2, D, 256])
    inp("ev_pool_w", [2, 4, 64, 64])
    inp("ev_pool_scale", [2, 128, 2])
    inp("ev_sink", [2, 12])
    inp("ev_wout", [2, D, D])
    inp("ropec", [128, S])
    inp("ropes", [128, S])
    inp("icnt", [2, 128, S])
    inp("amask", [128, 2, 384])
    inp("od_win", [2, D, 4128])
    inp("od_wout", [2, 1536, D])
    inp("cv_dww", [2, 128, 4, 31])
    inp("cv_vec", [2, 128, 3, 4])
    inp("ssm_cw", [2, 128, 16, 5])
    inp("ssm_dexp", [2, 128, 8])
    inp("ssm_nw", [2, 128, 8])
    inp("ssm_ab", [2, 48, 2])
    inp("ssm_A_log", [2, 32])
    inp("ssm_dt_bias", [2, 32])
    inp("tri", [128, 2, 128])
    inp("selm", [48, 4, 4, 128])
    inp("mbias", [128, 2, 512])
    inp("rmask", [48, 128])
    out = dt("out", [S, D], F32, kind="ExternalOutput").ap()
    X = dt("Xs", [8, 128, S], F32, kind="Internal").ap()
    QT = dt("QTs", [768, S], BF16, kind="Internal").ap()
    KT = dt("KTs", [256, S], BF16, kind="Internal").ap()
    V = dt("Vs", [S, 256], BF16, kind="Internal").ap()
    U = dt("Us", [2, 128, S], F32, kind="Internal").ap()
    AO = dt("AOs", [8, 128, S], BF16, kind="Internal").ap()
    Gs = dt("Gs", [4, 128, S], F32, kind="Internal").ap()
    ZS = dt("ZSs", [8, 128, S], F32, kind="Internal").ap()
    XBCr = dt("XBCrs", [16, 128, S], F32, kind="Internal").ap()
    XBC = dt("XBCs", [16, 128, S], F32, kind="Internal").ap()
    DTr = dt("DTrs", [32, S], F32, kind="Internal").ap()
    DTt = dt("DTts", [128, 32, 32], F32, kind="Internal").ap()
    Yf = dt("Yfs", [1024, S], F32, kind="Internal").ap()
    Yb = dt("Ybs", [1024, S], F32, kind="Internal").ap()
    MO = dt("MOs", [4, 128, S], BF16, kind="Internal").ap()
    B = Builder(nc, stop)
    B.init_consts(aps)
    B.stage_load(aps["x"], X)
    nst = 0
    seq = [(l, sub) for l in range(DEPTH) for sub in range(3)]
    if stop is not None:
        seq = seq[:stop]
    if only is not None:
        seq = only
    for (l, sub) in seq:
        base = l * 6 * 8
        if True:
            if sub == 0:
                B.sc.barrier(); B.stage_ffn(X, aps["wg"][l * 2], aps["wu"][l * 2], aps["wd"][l * 2], base + 0, base + 8)
            elif sub == 2:
                B.sc.barrier(); B.stage_ffn(X, aps["wg"][l * 2 + 1], aps["wu"][l * 2 + 1], aps["wd"][l * 2 + 1], base + 32, base + 40)
            elif l % 2 == 0:
                li = l // 2
                B.even_consts(aps, li)
                B.sc.barrier(); B.stage_even_in(X, aps, li, base + 16, QT, KT, V, U)
                B.sc.barrier(); B.stage_even_pool(aps, U, AO)
                B.sc.barrier(); B.stage_even_attn(aps, li, QT, KT, V, AO)
                B.sc.barrier(); B.stage_even_out(X, aps, li, base + 24, AO)
            else:
                li = l // 2
                import os
                ocut = int(os.environ.get("ODD_CUT", "99"))
                B.odd_consts(aps, li)
                oskip = int(os.environ.get("ODD_SKIP", "0"))
                if ocut >= 1 and oskip < 1:
                    B.sc.barrier(); B.stage_odd_in(X, aps, li, base + 16, Gs, ZS, XBCr, DTr, DTt)
                if ocut >= 2 and oskip < 2:
                    B.sc.barrier(); B.stage_odd_conv(aps, li, Gs, XBCr, XBC, MO)
                if ocut >= 3:
                    B.sc.barrier(); B.stage_odd_dt(aps, li, DTr, DTt)
                if ocut >= 4:
                    B.sc.barrier(); B.stage_odd_ssd(aps, XBC, Yf, 0)
                if ocut >= 5:
                    B.sc.barrier(); B.stage_odd_ssd(aps, XBC, Yb, 1)
                if ocut >= 6:
                    B.sc.barrier(); B.stage_odd_out(X, aps, li, base + 24, Yf, Yb, XBC, ZS, MO)
    B.sc.barrier(); B.stage_store(X, out)
    fin = [("out", g) for g in range(NG)]
    if os.environ.get("DBG"):
        def dbg(name, ap, keys):
            o = dt("dbg_" + name, list(ap.shape), ap.dtype, kind="ExternalOutput").ap()
            B.dma(o, ap, keys, [("dbg", name)])
            fin.append(("dbg", name))
        G4 = range(NG)
        dbg("G", Gs, [("G", g) for g in G4])
        dbg("XBC", XBC, [("XBC", g) for g in G4])
        dbg("MO", MO, [("MO", g) for g in G4])
        dbg("ZS", ZS, [("ZS", g) for g in G4])
        dbg("Yf", Yf, [("Y", 0, g) for g in G4])
        dbg("Yb", Yb, [("Y", 1, g) for g in G4])
        dbg("DTr", DTr, [("DTr", g) for g in G4])
        dbg("tok", B.bXN[:].bitcast(F32)[:, 0:3072], ["dtk", "dtd", "cdk"])
        dbg("acsT", B.bX[0:48, 0:S], ["acsT"])
    B.op("sp", lambda e: e.nop(), fin, [])
    with ExitStack() as st:
        B.sc.emit(st)
    return nc


def rope_tables_np():
    inv = (1.0 / (np.float32(10000.0) ** (np.arange(0, 64, 2, dtype=np.float32) / np.float32(64)))).astype(np.float32)
    ang = (np.arange(S, dtype=np.float32)[:, None] * inv[None, :]).astype(np.float32)
    cos = np.cos(ang).astype(np.float32)
    sin = np.sin(ang).astype(np.float32)
    cf = np.concatenate([cos, cos], axis=1)
    sf = np.concatenate([-sin, sin], axis=1)
    cT = np.ascontiguousarray(np.concatenate([cf, cf], axis=1).T)
    sT = np.ascontiguousarray(np.concatenate([sf, sf], axis=1).T)
    return cT, sT


def const_tables():
    cT, sT = rope_tables_np()
    t = np.arange(S)
    icnt = np.zeros((2, 128, S), np.float32)
    for gi, w in enumerate((2, 4, 8, 16)):
        lo = np.clip(t - w // 2, 0, S)
        hi = np.clip(t + w - w // 2, 0, S)
        icnt[gi // 2, (gi % 2) * 64:(gi % 2) * 64 + 64, :] = (1.0 / (hi - lo).astype(np.float32))[None, :]
    kl = np.arange(128)[:, None]
    ql = np.arange(128)[None, :]
    m1 = (ql <= kl).astype(np.float32)
    m2 = (kl <= ql).astype(np.float32)
    amask = np.stack([np.tile(m1, (1, 3)), np.tile(m2, (1, 3))], axis=1)
    s_ = np.arange(128)[:, None]
    l_ = np.arange(128)[None, :]
    tri = np.stack([(s_ <= l_), (s_ >= l_)], axis=1).astype(np.float32)
    selm = np.zeros((48, 4, 4, 128), np.float32)
    for k in range(16):
        selm[k, k // 4, k % 4, :] = 1.0
        selm[32 + k, k // 4, k % 4, :] = 1.0
    NEG = -30000.0
    mb = np.stack([np.tile(np.where(l_ >= s_, 0.0, NEG), (1, 4)), np.tile(np.where(s_ >= l_, 0.0, NEG), (1, 4))], axis=1)
    rmask = np.ones((48, 128), np.float32)
    rmask[:, 0] = 0.0
    return {"ropec": cT, "ropes": sT, "icnt": icnt, "amask": np.ascontiguousarray(amask),
            "tri": np.ascontiguousarray(tri), "selm": selm, "mbias": np.ascontiguousarray(mb.astype(np.float32)),
            "rmask": rmask}


def make_in_maps(inputs):
    nw = np.ascontiguousarray(
        inputs["norm_w"].reshape(DEPTH, 6, 8, 128).transpose(3, 0, 1, 2).reshape(128, DEPTH * 6 * 8))
    qk = np.arange(256, 1280)
    rel = (qk - 256) % 64
    partner = qk - rel + np.where(rel < 32, rel + 32, rel - 32)
    cols = []
    for i in range(8):
        cols.append(qk[i * 128:(i + 1) * 128])
        cols.append(partner[i * 128:(i + 1) * 128])
    cols.append(np.arange(0, 256))
    cols = np.concatenate(cols)
    shared = {
        "normw": nw,
        "wg": np.ascontiguousarray(inputs["ffn_w_gate"].reshape(DEPTH * 2, D, DFF)),
        "wu": np.ascontiguousarray(inputs["ffn_w_up"].reshape(DEPTH * 2, D, DFF)),
        "wd": np.ascontiguousarray(inputs["ffn_w_down"].reshape(DEPTH * 2, DFF, D)),
        "ev_win": np.ascontiguousarray(inputs["ev_w_in"][:, :, cols]),
        "ev_wv": np.ascontiguousarray(inputs["ev_w_in"][:, :, 1280:1536]),
        "ev_pool_w": np.ascontiguousarray(inputs["ev_pool_w"]),
        "ev_pool_scale": np.ascontiguousarray(inputs["ev_pool_scale"].reshape(2, 2, 128).transpose(0, 2, 1)),
        "ev_sink": np.ascontiguousarray(inputs["ev_sink"]),
        "ev_wout": np.ascontiguousarray(inputs["ev_w_out"]),
    }
    ocols = []
    for m in range(4):
        ocols.append(np.arange(m * 128, (m + 1) * 128))
        ocols.append(np.arange(512 + m * 128, 512 + (m + 1) * 128))
    ocols.append(np.arange(1024, 4128))
    ocols = np.concatenate(ocols)
    f32 = np.float32
    ab = np.zeros((2, 48, 2), f32)
    ab[:, 0:16, 0] = inputs["ssm_A_log"][:, 0]
    ab[:, 32:48, 0] = inputs["ssm_A_log"][:, 1]
    ab[:, 0:16, 1] = inputs["ssm_dt_bias"][:, 0]
    ab[:, 32:48, 1] = inputs["ssm_dt_bias"][:, 1]
    cw = np.concatenate([inputs["ssm_conv_w"].reshape(2, 4, 16, 128).transpose(0, 3, 2, 1),
                         inputs["ssm_conv_b"].reshape(2, 16, 128).transpose(0, 2, 1)[..., None]], axis=-1)
    shared.update({
        "od_win": np.ascontiguousarray(inputs["od_w_in"][:, :, ocols]),
        "od_wout": np.ascontiguousarray(inputs["od_w_out"]),
        "cv_dww": np.ascontiguousarray(inputs["cv_dw_w"].reshape(2, 31, 4, 128).transpose(0, 3, 2, 1)),
        "cv_vec": np.ascontiguousarray(np.stack([inputs["cv_dw_b"], inputs["cv_ln_g"], inputs["cv_ln_b"]], axis=1)
                                       .reshape(2, 3, 4, 128).transpose(0, 3, 1, 2)),
        "ssm_cw": np.ascontiguousarray(cw.astype(f32)),
        "ssm_dexp": np.ascontiguousarray(np.repeat(inputs["ssm_D"], 64, axis=1).reshape(2, 8, 128).transpose(0, 2, 1)),
        "ssm_nw": np.ascontiguousarray(inputs["ssm_norm_w"].reshape(2, 8, 128).transpose(0, 2, 1)),
        "ssm_ab": ab,
        "ssm_A_log": np.ascontiguousarray(inputs["ssm_A_log"].reshape(2, 32)),
        "ssm_dt_bias": np.ascontiguousarray(inputs["ssm_dt_bias"].reshape(2, 32)),
    })
    shared.update(const_tables())
    maps = []
    for b in range(NCORES):
        m = dict(shared)
        m["x"] = np.ascontiguousarray(inputs["x"][b])
        maps.append(m)
    return maps


def kernel(**inputs):
    inputs = {k: np.asarray(v) for k, v in inputs.items()}
    nc = build_program()
    in_maps = make_in_maps(inputs)
    res = run_bass_kernel_spmd(nc, in_maps, core_ids=list(range(NCORES)))
    return np.stack([np.asarray(r["out"]) for r in res.results], axis=0).astype(np.float32)
```

```python
import numpy as np
import concourse.bass as bass
import concourse.mybir as mybir
from concourse.bass_utils import run_bass_kernel_spmd

F32 = mybir.dt.float32
BF16 = mybir.dt.bfloat16
ALU = mybir.AluOpType
AF = mybir.ActivationFunctionType

D = 1024
S = 4096
DEPTH = 4
DFF = 2816
NCORES = 8
EPS = 1e-6
TG = 1024
NG = S // TG
NB = TG // 512


class Op:
    __slots__ = ("eng", "fn", "dma", "deps", "idx", "sem", "semval", "sig", "presem")


class Sched:
    ENG = ("pe", "act", "dve", "pool", "sp")

    def __init__(self, nc):
        self.nc = nc
        self.ops = []
        self.last_w = {}
        self.readers = {}
        self.dma_since = []
        self.last_on = {}

    def barrier(self):
        frontier = set(self.dma_since)
        for e, idx in self.last_on.items():
            frontier.add(idx)
        self.dma_since = []
        for e in self.ENG:
            self.add(e, lambda eng: eng.nop(), extra=frontier)

    def add(self, eng, fn, reads=(), writes=(), dma=False, extra=()):
        op = Op()
        op.eng = eng
        op.fn = fn
        op.dma = dma
        op.idx = len(self.ops)
        op.sig = False
        op.sem = None
        op.semval = 0
        op.presem = None
        deps = {}
        for k in reads:
            w = self.last_w.get(k)
            if w is not None:
                deps[w] = True
            if isinstance(k, tuple) and k[0] == "ps":
                for r in self.readers.get(k, ()):
                    if self.ops[r].eng != eng:
                        deps[r] = True
        for k in writes:
            w = self.last_w.get(k)
            if w is not None:
                deps.setdefault(w, False)
            for r in self.readers.get(k, ()):
                deps.setdefault(r, False)
        for k in reads:
            self.readers.setdefault(k, []).append(op.idx)
        for k in writes:
            self.last_w[k] = op.idx
            self.readers[k] = []
        for x in extra:
            deps[x] = True
        deps.pop(op.idx, None)
        op.deps = deps
        self.ops.append(op)
        if dma:
            self.dma_since.append(op.idx)
        else:
            self.last_on[eng] = op.idx
        return op

    def _needs_sync(self, c, p, raw):
        if p.dma:
            return True
        if c.eng != p.eng:
            return True
        if c.dma:
            return True
        if c.eng == "pe":
            return False
        return raw

    def emit(self, stack):
        nc = self.nc
        ops = self.ops
        for c in ops:
            for pi, raw in c.deps.items():
                p = ops[pi]
                if not p.dma and self._needs_sync(c, p, raw):
                    p.sig = True
        NDMA = 40
        dma_sems = [stack.enter_context(nc.semaphore("dq%d" % i)) for i in range(NDMA)]
        dma_cum = [0] * NDMA
        dma_rr = 0
        EPOCH = 20000
        eng_sems = {e: [] for e in self.ENG}
        eng_cnt = {e: 0 for e in self.ENG}
        for op in ops:
            if op.dma:
                i = dma_rr % NDMA
                dma_rr += 1
                op.sem = dma_sems[i]
                op.presem = (dma_sems[i], dma_cum[i]) if dma_cum[i] > 0 else None
                dma_cum[i] += 16
                op.semval = dma_cum[i]
            elif op.sig:
                e = op.eng
                if eng_cnt[e] % EPOCH == 0:
                    eng_sems[e].append(stack.enter_context(
                        nc.semaphore("e_%s_%d" % (e, len(eng_sems[e])))))
                eng_cnt[e] += 1
                op.sem = eng_sems[e][-1]
                op.semval = (eng_cnt[e] - 1) % EPOCH + 1
        per = {e: [] for e in self.ENG}
        for op in ops:
            per[op.eng].append(op)

        def run(engname, eng):
            waited = {}
            for op in per[engname]:
                ws = []
                for pi, raw in op.deps.items():
                    p = ops[pi]
                    if self._needs_sync(op, p, raw):
                        ws.append((p.sem, p.semval))
                if op.presem is not None:
                    ws.append(op.presem)
                for sem, val in ws:
                    key = id(sem)
                    if waited.get(key, 0) >= val:
                        continue
                    waited[key] = val
                    eng.wait_ge(sem, val)
                ins = op.fn(eng)
                if op.dma:
                    ins.then_inc(op.sem, 16)
                elif op.sig:
                    ins.then_inc(op.sem, 1)

        with nc.Block() as block:
            @block.tensor
            def _(e):
                run("pe", e)

            @block.scalar
            def _(e):
                run("act", e)

            @block.vector
            def _(e):
                run("dve", e)

            @block.gpsimd
            def _(e):
                run("pool", e)

            @block.sync
            def _(e):
                run("sp", e)


class Builder:
    def __init__(self, nc, stop=None, aps=None, wplan=None):
        self.nc = nc
        self.aps = aps
        self.wplan = wplan
        self.wrec = []
        self.w_issued = 0
        self.PF = 2
        self.sc = Sched(nc)
        self.stop = stop
        self.uid = 0
        nc_ = nc
        A = nc_.alloc_sbuf_tensor
        self.bX = A("bX", [128, 8 * TG], F32)
        self.bXN = A("bXN", [128, 8 * TG], BF16)
        self.bSQ = A("bSQ", [128, 8 * TG], BF16)
        self.bH = A("bH", [128, 22 * TG], BF16)
        self.NWS = 4
        self.wsF = [A("wsF%d" % i, [128, 2048], F32) for i in range(self.NWS)]
        self.wsB = [A("wsB%d" % i, [128, 2048], BF16) for i in range(self.NWS)]
        self.ws_i = 0
        self.bR = A("bR", [128, 2 * TG], F32)
        self.bSG = [A("bSG%d" % i, [128, TG], F32) for i in range(2)]
        self.sg_i = 0
        self.bT = [A("bT%d" % i, [128, TG], F32) for i in range(2)]
        self.t_i = 0
        self.ones_bf = A("ones_bf", [128, 128], BF16)
        self.ident_f = A("ident_f", [128, 128], F32)
        self.normw = A("normw_sb", [128, DEPTH * 6 * 8], F32)
        self.poolW = A("poolW", [128, 2, 128], BF16)
        self.poolS = A("poolS", [128, 2], F32)
        self.esink = A("esink", [128, 12], F32)
        self.amask = A("amask_sb", [128, 2, 384], F32)
        self.stg_i = 0
        self.ones_f = A("ones_f", [128, 128], F32)
        self.ident_b = A("ident_b", [128, 128], BF16)
        self.dww = A("dww", [128, 4, 31], F32)
        self.cvvec = A("cvvec", [128, 3, 4], F32)
        self.scw = A("scw", [128, 16, 5], F32)
        self.dexp = A("dexp", [128, 8], F32)
        self.snw = A("snw", [128, 8], F32)
        self.abcol = A("abcol", [48, 2], F32)
        self.abrow = A("abrow", [128, 2, 32], F32)
        self.tri = A("tri_sb", [128, 2, 128], F32)
        self.selm = A("selm_sb", [48, 4, 4, 128], F32)
        self.mbias = A("mbias_sb", [128, 2, 512], BF16)
        self.ps = nc_.alloc_psum_tensor("ps", [128, 8, 512], F32)
        self.ps_i = 0
        self.NPG = 8 // NB

    def op(self, eng, fn, reads=(), writes=(), dma=False):
        return self.sc.add(eng, fn, reads, writes, dma)

    def psum_group(self):
        g = self.ps_i % self.NPG
        self.ps_i += 1
        key = ("ps", g)
        view = self.ps[:, g * NB:(g + 1) * NB, :]
        return key, view

    def dma(self, out, in_, reads, writes, eng="sp"):
        return self.op(eng, lambda e, o=out, i=in_: e.dma_start(out=o, in_=i), reads, writes, dma=True)

    def _issue_w(self, j, spec):
        name, idx, r0, nk, c0, ncols = spec
        W = self.aps[name][idx]
        i = j % self.NWS
        fv = self.wsF[i][:, 0:nk * ncols].rearrange("p (k m) -> p k m", k=nk)
        bv = self.wsB[i][:, 0:nk * ncols].rearrange("p (k m) -> p k m", k=nk)
        src = W[r0:r0 + nk * 128, c0:c0 + ncols].rearrange("(k p) m -> p k m", p=128)
        self.dma(fv, src, reads=[], writes=[("wsF", i)])
        self.op("pool", lambda e, o=bv, s=fv: e.tensor_copy(out=o, in_=s),
                reads=[("wsF", i)], writes=[("wsB", i)])

    def load_w(self, Wd, r0, nk, c0, ncols):
        j = self.ws_i
        self.ws_i += 1
        spec = (Wd[0], Wd[1], r0, nk, c0, ncols)
        if self.wplan is None:
            self.wrec.append(spec)
            self._issue_w(j, spec)
        else:
            assert self.wplan[j] == spec, (j, self.wplan[j], spec)
            upto = min(j + self.PF, len(self.wplan) - 1)
            while self.w_issued <= upto:
                self._issue_w(self.w_issued, self.wplan[self.w_issued])
                self.w_issued += 1
        i = j % self.NWS
        bv = self.wsB[i][:, 0:nk * ncols].rearrange("p (k m) -> p k m", k=nk)
        return ("wsB", i), bv

    def init_consts(self, aps):
        nc = self.nc
        self.op("pool", lambda e: e.memset(self.ones_bf[:], 1.0), [], ["ones_bf"])
        self.op("pool", lambda e: e.memset(self.ident_f[:], 0.0), [], ["ident_f"])
        self.op("pool", lambda e: e.affine_select(
            out=self.ident_f[:], in_=self.ident_f[:], pattern=[[-1, 128]],
            compare_op=ALU.not_equal, fill=1.0, base=0, channel_multiplier=1),
            ["ident_f"], ["ident_f"])
        self.dma(self.normw[:], aps["normw"], [], ["normw"])
        self.dma(self.amask[:], aps["amask"], [], ["amask"])
        self.op("pool", lambda e: e.memset(self.ones_f[:], 1.0), [], ["ones_f"])
        self.op("pool", lambda e: e.tensor_copy(out=self.ident_b[:], in_=self.ident_f[:]), ["ident_f"], ["ident_b"])
        self.dma(self.tri[:], aps["tri"], [], ["tri"])
        self.dma(self.selm[:], aps["selm"], [], ["selm"])
        st_ = self.bX[:, 0:1024].rearrange("p (d t) -> p d t", d=2)
        self.dma(st_, aps["mbias"], [], [("bX", 0)])
        self.op("pool", lambda e: e.tensor_copy(out=self.mbias[:], in_=st_), [("bX", 0)], ["mbias"])

    def flat(self, pv):
        return pv.rearrange("p b t -> p (b t)")

    def stage_load(self, x, X):
        hf = self.bH[:].bitcast(F32)
        xv = self.bX[:].rearrange("p (c t) -> p c t", c=8)
        for g in range(NG):
            tv = hf[:, 0:8 * D].rearrange("p (tb f) -> p tb f", tb=8)
            self.dma(tv, x[g * TG:(g + 1) * TG, :].rearrange("(tb p) f -> p tb f", p=128),
                     [], ["bH"])
            n = 0
            for tb in range(8):
                for c0 in range(0, 8, 4):
                    key, pv = self.psum_group()
                    pf = self.flat(pv)
                    for cc in range(4):
                        c = c0 + cc
                        self.op("pe", lambda e, o=pf[:, cc * 128:(cc + 1) * 128], i=tv[:, tb, c * 128:(c + 1) * 128]:
                                e.transpose(out=o, in_=i, identity=self.ident_f[:]),
                                ["bH", "ident_f"], [key])
                    src = pf[:, 0:512].rearrange("p (c t) -> p c t", c=4)
                    dst = xv[:, c0:c0 + 4, tb * 128:(tb + 1) * 128]
                    if n % 2 == 0:
                        self.op("act", lambda e, o=dst, i=src: e.copy(out=o, in_=i), [key],
                                [("bX", c0 + k) for k in range(4)])
                    else:
                        self.op("dve", lambda e, o=dst, i=src: e.tensor_copy(out=o, in_=i), [key],
                                [("bX", c0 + k) for k in range(4)])
                    n += 1
            self.dma(X[:, :, g * TG:(g + 1) * TG].rearrange("c p t -> p c t"), xv,
                     [("bX", c) for c in range(8)], [("X", g, c) for c in range(8)])

    def stage_store(self, X, out):
        hf = self.bH[:].bitcast(F32)
        xv = self.bX[:].rearrange("p (c t) -> p c t", c=8)
        for g in range(NG):
            tv = hf[:, 0:8 * D].rearrange("p (tb f) -> p tb f", tb=8)
            self.dma(xv, X[:, :, g * TG:(g + 1) * TG].rearrange("c p t -> p c t"),
                     [("X", g, c) for c in range(8)], [("bX", c) for c in range(8)])
            n = 0
            for tb in range(8):
                for c0 in range(0, 8, 4):
                    key, pv = self.psum_group()
                    pf = self.flat(pv)
                    for cc in range(4):
                        c = c0 + cc
                        self.op("pe", lambda e, o=pf[:, cc * 128:(cc + 1) * 128], i=xv[:, c, tb * 128:(tb + 1) * 128]:
                                e.transpose(out=o, in_=i, identity=self.ident_f[:]),
                                [("bX", c), "ident_f"], [key])
                    dst = tv[:, tb, c0 * 128:(c0 + 4) * 128]
                    src = pf[:, 0:512]
                    if n % 2 == 0:
                        self.op("act", lambda e, o=dst, i=src: e.copy(out=o, in_=i), [key], ["bH"])
                    else:
                        self.op("dve", lambda e, o=dst, i=src: e.tensor_copy(out=o, in_=i), [key], ["bH"])
                    n += 1
            self.dma(out[g * TG:(g + 1) * TG, :].rearrange("(tb p) f -> p tb f", p=128), tv,
                     ["bH"], [("out", g)])

    def rms_rinv(self, sqv, nchunks, off, scale, bias, sqkeys):
        key, pv = self.psum_group()
        for b in range(NB):
            for c in range(nchunks):
                self.op("pe", lambda e, o=pv[:, b, :], r=sqv[:, c, b * 512:(b + 1) * 512], st=(c == 0), sp=(c == nchunks - 1):
                        e.matmul(o, lhsT=self.ones_bf[:], rhs=r, start=st, stop=sp),
                        ["ones_bf", sqkeys[c]], [key])
        rv = self.bR[:, off:off + TG]
        rk = ("bR", off)
        self.op("act", lambda e, o=rv, i=self.flat(pv): e.activation(out=o, in_=i, func=AF.Sqrt, scale=scale, bias=bias),
                [key], [rk])
        self.op("dve", lambda e, o=rv: e.reciprocal(out=o, in_=o), [rk], [rk])
        return rk, rv

    def views(self):
        xv = self.bX[:].rearrange("p (c t) -> p c t", c=8)
        xn = self.bXN[:].rearrange("p (c t) -> p c t", c=8)
        sq = self.bSQ[:].rearrange("p (c t) -> p c t", c=8)
        kX = [("bX", c) for c in range(8)]
        kXN = [("bXN", c) for c in range(8)]
        kSQ = [("bSQ", c) for c in range(8)]
        return xv, xn, sq, kX, kXN, kSQ

    def load_norm(self, X, g, npre):
        xv, xn, sq, kX, kXN, kSQ = self.views()
        t0 = g * TG
        self.dma(xv, X[:, :, t0:t0 + TG].rearrange("c p t -> p c t"),
                 [("X", g, c) for c in range(8)], kX)
        for c in range(8):
            self.op("act", lambda e, o=sq[:, c, :], i=xv[:, c, :]: e.activation(out=o, in_=i, func=AF.Square),
                    [kX[c]], [kSQ[c]])
        rk, rv = self.rms_rinv(sq, 8, 0, 1.0 / D, EPS, kSQ)
        for c in range(8):
            self.op("dve", lambda e, o=xn[:, c, :], i=xv[:, c, :], s=self.normw[:, npre + c:npre + c + 1], r=rv:
                    e.scalar_tensor_tensor(out=o, in0=i, scalar=s, in1=r, op0=ALU.mult, op1=ALU.mult),
                    [kX[c], "normw", rk], [kXN[c]])
        return xn, kXN

    def proj_post_residual(self, X, g, rhs, rkeys, KC, W, npost, half):
        xv, xn, sq, kX, kXN, kSQ = self.views()
        nkt = (KC + 7) // 8
        for ct in range(4):
            grp = [self.psum_group(), self.psum_group()]
            for kt in range(nkt):
                nk = min(8, KC - kt * 8)
                kw, w = self.load_w(W, kt * 1024, nk, ct * 256, 256)
                for mm in range(2):
                    kp, pv = grp[mm]
                    for kk in range(nk):
                        kc = kt * 8 + kk
                        for b in range(NB):
                            self.op("pe", lambda e, o=pv[:, b, :], w_=w[:, kk, mm * 128:(mm + 1) * 128], r=rhs[:, kc, b * 512:(b + 1) * 512], st=(kc == 0), sp=(kc == KC - 1):
                                    e.matmul(o, lhsT=w_, rhs=r, start=st, stop=sp), [kw, rkeys[kc]], [kp])
            for mm in range(2):
                oc = ct * 2 + mm
                kp, pv = grp[mm]
                self.op("dve", lambda e, o=xv[:, oc, :], i=self.flat(pv): e.tensor_copy(out=o, in_=i),
                        [kp], [kX[oc]])
                self.op("act", lambda e, o=sq[:, oc, :], i=xv[:, oc, :]: e.activation(out=o, in_=i, func=AF.Square),
                        [kX[oc]], [kSQ[oc]])
        if half:
            rk2, rv2 = self.rms_rinv(sq, 8, TG, 4.0 / D, 4.0 * EPS, kSQ)
        else:
            rk2, rv2 = self.rms_rinv(sq, 8, TG, 1.0 / D, EPS, kSQ)
        self.residual_add(X, g, xv, kX, rk2, rv2, npost)

    def stage_ffn(self, X, Wg, Wu, Wd, npre, npost):
        h = self.bH[:].rearrange("p (c t) -> p c t", c=22)
        for g in range(NG):
            xn, kXN = self.load_norm(X, g, npre)
            for ct in range(11):
                kg, wg = self.load_w(Wg, 0, 8, ct * 256, 256)
                ku, wu = self.load_w(Wu, 0, 8, ct * 256, 256)
                for mm in range(2):
                    m = ct * 2 + mm
                    kpg, pg = self.psum_group()
                    for k in range(8):
                        for b in range(NB):
                            self.op("pe", lambda e, o=pg[:, b, :], w=wg[:, k, mm * 128:(mm + 1) * 128], r=xn[:, k, b * 512:(b + 1) * 512], st=(k == 0), sp=(k == 7):
                                    e.matmul(o, lhsT=w, rhs=r, start=st, stop=sp), [kg, kXN[k]], [kpg])
                    kpu, pu = self.psum_group()
                    for k in range(8):
                        for b in range(NB):
                            self.op("pe", lambda e, o=pu[:, b, :], w=wu[:, k, mm * 128:(mm + 1) * 128], r=xn[:, k, b * 512:(b + 1) * 512], st=(k == 0), sp=(k == 7):
                                    e.matmul(o, lhsT=w, rhs=r, start=st, stop=sp), [ku, kXN[k]], [kpu])
                    si = self.sg_i % 2
                    self.sg_i += 1
                    sg = self.bSG[si]
                    self.op("act", lambda e, o=sg[:], i=self.flat(pg): e.activation(out=o, in_=i, func=AF.Silu),
                            [kpg], [("bSG", si)])
                    self.op("dve", lambda e, o=h[:, m, :], a=sg[:], b_=self.flat(pu): e.tensor_tensor(out=o, in0=a, in1=b_, op=ALU.mult),
                            [("bSG", si), kpu], [("bH", m)])
            self.proj_post_residual(X, g, h, [("bH", m) for m in range(22)], 22, Wd, npost, True)

    def residual_add(self, X, g, yv, kY, rk, rv, npost):
        t0 = g * TG
        for oc in range(8):
            ti = self.t_i % 2
            self.t_i += 1
            xr = self.bT[ti]
            kt_ = ("bT", ti)
            self.dma(xr[:], X[oc, :, t0:t0 + TG], [("X", g, oc)], [kt_])
            self.op("dve", lambda e, o=yv[:, oc, :], s=self.normw[:, npost + oc:npost + oc + 1], r=rv:
                    e.scalar_tensor_tensor(out=o, in0=o, scalar=s, in1=r, op0=ALU.mult, op1=ALU.mult),
                    [kY[oc], "normw", rk], [kY[oc]])
            self.op("pool", lambda e, o=xr[:], b_=yv[:, oc, :]: e.tensor_tensor(out=o, in0=o, in1=b_, op=ALU.add),
                    [kt_, kY[oc]], [kt_])
            self.dma(X[oc, :, t0:t0 + TG], xr[:], [kt_], [("X", g, oc)])


    def even_consts(self, aps, li):
        stg = self.wsF[0][:, 0:256].rearrange("p (c m) -> p c m", c=2)
        stg = self.bT[0][:, 0:256].rearrange("p (c m) -> p c m", c=2)
        kt_ = ("bT", 0)
        self.op("pool", lambda e, o=stg: e.memset(o, 0.0), [], [kt_])
        for c in range(2):
            for gg in range(2):
                self.dma(stg[gg * 64:(gg + 1) * 64, c, gg * 64:(gg + 1) * 64], aps["ev_pool_w"][li, 2 * c + gg],
                         [kt_], [kt_])
        self.op("pool", lambda e, o=self.poolW[:], i=stg: e.tensor_copy(out=o, in_=i), [kt_], ["poolW"])
        self.dma(self.poolS[:], aps["ev_pool_scale"][li], ["poolS"], ["poolS"])
        self.dma(self.esink[:], aps["ev_sink"][li].partition_broadcast(128), ["esink"], ["esink"])
        self.op("act", lambda e, o=self.esink[:]: e.activation(out=o, in_=o, func=AF.Exp), ["esink"], ["esink"])

    def stage_even_in(self, X, aps, li, npre, QT, KT, V, U):
        xv, xn_, sq, kX, kXN_, kSQ = self.views()
        Win = ("ev_win", li)
        Wv = ("ev_wv", li)
        sqb = sq
        vst = self.bH[:, 0:8 * 256].rearrange("p (tb f) -> p tb f", tb=8)
        ust = self.bR[:].rearrange("p (c t) -> p c t", c=2)
        for g in range(NG):
            t0 = g * TG
            xn, kXN = self.load_norm(X, g, npre)
            cosT = self.bSG[0]
            sinT = self.bSG[1]
            self.dma(cosT[:], aps["ropec"][:, t0:t0 + TG], [], [("bSG", 0)])
            self.dma(sinT[:], aps["ropes"][:, t0:t0 + TG], [], [("bSG", 1)])
            for i in range(8):
                kw, w = self.load_w(Win, 0, 8, i * 256, 256)
                kq, pq = self.psum_group()
                for k in range(8):
                    for b in range(NB):
                        self.op("pe", lambda e, o=pq[:, b, :], w_=w[:, k, 0:128], r=xn[:, k, b * 512:(b + 1) * 512], st=(k == 0), sp=(k == 7):
                                e.matmul(o, lhsT=w_, rhs=r, start=st, stop=sp), [kw, kXN[k]], [kq])
                kp, pp = self.psum_group()
                for k in range(8):
                    for b in range(NB):
                        self.op("pe", lambda e, o=pp[:, b, :], w_=w[:, k, 128:256], r=xn[:, k, b * 512:(b + 1) * 512], st=(k == 0), sp=(k == 7):
                                e.matmul(o, lhsT=w_, rhs=r, start=st, stop=sp), [kw, kXN[k]], [kp])
                t1 = self.bT[0]
                t2 = self.bT[1]
                self.op("dve", lambda e, o=t1[:], a=self.flat(pq), c_=cosT[:]: e.tensor_tensor(out=o, in0=a, in1=c_, op=ALU.mult),
                        [kq, ("bSG", 0)], [("bT", 0)])
                self.op("dve", lambda e, o=t2[:], a=self.flat(pp), c_=sinT[:]: e.tensor_tensor(out=o, in0=a, in1=c_, op=ALU.mult),
                        [kp, ("bSG", 1)], [("bT", 1)])
                self.op("pool", lambda e, o=sqb[:, i, :], a=t1[:], b_=t2[:]: e.tensor_tensor(out=o, in0=a, in1=b_, op=ALU.add),
                        [("bT", 0), ("bT", 1)], [kSQ[i]])
                if i < 6:
                    self.dma(QT[i * 128:(i + 1) * 128, t0:t0 + TG], sqb[:, i, :], [kSQ[i]], [("QT", g)])
                else:
                    self.dma(KT[(i - 6) * 128:(i - 5) * 128, t0:t0 + TG], sqb[:, i, :], [kSQ[i]], [("KT", g)])
            kw, w = self.load_w(Win, 0, 8, 2048, 256)
            for mm in range(2):
                kp, pv = self.psum_group()
                for k in range(8):
                    for b in range(NB):
                        self.op("pe", lambda e, o=pv[:, b, :], w_=w[:, k, mm * 128:(mm + 1) * 128], r=xn[:, k, b * 512:(b + 1) * 512], st=(k == 0), sp=(k == 7):
                                e.matmul(o, lhsT=w_, rhs=r, start=st, stop=sp), [kw, kXN[k]], [kp])
                self.op("act", lambda e, o=ust[:, mm, :], i_=self.flat(pv): e.copy(out=o, in_=i_), [kp], [("bR", mm * TG)])
                self.dma(U[mm, :, t0:t0 + TG], ust[:, mm, :], [("bR", mm * TG)], [("U", g)])
            kw, w = self.load_w(Wv, 0, 8, 0, 256)
            for tb in range(8):
                kp, pv = self.psum_group()
                pf = self.flat(pv)
                for k in range(8):
                    self.op("pe", lambda e, o=pf[:, 0:256], l=xn[:, k, tb * 128:(tb + 1) * 128], r=w[:, k, :], st=(k == 0), sp=(k == 7):
                            e.matmul(o, lhsT=l, rhs=r, start=st, stop=sp), [kw, kXN[k]], [kp])
                if tb % 2 == 0:
                    self.op("act", lambda e, o=vst[:, tb, :], i_=pf[:, 0:256]: e.copy(out=o, in_=i_), [kp], [("bH", "v")])
                else:
                    self.op("dve", lambda e, o=vst[:, tb, :], i_=pf[:, 0:256]: e.tensor_copy(out=o, in_=i_), [kp], [("bH", "v")])
            self.dma(V[t0:t0 + TG, :].rearrange("(tb p) f -> p tb f", p=128), vst, [("bH", "v")], [("V", g)])

    def stage_even_pool(self, aps, U, AO):
        L = TG + 16
        bx = self.bX
        up = bx[:, 0:2 * L].rearrange("p (c t) -> p c t", c=2)
        a1 = bx[:, 2 * L:4 * L].rearrange("p (c t) -> p c t", c=2)
        a2 = bx[:, 4 * L:6 * L].rearrange("p (c t) -> p c t", c=2)
        xnf = self.bXN[:].bitcast(F32)
        a3 = xnf[:, 0:L]
        a4 = xnf[:, L:2 * L]
        icnt = self.bR[:].rearrange("p (c t) -> p c t", c=2)
        res = self.bSQ[:].bitcast(F32)[:, 0:2 * TG].rearrange("p (c t) -> p c t", c=2)
        pb = self.bH[:, 0:2 * TG].rearrange("p (c t) -> p c t", c=2)
        ob = self.bH[:, 2 * TG:4 * TG].rearrange("p (c t) -> p c t", c=2)
        for g in range(NG):
            t0 = g * TG
            lo = max(t0 - 8, 0)
            hi = min(t0 + TG + 8, S)
            j0 = lo - (t0 - 8)
            self.op("pool", lambda e, o=up: e.memset(o, 0.0), [], ["pl_u"])
            self.dma(up[:, :, j0:j0 + hi - lo], U[:, :, lo:hi].rearrange("c p t -> p c t"),
                     ["pl_u"] + [("U", gg) for gg in (g - 1, g, g + 1) if 0 <= gg < NG], ["pl_u"])
            self.dma(icnt, aps["icnt"][:, :, t0:t0 + TG].rearrange("c p t -> p c t"), [], [("bR", 0), ("bR", TG)])
            self.op("dve", lambda e: e.tensor_tensor(out=a1[:, :, 1:L], in0=up[:, :, 0:L - 1], in1=up[:, :, 1:L], op=ALU.add),
                    ["pl_u"], ["pl_a1"])
            self.op("dve", lambda e: e.tensor_tensor(out=a2[:, :, 2:L - 1], in0=a1[:, :, 1:L - 2], in1=a1[:, :, 3:L], op=ALU.add),
                    ["pl_a1"], ["pl_a2"])
            self.op("dve", lambda e: e.tensor_tensor(out=a3[:, 4:L - 3], in0=a2[:, 1, 2:L - 5], in1=a2[:, 1, 6:L - 1], op=ALU.add),
                    ["pl_a2"], ["pl_a3"])
            self.op("dve", lambda e: e.tensor_tensor(out=a4[:, 8:L - 8], in0=a3[:, 4:L - 12], in1=a3[:, 12:L - 4], op=ALU.add),
                    ["pl_a3"], ["pl_a4"])
            self.op("dve", lambda e: e.tensor_tensor(out=res[0:64, 0, :], in0=a1[0:64, 0, 8:TG + 8], in1=icnt[0:64, 0, :], op=ALU.mult),
                    ["pl_a1", ("bR", 0)], ["pl_res"])
            self.op("dve", lambda e: e.tensor_tensor(out=res[64:128, 0, :], in0=a2[64:128, 0, 8:TG + 8], in1=icnt[64:128, 0, :], op=ALU.mult),
                    ["pl_a2", ("bR", 0)], ["pl_res"])
            self.op("dve", lambda e: e.tensor_tensor(out=res[0:64, 1, :], in0=a3[0:64, 8:TG + 8], in1=icnt[0:64, 1, :], op=ALU.mult),
                    ["pl_a3", ("bR", TG)], ["pl_res"])
            self.op("dve", lambda e: e.tensor_tensor(out=res[64:128, 1, :], in0=a4[64:128, 8:TG + 8], in1=icnt[64:128, 1, :], op=ALU.mult),
                    ["pl_a4", ("bR", TG)], ["pl_res"])
            self.op("dve", lambda e: e.tensor_tensor(out=pb, in0=res, in1=up[:, :, 8:TG + 8], op=ALU.subtract),
                    ["pl_res", "pl_u"], ["pl_pb"])
            for c in range(2):
                kp, pv = self.psum_group()
                for b in range(NB):
                    self.op("pe", lambda e, o=pv[:, b, :], l=self.poolW[:, c, :], r=pb[:, c, b * 512:(b + 1) * 512]:
                            e.matmul(o, lhsT=l, rhs=r, start=True, stop=True), ["poolW", "pl_pb"], [kp])
                self.op("dve", lambda e, o=ob[:, c, :], i_=self.flat(pv), s=self.poolS[:, c:c + 1]:
                        e.tensor_scalar(out=o, in0=i_, scalar1=s, scalar2=None, op0=ALU.mult), [kp, "poolS"], [("pl_ob", c)])
                self.dma(AO[c, :, t0:t0 + TG], ob[:, c, :], [("pl_ob", c)], [("AOp", g)])

    def stage_even_attn(self, aps, li, QT, KT, V, AO):
        KW = TG + 256
        q_sb = self.bH[0:64, 0:12 * TG].rearrange("p (h t) -> p h t", h=12)
        k_sb = self.bXN[0:64, 0:4 * KW].rearrange("p (h t) -> p h t", h=4)
        v_sb = self.bSQ[:, 0:10 * 256].rearrange("p (kb f) -> p kb f", kb=10)
        o_sb = self.bX[:].bitcast(BF16)[0:64, 0:12 * TG].rearrange("p (h t) -> p h t", h=12)
        Pb = [self.bSG[0][:].bitcast(BF16)[:, 0:384], self.bSG[1][:].bitcast(BF16)[:, 0:384],
              self.bT[0][:].bitcast(BF16)[:, 0:384], self.bT[1][:].bitcast(BF16)[:, 0:384]]
        Pk = [("bSG", 0), ("bSG", 1), ("bT", 0), ("bT", 1)]
        recs = [self.bR[0:64, 0:384], self.bR[0:64, 512:896]]
        AOf = AO.rearrange("c p t -> (c p) t")
        pi = 0
        ri = 0
        for g in range(NG):
            t0 = g * TG
            lo = max(t0 - 128, 0)
            hi = min(t0 + TG + 128, S)
            j0 = lo - (t0 - 128)
            nbr = [gg for gg in (g - 1, g, g + 1) if 0 <= gg < NG]
            self.dma(q_sb, QT[:, t0:t0 + TG].rearrange("(h d) t -> d h t", d=64), [("QT", g)], ["at_q"])
            self.dma(k_sb[:, :, j0:j0 + hi - lo], KT[:, lo:hi].rearrange("(h d) t -> d h t", d=64),
                     [("KT", gg) for gg in nbr], ["at_k"])
            self.dma(v_sb[:, j0 // 128:(j0 + hi - lo) // 128, :], V[lo:hi, :].rearrange("(kb p) f -> p kb f", p=128),
                     [("V", gg) for gg in nbr], ["at_v"])
            for qi in range(8):
                n = g * 8 + qi
                for gk in range(4):
                    kbs = [kb for kb in (n - 1, n, n + 1) if 0 <= kb < S // 128]
                    Ps = []
                    for kb in kbs:
                        lk = kb - (g * 8 - 1)
                        ks, pv = self.psum_group()
                        pf = self.flat(pv)
                        self.op("pe", lambda e, o=pf[:, 0:384].rearrange("p (h q) -> p h q", h=3), l=k_sb[:, gk, lk * 128:(lk + 1) * 128],
                                r=q_sb[:, 3 * gk:3 * gk + 3, qi * 128:(qi + 1) * 128]:
                                e.matmul(o, lhsT=l, rhs=r, start=True, stop=True), ["at_k", "at_q"], [ks])
                        P = Pb[pi % 4]
                        kP = Pk[pi % 4]
                        pi += 1
                        self.op("act", lambda e, o=P, i_=pf[:, 0:384]: e.activation(out=o, in_=i_, func=AF.Exp, scale=0.125),
                                [ks], [kP])
                        if kb == n - 1:
                            self.op("dve", lambda e, o=P: e.tensor_tensor(out=o, in0=o, in1=self.amask[:, 0, :], op=ALU.mult),
                                    [kP, "amask"], [kP])
                        elif kb == n + 1:
                            self.op("dve", lambda e, o=P: e.tensor_tensor(out=o, in0=o, in1=self.amask[:, 1, :], op=ALU.mult),
                                    [kP, "amask"], [kP])
                        Ps.append((P, kP, lk))
                    kn, pn = self.psum_group()
                    num = pn[0:64, 0, 0:384]
                    den = pn[0:64, 1, 0:384]
                    for idx, (P, kP, lk) in enumerate(Ps):
                        st = (idx == 0)
                        sp = (idx == len(Ps) - 1)
                        self.op("pe", lambda e, o=num, l=v_sb[:, lk, gk * 64:(gk + 1) * 64], r=P, st=st, sp=sp:
                                e.matmul(o, lhsT=l, rhs=r, start=st, stop=sp), ["at_v", kP], [kn])
                        self.op("pe", lambda e, o=den, l=self.ones_bf[:, 0:64], r=P, st=st, sp=sp:
                                e.matmul(o, lhsT=l, rhs=r, start=st, stop=sp), ["ones_bf", kP], [kn])
                    rec = recs[ri % 2]
                    kr = ("rec", ri % 2)
                    ri += 1
                    for j in range(3):
                        hh = 3 * gk + j
                        self.op("dve", lambda e, o=rec[:, j * 128:(j + 1) * 128], i_=den[:, j * 128:(j + 1) * 128],
                                s=self.esink[0:64, hh:hh + 1]:
                                e.tensor_scalar(out=o, in0=i_, scalar1=s, scalar2=None, op0=ALU.add), [kn, "esink"], [kr])
                    self.op("dve", lambda e, o=rec: e.reciprocal(out=o, in_=o), [kr], [kr])
                    self.op("dve", lambda e, o=o_sb[:, 3 * gk:3 * gk + 3, qi * 128:(qi + 1) * 128],
                            a=num.rearrange("p (h q) -> p h q", h=3), b_=rec.rearrange("p (h q) -> p h q", h=3):
                            e.tensor_tensor(out=o, in0=a, in1=b_, op=ALU.mult), [kn, kr], ["at_o"])
            self.dma(AOf[256:1024, t0:t0 + TG].rearrange("(h d) t -> d h t", d=64), o_sb, ["at_o"], [("AOa", g)])

    def stage_even_out(self, X, aps, li, npost, AO):
        xv, xn, sq, kX, kXN, kSQ = self.views()
        for g in range(NG):
            t0 = g * TG
            self.dma(xn, AO[:, :, t0:t0 + TG].rearrange("c p t -> p c t"), [("AOp", g), ("AOa", g)], kXN)
            self.proj_post_residual(X, g, xn, kXN, 8, ("ev_wout", li), npost, False)

    def stg4(self):
        i = self.stg_i % 4
        self.stg_i += 1
        t = [self.bT[0], self.bT[1], self.bSG[0], self.bSG[1]][i]
        k = [("bT", 0), ("bT", 1), ("bSG", 0), ("bSG", 1)][i]
        return t, k

    def odd_consts(self, aps, li):
        self.dma(self.dww[:], aps["cv_dww"][li], ["dww"], ["dww"])
        self.dma(self.cvvec[:], aps["cv_vec"][li], ["cvvec"], ["cvvec"])
        self.dma(self.scw[:], aps["ssm_cw"][li], ["scw"], ["scw"])
        self.dma(self.dexp[:], aps["ssm_dexp"][li], ["dexp"], ["dexp"])
        self.dma(self.snw[:], aps["ssm_nw"][li], ["snw"], ["snw"])
        self.dma(self.abcol[:], aps["ssm_ab"][li], ["abcol"], ["abcol"])
        self.op("act", lambda e, o=self.abcol[:, 0:1]: e.activation(out=o, in_=o, func=AF.Exp), ["abcol"], ["abcol"])
        self.op("dve", lambda e, o=self.abcol[:, 0:1]: e.tensor_scalar(out=o, in0=o, scalar1=-1.0, scalar2=None, op0=ALU.mult),
                ["abcol"], ["abcol"])
        self.dma(self.abrow[:, 0, :], aps["ssm_A_log"][li].partition_broadcast(128), ["abrow"], ["abrow"])
        self.dma(self.abrow[:, 1, :], aps["ssm_dt_bias"][li].partition_broadcast(128), ["abrow"], ["abrow"])
        self.op("act", lambda e, o=self.abrow[:, 0, :]: e.activation(out=o, in_=o, func=AF.Exp), ["abrow"], ["abrow"])
        self.op("dve", lambda e, o=self.abrow[:, 0, :]: e.tensor_scalar(out=o, in0=o, scalar1=-1.0, scalar2=None, op0=ALU.mult),
                ["abrow"], ["abrow"])

    def stage_odd_in(self, X, aps, li, npre, G, ZS, XBCr, DTr, DTt):
        Win = ("od_win", li)
        for g in range(NG):
            t0 = g * TG
            xn, kXN = self.load_norm(X, g, npre)
            for i in range(16):
                kw, w = self.load_w(Win, 0, 8, i * 256, 256)
                pgs = []
                for mm in range(2):
                    kp, pv = self.psum_group()
                    for k in range(8):
                        for b in range(NB):
                            self.op("pe", lambda e, o=pv[:, b, :], w_=w[:, k, mm * 128:(mm + 1) * 128], r=xn[:, k, b * 512:(b + 1) * 512], st=(k == 0), sp=(k == 7):
                                    e.matmul(o, lhsT=w_, rhs=r, start=st, stop=sp), [kw, kXN[k]], [kp])
                    pgs.append((kp, pv))
                if i < 4:
                    s1, k1 = self.stg4()
                    s2, k2 = self.stg4()
                    self.op("act", lambda e, o=s1[:], i_=self.flat(pgs[1][1]): e.activation(out=o, in_=i_, func=AF.Sigmoid),
                            [pgs[1][0]], [k1])
                    self.op("dve", lambda e, o=s2[:], a=self.flat(pgs[0][1]), b_=s1[:]: e.tensor_tensor(out=o, in0=a, in1=b_, op=ALU.mult),
                            [pgs[0][0], k1], [k2])
                    self.dma(G[i, :, t0:t0 + TG], s2[:], [k2], [("G", g)])
                elif i < 8:
                    for mm in range(2):
                        s1, k1 = self.stg4()
                        self.op("act", lambda e, o=s1[:], i_=self.flat(pgs[mm][1]): e.activation(out=o, in_=i_, func=AF.Silu),
                                [pgs[mm][0]], [k1])
                        self.dma(ZS[(i - 4) * 2 + mm, :, t0:t0 + TG], s1[:], [k1], [("ZS", g)])
                else:
                    for mm in range(2):
                        s1, k1 = self.stg4()
                        if mm == 0:
                            self.op("act", lambda e, o=s1[:], i_=self.flat(pgs[mm][1]): e.copy(out=o, in_=i_), [pgs[mm][0]], [k1])
                        else:
                            self.op("dve", lambda e, o=s1[:], i_=self.flat(pgs[mm][1]): e.tensor_copy(out=o, in_=i_), [pgs[mm][0]], [k1])
                        self.dma(XBCr[(i - 8) * 2 + mm, :, t0:t0 + TG], s1[:], [k1], [("XBCr", g)])
            kw, w = self.load_w(Win, 0, 8, 4096, 32)
            kp, pv = self.psum_group()
            for k in range(8):
                for b in range(NB):
                    self.op("pe", lambda e, o=pv[0:32, b, :], w_=w[:, k, :], r=xn[:, k, b * 512:(b + 1) * 512], st=(k == 0), sp=(k == 7):
                            e.matmul(o, lhsT=w_, rhs=r, start=st, stop=sp), [kw, kXN[k]], [kp])
            s1, k1 = self.stg4()
            self.op("act", lambda e, o=s1[0:32, :], i_=self.flat(pv)[0:32, :]: e.copy(out=o, in_=i_), [kp], [k1])
            self.dma(DTr[:, t0:t0 + TG], s1[0:32, :], [k1], [("DTr", g)])
            kp, pv = self.psum_group()
            pf = self.flat(pv)
            for tb in range(8):
                for k in range(8):
                    self.op("pe", lambda e, o=pf[:, tb * 32:(tb + 1) * 32], l=xn[:, k, tb * 128:(tb + 1) * 128], r=w[:, k, :], st=(k == 0), sp=(k == 7):
                            e.matmul(o, lhsT=l, rhs=r, start=st, stop=sp), [kw, kXN[k]], [kp])
            s1, k1 = self.stg4()
            self.op("dve", lambda e, o=s1[:, 0:256], i_=pf[:, 0:256]: e.tensor_copy(out=o, in_=i_), [kp], [k1])
            self.dma(DTt[:, g * 8:(g + 1) * 8, :],
                     s1[:, 0:256].rearrange("p (tb j) -> p tb j", tb=8), [k1], [("DTt", g)])

    def stage_odd_conv(self, aps, li, G, XBCr, XBC, MO):
        hf = self.bH[:].bitcast(F32)
        acc = hf[:, 0:4 * TG].rearrange("p (c t) -> p c t", c=4)
        sqr = hf[:, 4 * TG:8 * TG].rearrange("p (c t) -> p c t", c=4)
        mo = self.bXN[:, 0:4 * TG].rearrange("p (c t) -> p c t", c=4)
        mean = self.bR[:, 0:TG]
        rstd = self.bR[:, TG:2 * TG]
        LG = TG + 30
        gp = self.bX[:, 0:4 * LG].rearrange("p (c t) -> p c t", c=4)
        for g in range(NG):
            t0 = g * TG
            lo = max(t0 - 15, 0)
            hi = min(t0 + TG + 15, S)
            j0 = lo - (t0 - 15)
            self.op("pool", lambda e, o=gp: e.memset(o, 0.0), [], ["cv_gp"])
            self.dma(gp[:, :, j0:j0 + hi - lo], G[:, :, lo:hi].rearrange("c p t -> p c t"),
                     ["cv_gp"] + [("G", gg) for gg in (g - 1, g, g + 1) if 0 <= gg < NG], ["cv_gp"])
            for c in range(4):
                self.op("dve", lambda e, o=acc[:, c, :], i_=gp[:, c, 0:TG], s1=self.dww[:, c, 0:1], s2=self.cvvec[:, 0, c:c + 1]:
                        e.tensor_scalar(out=o, in0=i_, scalar1=s1, scalar2=s2, op0=ALU.mult, op1=ALU.add),
                        ["cv_gp", "dww", "cvvec"], [("cv_acc", c)])
            for k in range(1, 31):
                for c in range(4):
                    self.op("dve", lambda e, o=acc[:, c, :], i_=gp[:, c, k:k + TG], s=self.dww[:, c, k:k + 1]:
                            e.scalar_tensor_tensor(out=o, in0=i_, scalar=s, in1=o, op0=ALU.mult, op1=ALU.add),
                            ["cv_gp", "dww", ("cv_acc", c)], [("cv_acc", c)])
            for c in range(4):
                self.op("act", lambda e, o=sqr[:, c, :], i_=acc[:, c, :]: e.activation(out=o, in_=i_, func=AF.Square),
                        [("cv_acc", c)], [("cv_sq", c)])
            k1, p1 = self.psum_group()
            k2, p2 = self.psum_group()
            for b in range(NB):
                for c in range(4):
                    self.op("pe", lambda e, o=p1[:, b, :], r=acc[:, c, b * 512:(b + 1) * 512], st=(c == 0), sp=(c == 3):
                            e.matmul(o, lhsT=self.ones_f[:], rhs=r, start=st, stop=sp), ["ones_f", ("cv_acc", c)], [k1])
            for b in range(NB):
                for c in range(4):
                    self.op("pe", lambda e, o=p2[:, b, :], r=sqr[:, c, b * 512:(b + 1) * 512], st=(c == 0), sp=(c == 3):
                            e.matmul(o, lhsT=self.ones_f[:], rhs=r, start=st, stop=sp), ["ones_f", ("cv_sq", c)], [k2])
            self.op("act", lambda e, o=mean, i_=self.flat(p1): e.mul(out=o, in_=i_, mul=1.0 / 512), [k1], [("bR", 0)])
            msq = sqr[:, 0, :]
            self.op("act", lambda e, o=msq, i_=mean: e.activation(out=o, in_=i_, func=AF.Square), [("bR", 0), k2], [("cv_sq", 0)])
            self.op("dve", lambda e, o=rstd, i_=self.flat(p2), m=msq:
                    e.scalar_tensor_tensor(out=o, in0=i_, scalar=1.0 / 512, in1=m, op0=ALU.mult, op1=ALU.subtract),
                    [k2, ("cv_sq", 0)], [("bR", TG)])
            self.op("act", lambda e, o=rstd: e.activation(out=o, in_=o, func=AF.Sqrt, scale=1.0, bias=EPS), [("bR", TG)], [("bR", TG)])
            self.op("dve", lambda e, o=rstd: e.reciprocal(out=o, in_=o), [("bR", TG)], [("bR", TG)])
            for c in range(4):
                self.op("dve", lambda e, o=acc[:, c, :], m=mean: e.tensor_tensor(out=o, in0=o, in1=m, op=ALU.subtract),
                        [("cv_acc", c), ("bR", 0)], [("cv_acc", c)])
                self.op("dve", lambda e, o=acc[:, c, :], s=self.cvvec[:, 1, c:c + 1], r=rstd:
                        e.scalar_tensor_tensor(out=o, in0=o, scalar=s, in1=r, op0=ALU.mult, op1=ALU.mult),
                        [("cv_acc", c), ("bR", TG), "cvvec"], [("cv_acc", c)])
                self.op("act", lambda e, o=mo[:, c, :], i_=acc[:, c, :], bb=self.cvvec[:, 2, c:c + 1]:
                        e.activation(out=o, in_=i_, func=AF.Silu, bias=bb), [("cv_acc", c), "cvvec"], [("bXN", c)])
            self.dma(MO[:, :, t0:t0 + TG].rearrange("c p t -> p c t"), mo, [("bXN", c) for c in range(4)], [("MO", g)])
        LX = TG + 3
        xp = self.bX[:, 0:4 * LX].rearrange("p (c t) -> p c t", c=4)
        for g in range(NG):
            t0 = g * TG
            lo = max(t0 - 2, 0)
            hi = min(t0 + TG + 1, S)
            j0 = lo - (t0 - 2)
            for cb in range(4):
                self.op("pool", lambda e, o=xp: e.memset(o, 0.0), [], ["cv_gp"])
                self.dma(xp[:, :, j0:j0 + hi - lo], XBCr[cb * 4:cb * 4 + 4, :, lo:hi].rearrange("c p t -> p c t"),
                         ["cv_gp"] + [("XBCr", gg) for gg in (g - 1, g, g + 1) if 0 <= gg < NG], ["cv_gp"])
                for cc in range(4):
                    c = cb * 4 + cc
                    self.op("dve", lambda e, o=acc[:, cc, :], i_=xp[:, cc, 0:TG], s1=self.scw[:, c, 0:1], s2=self.scw[:, c, 4:5]:
                            e.tensor_scalar(out=o, in0=i_, scalar1=s1, scalar2=s2, op0=ALU.mult, op1=ALU.add),
                            ["cv_gp", "scw"], [("cv_acc", cc)])
                for k in range(1, 4):
                    for cc in range(4):
                        c = cb * 4 + cc
                        self.op("dve", lambda e, o=acc[:, cc, :], i_=xp[:, cc, k:k + TG], s=self.scw[:, c, k:k + 1]:
                                e.scalar_tensor_tensor(out=o, in0=i_, scalar=s, in1=o, op0=ALU.mult, op1=ALU.add),
                                ["cv_gp", "scw", ("cv_acc", cc)], [("cv_acc", cc)])
                for cc in range(4):
                    self.op("act", lambda e, o=acc[:, cc, :]: e.activation(out=o, in_=o, func=AF.Silu),
                            [("cv_acc", cc)], [("cv_acc", cc)])
                self.dma(XBC[cb * 4:cb * 4 + 4, :, t0:t0 + TG].rearrange("c p t -> p c t"), acc,
                         [("cv_acc", cc) for cc in range(4)], [("XBC", g)])

    def stage_odd_dt(self, aps, li, DTr, DTt):
        xf = self.bXN[:].bitcast(F32)
        dtk = xf[:, 0:1024]
        dtd = xf[:, 1024:2048]
        cdk = xf[:, 2048:3072]
        a_tok = xf[:, 3072:4096]
        sf = self.bSQ[:].bitcast(F32)
        acs_tok = sf[:, 0:1024]
        raw = sf[:, 1024:2048]
        hf = self.bH[:].bitcast(F32)
        AT = hf[0:48, 0:S]
        c1 = hf[0:48, S:2 * S]
        acsT = self.bX[0:48, 0:S]
        nacsT = self.bX[0:48, S:2 * S]
        dm = lambda t: t.rearrange("p (d c h) -> p c d h", d=2, h=16)
        self.dma(raw, DTt.rearrange("p c j -> p (c j)"), [("DTt", g) for g in range(NG)], ["dtraw"])
        for d in range(2):
            self.op("dve", lambda e, d=d: e.tensor_tensor(
                out=dtk[:, d * 512:(d + 1) * 512].rearrange("p (c h) -> p c h", h=16),
                in0=raw.rearrange("p (c j) -> p c j", j=32)[:, :, d * 16:(d + 1) * 16],
                in1=self.abrow[:, 1:2, d * 16:(d + 1) * 16].broadcast_to([128, 32, 16]), op=ALU.add),
                ["dtraw", "abrow"], ["dtk"])
        self.op("act", lambda e: e.activation(out=dtk, in_=dtk, func=AF.Exp), ["dtk"], ["dtk"])
        self.op("act", lambda e: e.activation(out=dtk, in_=dtk, func=AF.Ln, bias=1.0), ["dtk"], ["dtk"])
        for d in range(2):
            self.op("dve", lambda e, d=d: e.tensor_tensor(
                out=a_tok[:, d * 512:(d + 1) * 512].rearrange("p (c h) -> p c h", h=16),
                in0=dtk[:, d * 512:(d + 1) * 512].rearrange("p (c h) -> p c h", h=16),
                in1=self.abrow[:, 0:1, d * 16:(d + 1) * 16].broadcast_to([128, 32, 16]), op=ALU.mult),
                ["dtk", "abrow"], ["a_tok"])
        import os
        dcut = int(os.environ.get("DT_CUT", "99"))
        if dcut <= 1:
            return
        k1, p1 = self.psum_group()
        for d in range(2):
            self.op("pe", lambda e, d=d: e.matmul(p1[:, d, :], lhsT=self.tri[:, d, :], rhs=a_tok[:, d * 512:(d + 1) * 512],
                                                   start=True, stop=True), ["a_tok", "tri"], [k1])
        self.op("dve", lambda e: e.tensor_copy(out=acs_tok, in_=self.flat(p1)), [k1], ["acs_tok"])
        k2, p2 = self.psum_group()
        for d in range(2):
            self.op("pe", lambda e, d=d: e.matmul(p2[:, d, :], lhsT=self.ones_f[:], rhs=a_tok[:, d * 512:(d + 1) * 512],
                                                   start=True, stop=True), ["a_tok", "ones_f"], [k2])
        self.op("act", lambda e: e.activation(out=cdk, in_=self.flat(p2), func=AF.Exp), [k2], ["cdk"])
        self.op("dve", lambda e: e.tensor_tensor(out=dtd, in0=self.flat(p2), in1=acs_tok, op=ALU.subtract), [k2, "acs_tok"], ["dtd"])
        self.op("act", lambda e: e.activation(out=dtd, in_=dtd, func=AF.Exp), ["dtd"], ["dtd"])
        self.op("dve", lambda e: e.tensor_tensor(out=dtd, in0=dtd, in1=dtk, op=ALU.mult), ["dtd", "dtk"], ["dtd"])
        if dcut <= 2:
            return
        self.op("pool", lambda e: e.memset(AT, 0.0), [], ["AT"])
        self.dma(AT[0:16, :], DTr[0:16, :], ["AT"] + [("DTr", g) for g in range(NG)], ["AT"])
        self.dma(AT[32:48, :], DTr[16:32, :], ["AT"], ["AT"])
        self.op("act", lambda e: e.activation(out=AT, in_=AT, func=AF.Exp, bias=self.abcol[:, 1:2]), ["AT", "abcol"], ["AT"])
        self.op("act", lambda e: e.activation(out=AT, in_=AT, func=AF.Ln, bias=1.0), ["AT"], ["AT"])
        self.op("dve", lambda e: e.tensor_scalar(out=AT, in0=AT, scalar1=self.abcol[:, 0:1], scalar2=None, op0=ALU.mult),
                ["AT", "abcol"], ["AT"])
        if dcut <= 3:
            return
        c3 = lambda t: t.rearrange("p (c l) -> p c l", l=128)
        srcb, dstb = AT, c1
        names = {id(AT): "AT", id(c1): "c1", id(acsT): "acsT"}
        bufs = [c1, acsT]
        cur, curk = AT, "AT"
        step = 1
        n = 0
        while step < 128:
            dst = bufs[n % 2]
            dk = ["c1", "acsT"][n % 2]
            self.op("dve", lambda e, dst=dst, cur=cur, step=step: e.tensor_copy(out=c3(dst)[:, :, 0:step], in_=c3(cur)[:, :, 0:step]),
                    [curk], [dk])
            self.op("dve", lambda e, dst=dst, cur=cur, step=step: e.tensor_tensor(
                out=c3(dst)[:, :, step:128], in0=c3(cur)[:, :, step:128], in1=c3(cur)[:, :, 0:128 - step], op=ALU.add),
                [curk], [dk])
            cur, curk = dst, dk
            step *= 2
            n += 1
        cs = cur
        csk = curk
        self.op("dve", lambda e: e.tensor_copy(out=acsT[0:16, :], in_=cs[0:16, :]), [csk], ["acsT"])
        self.op("dve", lambda e: e.tensor_tensor(out=c3(acsT)[32:48], in0=c3(cs)[32:48, :, 127:128].broadcast_to([16, 32, 128]),
                                                 in1=c3(cs)[32:48], op=ALU.subtract), [csk, "acsT"], ["acsT"])
        self.op("dve", lambda e: e.tensor_tensor(out=acsT[32:48, :], in0=acsT[32:48, :], in1=AT[32:48, :], op=ALU.add),
                ["acsT", "AT"], ["acsT"])
        self.op("dve", lambda e: e.tensor_scalar(out=nacsT[0:16, :], in0=acsT[0:16, :], scalar1=-1.0, scalar2=None, op0=ALU.mult),
                ["acsT"], ["nacsT"])
        self.op("dve", lambda e: e.tensor_scalar(out=nacsT[32:48, :], in0=acsT[32:48, :], scalar1=-1.0, scalar2=None, op0=ALU.mult),
                ["acsT"], ["nacsT"])

    def stage_odd_ssd(self, aps, XBC, Y, d):
        xf = self.bXN[:].bitcast(F32)
        dtk = xf[:, 0:1024].rearrange("p (d c h) -> p d c h", d=2, h=16)
        dtd = xf[:, 1024:2048].rearrange("p (d c h) -> p d c h", d=2, h=16)
        cdk = xf[:, 2048:3072].rearrange("p (d c h) -> p d c h", d=2, h=16)
        acsT = self.bX[0:48, 0:S]
        nacsT = self.bX[0:48, S:2 * S]
        r0 = 0 if d == 0 else 32
        state = self.bR[:, 0:1024]
        state_bf = self.bR[:, 1024:1536].bitcast(BF16)
        hb = self.bH
        tokb = hb[:, 0:1536].rearrange("p (c f) -> p c f", c=12)
        Xdt = hb[:, 1536:2560].rearrange("p (h q) -> p h q", h=16)
        Xdd = hb[:, 2560:3584].rearrange("p (h q) -> p h q", h=16)
        MT = [hb[:, 3584 + i * 512:3584 + (i + 1) * 512].rearrange("p (h l) -> p h l", h=4) for i in range(2)]
        Cs = [hb[:, 4608 + i * 512:4608 + (i + 1) * 512].rearrange("p (h l) -> p h l", h=4) for i in range(2)]
        hf = hb[:].bitcast(F32)
        Lf = [hf[:, 3072 + i * 512:3072 + (i + 1) * 512] for i in range(2)]
        Ef = [hf[:, 4096 + i * 512:4096 + (i + 1) * 512] for i in range(2)]
        rhs2 = [hf[0:48, 5120 + i * 512:5120 + (i + 1) * 512] for i in range(2)]
        yst = [self.bT[0], self.bT[1], self.bSG[0], self.bSG[1]]
        ysk = [("bT", 0), ("bT", 1), ("bSG", 0), ("bSG", 1)]
        self.op("pool", lambda e: e.memset(self.bR[:, 0:1536], 0.0), [], [("st", g) for g in range(4)] + [("stb", g) for g in range(4)])
        order = range(S // 128) if d == 0 else range(S // 128 - 1, -1, -1)
        it = 0
        for c in order:
            sl = slice(c * 128, (c + 1) * 128)
            fi = it % 2
            it += 1
            fx = self.bSQ[:].bitcast(F32)[:, fi * 2048:(fi + 1) * 2048].rearrange("p (c t) -> p c t", c=16)
            fb = hb[:, 12288 + fi * 2048:12288 + (fi + 1) * 2048].rearrange("p (c t) -> p c t", c=16)
            self.dma(fx, XBC[:, :, sl].rearrange("c p t -> p c t"), [("XBC", c // (TG // 128))], [("sfx", fi)])
            self.op("pool", lambda e, o=fb, i_=fx: e.tensor_copy(out=o, in_=i_), [("sfx", fi)], [("sfb", fi)])
            kp, pv = self.psum_group()
            pt = self.flat(pv).bitcast(BF16)
            for i in range(12):
                self.op("pe", lambda e, o=pt[:, i * 128:(i + 1) * 128], i_=fb[:, i, :]: e.transpose(out=o, in_=i_, identity=self.ident_b[:]),
                        [("sfb", fi), "ident_b"], [kp])
            self.op("act", lambda e, o=hb[:, 0:1536], i_=pt[:, 0:1536]: e.copy(out=o, in_=i_), [kp], ["tokb"])
            xs3 = hb[:, 0:1024].rearrange("p (h q) -> p h q", h=16)
            self.op("dve", lambda e, c=c: e.tensor_tensor(out=Xdt, in0=xs3, in1=dtk[:, d, c, :].unsqueeze(2).broadcast_to([128, 16, 64]), op=ALU.mult),
                    ["tokb", "dtk"], ["Xdt"])
            self.op("pool", lambda e, c=c: e.tensor_tensor(out=Xdd, in0=xs3, in1=dtd[:, d, c, :].unsqueeze(2).broadcast_to([128, 16, 64]), op=ALU.mult),
                    ["tokb", "dtd"], ["Xdd"])
            for g in range(4):
                bi = (it * 4 + g) % 2
                ka, pa = self.psum_group()
                pfa = self.flat(pa)
                self.op("pe", lambda e, o=pfa[:, 0:128], l=fb[:, 8 + g, :], r=fb[:, 12 + g, :]: e.matmul(o, lhsT=l, rhs=r, start=True, stop=True),
                        [("sfb", fi)], [ka])
                r2 = rhs2[bi][r0:r0 + 16, :]
                kr2 = ("rhs2", bi)
                self.op("dve", lambda e, o=r2.rearrange("p (h l) -> p h l", h=4), sl=sl, g=g:
                        e.tensor_tensor(out=o, in0=acsT[r0:r0 + 16, sl].unsqueeze(1).broadcast_to([16, 4, 128]),
                                        in1=self.selm[r0:r0 + 16, g, :, :], op=ALU.mult), ["acsT", "selm"], [kr2])
                kb_, pb_ = self.psum_group()
                pfb = self.flat(pb_)
                self.op("pe", lambda e, o=pfb[:, 0:512], sl=sl, g=g: e.matmul(o, lhsT=nacsT[r0:r0 + 16, sl], rhs=self.selm[r0:r0 + 16, g, :, :].rearrange("p h l -> p (h l)"),
                                                                 start=True, stop=False), ["nacsT", "selm"], [kb_])
                self.op("pe", lambda e, o=pfb[:, 0:512], r=r2: e.matmul(o, lhsT=self.ones_f[r0:r0 + 16, :], rhs=r, start=False, stop=False),
                        ["ones_f", kr2], [kb_])
                self.op("pe", lambda e, o=pfb[:, 0:512]: e.matmul(o, lhsT=self.ident_b[:], rhs=self.mbias[:, d, :], start=False, stop=True),
                        ["ident_b", "mbias"], [kb_])
                L = Lf[bi]
                kL = ("Lf", bi)
                self.op("act", lambda e, o=L, i_=pfb[:, 0:512]: e.activation(out=o, in_=i_, func=AF.Exp), [kb_], [kL])
                mt = MT[bi]
                kM = ("MT", bi)
                self.op("dve", lambda e, o=mt, a=L.rearrange("p (h l) -> p h l", h=4), b_=pfa[:, 0:128].unsqueeze(1).broadcast_to([128, 4, 128]):
                        e.tensor_tensor(out=o, in0=a, in1=b_, op=ALU.mult), [kL, ka], [kM])
                kc_, pc_ = self.psum_group()
                pfc = self.flat(pc_)
                self.op("pe", lambda e, o=pfc[:, 0:512], r=r2: e.matmul(o, lhsT=self.ones_f[r0:r0 + 16, :], rhs=r, start=True, stop=True),
                        ["ones_f", kr2], [kc_])
                E = Ef[bi]
                kE = ("Ef", bi)
                self.op("act", lambda e, o=E, i_=pfc[:, 0:512]: e.activation(out=o, in_=i_, func=AF.Exp), [kc_], [kE])
                cs_ = Cs[bi]
                kC = ("Cs", bi)
                self.op("pool", lambda e, o=cs_, a=E.rearrange("p (h l) -> p h l", h=4), b_=fb[:, 12 + g, :].unsqueeze(1).broadcast_to([128, 4, 128]):
                        e.tensor_tensor(out=o, in0=a, in1=b_, op=ALU.mult), [kE, ("sfb", fi)], [kC])
                kd, pd = self.psum_group()
                pfd = self.flat(pd)
                for hh in range(4):
                    head = 4 * g + hh
                    self.op("pe", lambda e, o=pfd[0:64, hh * 128:(hh + 1) * 128], l=Xdt[:, head, :], r=mt[:, hh, :]:
                            e.matmul(o, lhsT=l, rhs=r, start=True, stop=False), ["Xdt", kM], [kd])
                    self.op("pe", lambda e, o=pfd[0:64, hh * 128:(hh + 1) * 128], l=state_bf[:, head * 64:(head + 1) * 64], r=cs_[:, hh, :]:
                            e.matmul(o, lhsT=l, rhs=r, start=False, stop=True), [("stb", g), kC], [kd])
                ys = yst[g][0:64, 0:512]
                self.op("act", lambda e, o=ys, i_=pfd[0:64, 0:512]: e.copy(out=o, in_=i_), [kd], [ysk[g]])
                self.dma(Y[g * 256:(g + 1) * 256, sl].rearrange("(h p) l -> p h l", p=64), ys.rearrange("p (h l) -> p h l", h=4),
                         [ysk[g]], [("Y", d, c // (TG // 128))])
                ke, pe_ = self.psum_group()
                pfe = self.flat(pe_)
                self.op("pe", lambda e, o=pfe[:, 0:256], l=tokb[:, 8 + g, :], r=Xdd[:, 4 * g:4 * g + 4, :]:
                        e.matmul(o, lhsT=l, rhs=r, start=True, stop=True), ["tokb", "Xdd"], [ke])
                stg = state[:, g * 256:(g + 1) * 256]
                self.op("dve", lambda e, o=stg.rearrange("p (h q) -> p h q", h=4), c=c, g=g:
                        e.tensor_tensor(out=o, in0=o, in1=cdk[:, d, c, 4 * g:4 * g + 4].unsqueeze(2).broadcast_to([128, 4, 64]), op=ALU.mult),
                        [("st", g), "cdk"], [("st", g)])
                self.op("dve", lambda e, o=stg, i_=pfe[:, 0:256]: e.tensor_tensor(out=o, in0=o, in1=i_, op=ALU.add),
                        [("st", g), ke], [("st", g)])
                self.op("act", lambda e, o=state_bf[:, g * 256:(g + 1) * 256], i_=stg: e.copy(out=o, in_=i_), [("st", g)], [("stb", g)])

    def stage_odd_out(self, X, aps, li, npost, Yf, Yb, XBC, ZS, MO):
        xv, xn_, sq, kX, kXN_, kSQ = self.views()
        rhs = self.bH[:, 0:12 * TG].rearrange("p (c t) -> p c t", c=12)
        rkeys = [("bH", m) for m in range(12)]
        for g in range(NG):
            t0 = g * TG
            self.dma(rhs[:, 0:4, :], MO[:, :, t0:t0 + TG].rearrange("c p t -> p c t"), [("MO", g)], rkeys[0:4])
            for c in range(8):
                a, ka = self.stg4()
                b, kb = self.stg4()
                xs, kx = self.stg4()
                z, kz = self.stg4()
                self.dma(a[:], Yf[c * 128:(c + 1) * 128, t0:t0 + TG], [("Y", 0, g)], [ka])
                self.dma(b[:], Yb[c * 128:(c + 1) * 128, t0:t0 + TG], [("Y", 1, g)], [kb])
                self.dma(xs[:], XBC[c, :, t0:t0 + TG], [("XBC", g)], [kx])
                self.dma(z[:], ZS[c, :, t0:t0 + TG], [("ZS", g)], [kz])
                self.op("pool", lambda e, o=a[:], b_=b[:]: e.tensor_tensor(out=o, in0=o, in1=b_, op=ALU.add), [ka, kb], [ka])
                self.op("dve", lambda e, o=a[:], x_=xs[:], s=self.dexp[:, c:c + 1]:
                        e.scalar_tensor_tensor(out=o, in0=x_, scalar=s, in1=o, op0=ALU.mult, op1=ALU.add), [ka, kx, "dexp"], [ka])
                self.op("dve", lambda e, o=xv[:, c, :], a_=a[:], z_=z[:]: e.tensor_tensor(out=o, in0=a_, in1=z_, op=ALU.mult),
                        [ka, kz], [kX[c]])
                self.op("act", lambda e, o=sq[:, c, :], i_=xv[:, c, :]: e.activation(out=o, in_=i_, func=AF.Square),
                        [kX[c]], [kSQ[c]])
            rk, rv = self.rms_rinv(sq, 8, 0, 1.0 / D, EPS, kSQ)
            for c in range(8):
                self.op("dve", lambda e, o=rhs[:, 4 + c, :], i_=xv[:, c, :], s=self.snw[:, c:c + 1], r=rv:
                        e.scalar_tensor_tensor(out=o, in0=i_, scalar=s, in1=r, op0=ALU.mult, op1=ALU.mult),
                        [kX[c], "snw", rk], [rkeys[4 + c]])
            self.proj_post_residual(X, g, rhs, rkeys, 12, ("od_wout", li), npost, False)

def build_program(stop=None, only=None):
    _, plan = _build(stop, only, None)
    nc, _ = _build(stop, only, plan)
    return nc


def _build(stop, only, wplan):
    from contextlib import ExitStack
    nc = bass.Bass("TRN2", target_bir_lowering=False)
    dt = nc.dram_tensor
    aps = {}

    def inp(name, shape):
        aps[name] = dt(name, shape, F32, kind="ExternalInput").ap()

    inp("x", [S, D])
    inp("normw", [128, DEPTH * 6 * 8])
    import os
    nff = 1 if os.environ.get("DEV_SMALL") else DEPTH * 2
    inp("wg", [nff, D, DFF])
    inp("wu", [nff, D, DFF])
    inp("wd", [nff, DFF, D])
    inp("ev_win", [2, D, 2304])
    inp("ev_wv", [2, D, 256])
    inp("ev_pool_w", [2, 4, 64, 64])
    inp("ev_pool_scale", [2, 128, 2])
    inp("ev_sink", [2, 12])
    inp("ev_wout", [2, D, D])
    inp("ropec", [128, S])
    inp("ropes", [128, S])
    inp("icnt", [2, 128, S])
    inp("amask", [128, 2, 384])
    inp("od_win", [2, D, 4128])
    inp("od_wout", [2, 1536, D])
    inp("cv_dww", [2, 128, 4, 31])
    inp("cv_vec", [2, 128, 3, 4])
    inp("ssm_cw", [2, 128, 16, 5])
    inp("ssm_dexp", [2, 128, 8])
    inp("ssm_nw", [2, 128, 8])
    inp("ssm_ab", [2, 48, 2])
    inp("ssm_A_log", [2, 32])
    inp("ssm_dt_bias", [2, 32])
    inp("tri", [128, 2, 128])
    inp("selm", [48, 4, 4, 128])
    inp("mbias", [128, 2, 512])
    inp("rmask", [48, 128])
    out = dt("out", [S, D], F32, kind="ExternalOutput").ap()
    X = dt("Xs", [8, 128, S], F32, kind="Internal").ap()
    QT = dt("QTs", [768, S], BF16, kind="Internal").ap()
    KT = dt("KTs", [256, S], BF16, kind="Internal").ap()
    V = dt("Vs", [S, 256], BF16, kind="Internal").ap()
    U = dt("Us", [2, 128, S], F32, kind="Internal").ap()
    AO = dt("AOs", [8, 128, S], BF16, kind="Internal").ap()
    Gs = dt("Gs", [4, 128, S], F32, kind="Internal").ap()
    ZS = dt("ZSs", [8, 128, S], F32, kind="Internal").ap()
    XBCr = dt("XBCrs", [16, 128, S], F32, kind="Internal").ap()
    XBC = dt("XBCs", [16, 128, S], F32, kind="Internal").ap()
    DTr = dt("DTrs", [32, S], F32, kind="Internal").ap()
    DTt = dt("DTts", [128, 32, 32], F32, kind="Internal").ap()
    Yf = dt("Yfs", [1024, S], F32, kind="Internal").ap()
    Yb = dt("Ybs", [1024, S], F32, kind="Internal").ap()
    MO = dt("MOs", [4, 128, S], BF16, kind="Internal").ap()
    B = Builder(nc, stop, aps, wplan)
    B.init_consts(aps)
    B.stage_load(aps["x"], X)
    nst = 0
    seq = [(l, sub) for l in range(DEPTH) for sub in range(3)]
    if stop is not None:
        seq = seq[:stop]
    if only is not None:
        seq = only
    for (l, sub) in seq:
        base = l * 6 * 8
        if True:
            if sub == 0:
                B.sc.barrier(); B.stage_ffn(X, ("wg", l * 2), ("wu", l * 2), ("wd", l * 2), base + 0, base + 8)
            elif sub == 2:
                B.sc.barrier(); B.stage_ffn(X, ("wg", l * 2 + 1), ("wu", l * 2 + 1), ("wd", l * 2 + 1), base + 32, base + 40)
            elif l % 2 == 0:
                li = l // 2
                B.even_consts(aps, li)
                B.sc.barrier(); B.stage_even_in(X, aps, li, base + 16, QT, KT, V, U)
                B.sc.barrier(); B.stage_even_pool(aps, U, AO)
                B.sc.barrier(); B.stage_even_attn(aps, li, QT, KT, V, AO)
                B.sc.barrier(); B.stage_even_out(X, aps, li, base + 24, AO)
            else:
                li = l // 2
                import os
                ocut = int(os.environ.get("ODD_CUT", "99"))
                B.odd_consts(aps, li)
                oskip = int(os.environ.get("ODD_SKIP", "0"))
                if ocut >= 1 and oskip < 1:
                    B.sc.barrier(); B.stage_odd_in(X, aps, li, base + 16, Gs, ZS, XBCr, DTr, DTt)
                if ocut >= 2 and oskip < 2:
                    B.sc.barrier(); B.stage_odd_conv(aps, li, Gs, XBCr, XBC, MO)
                if ocut >= 3:
                    B.sc.barrier(); B.stage_odd_dt(aps, li, DTr, DTt)
                if ocut >= 4:
                    B.sc.barrier(); B.stage_odd_ssd(aps, XBC, Yf, 0)
                if ocut >= 5:
                    B.sc.barrier(); B.stage_odd_ssd(aps, XBC, Yb, 1)
                if ocut >= 6:
                    B.sc.barrier(); B.stage_odd_out(X, aps, li, base + 24, Yf, Yb, XBC, ZS, MO)
    B.sc.barrier(); B.stage_store(X, out)
    fin = [("out", g) for g in range(NG)]
    if os.environ.get("DBG"):
        def dbg(name, ap, keys):
            o = dt("dbg_" + name, list(ap.shape), ap.dtype, kind="ExternalOutput").ap()
            B.dma(o, ap, keys, [("dbg", name)])
            fin.append(("dbg", name))
        G4 = range(NG)
        dbg("G", Gs, [("G", g) for g in G4])
        dbg("XBC", XBC, [("XBC", g) for g in G4])
        dbg("MO", MO, [("MO", g) for g in G4])
        dbg("ZS", ZS, [("ZS", g) for g in G4])
        dbg("Yf", Yf, [("Y", 0, g) for g in G4])
        dbg("Yb", Yb, [("Y", 1, g) for g in G4])
        dbg("DTr", DTr, [("DTr", g) for g in G4])
        dbg("tok", B.bXN[:].bitcast(F32)[:, 0:3072], ["dtk", "dtd", "cdk"])
        dbg("acsT", B.bX[0:48, 0:S], ["acsT"])
    B.op("sp", lambda e: e.nop(), fin, [])
    if wplan is None:
        return None, B.wrec
    with ExitStack() as st:
        B.sc.emit(st)
    return nc, B.wrec


def rope_tables_np():
    inv = (1.0 / (np.float32(10000.0) ** (np.arange(0, 64, 2, dtype=np.float32) / np.float32(64)))).astype(np.float32)
    ang = (np.arange(S, dtype=np.float32)[:, None] * inv[None, :]).astype(np.float32)
    cos = np.cos(ang).astype(np.float32)
    sin = np.sin(ang).astype(np.float32)
    cf = np.concatenate([cos, cos], axis=1)
    sf = np.concatenate([-sin, sin], axis=1)
    cT = np.ascontiguousarray(np.concatenate([cf, cf], axis=1).T)
    sT = np.ascontiguousarray(np.concatenate([sf, sf], axis=1).T)
    return cT, sT


def const_tables():
    cT, sT = rope_tables_np()
    t = np.arange(S)
    icnt = np.zeros((2, 128, S), np.float32)
    for gi, w in enumerate((2, 4, 8, 16)):
        lo = np.clip(t - w // 2, 0, S)
        hi = np.clip(t + w - w // 2, 0, S)
        icnt[gi // 2, (gi % 2) * 64:(gi % 2) * 64 + 64, :] = (1.0 / (hi - lo).astype(np.float32))[None, :]
    kl = np.arange(128)[:, None]
    ql = np.arange(128)[None, :]
    m1 = (ql <= kl).astype(np.float32)
    m2 = (kl <= ql).astype(np.float32)
    amask = np.stack([np.tile(m1, (1, 3)), np.tile(m2, (1, 3))], axis=1)
    s_ = np.arange(128)[:, None]
    l_ = np.arange(128)[None, :]
    tri = np.stack([(s_ <= l_), (s_ >= l_)], axis=1).astype(np.float32)
    selm = np.zeros((48, 4, 4, 128), np.float32)
    for k in range(16):
        selm[k, k // 4, k % 4, :] = 1.0
        selm[32 + k, k // 4, k % 4, :] = 1.0
    NEG = -30000.0
    mb = np.stack([np.tile(np.where(l_ >= s_, 0.0, NEG), (1, 4)), np.tile(np.where(s_ >= l_, 0.0, NEG), (1, 4))], axis=1)
    rmask = np.ones((48, 128), np.float32)
    rmask[:, 0] = 0.0
    return {"ropec": cT, "ropes": sT, "icnt": icnt, "amask": np.ascontiguousarray(amask),
            "tri": np.ascontiguousarray(tri), "selm": selm, "mbias": np.ascontiguousarray(mb.astype(np.float32)),
            "rmask": rmask}


def make_in_maps(inputs):
    nw = np.ascontiguousarray(
        inputs["norm_w"].reshape(DEPTH, 6, 8, 128).transpose(3, 0, 1, 2).reshape(128, DEPTH * 6 * 8))
    qk = np.arange(256, 1280)
    rel = (qk - 256) % 64
    partner = qk - rel + np.where(rel < 32, rel + 32, rel - 32)
    cols = []
    for i in range(8):
        cols.append(qk[i * 128:(i + 1) * 128])
        cols.append(partner[i * 128:(i + 1) * 128])
    cols.append(np.arange(0, 256))
    cols = np.concatenate(cols)
    shared = {
        "normw": nw,
        "wg": np.ascontiguousarray(inputs["ffn_w_gate"].reshape(DEPTH * 2, D, DFF)),
        "wu": np.ascontiguousarray(inputs["ffn_w_up"].reshape(DEPTH * 2, D, DFF)),
        "wd": np.ascontiguousarray(inputs["ffn_w_down"].reshape(DEPTH * 2, DFF, D)),
        "ev_win": np.ascontiguousarray(inputs["ev_w_in"][:, :, cols]),
        "ev_wv": np.ascontiguousarray(inputs["ev_w_in"][:, :, 1280:1536]),
        "ev_pool_w": np.ascontiguousarray(inputs["ev_pool_w"]),
        "ev_pool_scale": np.ascontiguousarray(inputs["ev_pool_scale"].reshape(2, 2, 128).transpose(0, 2, 1)),
        "ev_sink": np.ascontiguousarray(inputs["ev_sink"]),
        "ev_wout": np.ascontiguousarray(inputs["ev_w_out"]),
    }
    ocols = []
    for m in range(4):
        ocols.append(np.arange(m * 128, (m + 1) * 128))
        ocols.append(np.arange(512 + m * 128, 512 + (m + 1) * 128))
    ocols.append(np.arange(1024, 4128))
    ocols = np.concatenate(ocols)
    f32 = np.float32
    ab = np.zeros((2, 48, 2), f32)
    ab[:, 0:16, 0] = inputs["ssm_A_log"][:, 0]
    ab[:, 32:48, 0] = inputs["ssm_A_log"][:, 1]
    ab[:, 0:16, 1] = inputs["ssm_dt_bias"][:, 0]
    ab[:, 32:48, 1] = inputs["ssm_dt_bias"][:, 1]
    cw = np.concatenate([inputs["ssm_conv_w"].reshape(2, 4, 16, 128).transpose(0, 3, 2, 1),
                         inputs["ssm_conv_b"].reshape(2, 16, 128).transpose(0, 2, 1)[..., None]], axis=-1)
    shared.update({
        "od_win": np.ascontiguousarray(inputs["od_w_in"][:, :, ocols]),
        "od_wout": np.ascontiguousarray(inputs["od_w_out"]),
        "cv_dww": np.ascontiguousarray(inputs["cv_dw_w"].reshape(2, 31, 4, 128).transpose(0, 3, 2, 1)),
        "cv_vec": np.ascontiguousarray(np.stack([inputs["cv_dw_b"], inputs["cv_ln_g"], inputs["cv_ln_b"]], axis=1)
                                       .reshape(2, 3, 4, 128).transpose(0, 3, 1, 2)),
        "ssm_cw": np.ascontiguousarray(cw.astype(f32)),
        "ssm_dexp": np.ascontiguousarray(np.repeat(inputs["ssm_D"], 64, axis=1).reshape(2, 8, 128).transpose(0, 2, 1)),
        "ssm_nw": np.ascontiguousarray(inputs["ssm_norm_w"].reshape(2, 8, 128).transpose(0, 2, 1)),
        "ssm_ab": ab,
        "ssm_A_log": np.ascontiguousarray(inputs["ssm_A_log"].reshape(2, 32)),
        "ssm_dt_bias": np.ascontiguousarray(inputs["ssm_dt_bias"].reshape(2, 32)),
    })
    shared.update(const_tables())
    maps = []
    for b in range(NCORES):
        m = dict(shared)
        m["x"] = np.ascontiguousarray(inputs["x"][b])
        maps.append(m)
    return maps


def kernel(**inputs):
    inputs = {k: np.asarray(v) for k, v in inputs.items()}
    nc = build_program()
    in_maps = make_in_maps(inputs)
    res = run_bass_kernel_spmd(nc, in_maps, core_ids=list(range(NCORES)))
    return np.stack([np.asarray(r["out"]) for r in res.results], axis=0).astype(np.float32)
```

```python
import numpy as np
import concourse.bass as bass
import concourse.mybir as mybir
from concourse.bass_utils import run_bass_kernel_spmd

F32 = mybir.dt.float32
BF16 = mybir.dt.bfloat16
ALU = mybir.AluOpType
AF = mybir.ActivationFunctionType

D = 1024
S = 4096
DEPTH = 4
DFF = 2816
NCORES = 8
EPS = 1e-6
TG = 1024
NG = S // TG
NB = TG // 512


class Op:
    __slots__ = ("eng", "fn", "dma", "deps", "idx", "sem", "semval", "sig", "presem")


class Sched:
    ENG = ("pe", "act", "dve", "pool", "sp")

    def __init__(self, nc):
        self.nc = nc
        self.ops = []
        self.last_w = {}
        self.readers = {}
        self.dma_since = []
        self.last_on = {}

    def barrier(self):
        frontier = set(self.dma_since)
        for e, idx in self.last_on.items():
            frontier.add(idx)
        self.dma_since = []
        for e in self.ENG:
            self.add(e, lambda eng: eng.nop(), extra=frontier)

    def add(self, eng, fn, reads=(), writes=(), dma=False, extra=()):
        op = Op()
        op.eng = eng
        op.fn = fn
        op.dma = dma
        op.idx = len(self.ops)
        op.sig = False
        op.sem = None
        op.semval = 0
        op.presem = None
        deps = {}
        for k in reads:
            w = self.last_w.get(k)
            if w is not None:
                deps[w] = True
            if isinstance(k, tuple) and k[0] == "ps":
                for r in self.readers.get(k, ()):
                    if self.ops[r].eng != eng:
                        deps[r] = True
        for k in writes:
            w = self.last_w.get(k)
            if w is not None:
                deps.setdefault(w, False)
            for r in self.readers.get(k, ()):
                deps.setdefault(r, False)
        for k in reads:
            self.readers.setdefault(k, []).append(op.idx)
        for k in writes:
            self.last_w[k] = op.idx
            self.readers[k] = []
        for x in extra:
            deps[x] = True
        deps.pop(op.idx, None)
        op.deps = deps
        self.ops.append(op)
        if dma:
            self.dma_since.append(op.idx)
        else:
            self.last_on[eng] = op.idx
        return op

    def _needs_sync(self, c, p, raw):
        if p.dma:
            return True
        if c.eng != p.eng:
            return True
        if c.dma:
            return True
        if c.eng == "pe":
            return False
        return raw

    def emit(self, stack):
        nc = self.nc
        ops = self.ops
        for c in ops:
            for pi, raw in c.deps.items():
                p = ops[pi]
                if not p.dma and self._needs_sync(c, p, raw):
                    p.sig = True
        NDMA = 40
        dma_sems = [stack.enter_context(nc.semaphore("dq%d" % i)) for i in range(NDMA)]
        dma_cum = [0] * NDMA
        dma_rr = 0
        EPOCH = 20000
        eng_sems = {e: [] for e in self.ENG}
        eng_cnt = {e: 0 for e in self.ENG}
        for op in ops:
            if op.dma:
                i = dma_rr % NDMA
                dma_rr += 1
                op.sem = dma_sems[i]
                op.presem = (dma_sems[i], dma_cum[i]) if dma_cum[i] > 0 else None
                dma_cum[i] += 16
                op.semval = dma_cum[i]
            elif op.sig:
                e = op.eng
                if eng_cnt[e] % EPOCH == 0:
                    eng_sems[e].append(stack.enter_context(
                        nc.semaphore("e_%s_%d" % (e, len(eng_sems[e])))))
                eng_cnt[e] += 1
                op.sem = eng_sems[e][-1]
                op.semval = (eng_cnt[e] - 1) % EPOCH + 1
        per = {e: [] for e in self.ENG}
        for op in ops:
            per[op.eng].append(op)

        def run(engname, eng):
            waited = {}
            for op in per[engname]:
                ws = []
                for pi, raw in op.deps.items():
                    p = ops[pi]
                    if self._needs_sync(op, p, raw):
                        ws.append((p.sem, p.semval))
                if op.presem is not None:
                    ws.append(op.presem)
                for sem, val in ws:
                    key = id(sem)
                    if waited.get(key, 0) >= val:
                        continue
                    waited[key] = val
                    eng.wait_ge(sem, val)
                ins = op.fn(eng)
                if op.dma:
                    ins.then_inc(op.sem, 16)
                elif op.sig:
                    ins.then_inc(op.sem, 1)

        with nc.Block() as block:
            @block.tensor
            def _(e):
                run("pe", e)

            @block.scalar
            def _(e):
                run("act", e)

            @block.vector
            def _(e):
                run("dve", e)

            @block.gpsimd
            def _(e):
                run("pool", e)

            @block.sync
            def _(e):
                run("sp", e)


class Builder:
    def __init__(self, nc, stop=None, aps=None, wplan=None):
        self.nc = nc
        self.aps = aps
        self.wplan = wplan
        self.wrec = []
        self.w_issued = 0
        self.PF = 2
        self.sc = Sched(nc)
        self.stop = stop
        self.uid = 0
        nc_ = nc
        A = nc_.alloc_sbuf_tensor
        self.bX = A("bX", [128, 8 * TG], F32)
        self.bXN = A("bXN", [128, 8 * TG], BF16)
        self.bSQ = A("bSQ", [128, 8 * TG], BF16)
        self.bH = A("bH", [128, 22 * TG], BF16)
        self.NWS = 4
        self.wsF = [A("wsF%d" % i, [128, 2048], F32) for i in range(self.NWS)]
        self.wsB = [A("wsB%d" % i, [128, 2048], BF16) for i in range(self.NWS)]
        self.ws_i = 0
        self.bR = A("bR", [128, 2 * TG], F32)
        self.bSG = [A("bSG%d" % i, [128, TG], F32) for i in range(2)]
        self.sg_i = 0
        self.bT = [A("bT%d" % i, [128, TG], F32) for i in range(2)]
        self.t_i = 0
        self.ones_bf = A("ones_bf", [128, 128], BF16)
        self.ident_f = A("ident_f", [128, 128], F32)
        self.normw = A("normw_sb", [128, DEPTH * 6 * 8], F32)
        self.poolW = A("poolW", [128, 2, 128], BF16)
        self.poolS = A("poolS", [128, 2], F32)
        self.esink = A("esink", [128, 12], F32)
        self.amask = A("amask_sb", [128, 2, 384], F32)
        self.stg_i = 0
        self.ones_f = A("ones_f", [128, 128], F32)
        self.ident_b = A("ident_b", [128, 128], BF16)
        self.dww = A("dww", [128, 4, 31], F32)
        self.cvvec = A("cvvec", [128, 3, 4], F32)
        self.scw = A("scw", [128, 16, 5], F32)
        self.dexp = A("dexp", [128, 8], F32)
        self.snw = A("snw", [128, 8], F32)
        self.abcol = A("abcol", [48, 2], F32)
        self.abrow = A("abrow", [128, 2, 32], F32)
        self.tri = A("tri_sb", [128, 2, 128], F32)
        self.selm = A("selm_sb", [48, 4, 4, 128], F32)
        self.mbias = A("mbias_sb", [128, 2, 512], BF16)
        self.ps = nc_.alloc_psum_tensor("ps", [128, 8, 512], F32)
        self.ps_i = 0
        self.NPG = 8 // NB

    def op(self, eng, fn, reads=(), writes=(), dma=False):
        return self.sc.add(eng, fn, reads, writes, dma)

    def psum_group(self):
        g = self.ps_i % self.NPG
        self.ps_i += 1
        key = ("ps", g)
        view = self.ps[:, g * NB:(g + 1) * NB, :]
        return key, view

    def dma(self, out, in_, reads, writes, eng="sp"):
        return self.op(eng, lambda e, o=out, i=in_: e.dma_start(out=o, in_=i), reads, writes, dma=True)

    def _issue_w(self, j, spec):
        name, idx, r0, nk, c0, ncols = spec
        W = self.aps[name][idx]
        i = j % self.NWS
        fv = self.wsF[i][:, 0:nk * ncols].rearrange("p (k m) -> p k m", k=nk)
        bv = self.wsB[i][:, 0:nk * ncols].rearrange("p (k m) -> p k m", k=nk)
        src = W[r0:r0 + nk * 128, c0:c0 + ncols].rearrange("(k p) m -> p k m", p=128)
        self.dma(fv, src, reads=[], writes=[("wsF", i)])
        self.op("pool", lambda e, o=bv, s=fv: e.tensor_copy(out=o, in_=s),
                reads=[("wsF", i)], writes=[("wsB", i)])

    def load_w(self, Wd, r0, nk, c0, ncols):
        j = self.ws_i
        self.ws_i += 1
        spec = (Wd[0], Wd[1], r0, nk, c0, ncols)
        if self.wplan is None:
            self.wrec.append(spec)
            self._issue_w(j, spec)
        else:
            assert self.wplan[j] == spec, (j, self.wplan[j], spec)
            upto = min(j + self.PF, len(self.wplan) - 1)
            while self.w_issued <= upto:
                self._issue_w(self.w_issued, self.wplan[self.w_issued])
                self.w_issued += 1
        i = j % self.NWS
        bv = self.wsB[i][:, 0:nk * ncols].rearrange("p (k m) -> p k m", k=nk)
        return ("wsB", i), bv

    def init_consts(self, aps):
        nc = self.nc
        self.op("pool", lambda e: e.memset(self.ones_bf[:], 1.0), [], ["ones_bf"])
        self.op("pool", lambda e: e.memset(self.ident_f[:], 0.0), [], ["ident_f"])
        self.op("pool", lambda e: e.affine_select(
            out=self.ident_f[:], in_=self.ident_f[:], pattern=[[-1, 128]],
            compare_op=ALU.not_equal, fill=1.0, base=0, channel_multiplier=1),
            ["ident_f"], ["ident_f"])
        self.dma(self.normw[:], aps["normw"], [], ["normw"])
        self.dma(self.amask[:], aps["amask"], [], ["amask"])
        self.op("pool", lambda e: e.memset(self.ones_f[:], 1.0), [], ["ones_f"])
        self.op("pool", lambda e: e.tensor_copy(out=self.ident_b[:], in_=self.ident_f[:]), ["ident_f"], ["ident_b"])
        self.dma(self.tri[:], aps["tri"], [], ["tri"])
        self.dma(self.selm[:], aps["selm"], [], ["selm"])
        st_ = self.bX[:, 0:1024].rearrange("p (d t) -> p d t", d=2)
        self.dma(st_, aps["mbias"], [], [("bX", 0)])
        self.op("pool", lambda e: e.tensor_copy(out=self.mbias[:], in_=st_), [("bX", 0)], ["mbias"])

    def flat(self, pv):
        return pv.rearrange("p b t -> p (b t)")

    def stage_load(self, x, X):
        hf = self.bH[:].bitcast(F32)
        xv = self.bX[:].rearrange("p (c t) -> p c t", c=8)
        for g in range(NG):
            tv = hf[:, 0:8 * D].rearrange("p (tb f) -> p tb f", tb=8)
            self.dma(tv, x[g * TG:(g + 1) * TG, :].rearrange("(tb p) f -> p tb f", p=128),
                     [], ["bH"])
            n = 0
            for tb in range(8):
                for c0 in range(0, 8, 4):
                    key, pv = self.psum_group()
                    pf = self.flat(pv)
                    for cc in range(4):
                        c = c0 + cc
                        self.op("pe", lambda e, o=pf[:, cc * 128:(cc + 1) * 128], i=tv[:, tb, c * 128:(c + 1) * 128]:
                                e.transpose(out=o, in_=i, identity=self.ident_f[:]),
                                ["bH", "ident_f"], [key])
                    src = pf[:, 0:512].rearrange("p (c t) -> p c t", c=4)
                    dst = xv[:, c0:c0 + 4, tb * 128:(tb + 1) * 128]
                    if n % 2 == 0:
                        self.op("act", lambda e, o=dst, i=src: e.copy(out=o, in_=i), [key],
                                [("bX", c0 + k) for k in range(4)])
                    else:
                        self.op("dve", lambda e, o=dst, i=src: e.tensor_copy(out=o, in_=i), [key],
                                [("bX", c0 + k) for k in range(4)])
                    n += 1
            self.dma(X[:, :, g * TG:(g + 1) * TG].rearrange("c p t -> p c t"), xv,
                     [("bX", c) for c in range(8)], [("X", g, c) for c in range(8)])

    def stage_store(self, X, out):
        hf = self.bH[:].bitcast(F32)
        xv = self.bX[:].rearrange("p (c t) -> p c t", c=8)
        for g in range(NG):
            tv = hf[:, 0:8 * D].rearrange("p (tb f) -> p tb f", tb=8)
            self.dma(xv, X[:, :, g * TG:(g + 1) * TG].rearrange("c p t -> p c t"),
                     [("X", g, c) for c in range(8)], [("bX", c) for c in range(8)])
            n = 0
            for tb in range(8):
                for c0 in range(0, 8, 4):
                    key, pv = self.psum_group()
                    pf = self.flat(pv)
                    for cc in range(4):
                        c = c0 + cc
                        self.op("pe", lambda e, o=pf[:, cc * 128:(cc + 1) * 128], i=xv[:, c, tb * 128:(tb + 1) * 128]:
                                e.transpose(out=o, in_=i, identity=self.ident_f[:]),
                                [("bX", c), "ident_f"], [key])
                    dst = tv[:, tb, c0 * 128:(c0 + 4) * 128]
                    src = pf[:, 0:512]
                    if n % 2 == 0:
                        self.op("act", lambda e, o=dst, i=src: e.copy(out=o, in_=i), [key], ["bH"])
                    else:
                        self.op("dve", lambda e, o=dst, i=src: e.tensor_copy(out=o, in_=i), [key], ["bH"])
                    n += 1
            self.dma(out[g * TG:(g + 1) * TG, :].rearrange("(tb p) f -> p tb f", p=128), tv,
                     ["bH"], [("out", g)])

    def rms_rinv(self, sqv, nchunks, off, scale, bias, sqkeys):
        key, pv = self.psum_group()
        for b in range(NB):
            for c in range(nchunks):
                self.op("pe", lambda e, o=pv[:, b, :], r=sqv[:, c, b * 512:(b + 1) * 512], st=(c == 0), sp=(c == nchunks - 1):
                        e.matmul(o, lhsT=self.ones_bf[:], rhs=r, start=st, stop=sp),
                        ["ones_bf", sqkeys[c]], [key])
        rv = self.bR[:, off:off + TG]
        rk = ("bR", off)
        self.op("act", lambda e, o=rv, i=self.flat(pv): e.activation(out=o, in_=i, func=AF.Sqrt, scale=scale, bias=bias),
                [key], [rk])
        self.op("dve", lambda e, o=rv: e.reciprocal(out=o, in_=o), [rk], [rk])
        return rk, rv

    def views(self):
        xv = self.bX[:].rearrange("p (c t) -> p c t", c=8)
        xn = self.bXN[:].rearrange("p (c t) -> p c t", c=8)
        sq = self.bSQ[:].rearrange("p (c t) -> p c t", c=8)
        kX = [("bX", c) for c in range(8)]
        kXN = [("bXN", c) for c in range(8)]
        kSQ = [("bSQ", c) for c in range(8)]
        return xv, xn, sq, kX, kXN, kSQ

    def load_norm(self, X, g, npre):
        xv, xn, sq, kX, kXN, kSQ = self.views()
        t0 = g * TG
        self.dma(xv, X[:, :, t0:t0 + TG].rearrange("c p t -> p c t"),
                 [("X", g, c) for c in range(8)], kX)
        for c in range(8):
            self.op("act", lambda e, o=sq[:, c, :], i=xv[:, c, :]: e.activation(out=o, in_=i, func=AF.Square),
                    [kX[c]], [kSQ[c]])
        rk, rv = self.rms_rinv(sq, 8, 0, 1.0 / D, EPS, kSQ)
        for c in range(8):
            self.op("dve", lambda e, o=xn[:, c, :], i=xv[:, c, :], s=self.normw[:, npre + c:npre + c + 1], r=rv:
                    e.scalar_tensor_tensor(out=o, in0=i, scalar=s, in1=r, op0=ALU.mult, op1=ALU.mult),
                    [kX[c], "normw", rk], [kXN[c]])
        return xn, kXN

    def proj_post_residual(self, X, g, rhs, rkeys, KC, W, npost, half):
        xv, xn, sq, kX, kXN, kSQ = self.views()
        nkt = (KC + 7) // 8
        for ct in range(4):
            grp = [self.psum_group(), self.psum_group()]
            for kt in range(nkt):
                nk = min(8, KC - kt * 8)
                kw, w = self.load_w(W, kt * 1024, nk, ct * 256, 256)
                for mm in range(2):
                    kp, pv = grp[mm]
                    for kk in range(nk):
                        kc = kt * 8 + kk
                        for b in range(NB):
                            self.op("pe", lambda e, o=pv[:, b, :], w_=w[:, kk, mm * 128:(mm + 1) * 128], r=rhs[:, kc, b * 512:(b + 1) * 512], st=(kc == 0), sp=(kc == KC - 1):
                                    e.matmul(o, lhsT=w_, rhs=r, start=st, stop=sp), [kw, rkeys[kc]], [kp])
            for mm in range(2):
                oc = ct * 2 + mm
                kp, pv = grp[mm]
                self.op("dve", lambda e, o=xv[:, oc, :], i=self.flat(pv): e.tensor_copy(out=o, in_=i),
                        [kp], [kX[oc]])
                self.op("act", lambda e, o=sq[:, oc, :], i=xv[:, oc, :]: e.activation(out=o, in_=i, func=AF.Square),
                        [kX[oc]], [kSQ[oc]])
        if half:
            rk2, rv2 = self.rms_rinv(sq, 8, TG, 4.0 / D, 4.0 * EPS, kSQ)
        else:
            rk2, rv2 = self.rms_rinv(sq, 8, TG, 1.0 / D, EPS, kSQ)
        self.residual_add(X, g, xv, kX, rk2, rv2, npost)

    def stage_ffn(self, X, Wg, Wu, Wd, npre, npost):
        h = self.bH[:].rearrange("p (c t) -> p c t", c=22)
        for g in range(NG):
            xn, kXN = self.load_norm(X, g, npre)
            for ct in range(11):
                kg, wg = self.load_w(Wg, 0, 8, ct * 256, 256)
                ku, wu = self.load_w(Wu, 0, 8, ct * 256, 256)
                for mm in range(2):
                    m = ct * 2 + mm
                    kpg, pg = self.psum_group()
                    for k in range(8):
                        for b in range(NB):
                            self.op("pe", lambda e, o=pg[:, b, :], w=wg[:, k, mm * 128:(mm + 1) * 128], r=xn[:, k, b * 512:(b + 1) * 512], st=(k == 0), sp=(k == 7):
                                    e.matmul(o, lhsT=w, rhs=r, start=st, stop=sp), [kg, kXN[k]], [kpg])
                    kpu, pu = self.psum_group()
                    for k in range(8):
                        for b in range(NB):
                            self.op("pe", lambda e, o=pu[:, b, :], w=wu[:, k, mm * 128:(mm + 1) * 128], r=xn[:, k, b * 512:(b + 1) * 512], st=(k == 0), sp=(k == 7):
                                    e.matmul(o, lhsT=w, rhs=r, start=st, stop=sp), [ku, kXN[k]], [kpu])
                    si = self.sg_i % 2
                    self.sg_i += 1
                    sg = self.bSG[si]
                    self.op("act", lambda e, o=sg[:], i=self.flat(pg): e.activation(out=o, in_=i, func=AF.Silu),
                            [kpg], [("bSG", si)])
                    self.op("dve", lambda e, o=h[:, m, :], a=sg[:], b_=self.flat(pu): e.tensor_tensor(out=o, in0=a, in1=b_, op=ALU.mult),
                            [("bSG", si), kpu], [("bH", m)])
            self.proj_post_residual(X, g, h, [("bH", m) for m in range(22)], 22, Wd, npost, True)

    def residual_add(self, X, g, yv, kY, rk, rv, npost):
        t0 = g * TG
        for oc in range(8):
            ti = self.t_i % 2
            self.t_i += 1
            xr = self.bT[ti]
            kt_ = ("bT", ti)
            self.dma(xr[:], X[oc, :, t0:t0 + TG], [("X", g, oc)], [kt_])
            self.op("dve", lambda e, o=yv[:, oc, :], s=self.normw[:, npost + oc:npost + oc + 1], r=rv:
                    e.scalar_tensor_tensor(out=o, in0=o, scalar=s, in1=r, op0=ALU.mult, op1=ALU.mult),
                    [kY[oc], "normw", rk], [kY[oc]])
            self.op("pool", lambda e, o=xr[:], b_=yv[:, oc, :]: e.tensor_tensor(out=o, in0=o, in1=b_, op=ALU.add),
                    [kt_, kY[oc]], [kt_])
            self.dma(X[oc, :, t0:t0 + TG], xr[:], [kt_], [("X", g, oc)])


    def even_consts(self, aps, li):
        stg = self.wsF[0][:, 0:256].rearrange("p (c m) -> p c m", c=2)
        stg = self.bT[0][:, 0:256].rearrange("p (c m) -> p c m", c=2)
        kt_ = ("bT", 0)
        self.op("pool", lambda e, o=stg: e.memset(o, 0.0), [], [kt_])
        for c in range(2):
            for gg in range(2):
                self.dma(stg[gg * 64:(gg + 1) * 64, c, gg * 64:(gg + 1) * 64], aps["ev_pool_w"][li, 2 * c + gg],
                         [kt_], [kt_])
        self.op("pool", lambda e, o=self.poolW[:], i=stg: e.tensor_copy(out=o, in_=i), [kt_], ["poolW"])
        self.dma(self.poolS[:], aps["ev_pool_scale"][li], ["poolS"], ["poolS"])
        self.dma(self.esink[:], aps["ev_sink"][li].partition_broadcast(128), ["esink"], ["esink"])
        self.op("act", lambda e, o=self.esink[:]: e.activation(out=o, in_=o, func=AF.Exp), ["esink"], ["esink"])

    def stage_even_in(self, X, aps, li, npre, QT, KT, V, U):
        xv, xn_, sq, kX, kXN_, kSQ = self.views()
        Win = ("ev_win", li)
        Wv = ("ev_wv", li)
        sqb = sq
        vst = self.bH[:, 0:8 * 256].rearrange("p (tb f) -> p tb f", tb=8)
        ust = self.bR[:].rearrange("p (c t) -> p c t", c=2)
        for g in range(NG):
            t0 = g * TG
            xn, kXN = self.load_norm(X, g, npre)
            cosT = self.bSG[0]
            sinT = self.bSG[1]
            self.dma(cosT[:], aps["ropec"][:, t0:t0 + TG], [], [("bSG", 0)])
            self.dma(sinT[:], aps["ropes"][:, t0:t0 + TG], [], [("bSG", 1)])
            for i in range(8):
                kw, w = self.load_w(Win, 0, 8, i * 256, 256)
                kq, pq = self.psum_group()
                for k in range(8):
                    for b in range(NB):
                        self.op("pe", lambda e, o=pq[:, b, :], w_=w[:, k, 0:128], r=xn[:, k, b * 512:(b + 1) * 512], st=(k == 0), sp=(k == 7):
                                e.matmul(o, lhsT=w_, rhs=r, start=st, stop=sp), [kw, kXN[k]], [kq])
                kp, pp = self.psum_group()
                for k in range(8):
                    for b in range(NB):
                        self.op("pe", lambda e, o=pp[:, b, :], w_=w[:, k, 128:256], r=xn[:, k, b * 512:(b + 1) * 512], st=(k == 0), sp=(k == 7):
                                e.matmul(o, lhsT=w_, rhs=r, start=st, stop=sp), [kw, kXN[k]], [kp])
                t1 = self.bT[0]
                t2 = self.bT[1]
                self.op("dve", lambda e, o=t1[:], a=self.flat(pq), c_=cosT[:]: e.tensor_tensor(out=o, in0=a, in1=c_, op=ALU.mult),
                        [kq, ("bSG", 0)], [("bT", 0)])
                self.op("dve", lambda e, o=t2[:], a=self.flat(pp), c_=sinT[:]: e.tensor_tensor(out=o, in0=a, in1=c_, op=ALU.mult),
                        [kp, ("bSG", 1)], [("bT", 1)])
                self.op("pool", lambda e, o=sqb[:, i, :], a=t1[:], b_=t2[:]: e.tensor_tensor(out=o, in0=a, in1=b_, op=ALU.add),
                        [("bT", 0), ("bT", 1)], [kSQ[i]])
                if i < 6:
                    self.dma(QT[i * 128:(i + 1) * 128, t0:t0 + TG], sqb[:, i, :], [kSQ[i]], [("QT", g)])
                else:
                    self.dma(KT[(i - 6) * 128:(i - 5) * 128, t0:t0 + TG], sqb[:, i, :], [kSQ[i]], [("KT", g)])
            kw, w = self.load_w(Win, 0, 8, 2048, 256)
            for mm in range(2):
                kp, pv = self.psum_group()
                for k in range(8):
                    for b in range(NB):
                        self.op("pe", lambda e, o=pv[:, b, :], w_=w[:, k, mm * 128:(mm + 1) * 128], r=xn[:, k, b * 512:(b + 1) * 512], st=(k == 0), sp=(k == 7):
                                e.matmul(o, lhsT=w_, rhs=r, start=st, stop=sp), [kw, kXN[k]], [kp])
                self.op("act", lambda e, o=ust[:, mm, :], i_=self.flat(pv): e.copy(out=o, in_=i_), [kp], [("bR", mm * TG)])
                self.dma(U[mm, :, t0:t0 + TG], ust[:, mm, :], [("bR", mm * TG)], [("U", g)])
            kw, w = self.load_w(Wv, 0, 8, 0, 256)
            for tb in range(8):
                kp, pv = self.psum_group()
                pf = self.flat(pv)
                for k in range(8):
                    self.op("pe", lambda e, o=pf[:, 0:256], l=xn[:, k, tb * 128:(tb + 1) * 128], r=w[:, k, :], st=(k == 0), sp=(k == 7):
                            e.matmul(o, lhsT=l, rhs=r, start=st, stop=sp), [kw, kXN[k]], [kp])
                if tb % 2 == 0:
                    self.op("act", lambda e, o=vst[:, tb, :], i_=pf[:, 0:256]: e.copy(out=o, in_=i_), [kp], [("bH", "v")])
                else:
                    self.op("dve", lambda e, o=vst[:, tb, :], i_=pf[:, 0:256]: e.tensor_copy(out=o, in_=i_), [kp], [("bH", "v")])
            self.dma(V[t0:t0 + TG, :].rearrange("(tb p) f -> p tb f", p=128), vst, [("bH", "v")], [("V", g)])

    def stage_even_pool(self, aps, U, AO):
        L = TG + 16
        bx = self.bX
        up = bx[:, 0:2 * L].rearrange("p (c t) -> p c t", c=2)
        a1 = bx[:, 2 * L:4 * L].rearrange("p (c t) -> p c t", c=2)
        a2 = bx[:, 4 * L:6 * L].rearrange("p (c t) -> p c t", c=2)
        xnf = self.bXN[:].bitcast(F32)
        a3 = xnf[:, 0:L]
        a4 = xnf[:, L:2 * L]
        icnt = self.bR[:].rearrange("p (c t) -> p c t", c=2)
        res = self.bSQ[:].bitcast(F32)[:, 0:2 * TG].rearrange("p (c t) -> p c t", c=2)
        pb = self.bH[:, 0:2 * TG].rearrange("p (c t) -> p c t", c=2)
        ob = self.bH[:, 2 * TG:4 * TG].rearrange("p (c t) -> p c t", c=2)
        for g in range(NG):
            t0 = g * TG
            lo = max(t0 - 8, 0)
            hi = min(t0 + TG + 8, S)
            j0 = lo - (t0 - 8)
            self.op("pool", lambda e, o=up: e.memset(o, 0.0), [], ["pl_u"])
            self.dma(up[:, :, j0:j0 + hi - lo], U[:, :, lo:hi].rearrange("c p t -> p c t"),
                     ["pl_u"] + [("U", gg) for gg in (g - 1, g, g + 1) if 0 <= gg < NG], ["pl_u"])
            self.dma(icnt, aps["icnt"][:, :, t0:t0 + TG].rearrange("c p t -> p c t"), [], [("bR", 0), ("bR", TG)])
            self.op("dve", lambda e: e.tensor_tensor(out=a1[:, :, 1:L], in0=up[:, :, 0:L - 1], in1=up[:, :, 1:L], op=ALU.add),
                    ["pl_u"], ["pl_a1"])
            self.op("dve", lambda e: e.tensor_tensor(out=a2[:, :, 2:L - 1], in0=a1[:, :, 1:L - 2], in1=a1[:, :, 3:L], op=ALU.add),
                    ["pl_a1"], ["pl_a2"])
            self.op("dve", lambda e: e.tensor_tensor(out=a3[:, 4:L - 3], in0=a2[:, 1, 2:L - 5], in1=a2[:, 1, 6:L - 1], op=ALU.add),
                    ["pl_a2"], ["pl_a3"])
            self.op("dve", lambda e: e.tensor_tensor(out=a4[:, 8:L - 8], in0=a3[:, 4:L - 12], in1=a3[:, 12:L - 4], op=ALU.add),
                    ["pl_a3"], ["pl_a4"])
            self.op("dve", lambda e: e.tensor_tensor(out=res[0:64, 0, :], in0=a1[0:64, 0, 8:TG + 8], in1=icnt[0:64, 0, :], op=ALU.mult),
                    ["pl_a1", ("bR", 0)], ["pl_res"])
            self.op("dve", lambda e: e.tensor_tensor(out=res[64:128, 0, :], in0=a2[64:128, 0, 8:TG + 8], in1=icnt[64:128, 0, :], op=ALU.mult),
                    ["pl_a2", ("bR", 0)], ["pl_res"])
            self.op("dve", lambda e: e.tensor_tensor(out=res[0:64, 1, :], in0=a3[0:64, 8:TG + 8], in1=icnt[0:64, 1, :], op=ALU.mult),
                    ["pl_a3", ("bR", TG)], ["pl_res"])
            self.op("dve", lambda e: e.tensor_tensor(out=res[64:128, 1, :], in0=a4[64:128, 8:TG + 8], in1=icnt[64:128, 1, :], op=ALU.mult),
                    ["pl_a4", ("bR", TG)], ["pl_res"])
            self.op("dve", lambda e: e.tensor_tensor(out=pb, in0=res, in1=up[:, :, 8:TG + 8], op=ALU.subtract),
                    ["pl_res", "pl_u"], ["pl_pb"])
            for c in range(2):
                kp, pv = self.psum_group()
                for b in range(NB):
                    self.op("pe", lambda e, o=pv[:, b, :], l=self.poolW[:, c, :], r=pb[:, c, b * 512:(b + 1) * 512]:
                            e.matmul(o, lhsT=l, rhs=r, start=True, stop=True), ["poolW", "pl_pb"], [kp])
                self.op("dve", lambda e, o=ob[:, c, :], i_=self.flat(pv), s=self.poolS[:, c:c + 1]:
                        e.tensor_scalar(out=o, in0=i_, scalar1=s, scalar2=None, op0=ALU.mult), [kp, "poolS"], [("pl_ob", c)])
                self.dma(AO[c, :, t0:t0 + TG], ob[:, c, :], [("pl_ob", c)], [("AOp", g)])

    def stage_even_attn(self, aps, li, QT, KT, V, AO):
        KW = TG + 256
        q_sb = self.bH[0:64, 0:12 * TG].rearrange("p (h t) -> p h t", h=12)
        k_sb = self.bXN[0:64, 0:4 * KW].rearrange("p (h t) -> p h t", h=4)
        v_sb = self.bSQ[:, 0:10 * 256].rearrange("p (kb f) -> p kb f", kb=10)
        o_sb = self.bX[:].bitcast(BF16)[0:64, 0:12 * TG].rearrange("p (h t) -> p h t", h=12)
        Pb = [self.bSG[0][:].bitcast(BF16)[:, 0:384], self.bSG[1][:].bitcast(BF16)[:, 0:384],
              self.bT[0][:].bitcast(BF16)[:, 0:384], self.bT[1][:].bitcast(BF16)[:, 0:384]]
        Pk = [("bSG", 0), ("bSG", 1), ("bT", 0), ("bT", 1)]
        recs = [self.bR[0:64, 0:384], self.bR[0:64, 512:896]]
        AOf = AO.rearrange("c p t -> (c p) t")
        pi = 0
        ri = 0
        for g in range(NG):
            t0 = g * TG
            lo = max(t0 - 128, 0)
            hi = min(t0 + TG + 128, S)
            j0 = lo - (t0 - 128)
            nbr = [gg for gg in (g - 1, g, g + 1) if 0 <= gg < NG]
            self.dma(q_sb, QT[:, t0:t0 + TG].rearrange("(h d) t -> d h t", d=64), [("QT", g)], ["at_q"])
            self.dma(k_sb[:, :, j0:j0 + hi - lo], KT[:, lo:hi].rearrange("(h d) t -> d h t", d=64),
                     [("KT", gg) for gg in nbr], ["at_k"])
            self.dma(v_sb[:, j0 // 128:(j0 + hi - lo) // 128, :], V[lo:hi, :].rearrange("(kb p) f -> p kb f", p=128),
                     [("V", gg) for gg in nbr], ["at_v"])
            for qi in range(8):
                n = g * 8 + qi
                for gk in range(4):
                    kbs = [kb for kb in (n - 1, n, n + 1) if 0 <= kb < S // 128]
                    Ps = []
                    for kb in kbs:
                        lk = kb - (g * 8 - 1)
                        ks, pv = self.psum_group()
                        pf = self.flat(pv)
                        self.op("pe", lambda e, o=pf[:, 0:384].rearrange("p (h q) -> p h q", h=3), l=k_sb[:, gk, lk * 128:(lk + 1) * 128],
                                r=q_sb[:, 3 * gk:3 * gk + 3, qi * 128:(qi + 1) * 128]:
                                e.matmul(o, lhsT=l, rhs=r, start=True, stop=True), ["at_k", "at_q"], [ks])
                        P = Pb[pi % 4]
                        kP = Pk[pi % 4]
                        pi += 1
                        self.op("act", lambda e, o=P, i_=pf[:, 0:384]: e.activation(out=o, in_=i_, func=AF.Exp, scale=0.125),
                                [ks], [kP])
                        if kb == n - 1:
                            self.op("dve", lambda e, o=P: e.tensor_tensor(out=o, in0=o, in1=self.amask[:, 0, :], op=ALU.mult),
                                    [kP, "amask"], [kP])
                        elif kb == n + 1:
                            self.op("dve", lambda e, o=P: e.tensor_tensor(out=o, in0=o, in1=self.amask[:, 1, :], op=ALU.mult),
                                    [kP, "amask"], [kP])
                        Ps.append((P, kP, lk))
                    kn, pn = self.psum_group()
                    num = pn[0:64, 0, 0:384]
                    den = pn[0:64, 1, 0:384]
                    for idx, (P, kP, lk) in enumerate(Ps):
                        st = (idx == 0)
                        sp = (idx == len(Ps) - 1)
                        self.op("pe", lambda e, o=num, l=v_sb[:, lk, gk * 64:(gk + 1) * 64], r=P, st=st, sp=sp:
                                e.matmul(o, lhsT=l, rhs=r, start=st, stop=sp), ["at_v", kP], [kn])
                        self.op("pe", lambda e, o=den, l=self.ones_bf[:, 0:64], r=P, st=st, sp=sp:
                                e.matmul(o, lhsT=l, rhs=r, start=st, stop=sp), ["ones_bf", kP], [kn])
                    rec = recs[ri % 2]
                    kr = ("rec", ri % 2)
                    ri += 1
                    for j in range(3):
                        hh = 3 * gk + j
                        self.op("dve", lambda e, o=rec[:, j * 128:(j + 1) * 128], i_=den[:, j * 128:(j + 1) * 128],
                                s=self.esink[0:64, hh:hh + 1]:
                                e.tensor_scalar(out=o, in0=i_, scalar1=s, scalar2=None, op0=ALU.add), [kn, "esink"], [kr])
                    self.op("dve", lambda e, o=rec: e.reciprocal(out=o, in_=o), [kr], [kr])
                    self.op("dve", lambda e, o=o_sb[:, 3 * gk:3 * gk + 3, qi * 128:(qi + 1) * 128],
                            a=num.rearrange("p (h q) -> p h q", h=3), b_=rec.rearrange("p (h q) -> p h q", h=3):
                            e.tensor_tensor(out=o, in0=a, in1=b_, op=ALU.mult), [kn, kr], ["at_o"])
            self.dma(AOf[256:1024, t0:t0 + TG].rearrange("(h d) t -> d h t", d=64), o_sb, ["at_o"], [("AOa", g)])

    def stage_even_out(self, X, aps, li, npost, AO):
        xv, xn, sq, kX, kXN, kSQ = self.views()
        for g in range(NG):
            t0 = g * TG
            self.dma(xn, AO[:, :, t0:t0 + TG].rearrange("c p t -> p c t"), [("AOp", g), ("AOa", g)], kXN)
            self.proj_post_residual(X, g, xn, kXN, 8, ("ev_wout", li), npost, False)

    def stg4(self):
        i = self.stg_i % 4
        self.stg_i += 1
        t = [self.bT[0], self.bT[1], self.bSG[0], self.bSG[1]][i]
        k = [("bT", 0), ("bT", 1), ("bSG", 0), ("bSG", 1)][i]
        return t, k

    def odd_consts(self, aps, li):
        self.dma(self.dww[:], aps["cv_dww"][li], ["dww"], ["dww"])
        self.dma(self.cvvec[:], aps["cv_vec"][li], ["cvvec"], ["cvvec"])
        self.dma(self.scw[:], aps["ssm_cw"][li], ["scw"], ["scw"])
        self.dma(self.dexp[:], aps["ssm_dexp"][li], ["dexp"], ["dexp"])
        self.dma(self.snw[:], aps["ssm_nw"][li], ["snw"], ["snw"])
        self.dma(self.abcol[:], aps["ssm_ab"][li], ["abcol"], ["abcol"])
        self.op("act", lambda e, o=self.abcol[:, 0:1]: e.activation(out=o, in_=o, func=AF.Exp), ["abcol"], ["abcol"])
        self.op("dve", lambda e, o=self.abcol[:, 0:1]: e.tensor_scalar(out=o, in0=o, scalar1=-1.0, scalar2=None, op0=ALU.mult),
                ["abcol"], ["abcol"])
        self.dma(self.abrow[:, 0, :], aps["ssm_A_log"][li].partition_broadcast(128), ["abrow"], ["abrow"])
        self.dma(self.abrow[:, 1, :], aps["ssm_dt_bias"][li].partition_broadcast(128), ["abrow"], ["abrow"])
        self.op("act", lambda e, o=self.abrow[:, 0, :]: e.activation(out=o, in_=o, func=AF.Exp), ["abrow"], ["abrow"])
        self.op("dve", lambda e, o=self.abrow[:, 0, :]: e.tensor_scalar(out=o, in0=o, scalar1=-1.0, scalar2=None, op0=ALU.mult),
                ["abrow"], ["abrow"])

    def stage_odd_in(self, X, aps, li, npre, G, ZS, XBCr, DTr, DTt):
        Win = ("od_win", li)
        for g in range(NG):
            t0 = g * TG
            xn, kXN = self.load_norm(X, g, npre)
            for i in range(16):
                kw, w = self.load_w(Win, 0, 8, i * 256, 256)
                pgs = []
                for mm in range(2):
                    kp, pv = self.psum_group()
                    for k in range(8):
                        for b in range(NB):
                            self.op("pe", lambda e, o=pv[:, b, :], w_=w[:, k, mm * 128:(mm + 1) * 128], r=xn[:, k, b * 512:(b + 1) * 512], st=(k == 0), sp=(k == 7):
                                    e.matmul(o, lhsT=w_, rhs=r, start=st, stop=sp), [kw, kXN[k]], [kp])
                    pgs.append((kp, pv))
                if i < 4:
                    s1, k1 = self.stg4()
                    s2, k2 = self.stg4()
                    self.op("act", lambda e, o=s1[:], i_=self.flat(pgs[1][1]): e.activation(out=o, in_=i_, func=AF.Sigmoid),
                            [pgs[1][0]], [k1])
                    self.op("dve", lambda e, o=s2[:], a=self.flat(pgs[0][1]), b_=s1[:]: e.tensor_tensor(out=o, in0=a, in1=b_, op=ALU.mult),
                            [pgs[0][0], k1], [k2])
                    self.dma(G[i, :, t0:t0 + TG], s2[:], [k2], [("G", g)])
                elif i < 8:
                    for mm in range(2):
                        s1, k1 = self.stg4()
                        self.op("act", lambda e, o=s1[:], i_=self.flat(pgs[mm][1]): e.activation(out=o, in_=i_, func=AF.Silu),
                                [pgs[mm][0]], [k1])
                        self.dma(ZS[(i - 4) * 2 + mm, :, t0:t0 + TG], s1[:], [k1], [("ZS", g)])
                else:
                    for mm in range(2):
                        s1, k1 = self.stg4()
                        if mm == 0:
                            self.op("act", lambda e, o=s1[:], i_=self.flat(pgs[mm][1]): e.copy(out=o, in_=i_), [pgs[mm][0]], [k1])
                        else:
                            self.op("dve", lambda e, o=s1[:], i_=self.flat(pgs[mm][1]): e.tensor_copy(out=o, in_=i_), [pgs[mm][0]], [k1])
                        self.dma(XBCr[(i - 8) * 2 + mm, :, t0:t0 + TG], s1[:], [k1], [("XBCr", g)])
            kw, w = self.load_w(Win, 0, 8, 4096, 32)
            kp, pv = self.psum_group()
            for k in range(8):
                for b in range(NB):
                    self.op("pe", lambda e, o=pv[0:32, b, :], w_=w[:, k, :], r=xn[:, k, b * 512:(b + 1) * 512], st=(k == 0), sp=(k == 7):
                            e.matmul(o, lhsT=w_, rhs=r, start=st, stop=sp), [kw, kXN[k]], [kp])
            s1, k1 = self.stg4()
            self.op("act", lambda e, o=s1[0:32, :], i_=self.flat(pv)[0:32, :]: e.copy(out=o, in_=i_), [kp], [k1])
            self.dma(DTr[:, t0:t0 + TG], s1[0:32, :], [k1], [("DTr", g)])
            kp, pv = self.psum_group()
            pf = self.flat(pv)
            for tb in range(8):
                for k in range(8):
                    self.op("pe", lambda e, o=pf[:, tb * 32:(tb + 1) * 32], l=xn[:, k, tb * 128:(tb + 1) * 128], r=w[:, k, :], st=(k == 0), sp=(k == 7):
                            e.matmul(o, lhsT=l, rhs=r, start=st, stop=sp), [kw, kXN[k]], [kp])
            s1, k1 = self.stg4()
            self.op("dve", lambda e, o=s1[:, 0:256], i_=pf[:, 0:256]: e.tensor_copy(out=o, in_=i_), [kp], [k1])
            self.dma(DTt[:, g * 8:(g + 1) * 8, :],
                     s1[:, 0:256].rearrange("p (tb j) -> p tb j", tb=8), [k1], [("DTt", g)])

    def stage_odd_conv(self, aps, li, G, XBCr, XBC, MO):
        hf = self.bH[:].bitcast(F32)
        acc = hf[:, 0:4 * TG].rearrange("p (c t) -> p c t", c=4)
        sqr = hf[:, 4 * TG:8 * TG].rearrange("p (c t) -> p c t", c=4)
        mo = self.bXN[:, 0:4 * TG].rearrange("p (c t) -> p c t", c=4)
        mean = self.bR[:, 0:TG]
        rstd = self.bR[:, TG:2 * TG]
        LG = TG + 30
        gp = self.bX[:, 0:4 * LG].rearrange("p (c t) -> p c t", c=4)
        for g in range(NG):
            t0 = g * TG
            lo = max(t0 - 15, 0)
            hi = min(t0 + TG + 15, S)
            j0 = lo - (t0 - 15)
            self.op("pool", lambda e, o=gp: e.memset(o, 0.0), [], ["cv_gp"])
            self.dma(gp[:, :, j0:j0 + hi - lo], G[:, :, lo:hi].rearrange("c p t -> p c t"),
                     ["cv_gp"] + [("G", gg) for gg in (g - 1, g, g + 1) if 0 <= gg < NG], ["cv_gp"])
            for c in range(4):
                self.op("dve", lambda e, o=acc[:, c, :], i_=gp[:, c, 0:TG], s1=self.dww[:, c, 0:1], s2=self.cvvec[:, 0, c:c + 1]:
                        e.tensor_scalar(out=o, in0=i_, scalar1=s1, scalar2=s2, op0=ALU.mult, op1=ALU.add),
                        ["cv_gp", "dww", "cvvec"], [("cv_acc", c)])
            for k in range(1, 31):
                for c in range(4):
                    self.op("dve", lambda e, o=acc[:, c, :], i_=gp[:, c, k:k + TG], s=self.dww[:, c, k:k + 1]:
                            e.scalar_tensor_tensor(out=o, in0=i_, scalar=s, in1=o, op0=ALU.mult, op1=ALU.add),
                            ["cv_gp", "dww", ("cv_acc", c)], [("cv_acc", c)])
            for c in range(4):
                self.op("act", lambda e, o=sqr[:, c, :], i_=acc[:, c, :]: e.activation(out=o, in_=i_, func=AF.Square),
                        [("cv_acc", c)], [("cv_sq", c)])
            k1, p1 = self.psum_group()
            k2, p2 = self.psum_group()
            for b in range(NB):
                for c in range(4):
                    self.op("pe", lambda e, o=p1[:, b, :], r=acc[:, c, b * 512:(b + 1) * 512], st=(c == 0), sp=(c == 3):
                            e.matmul(o, lhsT=self.ones_f[:], rhs=r, start=st, stop=sp), ["ones_f", ("cv_acc", c)], [k1])
            for b in range(NB):
                for c in range(4):
                    self.op("pe", lambda e, o=p2[:, b, :], r=sqr[:, c, b * 512:(b + 1) * 512], st=(c == 0), sp=(c == 3):
                            e.matmul(o, lhsT=self.ones_f[:], rhs=r, start=st, stop=sp), ["ones_f", ("cv_sq", c)], [k2])
            self.op("act", lambda e, o=mean, i_=self.flat(p1): e.mul(out=o, in_=i_, mul=1.0 / 512), [k1], [("bR", 0)])
            msq = sqr[:, 0, :]
            self.op("act", lambda e, o=msq, i_=mean: e.activation(out=o, in_=i_, func=AF.Square), [("bR", 0), k2], [("cv_sq", 0)])
            self.op("dve", lambda e, o=rstd, i_=self.flat(p2), m=msq:
                    e.scalar_tensor_tensor(out=o, in0=i_, scalar=1.0 / 512, in1=m, op0=ALU.mult, op1=ALU.subtract),
                    [k2, ("cv_sq", 0)], [("bR", TG)])
            self.op("act", lambda e, o=rstd: e.activation(out=o, in_=o, func=AF.Sqrt, scale=1.0, bias=EPS), [("bR", TG)], [("bR", TG)])
            self.op("dve", lambda e, o=rstd: e.reciprocal(out=o, in_=o), [("bR", TG)], [("bR", TG)])
            for c in range(4):
                self.op("dve", lambda e, o=acc[:, c, :], m=mean: e.tensor_tensor(out=o, in0=o, in1=m, op=ALU.subtract),
                        [("cv_acc", c), ("bR", 0)], [("cv_acc", c)])
                self.op("dve", lambda e, o=acc[:, c, :], s=self.cvvec[:, 1, c:c + 1], r=rstd:
                        e.scalar_tensor_tensor(out=o, in0=o, scalar=s, in1=r, op0=ALU.mult, op1=ALU.mult),
                        [("cv_acc", c), ("bR", TG), "cvvec"], [("cv_acc", c)])
                self.op("act", lambda e, o=mo[:, c, :], i_=acc[:, c, :], bb=self.cvvec[:, 2, c:c + 1]:
                        e.activation(out=o, in_=i_, func=AF.Silu, bias=bb), [("cv_acc", c), "cvvec"], [("bXN", c)])
            self.dma(MO[:, :, t0:t0 + TG].rearrange("c p t -> p c t"), mo, [("bXN", c) for c in range(4)], [("MO", g)])
        LX = TG + 3
        xp = self.bX[:, 0:4 * LX].rearrange("p (c t) -> p c t", c=4)
        for g in range(NG):
            t0 = g * TG
            lo = max(t0 - 2, 0)
            hi = min(t0 + TG + 1, S)
            j0 = lo - (t0 - 2)
            for cb in range(4):
                self.op("pool", lambda e, o=xp: e.memset(o, 0.0), [], ["cv_gp"])
                self.dma(xp[:, :, j0:j0 + hi - lo], XBCr[cb * 4:cb * 4 + 4, :, lo:hi].rearrange("c p t -> p c t"),
                         ["cv_gp"] + [("XBCr", gg) for gg in (g - 1, g, g + 1) if 0 <= gg < NG], ["cv_gp"])
                for cc in range(4):
                    c = cb * 4 + cc
                    self.op("dve", lambda e, o=acc[:, cc, :], i_=xp[:, cc, 0:TG], s1=self.scw[:, c, 0:1], s2=self.scw[:, c, 4:5]:
                            e.tensor_scalar(out=o, in0=i_, scalar1=s1, scalar2=s2, op0=ALU.mult, op1=ALU.add),
                            ["cv_gp", "scw"], [("cv_acc", cc)])
                for k in range(1, 4):
                    for cc in range(4):
                        c = cb * 4 + cc
                        self.op("dve", lambda e, o=acc[:, cc, :], i_=xp[:, cc, k:k + TG], s=self.scw[:, c, k:k + 1]:
                                e.scalar_tensor_tensor(out=o, in0=i_, scalar=s, in1=o, op0=ALU.mult, op1=ALU.add),
                                ["cv_gp", "scw", ("cv_acc", cc)], [("cv_acc", cc)])
                for cc in range(4):
                    self.op("act", lambda e, o=acc[:, cc, :]: e.activation(out=o, in_=o, func=AF.Silu),
                            [("cv_acc", cc)], [("cv_acc", cc)])
                self.dma(XBC[cb * 4:cb * 4 + 4, :, t0:t0 + TG].rearrange("c p t -> p c t"), acc,
                         [("cv_acc", cc) for cc in range(4)], [("XBC", g)])

    def stage_odd_dt(self, aps, li, DTr, DTt):
        xf = self.bXN[:].bitcast(F32)
        dtk = xf[:, 0:1024]
        dtd = xf[:, 1024:2048]
        cdk = xf[:, 2048:3072]
        a_tok = xf[:, 3072:4096]
        sf = self.bSQ[:].bitcast(F32)
        acs_tok = sf[:, 0:1024]
        raw = sf[:, 1024:2048]
        hf = self.bH[:].bitcast(F32)
        AT = hf[0:48, 0:S]
        c1 = hf[0:48, S:2 * S]
        acsT = self.bX[0:48, 0:S]
        nacsT = self.bX[0:48, S:2 * S]
        dm = lambda t: t.rearrange("p (d c h) -> p c d h", d=2, h=16)
        self.dma(raw, DTt.rearrange("p c j -> p (c j)"), [("DTt", g) for g in range(NG)], ["dtraw"])
        for d in range(2):
            self.op("dve", lambda e, d=d: e.tensor_tensor(
                out=dtk[:, d * 512:(d + 1) * 512].rearrange("p (c h) -> p c h", h=16),
                in0=raw.rearrange("p (c j) -> p c j", j=32)[:, :, d * 16:(d + 1) * 16],
                in1=self.abrow[:, 1:2, d * 16:(d + 1) * 16].broadcast_to([128, 32, 16]), op=ALU.add),
                ["dtraw", "abrow"], ["dtk"])
        self.op("act", lambda e: e.activation(out=dtk, in_=dtk, func=AF.Exp), ["dtk"], ["dtk"])
        self.op("act", lambda e: e.activation(out=dtk, in_=dtk, func=AF.Ln, bias=1.0), ["dtk"], ["dtk"])
        for d in range(2):
            self.op("dve", lambda e, d=d: e.tensor_tensor(
                out=a_tok[:, d * 512:(d + 1) * 512].rearrange("p (c h) -> p c h", h=16),
                in0=dtk[:, d * 512:(d + 1) * 512].rearrange("p (c h) -> p c h", h=16),
                in1=self.abrow[:, 0:1, d * 16:(d + 1) * 16].broadcast_to([128, 32, 16]), op=ALU.mult),
                ["dtk", "abrow"], ["a_tok"])
        import os
        dcut = int(os.environ.get("DT_CUT", "99"))
        if dcut <= 1:
            return
        k1, p1 = self.psum_group()
        for d in range(2):
            self.op("pe", lambda e, d=d: e.matmul(p1[:, d, :], lhsT=self.tri[:, d, :], rhs=a_tok[:, d * 512:(d + 1) * 512],
                                                   start=True, stop=True), ["a_tok", "tri"], [k1])
        self.op("dve", lambda e: e.tensor_copy(out=acs_tok, in_=self.flat(p1)), [k1], ["acs_tok"])
        k2, p2 = self.psum_group()
        for d in range(2):
            self.op("pe", lambda e, d=d: e.matmul(p2[:, d, :], lhsT=self.ones_f[:], rhs=a_tok[:, d * 512:(d + 1) * 512],
                                                   start=True, stop=True), ["a_tok", "ones_f"], [k2])
        self.op("act", lambda e: e.activation(out=cdk, in_=self.flat(p2), func=AF.Exp), [k2], ["cdk"])
        self.op("dve", lambda e: e.tensor_tensor(out=dtd, in0=self.flat(p2), in1=acs_tok, op=ALU.subtract), [k2, "acs_tok"], ["dtd"])
        self.op("act", lambda e: e.activation(out=dtd, in_=dtd, func=AF.Exp), ["dtd"], ["dtd"])
        self.op("dve", lambda e: e.tensor_tensor(out=dtd, in0=dtd, in1=dtk, op=ALU.mult), ["dtd", "dtk"], ["dtd"])
        if dcut <= 2:
            return
        self.op("pool", lambda e: e.memset(AT, 0.0), [], ["AT"])
        self.dma(AT[0:16, :], DTr[0:16, :], ["AT"] + [("DTr", g) for g in range(NG)], ["AT"])
        self.dma(AT[32:48, :], DTr[16:32, :], ["AT"], ["AT"])
        self.op("act", lambda e: e.activation(out=AT, in_=AT, func=AF.Exp, bias=self.abcol[:, 1:2]), ["AT", "abcol"], ["AT"])
        self.op("act", lambda e: e.activation(out=AT, in_=AT, func=AF.Ln, bias=1.0), ["AT"], ["AT"])
        self.op("dve", lambda e: e.tensor_scalar(out=AT, in0=AT, scalar1=self.abcol[:, 0:1], scalar2=None, op0=ALU.mult),
                ["AT", "abcol"], ["AT"])
        if dcut <= 3:
            return
        c3 = lambda t: t.rearrange("p (c l) -> p c l", l=128)
        srcb, dstb = AT, c1
        names = {id(AT): "AT", id(c1): "c1", id(acsT): "acsT"}
        bufs = [c1, acsT]
        cur, curk = AT, "AT"
        step = 1
        n = 0
        while step < 128:
            dst = bufs[n % 2]
            dk = ["c1", "acsT"][n % 2]
            self.op("dve", lambda e, dst=dst, cur=cur, step=step: e.tensor_copy(out=c3(dst)[:, :, 0:step], in_=c3(cur)[:, :, 0:step]),
                    [curk], [dk])
            self.op("dve", lambda e, dst=dst, cur=cur, step=step: e.tensor_tensor(
                out=c3(dst)[:, :, step:128], in0=c3(cur)[:, :, step:128], in1=c3(cur)[:, :, 0:128 - step], op=ALU.add),
                [curk], [dk])
            cur, curk = dst, dk
            step *= 2
            n += 1
        cs = cur
        csk = curk
        self.op("dve", lambda e: e.tensor_copy(out=acsT[0:16, :], in_=cs[0:16, :]), [csk], ["acsT"])
        self.op("dve", lambda e: e.tensor_tensor(out=c3(acsT)[32:48], in0=c3(cs)[32:48, :, 127:128].broadcast_to([16, 32, 128]),
                                                 in1=c3(cs)[32:48], op=ALU.subtract), [csk, "acsT"], ["acsT"])
        self.op("dve", lambda e: e.tensor_tensor(out=acsT[32:48, :], in0=acsT[32:48, :], in1=AT[32:48, :], op=ALU.add),
                ["acsT", "AT"], ["acsT"])
        self.op("dve", lambda e: e.tensor_scalar(out=nacsT[0:16, :], in0=acsT[0:16, :], scalar1=-1.0, scalar2=None, op0=ALU.mult),
                ["acsT"], ["nacsT"])
        self.op("dve", lambda e: e.tensor_scalar(out=nacsT[32:48, :], in0=acsT[32:48, :], scalar1=-1.0, scalar2=None, op0=ALU.mult),
                ["acsT"], ["nacsT"])

    def stage_odd_ssd(self, aps, XBC, Y, d):
        xf = self.bXN[:].bitcast(F32)
        dtk = xf[:, 0:1024].rearrange("p (d c h) -> p d c h", d=2, h=16)
        dtd = xf[:, 1024:2048].rearrange("p (d c h) -> p d c h", d=2, h=16)
        cdk = xf[:, 2048:3072].rearrange("p (d c h) -> p d c h", d=2, h=16)
        acsT = self.bX[0:48, 0:S]
        nacsT = self.bX[0:48, S:2 * S]
        r0 = 0 if d == 0 else 32
        state = self.bR[:, 0:1024]
        state_bf = self.bR[:, 1024:1536].bitcast(BF16)
        hb = self.bH
        hf = hb[:].bitcast(F32)
        CB0 = [0, 16384]
        tokb_ = [hb[:, o:o + 1536] for o in CB0]
        Xdt_ = [hb[:, o + 1536:o + 2560].rearrange("p (h q) -> p h q", h=16) for o in CB0]
        Xdd_ = [hb[:, o + 2560:o + 3584].rearrange("p (h q) -> p h q", h=16) for o in CB0]
        MT = [hb[:, 3584 + i * 512:3584 + (i + 1) * 512].rearrange("p (h l) -> p h l", h=4) for i in range(2)]
        Cs = [hb[:, 4608 + i * 512:4608 + (i + 1) * 512].rearrange("p (h l) -> p h l", h=4) for i in range(2)]
        Lf = [hf[:, 3072 + i * 512:3072 + (i + 1) * 512] for i in range(2)]
        Ef = [hf[:, 4096 + i * 512:4096 + (i + 1) * 512] for i in range(2)]
        rhs2 = [hf[0:48, 5120 + i * 512:5120 + (i + 1) * 512] for i in range(2)]
        fbs = [hb[:, 12288 + i * 2048:12288 + (i + 1) * 2048].rearrange("p (c t) -> p c t", c=16) for i in range(2)]
        fxs = [self.bSQ[:].bitcast(F32)[:, i * 2048:(i + 1) * 2048].rearrange("p (c t) -> p c t", c=16) for i in range(2)]
        yst = [self.bT[0], self.bT[1], self.bSG[0], self.bSG[1]]
        ysk = [("bT", 0), ("bT", 1), ("bSG", 0), ("bSG", 1)]
        PS = self.ps
        self.op("pool", lambda e: e.memset(self.bR[:, 0:1536], 0.0), [], [("st", g) for g in range(4)] + [("stb", g) for g in range(4)])
        order = list(range(S // 128)) if d == 0 else list(range(S // 128 - 1, -1, -1))
        items = [(ci, c, g) for ci, c in enumerate(order) for g in range(4)]

        def prep(ci, c):
            fi = ci % 2
            sl = slice(c * 128, (c + 1) * 128)
            fx, fb = fxs[fi], fbs[fi]
            self.dma(fx, XBC[:, :, sl].rearrange("c p t -> p c t"), [("XBC", c // (TG // 128))], [("sfx", fi)])
            self.op("pool", lambda e, o=fb, i_=fx: e.tensor_copy(out=o, in_=i_), [("sfx", fi)], [("sfb", fi)])
            tk = tokb_[fi]
            for half in range(2):
                pt = PS[:, 6, :].bitcast(BF16)
                for i in range(6):
                    ii = half * 6 + i
                    self.op("pe", lambda e, o=pt[:, i * 128:(i + 1) * 128], i_=fb[:, ii, :]: e.transpose(out=o, in_=i_, identity=self.ident_b[:]),
                            [("sfb", fi), "ident_b"], [("psb", 6)])
                self.op("act", lambda e, o=tk[:, half * 768:(half + 1) * 768], i_=pt[:, 0:768]: e.copy(out=o, in_=i_),
                        [("psb", 6)], [("tokb", fi, half)])
            xs3 = tk[:, 0:1024].rearrange("p (h q) -> p h q", h=16)
            self.op("dve", lambda e, o=Xdt_[fi], c=c: e.tensor_tensor(out=o, in0=xs3, in1=dtk[:, d, c, :].unsqueeze(2).broadcast_to([128, 16, 64]), op=ALU.mult),
                    [("tokb", fi, 0), ("tokb", fi, 1), "dtk"], [("Xdt", fi)])
            self.op("pool", lambda e, o=Xdd_[fi], c=c: e.tensor_tensor(out=o, in0=xs3, in1=dtd[:, d, c, :].unsqueeze(2).broadcast_to([128, 16, 64]), op=ALU.mult),
                    [("tokb", fi, 0), ("tokb", fi, 1), "dtd"], [("Xdd", fi)])

        def front(idx):
            ci, c, g = items[idx]
            if g == 0:
                prep(ci, c)
            fi = ci % 2
            bi = idx % 2
            sl = slice(c * 128, (c + 1) * 128)
            fb = fbs[fi]
            slot = idx % 4
            cbt = PS[:, 0, slot * 128:(slot + 1) * 128]
            kcb = ("psb", 0, slot)
            self.op("pe", lambda e, o=cbt, l=fb[:, 8 + g, :], r=fb[:, 12 + g, :]: e.matmul(o, lhsT=l, rhs=r, start=True, stop=True),
                    [("sfb", fi)], [kcb])
            r2 = rhs2[bi][r0:r0 + 16, :]
            kr2 = ("rhs2", bi)
            self.op("dve", lambda e, o=r2.rearrange("p (h l) -> p h l", h=4), sl=sl, g=g:
                    e.tensor_tensor(out=o, in0=acsT[r0:r0 + 16, sl].unsqueeze(1).broadcast_to([16, 4, 128]),
                                    in1=self.selm[r0:r0 + 16, g, :, :], op=ALU.mult), ["acsT", "selm"], [kr2])
            pdiff = PS[:, 1 + bi, :]
            kdf = ("psb", 1 + bi)
            self.op("pe", lambda e, o=pdiff, sl=sl, g=g: e.matmul(o, lhsT=nacsT[r0:r0 + 16, sl], rhs=self.selm[r0:r0 + 16, g, :, :].rearrange("p h l -> p (h l)"),
                                                       start=True, stop=False), ["nacsT", "selm"], [kdf])
            self.op("pe", lambda e, o=pdiff, r=r2: e.matmul(o, lhsT=self.ones_f[r0:r0 + 16, :], rhs=r, start=False, stop=False),
                    ["ones_f", kr2], [kdf])
            self.op("pe", lambda e, o=pdiff: e.matmul(o, lhsT=self.ident_b[:], rhs=self.mbias[:, d, :], start=False, stop=True),
                    ["ident_b", "mbias"], [kdf])
            pE = PS[:, 3 + bi, :]
            kpe = ("psb", 3 + bi)
            self.op("pe", lambda e, o=pE, r=r2: e.matmul(o, lhsT=self.ones_f[r0:r0 + 16, :], rhs=r, start=True, stop=True),
                    ["ones_f", kr2], [kpe])
            L = Lf[bi]
            kL = ("Lf", bi)
            self.op("act", lambda e, o=L, i_=pdiff: e.activation(out=o, in_=i_, func=AF.Exp), [kdf], [kL])
            E = Ef[bi]
            kE = ("Ef", bi)
            self.op("act", lambda e, o=E, i_=pE: e.activation(out=o, in_=i_, func=AF.Exp), [kpe], [kE])
            mt = MT[bi]
            self.op("dve", lambda e, o=mt, a=L.rearrange("p (h l) -> p h l", h=4), b_=cbt.unsqueeze(1).broadcast_to([128, 4, 128]):
                    e.tensor_tensor(out=o, in0=a, in1=b_, op=ALU.mult), [kL, kcb], [("MT", bi)])
            self.op("pool", lambda e, o=Cs[bi], a=E.rearrange("p (h l) -> p h l", h=4), b_=fb[:, 12 + g, :].unsqueeze(1).broadcast_to([128, 4, 128]):
                    e.tensor_tensor(out=o, in0=a, in1=b_, op=ALU.mult), [kE, ("sfb", fi)], [("Cs", bi)])

        def back(idx):
            ci, c, g = items[idx]
            fi = ci % 2
            bi = idx % 2
            sl = slice(c * 128, (c + 1) * 128)
            mt, cs_ = MT[bi], Cs[bi]
            Xdt, Xdd, tk = Xdt_[fi], Xdd_[fi], tokb_[fi]
            py = PS[0:64, 5, :]
            ky = ("psb", 5)
            for hh in range(4):
                head = 4 * g + hh
                self.op("pe", lambda e, o=py[:, hh * 128:(hh + 1) * 128], l=Xdt[:, head, :], r=mt[:, hh, :]:
                        e.matmul(o, lhsT=l, rhs=r, start=True, stop=False), [("Xdt", fi), ("MT", bi)], [ky])
                self.op("pe", lambda e, o=py[:, hh * 128:(hh + 1) * 128], l=state_bf[:, head * 64:(head + 1) * 64], r=cs_[:, hh, :]:
                        e.matmul(o, lhsT=l, rhs=r, start=False, stop=True), [("stb", g), ("Cs", bi)], [ky])
            ys = yst[g][0:64, 0:512]
            self.op("act", lambda e, o=ys, i_=py: e.copy(out=o, in_=i_), [ky], [ysk[g]])
            self.dma(Y[g * 256:(g + 1) * 256, sl].rearrange("(h p) l -> p h l", p=64), ys.rearrange("p (h l) -> p h l", h=4),
                     [ysk[g]], [("Y", d, c // (TG // 128))])
            pst = PS[:, 7, bi * 256:(bi + 1) * 256]
            kst = ("psb", 7, bi)
            self.op("pe", lambda e, o=pst, l=tk[:, (8 + g) * 128:(9 + g) * 128], r=Xdd[:, 4 * g:4 * g + 4, :]:
                    e.matmul(o, lhsT=l, rhs=r, start=True, stop=True), [("tokb", fi, 1), ("Xdd", fi)], [kst])
            stg = state[:, g * 256:(g + 1) * 256]
            self.op("dve", lambda e, o=stg.rearrange("p (h q) -> p h q", h=4), c=c, g=g:
                    e.tensor_tensor(out=o, in0=o, in1=cdk[:, d, c, 4 * g:4 * g + 4].unsqueeze(2).broadcast_to([128, 4, 64]), op=ALU.mult),
                    [("st", g), "cdk"], [("st", g)])
            self.op("dve", lambda e, o=stg, i_=pst: e.tensor_tensor(out=o, in0=o, in1=i_, op=ALU.add),
                    [("st", g), kst], [("st", g)])
            self.op("act", lambda e, o=state_bf[:, g * 256:(g + 1) * 256], i_=stg: e.copy(out=o, in_=i_), [("st", g)], [("stb", g)])

        n = len(items)
        for idx in range(n):
            front(idx)
            if idx >= 1:
                back(idx - 1)
        back(n - 1)

    def stage_odd_out(self, X, aps, li, npost, Yf, Yb, XBC, ZS, MO):
        xv, xn_, sq, kX, kXN_, kSQ = self.views()
        rhs = self.bH[:, 0:12 * TG].rearrange("p (c t) -> p c t", c=12)
        rkeys = [("bH", m) for m in range(12)]
        for g in range(NG):
            t0 = g * TG
            self.dma(rhs[:, 0:4, :], MO[:, :, t0:t0 + TG].rearrange("c p t -> p c t"), [("MO", g)], rkeys[0:4])
            for c in range(8):
                a, ka = self.stg4()
                b, kb = self.stg4()
                xs, kx = self.stg4()
                z, kz = self.stg4()
                self.dma(a[:], Yf[c * 128:(c + 1) * 128, t0:t0 + TG], [("Y", 0, g)], [ka])
                self.dma(b[:], Yb[c * 128:(c + 1) * 128, t0:t0 + TG], [("Y", 1, g)], [kb])
                self.dma(xs[:], XBC[c, :, t0:t0 + TG], [("XBC", g)], [kx])
                self.dma(z[:], ZS[c, :, t0:t0 + TG], [("ZS", g)], [kz])
                self.op("pool", lambda e, o=a[:], b_=b[:]: e.tensor_tensor(out=o, in0=o, in1=b_, op=ALU.add), [ka, kb], [ka])
                self.op("dve", lambda e, o=a[:], x_=xs[:], s=self.dexp[:, c:c + 1]:
                        e.scalar_tensor_tensor(out=o, in0=x_, scalar=s, in1=o, op0=ALU.mult, op1=ALU.add), [ka, kx, "dexp"], [ka])
                self.op("dve", lambda e, o=xv[:, c, :], a_=a[:], z_=z[:]: e.tensor_tensor(out=o, in0=a_, in1=z_, op=ALU.mult),
                        [ka, kz], [kX[c]])
                self.op("act", lambda e, o=sq[:, c, :], i_=xv[:, c, :]: e.activation(out=o, in_=i_, func=AF.Square),
                        [kX[c]], [kSQ[c]])
            rk, rv = self.rms_rinv(sq, 8, 0, 1.0 / D, EPS, kSQ)
            for c in range(8):
                self.op("dve", lambda e, o=rhs[:, 4 + c, :], i_=xv[:, c, :], s=self.snw[:, c:c + 1], r=rv:
                        e.scalar_tensor_tensor(out=o, in0=i_, scalar=s, in1=r, op0=ALU.mult, op1=ALU.mult),
                        [kX[c], "snw", rk], [rkeys[4 + c]])
            self.proj_post_residual(X, g, rhs, rkeys, 12, ("od_wout", li), npost, False)

def build_program(stop=None, only=None):
    _, plan = _build(stop, only, None)
    nc, _ = _build(stop, only, plan)
    return nc


def _build(stop, only, wplan):
    from contextlib import ExitStack
    nc = bass.Bass("TRN2", target_bir_lowering=False)
    dt = nc.dram_tensor
    aps = {}

    def inp(name, shape):
        aps[name] = dt(name, shape, F32, kind="ExternalInput").ap()

    inp("x", [S, D])
    inp("normw", [128, DEPTH * 6 * 8])
    import os
    nff = 1 if os.environ.get("DEV_SMALL") else DEPTH * 2
    inp("wg", [nff, D, DFF])
    inp("wu", [nff, D, DFF])
    inp("wd", [nff, DFF, D])
    inp("ev_win", [2, D, 2304])
    inp("ev_wv", [2, D, 256])
    inp("ev_pool_w", [2, 4, 64, 64])
    inp("ev_pool_scale", [2, 128, 2])
    inp("ev_sink", [2, 12])
    inp("ev_wout", [2, D, D])
    inp("ropec", [128, S])
    inp("ropes", [128, S])
    inp("icnt", [2, 128, S])
    inp("amask", [128, 2, 384])
    inp("od_win", [2, D, 4128])
    inp("od_wout", [2, 1536, D])
    inp("cv_dww", [2, 128, 4, 31])
    inp("cv_vec", [2, 128, 3, 4])
    inp("ssm_cw", [2, 128, 16, 5])
    inp("ssm_dexp", [2, 128, 8])
    inp("ssm_nw", [2, 128, 8])
    inp("ssm_ab", [2, 48, 2])
    inp("ssm_A_log", [2, 32])
    inp("ssm_dt_bias", [2, 32])
    inp("tri", [128, 2, 128])
    inp("selm", [48, 4, 4, 128])
    inp("mbias", [128, 2, 512])
    inp("rmask", [48, 128])
    out = dt("out", [S, D], F32, kind="ExternalOutput").ap()
    X = dt("Xs", [8, 128, S], F32, kind="Internal").ap()
    QT = dt("QTs", [768, S], BF16, kind="Internal").ap()
    KT = dt("KTs", [256, S], BF16, kind="Internal").ap()
    V = dt("Vs", [S, 256], BF16, kind="Internal").ap()
    U = dt("Us", [2, 128, S], F32, kind="Internal").ap()
    AO = dt("AOs", [8, 128, S], BF16, kind="Internal").ap()
    Gs = dt("Gs", [4, 128, S], F32, kind="Internal").ap()
    ZS = dt("ZSs", [8, 128, S], F32, kind="Internal").ap()
    XBCr = dt("XBCrs", [16, 128, S], F32, kind="Internal").ap()
    XBC = dt("XBCs", [16, 128, S], F32, kind="Internal").ap()
    DTr = dt("DTrs", [32, S], F32, kind="Internal").ap()
    DTt = dt("DTts", [128, 32, 32], F32, kind="Internal").ap()
    Yf = dt("Yfs", [1024, S], F32, kind="Internal").ap()
    Yb = dt("Ybs", [1024, S], F32, kind="Internal").ap()
    MO = dt("MOs", [4, 128, S], BF16, kind="Internal").ap()
    B = Builder(nc, stop, aps, wplan)
    B.init_consts(aps)
    B.stage_load(aps["x"], X)
    nst = 0
    seq = [(l, sub) for l in range(DEPTH) for sub in range(3)]
    if stop is not None:
        seq = seq[:stop]
    if only is not None:
        seq = only
    for (l, sub) in seq:
        base = l * 6 * 8
        if True:
            if sub == 0:
                B.sc.barrier(); B.stage_ffn(X, ("wg", l * 2), ("wu", l * 2), ("wd", l * 2), base + 0, base + 8)
            elif sub == 2:
                B.sc.barrier(); B.stage_ffn(X, ("wg", l * 2 + 1), ("wu", l * 2 + 1), ("wd", l * 2 + 1), base + 32, base + 40)
            elif l % 2 == 0:
                li = l // 2
                B.even_consts(aps, li)
                B.sc.barrier(); B.stage_even_in(X, aps, li, base + 16, QT, KT, V, U)
                B.sc.barrier(); B.stage_even_pool(aps, U, AO)
                B.sc.barrier(); B.stage_even_attn(aps, li, QT, KT, V, AO)
                B.sc.barrier(); B.stage_even_out(X, aps, li, base + 24, AO)
            else:
                li = l // 2
                import os
                ocut = int(os.environ.get("ODD_CUT", "99"))
                B.odd_consts(aps, li)
                oskip = int(os.environ.get("ODD_SKIP", "0"))
                if ocut >= 1 and oskip < 1:
                    B.sc.barrier(); B.stage_odd_in(X, aps, li, base + 16, Gs, ZS, XBCr, DTr, DTt)
                if ocut >= 2 and oskip < 2:
                    B.sc.barrier(); B.stage_odd_conv(aps, li, Gs, XBCr, XBC, MO)
                if ocut >= 3:
                    B.sc.barrier(); B.stage_odd_dt(aps, li, DTr, DTt)
                if ocut >= 4:
                    B.sc.barrier(); B.stage_odd_ssd(aps, XBC, Yf, 0)
                if ocut >= 5:
                    B.sc.barrier(); B.stage_odd_ssd(aps, XBC, Yb, 1)
                if ocut >= 6:
                    B.sc.barrier(); B.stage_odd_out(X, aps, li, base + 24, Yf, Yb, XBC, ZS, MO)
    B.sc.barrier(); B.stage_store(X, out)
    fin = [("out", g) for g in range(NG)]
    if os.environ.get("DBG"):
        def dbg(name, ap, keys):
            o = dt("dbg_" + name, list(ap.shape), ap.dtype, kind="ExternalOutput").ap()
            B.dma(o, ap, keys, [("dbg", name)])
            fin.append(("dbg", name))
        G4 = range(NG)
        dbg("G", Gs, [("G", g) for g in G4])
        dbg("XBC", XBC, [("XBC", g) for g in G4])
        dbg("MO", MO, [("MO", g) for g in G4])
        dbg("ZS", ZS, [("ZS", g) for g in G4])
        dbg("Yf", Yf, [("Y", 0, g) for g in G4])
        dbg("Yb", Yb, [("Y", 1, g) for g in G4])
        dbg("DTr", DTr, [("DTr", g) for g in G4])
        dbg("tok", B.bXN[:].bitcast(F32)[:, 0:3072], ["dtk", "dtd", "cdk"])
        dbg("acsT", B.bX[0:48, 0:S], ["acsT"])
    B.op("sp", lambda e: e.nop(), fin, [])
    if wplan is None:
        return None, B.wrec
    with ExitStack() as st:
        B.sc.emit(st)
    return nc, B.wrec


def rope_tables_np():
    inv = (1.0 / (np.float32(10000.0) ** (np.arange(0, 64, 2, dtype=np.float32) / np.float32(64)))).astype(np.float32)
    ang = (np.arange(S, dtype=np.float32)[:, None] * inv[None, :]).astype(np.float32)
    cos = np.cos(ang).astype(np.float32)
    sin = np.sin(ang).astype(np.float32)
    cf = np.concatenate([cos, cos], axis=1)
    sf = np.concatenate([-sin, sin], axis=1)
    cT = np.ascontiguousarray(np.concatenate([cf, cf], axis=1).T)
    sT = np.ascontiguousarray(np.concatenate([sf, sf], axis=1).T)
    return cT, sT


def const_tables():
    cT, sT = rope_tables_np()
    t = np.arange(S)
    icnt = np.zeros((2, 128, S), np.float32)
    for gi, w in enumerate((2, 4, 8, 16)):
        lo = np.clip(t - w // 2, 0, S)
        hi = np.clip(t + w - w // 2, 0, S)
        icnt[gi // 2, (gi % 2) * 64:(gi % 2) * 64 + 64, :] = (1.0 / (hi - lo).astype(np.float32))[None, :]
    kl = np.arange(128)[:, None]
    ql = np.arange(128)[None, :]
    m1 = (ql <= kl).astype(np.float32)
    m2 = (kl <= ql).astype(np.float32)
    amask = np.stack([np.tile(m1, (1, 3)), np.tile(m2, (1, 3))], axis=1)
    s_ = np.arange(128)[:, None]
    l_ = np.arange(128)[None, :]
    tri = np.stack([(s_ <= l_), (s_ >= l_)], axis=1).astype(np.float32)
    selm = np.zeros((48, 4, 4, 128), np.float32)
    for k in range(16):
        selm[k, k // 4, k % 4, :] = 1.0
        selm[32 + k, k // 4, k % 4, :] = 1.0
    NEG = -30000.0
    mb = np.stack([np.tile(np.where(l_ >= s_, 0.0, NEG), (1, 4)), np.tile(np.where(s_ >= l_, 0.0, NEG), (1, 4))], axis=1)
    rmask = np.ones((48, 128), np.float32)
    rmask[:, 0] = 0.0
    return {"ropec": cT, "ropes": sT, "icnt": icnt, "amask": np.ascontiguousarray(amask),
            "tri": np.ascontiguousarray(tri), "selm": selm, "mbias": np.ascontiguousarray(mb.astype(np.float32)),
            "rmask": rmask}


def make_in_maps(inputs):
    nw = np.ascontiguousarray(
        inputs["norm_w"].reshape(DEPTH, 6, 8, 128).transpose(3, 0, 1, 2).reshape(128, DEPTH * 6 * 8))
    qk = np.arange(256, 1280)
    rel = (qk - 256) % 64
    partner = qk - rel + np.where(rel < 32, rel + 32, rel - 32)
    cols = []
    for i in range(8):
        cols.append(qk[i * 128:(i + 1) * 128])
        cols.append(partner[i * 128:(i + 1) * 128])
    cols.append(np.arange(0, 256))
    cols = np.concatenate(cols)
    shared = {
        "normw": nw,
        "wg": np.ascontiguousarray(inputs["ffn_w_gate"].reshape(DEPTH * 2, D, DFF)),
        "wu": np.ascontiguousarray(inputs["ffn_w_up"].reshape(DEPTH * 2, D, DFF)),
        "wd": np.ascontiguousarray(inputs["ffn_w_down"].reshape(DEPTH * 2, DFF, D)),
        "ev_win": np.ascontiguousarray(inputs["ev_w_in"][:, :, cols]),
        "ev_wv": np.ascontiguousarray(inputs["ev_w_in"][:, :, 1280:1536]),
        "ev_pool_w": np.ascontiguousarray(inputs["ev_pool_w"]),
        "ev_pool_scale": np.ascontiguousarray(inputs["ev_pool_scale"].reshape(2, 2, 128).transpose(0, 2, 1)),
        "ev_sink": np.ascontiguousarray(inputs["ev_sink"]),
        "ev_wout": np.ascontiguousarray(inputs["ev_w_out"]),
    }
    ocols = []
    for m in range(4):
        ocols.append(np.arange(m * 128, (m + 1) * 128))
        ocols.append(np.arange(512 + m * 128, 512 + (m + 1) * 128))
    ocols.append(np.arange(1024, 4128))
    ocols = np.concatenate(ocols)
    f32 = np.float32
    ab = np.zeros((2, 48, 2), f32)
    ab[:, 0:16, 0] = inputs["ssm_A_log"][:, 0]
    ab[:, 32:48, 0] = inputs["ssm_A_log"][:, 1]
    ab[:, 0:16, 1] = inputs["ssm_dt_bias"][:, 0]
    ab[:, 32:48, 1] = inputs["ssm_dt_bias"][:, 1]
    cw = np.concatenate([inputs["ssm_conv_w"].reshape(2, 4, 16, 128).transpose(0, 3, 2, 1),
                         inputs["ssm_conv_b"].reshape(2, 16, 128).transpose(0, 2, 1)[..., None]], axis=-1)
    shared.update({
        "od_win": np.ascontiguousarray(inputs["od_w_in"][:, :, ocols]),
        "od_wout": np.ascontiguousarray(inputs["od_w_out"]),
        "cv_dww": np.ascontiguousarray(inputs["cv_dw_w"].reshape(2, 31, 4, 128).transpose(0, 3, 2, 1)),
        "cv_vec": np.ascontiguousarray(np.stack([inputs["cv_dw_b"], inputs["cv_ln_g"], inputs["cv_ln_b"]], axis=1)
                                       .reshape(2, 3, 4, 128).transpose(0, 3, 1, 2)),
        "ssm_cw": np.ascontiguousarray(cw.astype(f32)),
        "ssm_dexp": np.ascontiguousarray(np.repeat(inputs["ssm_D"], 64, axis=1).reshape(2, 8, 128).transpose(0, 2, 1)),
        "ssm_nw": np.ascontiguousarray(inputs["ssm_norm_w"].reshape(2, 8, 128).transpose(0, 2, 1)),
        "ssm_ab": ab,
        "ssm_A_log": np.ascontiguousarray(inputs["ssm_A_log"].reshape(2, 32)),
        "ssm_dt_bias": np.ascontiguousarray(inputs["ssm_dt_bias"].reshape(2, 32)),
    })
    shared.update(const_tables())
    maps = []
    for b in range(NCORES):
        m = dict(shared)
        m["x"] = np.ascontiguousarray(inputs["x"][b])
        maps.append(m)
    return maps


def kernel(**inputs):
    inputs = {k: np.asarray(v) for k, v in inputs.items()}
    nc = build_program()
    in_maps = make_in_maps(inputs)
    res = run_bass_kernel_spmd(nc, in_maps, core_ids=list(range(NCORES)))
    return np.stack([np.asarray(r["out"]) for r in res.results], axis=0).astype(np.float32)
```

```python
import numpy as np
import concourse.bass as bass
import concourse.mybir as mybir
from concourse.bass_utils import run_bass_kernel_spmd

F32 = mybir.dt.float32
BF16 = mybir.dt.bfloat16
ALU = mybir.AluOpType
AF = mybir.ActivationFunctionType

D = 1024
S = 4096
DEPTH = 4
DFF = 2816
NCORES = 8
EPS = 1e-6
TG = 1024
NG = S // TG
NB = TG // 512


W_TILED = {"wg": (D, DFF), "wu": (D, DFF), "wd": (DFF, D), "ev_win": (D, 2304), "ev_wv": (D, 256),
           "ev_wout": (D, D), "od_win": (D, 4096), "od_wout": (1536, D)}


def tile_offset(K, M, kt, ct):
    nct = M // 256
    off = 0
    for k_ in range(kt):
        off += nct * 128 * min(8, (K - k_ * 1024) // 128) * 256
    return off + ct * 128 * min(8, (K - kt * 1024) // 128) * 256


def tileize(W):
    lead = W.shape[:-2]
    K, M = W.shape[-2:]
    W2 = W.reshape((-1, K, M))
    outs = []
    for kt in range((K + 1023) // 1024):
        nk = min(8, (K - kt * 1024) // 128)
        blk = W2[:, kt * 1024:kt * 1024 + nk * 128, :].reshape(-1, nk, 128, M // 256, 256)
        outs.append(np.ascontiguousarray(blk.transpose(0, 3, 2, 1, 4)).reshape(W2.shape[0], -1))
    return np.ascontiguousarray(np.concatenate(outs, axis=1)).reshape(lead + (K * M,))


class Op:
    __slots__ = ("eng", "fn", "dma", "deps", "idx", "sem", "semval", "sig", "presem")


class Sched:
    ENG = ("pe", "act", "dve", "pool", "sp")

    def __init__(self, nc):
        self.nc = nc
        self.ops = []
        self.last_w = {}
        self.readers = {}
        self.dma_since = []
        self.last_on = {}

    def barrier(self):
        frontier = set(self.dma_since)
        for e, idx in self.last_on.items():
            frontier.add(idx)
        self.dma_since = []
        for e in self.ENG:
            self.add(e, lambda eng: eng.nop(), extra=frontier)

    def add(self, eng, fn, reads=(), writes=(), dma=False, extra=()):
        op = Op()
        op.eng = eng
        op.fn = fn
        op.dma = dma
        op.idx = len(self.ops)
        op.sig = False
        op.sem = None
        op.semval = 0
        op.presem = None
        deps = {}
        for k in reads:
            w = self.last_w.get(k)
            if w is not None:
                deps[w] = True
            if isinstance(k, tuple) and k[0] == "ps":
                for r in self.readers.get(k, ()):
                    if self.ops[r].eng != eng:
                        deps[r] = True
        for k in writes:
            w = self.last_w.get(k)
            if w is not None:
                deps.setdefault(w, False)
            for r in self.readers.get(k, ()):
                deps.setdefault(r, False)
        for k in reads:
            self.readers.setdefault(k, []).append(op.idx)
        for k in writes:
            self.last_w[k] = op.idx
            self.readers[k] = []
        for x in extra:
            deps[x] = True
        deps.pop(op.idx, None)
        op.deps = deps
        self.ops.append(op)
        if dma:
            self.dma_since.append(op.idx)
        else:
            self.last_on[eng] = op.idx
        return op

    def _needs_sync(self, c, p, raw):
        if p.dma:
            return True
        if c.eng != p.eng:
            return True
        if c.dma:
            return True
        if c.eng == "pe":
            return False
        return raw

    def emit(self, stack):
        nc = self.nc
        ops = self.ops
        for c in ops:
            for pi, raw in c.deps.items():
                p = ops[pi]
                if not p.dma and self._needs_sync(c, p, raw):
                    p.sig = True
        NDMA = 40
        dma_sems = [stack.enter_context(nc.semaphore("dq%d" % i)) for i in range(NDMA)]
        dma_cum = [0] * NDMA
        dma_rr = 0
        EPOCH = 20000
        eng_sems = {e: [] for e in self.ENG}
        eng_cnt = {e: 0 for e in self.ENG}
        for op in ops:
            if op.dma:
                i = dma_rr % NDMA
                dma_rr += 1
                op.sem = dma_sems[i]
                op.presem = (dma_sems[i], dma_cum[i]) if dma_cum[i] > 0 else None
                dma_cum[i] += 16
                op.semval = dma_cum[i]
            elif op.sig:
                e = op.eng
                if eng_cnt[e] % EPOCH == 0:
                    eng_sems[e].append(stack.enter_context(
                        nc.semaphore("e_%s_%d" % (e, len(eng_sems[e])))))
                eng_cnt[e] += 1
                op.sem = eng_sems[e][-1]
                op.semval = (eng_cnt[e] - 1) % EPOCH + 1
        per = {e: [] for e in self.ENG}
        for op in ops:
            per[op.eng].append(op)

        def run(engname, eng):
            waited = {}
            for op in per[engname]:
                ws = []
                for pi, raw in op.deps.items():
                    p = ops[pi]
                    if self._needs_sync(op, p, raw):
                        ws.append((p.sem, p.semval))
                if op.presem is not None:
                    ws.append(op.presem)
                for sem, val in ws:
                    key = id(sem)
                    if waited.get(key, 0) >= val:
                        continue
                    waited[key] = val
                    eng.wait_ge(sem, val)
                ins = op.fn(eng)
                if op.dma:
                    ins.then_inc(op.sem, 16)
                elif op.sig:
                    ins.then_inc(op.sem, 1)

        with nc.Block() as block:
            @block.tensor
            def _(e):
                run("pe", e)

            @block.scalar
            def _(e):
                run("act", e)

            @block.vector
            def _(e):
                run("dve", e)

            @block.gpsimd
            def _(e):
                run("pool", e)

            @block.sync
            def _(e):
                run("sp", e)


class Builder:
    def __init__(self, nc, stop=None, aps=None, wplan=None):
        self.nc = nc
        self.aps = aps
        self.wplan = wplan
        self.wrec = []
        self.w_issued = 0
        self.PF = 2
        self.sc = Sched(nc)
        self.stop = stop
        self.uid = 0
        nc_ = nc
        A = nc_.alloc_sbuf_tensor
        self.bX = A("bX", [128, 8 * TG], F32)
        self.bXN = A("bXN", [128, 8 * TG], BF16)
        self.bSQ = A("bSQ", [128, 8 * TG], BF16)
        self.bH = A("bH", [128, 22 * TG], BF16)
        self.NWS = 4
        self.wsF = [A("wsF%d" % i, [128, 2048], F32) for i in range(self.NWS)]
        self.wsB = [A("wsB%d" % i, [128, 2048], BF16) for i in range(self.NWS)]
        self.ws_i = 0
        self.bR = A("bR", [128, 2 * TG], F32)
        self.bSG = [A("bSG%d" % i, [128, TG], F32) for i in range(2)]
        self.sg_i = 0
        self.bT = [A("bT%d" % i, [128, TG], F32) for i in range(2)]
        self.t_i = 0
        self.ones_bf = A("ones_bf", [128, 128], BF16)
        self.ident_f = A("ident_f", [128, 128], F32)
        self.normw = A("normw_sb", [128, DEPTH * 6 * 8], F32)
        self.poolW = A("poolW", [128, 2, 128], BF16)
        self.poolS = A("poolS", [128, 2], F32)
        self.esink = A("esink", [128, 12], F32)
        self.amask = A("amask_sb", [128, 2, 384], F32)
        self.stg_i = 0
        self.ones_f = A("ones_f", [128, 128], F32)
        self.ident_b = A("ident_b", [128, 128], BF16)
        self.dww = A("dww", [128, 4, 31], F32)
        self.cvvec = A("cvvec", [128, 3, 4], F32)
        self.scw = A("scw", [128, 16, 5], F32)
        self.dexp = A("dexp", [128, 8], F32)
        self.snw = A("snw", [128, 8], F32)
        self.abcol = A("abcol", [48, 2], F32)
        self.abrow = A("abrow", [128, 2, 32], F32)
        self.tri = A("tri_sb", [128, 2, 128], F32)
        self.selm = A("selm_sb", [48, 4, 4, 128], F32)
        self.mbias = A("mbias_sb", [128, 2, 512], BF16)
        self.ps = nc_.alloc_psum_tensor("ps", [128, 8, 512], F32)
        self.ps_i = 0
        self.NPG = 8 // NB

    def op(self, eng, fn, reads=(), writes=(), dma=False):
        return self.sc.add(eng, fn, reads, writes, dma)

    def psum_group(self):
        g = self.ps_i % self.NPG
        self.ps_i += 1
        key = ("ps", g)
        view = self.ps[:, g * NB:(g + 1) * NB, :]
        return key, view

    def dma(self, out, in_, reads, writes, eng="sp"):
        return self.op(eng, lambda e, o=out, i=in_: e.dma_start(out=o, in_=i), reads, writes, dma=True)

    def _issue_w(self, j, spec):
        name, idx, r0, nk, c0, ncols = spec
        W = self.aps[name][idx]
        i = j % self.NWS
        fv = self.wsF[i][:, 0:nk * ncols].rearrange("p (k m) -> p k m", k=nk)
        bv = self.wsB[i][:, 0:nk * ncols].rearrange("p (k m) -> p k m", k=nk)
        if name in W_TILED:
            K, M = W_TILED[name]
            off = tile_offset(K, M, r0 // 1024, c0 // 256)
            assert ncols == 256 and nk == min(8, (K - r0) // 128)
            src = W[off:off + 128 * nk * 256].rearrange("(p k m) -> p k m", p=128, k=nk)
        else:
            src = W[r0:r0 + nk * 128, c0:c0 + ncols].rearrange("(k p) m -> p k m", p=128)
        self.dma(fv, src, reads=[], writes=[("wsF", i)])
        if j % 2 == 0:
            self.op("act", lambda e, o=bv, s=fv: e.copy(out=o, in_=s),
                    reads=[("wsF", i)], writes=[("wsB", i)])
        else:
            self.op("dve", lambda e, o=bv, s=fv: e.tensor_copy(out=o, in_=s),
                    reads=[("wsF", i)], writes=[("wsB", i)])

    def load_w(self, Wd, r0, nk, c0, ncols):
        j = self.ws_i
        self.ws_i += 1
        spec = (Wd[0], Wd[1], r0, nk, c0, ncols)
        if self.wplan is None:
            self.wrec.append(spec)
            self._issue_w(j, spec)
        else:
            assert self.wplan[j] == spec, (j, self.wplan[j], spec)
            upto = min(j + self.PF, len(self.wplan) - 1)
            while self.w_issued <= upto:
                self._issue_w(self.w_issued, self.wplan[self.w_issued])
                self.w_issued += 1
        i = j % self.NWS
        bv = self.wsB[i][:, 0:nk * ncols].rearrange("p (k m) -> p k m", k=nk)
        return ("wsB", i), bv

    def init_consts(self, aps):
        nc = self.nc
        self.op("pool", lambda e: e.memset(self.ones_bf[:], 1.0), [], ["ones_bf"])
        self.op("pool", lambda e: e.memset(self.ident_f[:], 0.0), [], ["ident_f"])
        self.op("pool", lambda e: e.affine_select(
            out=self.ident_f[:], in_=self.ident_f[:], pattern=[[-1, 128]],
            compare_op=ALU.not_equal, fill=1.0, base=0, channel_multiplier=1),
            ["ident_f"], ["ident_f"])
        self.dma(self.normw[:], aps["normw"], [], ["normw"])
        self.dma(self.amask[:], aps["amask"], [], ["amask"])
        self.op("pool", lambda e: e.memset(self.ones_f[:], 1.0), [], ["ones_f"])
        self.op("pool", lambda e: e.tensor_copy(out=self.ident_b[:], in_=self.ident_f[:]), ["ident_f"], ["ident_b"])
        self.dma(self.tri[:], aps["tri"], [], ["tri"])
        self.dma(self.selm[:], aps["selm"], [], ["selm"])
        st_ = self.bX[:, 0:1024].rearrange("p (d t) -> p d t", d=2)
        self.dma(st_, aps["mbias"], [], [("bX", 0)])
        self.op("pool", lambda e: e.tensor_copy(out=self.mbias[:], in_=st_), [("bX", 0)], ["mbias"])

    def flat(self, pv):
        return pv.rearrange("p b t -> p (b t)")

    def stage_load(self, x, X):
        hf = self.bH[:].bitcast(F32)
        xv = self.bX[:].rearrange("p (c t) -> p c t", c=8)
        for g in range(NG):
            tv = hf[:, 0:8 * D].rearrange("p (tb f) -> p tb f", tb=8)
            self.dma(tv, x[g * TG:(g + 1) * TG, :].rearrange("(tb p) f -> p tb f", p=128),
                     [], ["bH"])
            n = 0
            for tb in range(8):
                for c0 in range(0, 8, 4):
                    key, pv = self.psum_group()
                    pf = self.flat(pv)
                    for cc in range(4):
                        c = c0 + cc
                        self.op("pe", lambda e, o=pf[:, cc * 128:(cc + 1) * 128], i=tv[:, tb, c * 128:(c + 1) * 128]:
                                e.transpose(out=o, in_=i, identity=self.ident_f[:]),
                                ["bH", "ident_f"], [key])
                    src = pf[:, 0:512].rearrange("p (c t) -> p c t", c=4)
                    dst = xv[:, c0:c0 + 4, tb * 128:(tb + 1) * 128]
                    if n % 2 == 0:
                        self.op("act", lambda e, o=dst, i=src: e.copy(out=o, in_=i), [key],
                                [("bX", c0 + k) for k in range(4)])
                    else:
                        self.op("dve", lambda e, o=dst, i=src: e.tensor_copy(out=o, in_=i), [key],
                                [("bX", c0 + k) for k in range(4)])
                    n += 1
            self.dma(X[:, :, g * TG:(g + 1) * TG].rearrange("c p t -> p c t"), xv,
                     [("bX", c) for c in range(8)], [("X", g, c) for c in range(8)])

    def stage_store(self, X, out):
        hf = self.bH[:].bitcast(F32)
        xv = self.bX[:].rearrange("p (c t) -> p c t", c=8)
        for g in range(NG):
            tv = hf[:, 0:8 * D].rearrange("p (tb f) -> p tb f", tb=8)
            self.dma(xv, X[:, :, g * TG:(g + 1) * TG].rearrange("c p t -> p c t"),
                     [("X", g, c) for c in range(8)], [("bX", c) for c in range(8)])
            n = 0
            for tb in range(8):
                for c0 in range(0, 8, 4):
                    key, pv = self.psum_group()
                    pf = self.flat(pv)
                    for cc in range(4):
                        c = c0 + cc
                        self.op("pe", lambda e, o=pf[:, cc * 128:(cc + 1) * 128], i=xv[:, c, tb * 128:(tb + 1) * 128]:
                                e.transpose(out=o, in_=i, identity=self.ident_f[:]),
                                [("bX", c), "ident_f"], [key])
                    dst = tv[:, tb, c0 * 128:(c0 + 4) * 128]
                    src = pf[:, 0:512]
                    if n % 2 == 0:
                        self.op("act", lambda e, o=dst, i=src: e.copy(out=o, in_=i), [key], ["bH"])
                    else:
                        self.op("dve", lambda e, o=dst, i=src: e.tensor_copy(out=o, in_=i), [key], ["bH"])
                    n += 1
            self.dma(out[g * TG:(g + 1) * TG, :].rearrange("(tb p) f -> p tb f", p=128), tv,
                     ["bH"], [("out", g)])

    def rms_rinv(self, sqv, nchunks, off, scale, bias, sqkeys):
        key, pv = self.psum_group()
        for b in range(NB):
            for c in range(nchunks):
                self.op("pe", lambda e, o=pv[:, b, :], r=sqv[:, c, b * 512:(b + 1) * 512], st=(c == 0), sp=(c == nchunks - 1):
                        e.matmul(o, lhsT=self.ones_bf[:], rhs=r, start=st, stop=sp),
                        ["ones_bf", sqkeys[c]], [key])
        rv = self.bR[:, off:off + TG]
        rk = ("bR", off)
        self.op("act", lambda e, o=rv, i=self.flat(pv): e.activation(out=o, in_=i, func=AF.Sqrt, scale=scale, bias=bias),
                [key], [rk])
        self.op("dve", lambda e, o=rv: e.reciprocal(out=o, in_=o), [rk], [rk])
        return rk, rv

    def views(self):
        xv = self.bX[:].rearrange("p (c t) -> p c t", c=8)
        xn = self.bXN[:].rearrange("p (c t) -> p c t", c=8)
        sq = self.bSQ[:].rearrange("p (c t) -> p c t", c=8)
        kX = [("bX", c) for c in range(8)]
        kXN = [("bXN", c) for c in range(8)]
        kSQ = [("bSQ", c) for c in range(8)]
        return xv, xn, sq, kX, kXN, kSQ

    def load_norm(self, X, g, npre):
        xv, xn, sq, kX, kXN, kSQ = self.views()
        t0 = g * TG
        self.dma(xv, X[:, :, t0:t0 + TG].rearrange("c p t -> p c t"),
                 [("X", g, c) for c in range(8)], kX)
        for c in range(8):
            self.op("act", lambda e, o=sq[:, c, :], i=xv[:, c, :]: e.activation(out=o, in_=i, func=AF.Square),
                    [kX[c]], [kSQ[c]])
        rk, rv = self.rms_rinv(sq, 8, 0, 1.0 / D, EPS, kSQ)
        for c in range(8):
            self.op("dve", lambda e, o=xn[:, c, :], i=xv[:, c, :], s=self.normw[:, npre + c:npre + c + 1], r=rv:
                    e.scalar_tensor_tensor(out=o, in0=i, scalar=s, in1=r, op0=ALU.mult, op1=ALU.mult),
                    [kX[c], "normw", rk], [kXN[c]])
        return xn, kXN

    def proj_post_residual(self, X, g, rhs, rkeys, KC, W, npost, half):
        xv, xn, sq, kX, kXN, kSQ = self.views()
        nkt = (KC + 7) // 8
        for ct in range(4):
            grp = [self.psum_group(), self.psum_group()]
            for kt in range(nkt):
                nk = min(8, KC - kt * 8)
                kw, w = self.load_w(W, kt * 1024, nk, ct * 256, 256)
                for mm in range(2):
                    kp, pv = grp[mm]
                    for kk in range(nk):
                        kc = kt * 8 + kk
                        for b in range(NB):
                            self.op("pe", lambda e, o=pv[:, b, :], w_=w[:, kk, mm * 128:(mm + 1) * 128], r=rhs[:, kc, b * 512:(b + 1) * 512], st=(kc == 0), sp=(kc == KC - 1):
                                    e.matmul(o, lhsT=w_, rhs=r, start=st, stop=sp), [kw, rkeys[kc]], [kp])
            for mm in range(2):
                oc = ct * 2 + mm
                kp, pv = grp[mm]
                self.op("dve", lambda e, o=xv[:, oc, :], i=self.flat(pv): e.tensor_copy(out=o, in_=i),
                        [kp], [kX[oc]])
                self.op("act", lambda e, o=sq[:, oc, :], i=xv[:, oc, :]: e.activation(out=o, in_=i, func=AF.Square),
                        [kX[oc]], [kSQ[oc]])
        if half:
            rk2, rv2 = self.rms_rinv(sq, 8, TG, 4.0 / D, 4.0 * EPS, kSQ)
        else:
            rk2, rv2 = self.rms_rinv(sq, 8, TG, 1.0 / D, EPS, kSQ)
        self.residual_add(X, g, xv, kX, rk2, rv2, npost)

    def stage_ffn(self, X, Wg, Wu, Wd, npre, npost):
        h = self.bH[:].rearrange("p (c t) -> p c t", c=22)
        xv, xn, sq, kX, kXN, kSQ = self.views()
        sqs = [self.bSG[i // 2][:].bitcast(BF16)[:, (i % 2) * TG:(i % 2 + 1) * TG] for i in range(4)]
        ksqs = [("sqs", i) for i in range(4)]
        self.load_norm(X, 0, npre)
        for g in range(NG):
            for ct in range(11):
                kg, wg = self.load_w(Wg, 0, 8, ct * 256, 256)
                ku, wu = self.load_w(Wu, 0, 8, ct * 256, 256)
                for mm in range(2):
                    m = ct * 2 + mm
                    kpg, pg = self.psum_group()
                    for k in range(8):
                        for b in range(NB):
                            self.op("pe", lambda e, o=pg[:, b, :], w=wg[:, k, mm * 128:(mm + 1) * 128], r=xn[:, k, b * 512:(b + 1) * 512], st=(k == 0), sp=(k == 7):
                                    e.matmul(o, lhsT=w, rhs=r, start=st, stop=sp), [kg, kXN[k]], [kpg])
                    kpu, pu = self.psum_group()
                    for k in range(8):
                        for b in range(NB):
                            self.op("pe", lambda e, o=pu[:, b, :], w=wu[:, k, mm * 128:(mm + 1) * 128], r=xn[:, k, b * 512:(b + 1) * 512], st=(k == 0), sp=(k == 7):
                                    e.matmul(o, lhsT=w, rhs=r, start=st, stop=sp), [ku, kXN[k]], [kpu])
                    si = self.sg_i % 2
                    self.sg_i += 1
                    sg = self.bSG[si]
                    self.op("act", lambda e, o=sg[:], i=self.flat(pg): e.activation(out=o, in_=i, func=AF.Silu),
                            [kpg], [("bSG", si), ("sqs", 2 * si), ("sqs", 2 * si + 1)])
                    self.op("dve", lambda e, o=h[:, m, :], a=sg[:], b_=self.flat(pu): e.tensor_tensor(out=o, in0=a, in1=b_, op=ALU.mult),
                            [("bSG", si), ("sqs", 2 * si), ("sqs", 2 * si + 1), kpu], [("bH", m)])
            nxt = g + 1 if g + 1 < NG else None
            t1 = (g + 1) * TG
            stk = None

            def sq_chunks(c0):
                for c in range(c0, c0 + 4):
                    ti = c % 2
                    self.dma(self.bT[ti][:], X[c, :, t1:t1 + TG], [("X", nxt, c)], [("bT", ti)])
                    self.op("act", lambda e, o=sqs[c % 4], i_=self.bT[ti][:]: e.activation(out=o, in_=i_, func=AF.Square),
                            [("bT", ti)], [ksqs[c % 4], ("bSG", (c % 4) // 2)])

            def stat_mms(c0, key, pv):
                for c in range(c0, c0 + 4):
                    for b in range(NB):
                        self.op("pe", lambda e, o=pv[:, b, :], r=sqs[c % 4][:, b * 512:(b + 1) * 512], st=(c == 0), sp=(c == 7):
                                e.matmul(o, lhsT=self.ones_bf[:], rhs=r, start=st, stop=sp),
                                ["ones_bf", ksqs[c % 4], ("bSG", (c % 4) // 2)], [key])

            for ct in range(4):
                if nxt is not None and ct == 0:
                    sq_chunks(0)
                grp = [self.psum_group(), self.psum_group()]
                for kt in range(3):
                    nk = 8 if kt < 2 else 6
                    kw, w = self.load_w(Wd, kt * 1024, nk, ct * 256, 256)
                    for mm in range(2):
                        kp, pv = grp[mm]
                        for kk in range(nk):
                            kc = kt * 8 + kk
                            for b in range(NB):
                                self.op("pe", lambda e, o=pv[:, b, :], w_=w[:, kk, mm * 128:(mm + 1) * 128], r=h[:, kc, b * 512:(b + 1) * 512], st=(kc == 0), sp=(kc == 21):
                                        e.matmul(o, lhsT=w_, rhs=r, start=st, stop=sp), [kw, ("bH", kc)], [kp])
                if nxt is not None and ct == 0:
                    stk = self.psum_group()
                    stat_mms(0, stk[0], stk[1])
                    sq_chunks(4)
                if nxt is not None and ct == 1:
                    stat_mms(4, stk[0], stk[1])
                for mm in range(2):
                    oc = ct * 2 + mm
                    kp, pv = grp[mm]
                    self.op("dve", lambda e, o=xv[:, oc, :], i=self.flat(pv): e.tensor_copy(out=o, in_=i),
                            [kp], [kX[oc]])
                    self.op("act", lambda e, o=sq[:, oc, :], i=xv[:, oc, :]: e.activation(out=o, in_=i, func=AF.Square),
                            [kX[oc]], [kSQ[oc]])
                if nxt is not None and ct == 1:
                    rv = self.bR[:, 0:TG]
                    rk = ("bR", 0)
                    self.op("act", lambda e, o=rv, i=self.flat(stk[1]): e.activation(out=o, in_=i, func=AF.Sqrt, scale=1.0 / D, bias=EPS),
                            [stk[0]], [rk])
                    self.op("dve", lambda e, o=rv: e.reciprocal(out=o, in_=o), [rk], [rk])
                    for c in range(8):
                        ti = c % 2
                        self.dma(self.bT[ti][:], X[c, :, t1:t1 + TG], [("X", nxt, c)], [("bT", ti)])
                        self.op("dve", lambda e, o=xn[:, c, :], i=self.bT[ti][:], s=self.normw[:, npre + c:npre + c + 1], r=rv:
                                e.scalar_tensor_tensor(out=o, in0=i, scalar=s, in1=r, op0=ALU.mult, op1=ALU.mult),
                                [("bT", ti), "normw", rk], [kXN[c]])
            rk2, rv2 = self.rms_rinv(sq, 8, TG, 4.0 / D, 4.0 * EPS, kSQ)
            self.residual_add(X, g, xv, kX, rk2, rv2, npost)

    def residual_add(self, X, g, yv, kY, rk, rv, npost):
        t0 = g * TG
        for oc in range(8):
            ti = self.t_i % 2
            self.t_i += 1
            xr = self.bT[ti]
            kt_ = ("bT", ti)
            self.dma(xr[:], X[oc, :, t0:t0 + TG], [("X", g, oc)], [kt_])
            self.op("dve", lambda e, o=yv[:, oc, :], s=self.normw[:, npost + oc:npost + oc + 1], r=rv:
                    e.scalar_tensor_tensor(out=o, in0=o, scalar=s, in1=r, op0=ALU.mult, op1=ALU.mult),
                    [kY[oc], "normw", rk], [kY[oc]])
            self.op("pool", lambda e, o=xr[:], b_=yv[:, oc, :]: e.tensor_tensor(out=o, in0=o, in1=b_, op=ALU.add),
                    [kt_, kY[oc]], [kt_])
            self.dma(X[oc, :, t0:t0 + TG], xr[:], [kt_], [("X", g, oc)])


    def even_consts(self, aps, li):
        stg = self.wsF[0][:, 0:256].rearrange("p (c m) -> p c m", c=2)
        stg = self.bT[0][:, 0:256].rearrange("p (c m) -> p c m", c=2)
        kt_ = ("bT", 0)
        self.op("pool", lambda e, o=stg: e.memset(o, 0.0), [], [kt_])
        for c in range(2):
            for gg in range(2):
                self.dma(stg[gg * 64:(gg + 1) * 64, c, gg * 64:(gg + 1) * 64], aps["ev_pool_w"][li, 2 * c + gg],
                         [kt_], [kt_])
        self.op("pool", lambda e, o=self.poolW[:], i=stg: e.tensor_copy(out=o, in_=i), [kt_], ["poolW"])
        self.dma(self.poolS[:], aps["ev_pool_scale"][li], ["poolS"], ["poolS"])
        self.dma(self.esink[:], aps["ev_sink"][li].partition_broadcast(128), ["esink"], ["esink"])
        self.op("act", lambda e, o=self.esink[:]: e.activation(out=o, in_=o, func=AF.Exp), ["esink"], ["esink"])

    def stage_even_in(self, X, aps, li, npre, QT, KT, V, U):
        xv, xn_, sq, kX, kXN_, kSQ = self.views()
        Win = ("ev_win", li)
        Wv = ("ev_wv", li)
        sqb = sq
        vst = self.bH[:, 0:8 * 256].rearrange("p (tb f) -> p tb f", tb=8)
        ust = self.bR[:].rearrange("p (c t) -> p c t", c=2)
        for g in range(NG):
            t0 = g * TG
            xn, kXN = self.load_norm(X, g, npre)
            cosT = self.bSG[0]
            sinT = self.bSG[1]
            self.dma(cosT[:], aps["ropec"][:, t0:t0 + TG], [], [("bSG", 0)])
            self.dma(sinT[:], aps["ropes"][:, t0:t0 + TG], [], [("bSG", 1)])
            for i in range(8):
                kw, w = self.load_w(Win, 0, 8, i * 256, 256)
                kq, pq = self.psum_group()
                for k in range(8):
                    for b in range(NB):
                        self.op("pe", lambda e, o=pq[:, b, :], w_=w[:, k, 0:128], r=xn[:, k, b * 512:(b + 1) * 512], st=(k == 0), sp=(k == 7):
                                e.matmul(o, lhsT=w_, rhs=r, start=st, stop=sp), [kw, kXN[k]], [kq])
                kp, pp = self.psum_group()
                for k in range(8):
                    for b in range(NB):
                        self.op("pe", lambda e, o=pp[:, b, :], w_=w[:, k, 128:256], r=xn[:, k, b * 512:(b + 1) * 512], st=(k == 0), sp=(k == 7):
                                e.matmul(o, lhsT=w_, rhs=r, start=st, stop=sp), [kw, kXN[k]], [kp])
                t1 = self.bT[0]
                t2 = self.bT[1]
                self.op("dve", lambda e, o=t1[:], a=self.flat(pq), c_=cosT[:]: e.tensor_tensor(out=o, in0=a, in1=c_, op=ALU.mult),
                        [kq, ("bSG", 0)], [("bT", 0)])
                self.op("dve", lambda e, o=t2[:], a=self.flat(pp), c_=sinT[:]: e.tensor_tensor(out=o, in0=a, in1=c_, op=ALU.mult),
                        [kp, ("bSG", 1)], [("bT", 1)])
                self.op("pool", lambda e, o=sqb[:, i, :], a=t1[:], b_=t2[:]: e.tensor_tensor(out=o, in0=a, in1=b_, op=ALU.add),
                        [("bT", 0), ("bT", 1)], [kSQ[i]])
                if i < 6:
                    self.dma(QT[i * 128:(i + 1) * 128, t0:t0 + TG], sqb[:, i, :], [kSQ[i]], [("QT", g)])
                else:
                    self.dma(KT[(i - 6) * 128:(i - 5) * 128, t0:t0 + TG], sqb[:, i, :], [kSQ[i]], [("KT", g)])
            kw, w = self.load_w(Win, 0, 8, 2048, 256)
            for mm in range(2):
                kp, pv = self.psum_group()
                for k in range(8):
                    for b in range(NB):
                        self.op("pe", lambda e, o=pv[:, b, :], w_=w[:, k, mm * 128:(mm + 1) * 128], r=xn[:, k, b * 512:(b + 1) * 512], st=(k == 0), sp=(k == 7):
                                e.matmul(o, lhsT=w_, rhs=r, start=st, stop=sp), [kw, kXN[k]], [kp])
                self.op("act", lambda e, o=ust[:, mm, :], i_=self.flat(pv): e.copy(out=o, in_=i_), [kp], [("bR", mm * TG)])
                self.dma(U[mm, :, t0:t0 + TG], ust[:, mm, :], [("bR", mm * TG)], [("U", g)])
            kw, w = self.load_w(Wv, 0, 8, 0, 256)
            for tb in range(8):
                kp, pv = self.psum_group()
                pf = self.flat(pv)
                for k in range(8):
                    self.op("pe", lambda e, o=pf[:, 0:256], l=xn[:, k, tb * 128:(tb + 1) * 128], r=w[:, k, :], st=(k == 0), sp=(k == 7):
                            e.matmul(o, lhsT=l, rhs=r, start=st, stop=sp), [kw, kXN[k]], [kp])
                if tb % 2 == 0:
                    self.op("act", lambda e, o=vst[:, tb, :], i_=pf[:, 0:256]: e.copy(out=o, in_=i_), [kp], [("bH", "v")])
                else:
                    self.op("dve", lambda e, o=vst[:, tb, :], i_=pf[:, 0:256]: e.tensor_copy(out=o, in_=i_), [kp], [("bH", "v")])
            self.dma(V[t0:t0 + TG, :].rearrange("(tb p) f -> p tb f", p=128), vst, [("bH", "v")], [("V", g)])

    def stage_even_pool(self, aps, U, AO):
        L = TG + 16
        bx = self.bX
        up = bx[:, 0:2 * L].rearrange("p (c t) -> p c t", c=2)
        a1 = bx[:, 2 * L:4 * L].rearrange("p (c t) -> p c t", c=2)
        a2 = bx[:, 4 * L:6 * L].rearrange("p (c t) -> p c t", c=2)
        xnf = self.bXN[:].bitcast(F32)
        a3 = xnf[:, 0:L]
        a4 = xnf[:, L:2 * L]
        icnt = self.bR[:].rearrange("p (c t) -> p c t", c=2)
        res = self.bSQ[:].bitcast(F32)[:, 0:2 * TG].rearrange("p (c t) -> p c t", c=2)
        pb = self.bH[:, 0:2 * TG].rearrange("p (c t) -> p c t", c=2)
        ob = self.bH[:, 2 * TG:4 * TG].rearrange("p (c t) -> p c t", c=2)
        for g in range(NG):
            t0 = g * TG
            lo = max(t0 - 8, 0)
            hi = min(t0 + TG + 8, S)
            j0 = lo - (t0 - 8)
            self.op("pool", lambda e, o=up: e.memset(o, 0.0), [], ["pl_u"])
            self.dma(up[:, :, j0:j0 + hi - lo], U[:, :, lo:hi].rearrange("c p t -> p c t"),
                     ["pl_u"] + [("U", gg) for gg in (g - 1, g, g + 1) if 0 <= gg < NG], ["pl_u"])
            self.dma(icnt, aps["icnt"][:, :, t0:t0 + TG].rearrange("c p t -> p c t"), [], [("bR", 0), ("bR", TG)])
            self.op("dve", lambda e: e.tensor_tensor(out=a1[:, :, 1:L], in0=up[:, :, 0:L - 1], in1=up[:, :, 1:L], op=ALU.add),
                    ["pl_u"], ["pl_a1"])
            self.op("dve", lambda e: e.tensor_tensor(out=a2[:, :, 2:L - 1], in0=a1[:, :, 1:L - 2], in1=a1[:, :, 3:L], op=ALU.add),
                    ["pl_a1"], ["pl_a2"])
            self.op("dve", lambda e: e.tensor_tensor(out=a3[:, 4:L - 3], in0=a2[:, 1, 2:L - 5], in1=a2[:, 1, 6:L - 1], op=ALU.add),
                    ["pl_a2"], ["pl_a3"])
            self.op("dve", lambda e: e.tensor_tensor(out=a4[:, 8:L - 8], in0=a3[:, 4:L - 12], in1=a3[:, 12:L - 4], op=ALU.add),
                    ["pl_a3"], ["pl_a4"])
            self.op("dve", lambda e: e.tensor_tensor(out=res[0:64, 0, :], in0=a1[0:64, 0, 8:TG + 8], in1=icnt[0:64, 0, :], op=ALU.mult),
                    ["pl_a1", ("bR", 0)], ["pl_res"])
            self.op("dve", lambda e: e.tensor_tensor(out=res[64:128, 0, :], in0=a2[64:128, 0, 8:TG + 8], in1=icnt[64:128, 0, :], op=ALU.mult),
                    ["pl_a2", ("bR", 0)], ["pl_res"])
            self.op("dve", lambda e: e.tensor_tensor(out=res[0:64, 1, :], in0=a3[0:64, 8:TG + 8], in1=icnt[0:64, 1, :], op=ALU.mult),
                    ["pl_a3", ("bR", TG)], ["pl_res"])
            self.op("dve", lambda e: e.tensor_tensor(out=res[64:128, 1, :], in0=a4[64:128, 8:TG + 8], in1=icnt[64:128, 1, :], op=ALU.mult),
                    ["pl_a4", ("bR", TG)], ["pl_res"])
            self.op("dve", lambda e: e.tensor_tensor(out=pb, in0=res, in1=up[:, :, 8:TG + 8], op=ALU.subtract),
                    ["pl_res", "pl_u"], ["pl_pb"])
            for c in range(2):
                kp, pv = self.psum_group()
                for b in range(NB):
                    self.op("pe", lambda e, o=pv[:, b, :], l=self.poolW[:, c, :], r=pb[:, c, b * 512:(b + 1) * 512]:
                            e.matmul(o, lhsT=l, rhs=r, start=True, stop=True), ["poolW", "pl_pb"], [kp])
                self.op("dve", lambda e, o=ob[:, c, :], i_=self.flat(pv), s=self.poolS[:, c:c + 1]:
                        e.tensor_scalar(out=o, in0=i_, scalar1=s, scalar2=None, op0=ALU.mult), [kp, "poolS"], [("pl_ob", c)])
                self.dma(AO[c, :, t0:t0 + TG], ob[:, c, :], [("pl_ob", c)], [("AOp", g)])

    def stage_even_attn(self, aps, li, QT, KT, V, AO):
        KW = TG + 256
        q_sb = self.bH[0:64, 0:12 * TG].rearrange("p (h t) -> p h t", h=12)
        k_sb = self.bXN[0:64, 0:4 * KW].rearrange("p (h t) -> p h t", h=4)
        v_sb = self.bSQ[:, 0:10 * 256].rearrange("p (kb f) -> p kb f", kb=10)
        o_sb = self.bX[:].bitcast(BF16)[0:64, 0:12 * TG].rearrange("p (h t) -> p h t", h=12)
        Pb = [self.bSG[0][:].bitcast(BF16)[:, 0:384], self.bSG[1][:].bitcast(BF16)[:, 0:384],
              self.bT[0][:].bitcast(BF16)[:, 0:384], self.bT[1][:].bitcast(BF16)[:, 0:384]]
        Pk = [("bSG", 0), ("bSG", 1), ("bT", 0), ("bT", 1)]
        recs = [self.bR[0:64, 0:384], self.bR[0:64, 512:896]]
        AOf = AO.rearrange("c p t -> (c p) t")
        pi = 0
        ri = 0
        for g in range(NG):
            t0 = g * TG
            lo = max(t0 - 128, 0)
            hi = min(t0 + TG + 128, S)
            j0 = lo - (t0 - 128)
            nbr = [gg for gg in (g - 1, g, g + 1) if 0 <= gg < NG]
            self.dma(q_sb, QT[:, t0:t0 + TG].rearrange("(h d) t -> d h t", d=64), [("QT", g)], ["at_q"])
            self.dma(k_sb[:, :, j0:j0 + hi - lo], KT[:, lo:hi].rearrange("(h d) t -> d h t", d=64),
                     [("KT", gg) for gg in nbr], ["at_k"])
            self.dma(v_sb[:, j0 // 128:(j0 + hi - lo) // 128, :], V[lo:hi, :].rearrange("(kb p) f -> p kb f", p=128),
                     [("V", gg) for gg in nbr], ["at_v"])
            for qi in range(8):
                n = g * 8 + qi
                for gk in range(4):
                    kbs = [kb for kb in (n - 1, n, n + 1) if 0 <= kb < S // 128]
                    Ps = []
                    for kb in kbs:
                        lk = kb - (g * 8 - 1)
                        ks, pv = self.psum_group()
                        pf = self.flat(pv)
                        self.op("pe", lambda e, o=pf[:, 0:384].rearrange("p (h q) -> p h q", h=3), l=k_sb[:, gk, lk * 128:(lk + 1) * 128],
                                r=q_sb[:, 3 * gk:3 * gk + 3, qi * 128:(qi + 1) * 128]:
                                e.matmul(o, lhsT=l, rhs=r, start=True, stop=True), ["at_k", "at_q"], [ks])
                        P = Pb[pi % 4]
                        kP = Pk[pi % 4]
                        pi += 1
                        self.op("act", lambda e, o=P, i_=pf[:, 0:384]: e.activation(out=o, in_=i_, func=AF.Exp, scale=0.125),
                                [ks], [kP])
                        if kb == n - 1:
                            self.op("dve", lambda e, o=P: e.tensor_tensor(out=o, in0=o, in1=self.amask[:, 0, :], op=ALU.mult),
                                    [kP, "amask"], [kP])
                        elif kb == n + 1:
                            self.op("dve", lambda e, o=P: e.tensor_tensor(out=o, in0=o, in1=self.amask[:, 1, :], op=ALU.mult),
                                    [kP, "amask"], [kP])
                        Ps.append((P, kP, lk))
                    kn, pn = self.psum_group()
                    num = pn[0:64, 0, 0:384]
                    den = pn[0:64, 1, 0:384]
                    for idx, (P, kP, lk) in enumerate(Ps):
                        st = (idx == 0)
                        sp = (idx == len(Ps) - 1)
                        self.op("pe", lambda e, o=num, l=v_sb[:, lk, gk * 64:(gk + 1) * 64], r=P, st=st, sp=sp:
                                e.matmul(o, lhsT=l, rhs=r, start=st, stop=sp), ["at_v", kP], [kn])
                        self.op("pe", lambda e, o=den, l=self.ones_bf[:, 0:64], r=P, st=st, sp=sp:
                                e.matmul(o, lhsT=l, rhs=r, start=st, stop=sp), ["ones_bf", kP], [kn])
                    rec = recs[ri % 2]
                    kr = ("rec", ri % 2)
                    ri += 1
                    for j in range(3):
                        hh = 3 * gk + j
                        self.op("dve", lambda e, o=rec[:, j * 128:(j + 1) * 128], i_=den[:, j * 128:(j + 1) * 128],
                                s=self.esink[0:64, hh:hh + 1]:
                                e.tensor_scalar(out=o, in0=i_, scalar1=s, scalar2=None, op0=ALU.add), [kn, "esink"], [kr])
                    self.op("dve", lambda e, o=rec: e.reciprocal(out=o, in_=o), [kr], [kr])
                    self.op("dve", lambda e, o=o_sb[:, 3 * gk:3 * gk + 3, qi * 128:(qi + 1) * 128],
                            a=num.rearrange("p (h q) -> p h q", h=3), b_=rec.rearrange("p (h q) -> p h q", h=3):
                            e.tensor_tensor(out=o, in0=a, in1=b_, op=ALU.mult), [kn, kr], ["at_o"])
            self.dma(AOf[256:1024, t0:t0 + TG].rearrange("(h d) t -> d h t", d=64), o_sb, ["at_o"], [("AOa", g)])

    def stage_even_out(self, X, aps, li, npost, AO):
        xv, xn, sq, kX, kXN, kSQ = self.views()
        for g in range(NG):
            t0 = g * TG
            self.dma(xn, AO[:, :, t0:t0 + TG].rearrange("c p t -> p c t"), [("AOp", g), ("AOa", g)], kXN)
            self.proj_post_residual(X, g, xn, kXN, 8, ("ev_wout", li), npost, False)

    def stg4(self):
        i = self.stg_i % 4
        self.stg_i += 1
        t = [self.bT[0], self.bT[1], self.bSG[0], self.bSG[1]][i]
        k = [("bT", 0), ("bT", 1), ("bSG", 0), ("bSG", 1)][i]
        return t, k

    def odd_consts(self, aps, li):
        self.dma(self.dww[:], aps["cv_dww"][li], ["dww"], ["dww"])
        self.dma(self.cvvec[:], aps["cv_vec"][li], ["cvvec"], ["cvvec"])
        self.dma(self.scw[:], aps["ssm_cw"][li], ["scw"], ["scw"])
        self.dma(self.dexp[:], aps["ssm_dexp"][li], ["dexp"], ["dexp"])
        self.dma(self.snw[:], aps["ssm_nw"][li], ["snw"], ["snw"])
        self.dma(self.abcol[:], aps["ssm_ab"][li], ["abcol"], ["abcol"])
        self.op("act", lambda e, o=self.abcol[:, 0:1]: e.activation(out=o, in_=o, func=AF.Exp), ["abcol"], ["abcol"])
        self.op("dve", lambda e, o=self.abcol[:, 0:1]: e.tensor_scalar(out=o, in0=o, scalar1=-1.0, scalar2=None, op0=ALU.mult),
                ["abcol"], ["abcol"])
        self.dma(self.abrow[:, 0, :], aps["ssm_A_log"][li].partition_broadcast(128), ["abrow"], ["abrow"])
        self.dma(self.abrow[:, 1, :], aps["ssm_dt_bias"][li].partition_broadcast(128), ["abrow"], ["abrow"])
        self.op("act", lambda e, o=self.abrow[:, 0, :]: e.activation(out=o, in_=o, func=AF.Exp), ["abrow"], ["abrow"])
        self.op("dve", lambda e, o=self.abrow[:, 0, :]: e.tensor_scalar(out=o, in0=o, scalar1=-1.0, scalar2=None, op0=ALU.mult),
                ["abrow"], ["abrow"])

    def stage_odd_in(self, X, aps, li, npre, G, ZS, XBCr, DTr, DTt):
        Win = ("od_win", li)
        for g in range(NG):
            t0 = g * TG
            xn, kXN = self.load_norm(X, g, npre)
            for i in range(16):
                kw, w = self.load_w(Win, 0, 8, i * 256, 256)
                pgs = []
                for mm in range(2):
                    kp, pv = self.psum_group()
                    for k in range(8):
                        for b in range(NB):
                            self.op("pe", lambda e, o=pv[:, b, :], w_=w[:, k, mm * 128:(mm + 1) * 128], r=xn[:, k, b * 512:(b + 1) * 512], st=(k == 0), sp=(k == 7):
                                    e.matmul(o, lhsT=w_, rhs=r, start=st, stop=sp), [kw, kXN[k]], [kp])
                    pgs.append((kp, pv))
                if i < 4:
                    s1, k1 = self.stg4()
                    s2, k2 = self.stg4()
                    self.op("act", lambda e, o=s1[:], i_=self.flat(pgs[1][1]): e.activation(out=o, in_=i_, func=AF.Sigmoid),
                            [pgs[1][0]], [k1])
                    self.op("dve", lambda e, o=s2[:], a=self.flat(pgs[0][1]), b_=s1[:]: e.tensor_tensor(out=o, in0=a, in1=b_, op=ALU.mult),
                            [pgs[0][0], k1], [k2])
                    self.dma(G[i, :, t0:t0 + TG], s2[:], [k2], [("G", g)])
                elif i < 8:
                    for mm in range(2):
                        s1, k1 = self.stg4()
                        self.op("act", lambda e, o=s1[:], i_=self.flat(pgs[mm][1]): e.activation(out=o, in_=i_, func=AF.Silu),
                                [pgs[mm][0]], [k1])
                        self.dma(ZS[(i - 4) * 2 + mm, :, t0:t0 + TG], s1[:], [k1], [("ZS", g)])
                else:
                    for mm in range(2):
                        s1, k1 = self.stg4()
                        if mm == 0:
                            self.op("act", lambda e, o=s1[:], i_=self.flat(pgs[mm][1]): e.copy(out=o, in_=i_), [pgs[mm][0]], [k1])
                        else:
                            self.op("dve", lambda e, o=s1[:], i_=self.flat(pgs[mm][1]): e.tensor_copy(out=o, in_=i_), [pgs[mm][0]], [k1])
                        self.dma(XBCr[(i - 8) * 2 + mm, :, t0:t0 + TG], s1[:], [k1], [("XBCr", g)])
            kw, w = self.load_w(("od_wdt", li), 0, 8, 0, 32)
            kp, pv = self.psum_group()
            for k in range(8):
                for b in range(NB):
                    self.op("pe", lambda e, o=pv[0:32, b, :], w_=w[:, k, :], r=xn[:, k, b * 512:(b + 1) * 512], st=(k == 0), sp=(k == 7):
                            e.matmul(o, lhsT=w_, rhs=r, start=st, stop=sp), [kw, kXN[k]], [kp])
            s1, k1 = self.stg4()
            self.op("act", lambda e, o=s1[0:32, :], i_=self.flat(pv)[0:32, :]: e.copy(out=o, in_=i_), [kp], [k1])
            self.dma(DTr[:, t0:t0 + TG], s1[0:32, :], [k1], [("DTr", g)])
            kp, pv = self.psum_group()
            pf = self.flat(pv)
            for tb in range(8):
                for k in range(8):
                    self.op("pe", lambda e, o=pf[:, tb * 32:(tb + 1) * 32], l=xn[:, k, tb * 128:(tb + 1) * 128], r=w[:, k, :], st=(k == 0), sp=(k == 7):
                            e.matmul(o, lhsT=l, rhs=r, start=st, stop=sp), [kw, kXN[k]], [kp])
            s1, k1 = self.stg4()
            self.op("dve", lambda e, o=s1[:, 0:256], i_=pf[:, 0:256]: e.tensor_copy(out=o, in_=i_), [kp], [k1])
            self.dma(DTt[:, g * 8:(g + 1) * 8, :],
                     s1[:, 0:256].rearrange("p (tb j) -> p tb j", tb=8), [k1], [("DTt", g)])

    def stage_odd_conv(self, aps, li, G, XBCr, XBC, MO):
        hf = self.bH[:].bitcast(F32)
        acc = hf[:, 0:4 * TG].rearrange("p (c t) -> p c t", c=4)
        sqr = hf[:, 4 * TG:8 * TG].rearrange("p (c t) -> p c t", c=4)
        mo = self.bXN[:, 0:4 * TG].rearrange("p (c t) -> p c t", c=4)
        mean = self.bR[:, 0:TG]
        rstd = self.bR[:, TG:2 * TG]
        LG = TG + 30
        gp = self.bX[:, 0:4 * LG].rearrange("p (c t) -> p c t", c=4)
        for g in range(NG):
            t0 = g * TG
            lo = max(t0 - 15, 0)
            hi = min(t0 + TG + 15, S)
            j0 = lo - (t0 - 15)
            self.op("pool", lambda e, o=gp: e.memset(o, 0.0), [], ["cv_gp"])
            self.dma(gp[:, :, j0:j0 + hi - lo], G[:, :, lo:hi].rearrange("c p t -> p c t"),
                     ["cv_gp"] + [("G", gg) for gg in (g - 1, g, g + 1) if 0 <= gg < NG], ["cv_gp"])
            for c in range(4):
                self.op("dve", lambda e, o=acc[:, c, :], i_=gp[:, c, 0:TG], s1=self.dww[:, c, 0:1], s2=self.cvvec[:, 0, c:c + 1]:
                        e.tensor_scalar(out=o, in0=i_, scalar1=s1, scalar2=s2, op0=ALU.mult, op1=ALU.add),
                        ["cv_gp", "dww", "cvvec"], [("cv_acc", c)])
            for k in range(1, 31):
                for c in range(4):
                    self.op("dve", lambda e, o=acc[:, c, :], i_=gp[:, c, k:k + TG], s=self.dww[:, c, k:k + 1]:
                            e.scalar_tensor_tensor(out=o, in0=i_, scalar=s, in1=o, op0=ALU.mult, op1=ALU.add),
                            ["cv_gp", "dww", ("cv_acc", c)], [("cv_acc", c)])
            for c in range(4):
                self.op("act", lambda e, o=sqr[:, c, :], i_=acc[:, c, :]: e.activation(out=o, in_=i_, func=AF.Square),
                        [("cv_acc", c)], [("cv_sq", c)])
            k1, p1 = self.psum_group()
            k2, p2 = self.psum_group()
            for b in range(NB):
                for c in range(4):
                    self.op("pe", lambda e, o=p1[:, b, :], r=acc[:, c, b * 512:(b + 1) * 512], st=(c == 0), sp=(c == 3):
                            e.matmul(o, lhsT=self.ones_f[:], rhs=r, start=st, stop=sp), ["ones_f", ("cv_acc", c)], [k1])
            for b in range(NB):
                for c in range(4):
                    self.op("pe", lambda e, o=p2[:, b, :], r=sqr[:, c, b * 512:(b + 1) * 512], st=(c == 0), sp=(c == 3):
                            e.matmul(o, lhsT=self.ones_f[:], rhs=r, start=st, stop=sp), ["ones_f", ("cv_sq", c)], [k2])
            self.op("act", lambda e, o=mean, i_=self.flat(p1): e.mul(out=o, in_=i_, mul=1.0 / 512), [k1], [("bR", 0)])
            msq = sqr[:, 0, :]
            self.op("act", lambda e, o=msq, i_=mean: e.activation(out=o, in_=i_, func=AF.Square), [("bR", 0), k2], [("cv_sq", 0)])
            self.op("dve", lambda e, o=rstd, i_=self.flat(p2), m=msq:
                    e.scalar_tensor_tensor(out=o, in0=i_, scalar=1.0 / 512, in1=m, op0=ALU.mult, op1=ALU.subtract),
                    [k2, ("cv_sq", 0)], [("bR", TG)])
            self.op("act", lambda e, o=rstd: e.activation(out=o, in_=o, func=AF.Sqrt, scale=1.0, bias=EPS), [("bR", TG)], [("bR", TG)])
            self.op("dve", lambda e, o=rstd: e.reciprocal(out=o, in_=o), [("bR", TG)], [("bR", TG)])
            for c in range(4):
                self.op("dve", lambda e, o=acc[:, c, :], m=mean: e.tensor_tensor(out=o, in0=o, in1=m, op=ALU.subtract),
                        [("cv_acc", c), ("bR", 0)], [("cv_acc", c)])
                self.op("dve", lambda e, o=acc[:, c, :], s=self.cvvec[:, 1, c:c + 1], r=rstd:
                        e.scalar_tensor_tensor(out=o, in0=o, scalar=s, in1=r, op0=ALU.mult, op1=ALU.mult),
                        [("cv_acc", c), ("bR", TG), "cvvec"], [("cv_acc", c)])
                self.op("act", lambda e, o=mo[:, c, :], i_=acc[:, c, :], bb=self.cvvec[:, 2, c:c + 1]:
                        e.activation(out=o, in_=i_, func=AF.Silu, bias=bb), [("cv_acc", c), "cvvec"], [("bXN", c)])
            self.dma(MO[:, :, t0:t0 + TG].rearrange("c p t -> p c t"), mo, [("bXN", c) for c in range(4)], [("MO", g)])
        LX = TG + 3
        xp = self.bX[:, 0:4 * LX].rearrange("p (c t) -> p c t", c=4)
        for g in range(NG):
            t0 = g * TG
            lo = max(t0 - 2, 0)
            hi = min(t0 + TG + 1, S)
            j0 = lo - (t0 - 2)
            for cb in range(4):
                self.op("pool", lambda e, o=xp: e.memset(o, 0.0), [], ["cv_gp"])
                self.dma(xp[:, :, j0:j0 + hi - lo], XBCr[cb * 4:cb * 4 + 4, :, lo:hi].rearrange("c p t -> p c t"),
                         ["cv_gp"] + [("XBCr", gg) for gg in (g - 1, g, g + 1) if 0 <= gg < NG], ["cv_gp"])
                for cc in range(4):
                    c = cb * 4 + cc
                    self.op("dve", lambda e, o=acc[:, cc, :], i_=xp[:, cc, 0:TG], s1=self.scw[:, c, 0:1], s2=self.scw[:, c, 4:5]:
                            e.tensor_scalar(out=o, in0=i_, scalar1=s1, scalar2=s2, op0=ALU.mult, op1=ALU.add),
                            ["cv_gp", "scw"], [("cv_acc", cc)])
                for k in range(1, 4):
                    for cc in range(4):
                        c = cb * 4 + cc
                        self.op("dve", lambda e, o=acc[:, cc, :], i_=xp[:, cc, k:k + TG], s=self.scw[:, c, k:k + 1]:
                                e.scalar_tensor_tensor(out=o, in0=i_, scalar=s, in1=o, op0=ALU.mult, op1=ALU.add),
                                ["cv_gp", "scw", ("cv_acc", cc)], [("cv_acc", cc)])
                for cc in range(4):
                    self.op("act", lambda e, o=acc[:, cc, :]: e.activation(out=o, in_=o, func=AF.Silu),
                            [("cv_acc", cc)], [("cv_acc", cc)])
                self.dma(XBC[cb * 4:cb * 4 + 4, :, t0:t0 + TG].rearrange("c p t -> p c t"), acc,
                         [("cv_acc", cc) for cc in range(4)], [("XBC", g)])

    def stage_odd_dt(self, aps, li, DTr, DTt):
        xf = self.bXN[:].bitcast(F32)
        dtk = xf[:, 0:1024]
        dtd = xf[:, 1024:2048]
        cdk = xf[:, 2048:3072]
        a_tok = xf[:, 3072:4096]
        sf = self.bSQ[:].bitcast(F32)
        acs_tok = sf[:, 0:1024]
        raw = sf[:, 1024:2048]
        hf = self.bH[:].bitcast(F32)
        AT = hf[0:48, 0:S]
        c1 = hf[0:48, S:2 * S]
        acsT = self.bX[0:48, 0:S]
        nacsT = self.bX[0:48, S:2 * S]
        dm = lambda t: t.rearrange("p (d c h) -> p c d h", d=2, h=16)
        self.dma(raw, DTt.rearrange("p c j -> p (c j)"), [("DTt", g) for g in range(NG)], ["dtraw"])
        for d in range(2):
            self.op("dve", lambda e, d=d: e.tensor_tensor(
                out=dtk[:, d * 512:(d + 1) * 512].rearrange("p (c h) -> p c h", h=16),
                in0=raw.rearrange("p (c j) -> p c j", j=32)[:, :, d * 16:(d + 1) * 16],
                in1=self.abrow[:, 1:2, d * 16:(d + 1) * 16].broadcast_to([128, 32, 16]), op=ALU.add),
                ["dtraw", "abrow"], ["dtk"])
        self.op("act", lambda e: e.activation(out=dtk, in_=dtk, func=AF.Exp), ["dtk"], ["dtk"])
        self.op("act", lambda e: e.activation(out=dtk, in_=dtk, func=AF.Ln, bias=1.0), ["dtk"], ["dtk"])
        for d in range(2):
            self.op("dve", lambda e, d=d: e.tensor_tensor(
                out=a_tok[:, d * 512:(d + 1) * 512].rearrange("p (c h) -> p c h", h=16),
                in0=dtk[:, d * 512:(d + 1) * 512].rearrange("p (c h) -> p c h", h=16),
                in1=self.abrow[:, 0:1, d * 16:(d + 1) * 16].broadcast_to([128, 32, 16]), op=ALU.mult),
                ["dtk", "abrow"], ["a_tok"])
        import os
        dcut = int(os.environ.get("DT_CUT", "99"))
        if dcut <= 1:
            return
        k1, p1 = self.psum_group()
        for d in range(2):
            self.op("pe", lambda e, d=d: e.matmul(p1[:, d, :], lhsT=self.tri[:, d, :], rhs=a_tok[:, d * 512:(d + 1) * 512],
                                                   start=True, stop=True), ["a_tok", "tri"], [k1])
        self.op("dve", lambda e: e.tensor_copy(out=acs_tok, in_=self.flat(p1)), [k1], ["acs_tok"])
        k2, p2 = self.psum_group()
        for d in range(2):
            self.op("pe", lambda e, d=d: e.matmul(p2[:, d, :], lhsT=self.ones_f[:], rhs=a_tok[:, d * 512:(d + 1) * 512],
                                                   start=True, stop=True), ["a_tok", "ones_f"], [k2])
        self.op("act", lambda e: e.activation(out=cdk, in_=self.flat(p2), func=AF.Exp), [k2], ["cdk"])
        self.op("dve", lambda e: e.tensor_tensor(out=dtd, in0=self.flat(p2), in1=acs_tok, op=ALU.subtract), [k2, "acs_tok"], ["dtd"])
        self.op("act", lambda e: e.activation(out=dtd, in_=dtd, func=AF.Exp), ["dtd"], ["dtd"])
        self.op("dve", lambda e: e.tensor_tensor(out=dtd, in0=dtd, in1=dtk, op=ALU.mult), ["dtd", "dtk"], ["dtd"])
        if dcut <= 2:
            return
        self.op("pool", lambda e: e.memset(AT, 0.0), [], ["AT"])
        self.dma(AT[0:16, :], DTr[0:16, :], ["AT"] + [("DTr", g) for g in range(NG)], ["AT"])
        self.dma(AT[32:48, :], DTr[16:32, :], ["AT"], ["AT"])
        self.op("act", lambda e: e.activation(out=AT, in_=AT, func=AF.Exp, bias=self.abcol[:, 1:2]), ["AT", "abcol"], ["AT"])
        self.op("act", lambda e: e.activation(out=AT, in_=AT, func=AF.Ln, bias=1.0), ["AT"], ["AT"])
        self.op("dve", lambda e: e.tensor_scalar(out=AT, in0=AT, scalar1=self.abcol[:, 0:1], scalar2=None, op0=ALU.mult),
                ["AT", "abcol"], ["AT"])
        if dcut <= 3:
            return
        c3 = lambda t: t.rearrange("p (c l) -> p c l", l=128)
        srcb, dstb = AT, c1
        names = {id(AT): "AT", id(c1): "c1", id(acsT): "acsT"}
        bufs = [c1, acsT]
        cur, curk = AT, "AT"
        step = 1
        n = 0
        while step < 128:
            dst = bufs[n % 2]
            dk = ["c1", "acsT"][n % 2]
            self.op("dve", lambda e, dst=dst, cur=cur, step=step: e.tensor_copy(out=c3(dst)[:, :, 0:step], in_=c3(cur)[:, :, 0:step]),
                    [curk], [dk])
            self.op("dve", lambda e, dst=dst, cur=cur, step=step: e.tensor_tensor(
                out=c3(dst)[:, :, step:128], in0=c3(cur)[:, :, step:128], in1=c3(cur)[:, :, 0:128 - step], op=ALU.add),
                [curk], [dk])
            cur, curk = dst, dk
            step *= 2
            n += 1
        cs = cur
        csk = curk
        self.op("dve", lambda e: e.tensor_copy(out=acsT[0:16, :], in_=cs[0:16, :]), [csk], ["acsT"])
        self.op("dve", lambda e: e.tensor_tensor(out=c3(acsT)[32:48], in0=c3(cs)[32:48, :, 127:128].broadcast_to([16, 32, 128]),
                                                 in1=c3(cs)[32:48], op=ALU.subtract), [csk, "acsT"], ["acsT"])
        self.op("dve", lambda e: e.tensor_tensor(out=acsT[32:48, :], in0=acsT[32:48, :], in1=AT[32:48, :], op=ALU.add),
                ["acsT", "AT"], ["acsT"])
        self.op("dve", lambda e: e.tensor_scalar(out=nacsT[0:16, :], in0=acsT[0:16, :], scalar1=-1.0, scalar2=None, op0=ALU.mult),
                ["acsT"], ["nacsT"])
        self.op("dve", lambda e: e.tensor_scalar(out=nacsT[32:48, :], in0=acsT[32:48, :], scalar1=-1.0, scalar2=None, op0=ALU.mult),
                ["acsT"], ["nacsT"])

    def stage_odd_ssd(self, aps, XBC, Y, d):
        xf = self.bXN[:].bitcast(F32)
        dtk = xf[:, 0:1024].rearrange("p (d c h) -> p d c h", d=2, h=16)
        dtd = xf[:, 1024:2048].rearrange("p (d c h) -> p d c h", d=2, h=16)
        cdk = xf[:, 2048:3072].rearrange("p (d c h) -> p d c h", d=2, h=16)
        acsT = self.bX[0:48, 0:S]
        nacsT = self.bX[0:48, S:2 * S]
        r0 = 0 if d == 0 else 32
        state = self.bR[:, 0:1024]
        state_bf = self.bR[:, 1024:1536].bitcast(BF16)
        hb = self.bH
        hf = hb[:].bitcast(F32)
        CB0 = [0, 16384]
        tokb_ = [hb[:, o:o + 1536] for o in CB0]
        Xdt_ = [hb[:, o + 1536:o + 2560].rearrange("p (h q) -> p h q", h=16) for o in CB0]
        Xdd_ = [hb[:, o + 2560:o + 3584].rearrange("p (h q) -> p h q", h=16) for o in CB0]
        MT = [hb[:, 3584 + i * 512:3584 + (i + 1) * 512].rearrange("p (h l) -> p h l", h=4) for i in range(2)]
        Cs = [hb[:, 4608 + i * 512:4608 + (i + 1) * 512].rearrange("p (h l) -> p h l", h=4) for i in range(2)]
        Lf = [hf[:, 3072 + i * 512:3072 + (i + 1) * 512] for i in range(2)]
        Ef = [hf[:, 4096 + i * 512:4096 + (i + 1) * 512] for i in range(2)]
        rhs2 = [hf[0:48, 5120 + i * 512:5120 + (i + 1) * 512] for i in range(2)]
        fbs = [hb[:, 12288 + i * 2048:12288 + (i + 1) * 2048].rearrange("p (c t) -> p c t", c=16) for i in range(2)]
        fxs = [self.bSQ[:].bitcast(F32)[:, i * 2048:(i + 1) * 2048].rearrange("p (c t) -> p c t", c=16) for i in range(2)]
        yst = [self.bT[0], self.bT[1], self.bSG[0], self.bSG[1]]
        ysk = [("bT", 0), ("bT", 1), ("bSG", 0), ("bSG", 1)]
        PS = self.ps
        self.op("pool", lambda e: e.memset(self.bR[:, 0:1536], 0.0), [], [("st", g) for g in range(4)] + [("stb", g) for g in range(4)])
        order = list(range(S // 128)) if d == 0 else list(range(S // 128 - 1, -1, -1))
        items = [(ci, c, g) for ci, c in enumerate(order) for g in range(4)]

        def prep(ci, c):
            fi = ci % 2
            sl = slice(c * 128, (c + 1) * 128)
            fx, fb = fxs[fi], fbs[fi]
            self.dma(fx, XBC[:, :, sl].rearrange("c p t -> p c t"), [("XBC", c // (TG // 128))], [("sfx", fi)])
            self.op("pool", lambda e, o=fb, i_=fx: e.tensor_copy(out=o, in_=i_), [("sfx", fi)], [("sfb", fi)])
            tk = tokb_[fi]
            for half in range(2):
                pt = PS[:, 6, :].bitcast(BF16)
                for i in range(6):
                    ii = half * 6 + i
                    self.op("pe", lambda e, o=pt[:, i * 128:(i + 1) * 128], i_=fb[:, ii, :]: e.transpose(out=o, in_=i_, identity=self.ident_b[:]),
                            [("sfb", fi), "ident_b"], [("psb", 6)])
                self.op("act", lambda e, o=tk[:, half * 768:(half + 1) * 768], i_=pt[:, 0:768]: e.copy(out=o, in_=i_),
                        [("psb", 6)], [("tokb", fi, half)])
            xs3 = tk[:, 0:1024].rearrange("p (h q) -> p h q", h=16)
            self.op("dve", lambda e, o=Xdt_[fi], c=c: e.tensor_tensor(out=o, in0=xs3, in1=dtk[:, d, c, :].unsqueeze(2).broadcast_to([128, 16, 64]), op=ALU.mult),
                    [("tokb", fi, 0), ("tokb", fi, 1), "dtk"], [("Xdt", fi)])
            self.op("pool", lambda e, o=Xdd_[fi], c=c: e.tensor_tensor(out=o, in0=xs3, in1=dtd[:, d, c, :].unsqueeze(2).broadcast_to([128, 16, 64]), op=ALU.mult),
                    [("tokb", fi, 0), ("tokb", fi, 1), "dtd"], [("Xdd", fi)])

        def front(idx):
            ci, c, g = items[idx]
            if g == 0:
                prep(ci, c)
            fi = ci % 2
            bi = idx % 2
            sl = slice(c * 128, (c + 1) * 128)
            fb = fbs[fi]
            slot = idx % 4
            cbt = PS[:, 0, slot * 128:(slot + 1) * 128]
            kcb = ("psb", 0, slot)
            self.op("pe", lambda e, o=cbt, l=fb[:, 8 + g, :], r=fb[:, 12 + g, :]: e.matmul(o, lhsT=l, rhs=r, start=True, stop=True),
                    [("sfb", fi)], [kcb])
            r2 = rhs2[bi][r0:r0 + 16, :]
            kr2 = ("rhs2", bi)
            self.op("dve", lambda e, o=r2.rearrange("p (h l) -> p h l", h=4), sl=sl, g=g:
                    e.tensor_tensor(out=o, in0=acsT[r0:r0 + 16, sl].unsqueeze(1).broadcast_to([16, 4, 128]),
                                    in1=self.selm[r0:r0 + 16, g, :, :], op=ALU.mult), ["acsT", "selm"], [kr2])
            pdiff = PS[:, 1 + bi, :]
            kdf = ("psb", 1 + bi)
            self.op("pe", lambda e, o=pdiff, sl=sl, g=g: e.matmul(o, lhsT=nacsT[r0:r0 + 16, sl], rhs=self.selm[r0:r0 + 16, g, :, :].rearrange("p h l -> p (h l)"),
                                                       start=True, stop=False), ["nacsT", "selm"], [kdf])
            self.op("pe", lambda e, o=pdiff, r=r2: e.matmul(o, lhsT=self.ones_f[r0:r0 + 16, :], rhs=r, start=False, stop=False),
                    ["ones_f", kr2], [kdf])
            self.op("pe", lambda e, o=pdiff: e.matmul(o, lhsT=self.ident_b[:], rhs=self.mbias[:, d, :], start=False, stop=True),
                    ["ident_b", "mbias"], [kdf])
            pE = PS[:, 3 + bi, :]
            kpe = ("psb", 3 + bi)
            self.op("pe", lambda e, o=pE, r=r2: e.matmul(o, lhsT=self.ones_f[r0:r0 + 16, :], rhs=r, start=True, stop=True),
                    ["ones_f", kr2], [kpe])
            L = Lf[bi]
            kL = ("Lf", bi)
            self.op("act", lambda e, o=L, i_=pdiff: e.activation(out=o, in_=i_, func=AF.Exp), [kdf], [kL])
            E = Ef[bi]
            kE = ("Ef", bi)
            self.op("act", lambda e, o=E, i_=pE: e.activation(out=o, in_=i_, func=AF.Exp), [kpe], [kE])
            mt = MT[bi]
            self.op("dve", lambda e, o=mt, a=L.rearrange("p (h l) -> p h l", h=4), b_=cbt.unsqueeze(1).broadcast_to([128, 4, 128]):
                    e.tensor_tensor(out=o, in0=a, in1=b_, op=ALU.mult), [kL, kcb], [("MT", bi)])
            self.op("pool", lambda e, o=Cs[bi], a=E.rearrange("p (h l) -> p h l", h=4), b_=fb[:, 12 + g, :].unsqueeze(1).broadcast_to([128, 4, 128]):
                    e.tensor_tensor(out=o, in0=a, in1=b_, op=ALU.mult), [kE, ("sfb", fi)], [("Cs", bi)])

        def back(idx):
            ci, c, g = items[idx]
            fi = ci % 2
            bi = idx % 2
            sl = slice(c * 128, (c + 1) * 128)
            mt, cs_ = MT[bi], Cs[bi]
            Xdt, Xdd, tk = Xdt_[fi], Xdd_[fi], tokb_[fi]
            py = PS[0:64, 5, :]
            ky = ("psb", 5)
            for hh in range(4):
                head = 4 * g + hh
                self.op("pe", lambda e, o=py[:, hh * 128:(hh + 1) * 128], l=Xdt[:, head, :], r=mt[:, hh, :]:
                        e.matmul(o, lhsT=l, rhs=r, start=True, stop=False), [("Xdt", fi), ("MT", bi)], [ky])
                self.op("pe", lambda e, o=py[:, hh * 128:(hh + 1) * 128], l=state_bf[:, head * 64:(head + 1) * 64], r=cs_[:, hh, :]:
                        e.matmul(o, lhsT=l, rhs=r, start=False, stop=True), [("stb", g), ("Cs", bi)], [ky])
            ys = yst[g][0:64, 0:512]
            self.op("act", lambda e, o=ys, i_=py: e.copy(out=o, in_=i_), [ky], [ysk[g]])
            self.dma(Y[g * 256:(g + 1) * 256, sl].rearrange("(h p) l -> p h l", p=64), ys.rearrange("p (h l) -> p h l", h=4),
                     [ysk[g]], [("Y", d, c // (TG // 128))])
            pst = PS[:, 7, bi * 256:(bi + 1) * 256]
            kst = ("psb", 7, bi)
            self.op("pe", lambda e, o=pst, l=tk[:, (8 + g) * 128:(9 + g) * 128], r=Xdd[:, 4 * g:4 * g + 4, :]:
                    e.matmul(o, lhsT=l, rhs=r, start=True, stop=True), [("tokb", fi, 1), ("Xdd", fi)], [kst])
            stg = state[:, g * 256:(g + 1) * 256]
            self.op("dve", lambda e, o=stg.rearrange("p (h q) -> p h q", h=4), c=c, g=g:
                    e.tensor_tensor(out=o, in0=o, in1=cdk[:, d, c, 4 * g:4 * g + 4].unsqueeze(2).broadcast_to([128, 4, 64]), op=ALU.mult),
                    [("st", g), "cdk"], [("st", g)])
            self.op("dve", lambda e, o=stg, i_=pst: e.tensor_tensor(out=o, in0=o, in1=i_, op=ALU.add),
                    [("st", g), kst], [("st", g)])
            self.op("act", lambda e, o=state_bf[:, g * 256:(g + 1) * 256], i_=stg: e.copy(out=o, in_=i_), [("st", g)], [("stb", g)])

        n = len(items)
        for idx in range(n):
            front(idx)
            if idx >= 1:
                back(idx - 1)
        back(n - 1)

    def stage_odd_out(self, X, aps, li, npost, Yf, Yb, XBC, ZS, MO):
        xv, xn_, sq, kX, kXN_, kSQ = self.views()
        rhs = self.bH[:, 0:12 * TG].rearrange("p (c t) -> p c t", c=12)
        rkeys = [("bH", m) for m in range(12)]
        for g in range(NG):
            t0 = g * TG
            self.dma(rhs[:, 0:4, :], MO[:, :, t0:t0 + TG].rearrange("c p t -> p c t"), [("MO", g)], rkeys[0:4])
            for c in range(8):
                a, ka = self.stg4()
                b, kb = self.stg4()
                xs, kx = self.stg4()
                z, kz = self.stg4()
                self.dma(a[:], Yf[c * 128:(c + 1) * 128, t0:t0 + TG], [("Y", 0, g)], [ka])
                self.dma(b[:], Yb[c * 128:(c + 1) * 128, t0:t0 + TG], [("Y", 1, g)], [kb])
                self.dma(xs[:], XBC[c, :, t0:t0 + TG], [("XBC", g)], [kx])
                self.dma(z[:], ZS[c, :, t0:t0 + TG], [("ZS", g)], [kz])
                self.op("pool", lambda e, o=a[:], b_=b[:]: e.tensor_tensor(out=o, in0=o, in1=b_, op=ALU.add), [ka, kb], [ka])
                self.op("dve", lambda e, o=a[:], x_=xs[:], s=self.dexp[:, c:c + 1]:
                        e.scalar_tensor_tensor(out=o, in0=x_, scalar=s, in1=o, op0=ALU.mult, op1=ALU.add), [ka, kx, "dexp"], [ka])
                self.op("dve", lambda e, o=xv[:, c, :], a_=a[:], z_=z[:]: e.tensor_tensor(out=o, in0=a_, in1=z_, op=ALU.mult),
                        [ka, kz], [kX[c]])
                self.op("act", lambda e, o=sq[:, c, :], i_=xv[:, c, :]: e.activation(out=o, in_=i_, func=AF.Square),
                        [kX[c]], [kSQ[c]])
            rk, rv = self.rms_rinv(sq, 8, 0, 1.0 / D, EPS, kSQ)
            for c in range(8):
                self.op("dve", lambda e, o=rhs[:, 4 + c, :], i_=xv[:, c, :], s=self.snw[:, c:c + 1], r=rv:
                        e.scalar_tensor_tensor(out=o, in0=i_, scalar=s, in1=r, op0=ALU.mult, op1=ALU.mult),
                        [kX[c], "snw", rk], [rkeys[4 + c]])
            self.proj_post_residual(X, g, rhs, rkeys, 12, ("od_wout", li), npost, False)

def build_program(stop=None, only=None):
    _, plan = _build(stop, only, None)
    nc, _ = _build(stop, only, plan)
    return nc


def _build(stop, only, wplan):
    from contextlib import ExitStack
    nc = bass.Bass("TRN2", target_bir_lowering=False)
    dt = nc.dram_tensor
    aps = {}

    def inp(name, shape):
        aps[name] = dt(name, shape, F32, kind="ExternalInput").ap()

    inp("x", [S, D])
    inp("normw", [128, DEPTH * 6 * 8])
    import os
    nff = 1 if os.environ.get("DEV_SMALL") else DEPTH * 2
    inp("wg", [nff, D * DFF])
    inp("wu", [nff, D * DFF])
    inp("wd", [nff, DFF * D])
    inp("ev_win", [2, D * 2304])
    inp("ev_wv", [2, D * 256])
    inp("ev_pool_w", [2, 4, 64, 64])
    inp("ev_pool_scale", [2, 128, 2])
    inp("ev_sink", [2, 12])
    inp("ev_wout", [2, D * D])
    inp("ropec", [128, S])
    inp("ropes", [128, S])
    inp("icnt", [2, 128, S])
    inp("amask", [128, 2, 384])
    inp("od_win", [2, D * 4096])
    inp("od_wdt", [2, D, 32])
    inp("od_wout", [2, 1536 * D])
    inp("cv_dww", [2, 128, 4, 31])
    inp("cv_vec", [2, 128, 3, 4])
    inp("ssm_cw", [2, 128, 16, 5])
    inp("ssm_dexp", [2, 128, 8])
    inp("ssm_nw", [2, 128, 8])
    inp("ssm_ab", [2, 48, 2])
    inp("ssm_A_log", [2, 32])
    inp("ssm_dt_bias", [2, 32])
    inp("tri", [128, 2, 128])
    inp("selm", [48, 4, 4, 128])
    inp("mbias", [128, 2, 512])
    inp("rmask", [48, 128])
    out = dt("out", [S, D], F32, kind="ExternalOutput").ap()
    X = dt("Xs", [8, 128, S], F32, kind="Internal").ap()
    QT = dt("QTs", [768, S], BF16, kind="Internal").ap()
    KT = dt("KTs", [256, S], BF16, kind="Internal").ap()
    V = dt("Vs", [S, 256], BF16, kind="Internal").ap()
    U = dt("Us", [2, 128, S], F32, kind="Internal").ap()
    AO = dt("AOs", [8, 128, S], BF16, kind="Internal").ap()
    Gs = dt("Gs", [4, 128, S], F32, kind="Internal").ap()
    ZS = dt("ZSs", [8, 128, S], F32, kind="Internal").ap()
    XBCr = dt("XBCrs", [16, 128, S], F32, kind="Internal").ap()
    XBC = dt("XBCs", [16, 128, S], F32, kind="Internal").ap()
    DTr = dt("DTrs", [32, S], F32, kind="Internal").ap()
    DTt = dt("DTts", [128, 32, 32], F32, kind="Internal").ap()
    Yf = dt("Yfs", [1024, S], F32, kind="Internal").ap()
    Yb = dt("Ybs", [1024, S], F32, kind="Internal").ap()
    MO = dt("MOs", [4, 128, S], BF16, kind="Internal").ap()
    B = Builder(nc, stop, aps, wplan)
    B.init_consts(aps)
    B.stage_load(aps["x"], X)
    nst = 0
    seq = [(l, sub) for l in range(DEPTH) for sub in range(3)]
    if stop is not None:
        seq = seq[:stop]
    if only is not None:
        seq = only
    for (l, sub) in seq:
        base = l * 6 * 8
        if True:
            if sub == 0:
                B.sc.barrier(); B.stage_ffn(X, ("wg", l * 2), ("wu", l * 2), ("wd", l * 2), base + 0, base + 8)
            elif sub == 2:
                B.sc.barrier(); B.stage_ffn(X, ("wg", l * 2 + 1), ("wu", l * 2 + 1), ("wd", l * 2 + 1), base + 32, base + 40)
            elif l % 2 == 0:
                li = l // 2
                B.even_consts(aps, li)
                B.sc.barrier(); B.stage_even_in(X, aps, li, base + 16, QT, KT, V, U)
                B.sc.barrier(); B.stage_even_pool(aps, U, AO)
                B.sc.barrier(); B.stage_even_attn(aps, li, QT, KT, V, AO)
                B.sc.barrier(); B.stage_even_out(X, aps, li, base + 24, AO)
            else:
                li = l // 2
                import os
                ocut = int(os.environ.get("ODD_CUT", "99"))
                B.odd_consts(aps, li)
                oskip = int(os.environ.get("ODD_SKIP", "0"))
                if ocut >= 1 and oskip < 1:
                    B.sc.barrier(); B.stage_odd_in(X, aps, li, base + 16, Gs, ZS, XBCr, DTr, DTt)
                if ocut >= 2 and oskip < 2:
                    B.sc.barrier(); B.stage_odd_conv(aps, li, Gs, XBCr, XBC, MO)
                if ocut >= 3:
                    B.sc.barrier(); B.stage_odd_dt(aps, li, DTr, DTt)
                if ocut >= 4:
                    B.sc.barrier(); B.stage_odd_ssd(aps, XBC, Yf, 0)
                if ocut >= 5:
                    B.sc.barrier(); B.stage_odd_ssd(aps, XBC, Yb, 1)
                if ocut >= 6:
                    B.sc.barrier(); B.stage_odd_out(X, aps, li, base + 24, Yf, Yb, XBC, ZS, MO)
    B.sc.barrier(); B.stage_store(X, out)
    fin = [("out", g) for g in range(NG)]
    if os.environ.get("DBG"):
        def dbg(name, ap, keys):
            o = dt("dbg_" + name, list(ap.shape), ap.dtype, kind="ExternalOutput").ap()
            B.dma(o, ap, keys, [("dbg", name)])
            fin.append(("dbg", name))
        G4 = range(NG)
        dbg("G", Gs, [("G", g) for g in G4])
        dbg("XBC", XBC, [("XBC", g) for g in G4])
        dbg("MO", MO, [("MO", g) for g in G4])
        dbg("ZS", ZS, [("ZS", g) for g in G4])
        dbg("Yf", Yf, [("Y", 0, g) for g in G4])
        dbg("Yb", Yb, [("Y", 1, g) for g in G4])
        dbg("DTr", DTr, [("DTr", g) for g in G4])
        dbg("tok", B.bXN[:].bitcast(F32)[:, 0:3072], ["dtk", "dtd", "cdk"])
        dbg("acsT", B.bX[0:48, 0:S], ["acsT"])
    B.op("sp", lambda e: e.nop(), fin, [])
    if wplan is None:
        return None, B.wrec
    with ExitStack() as st:
        B.sc.emit(st)
    return nc, B.wrec


def rope_tables_np():
    inv = (1.0 / (np.float32(10000.0) ** (np.arange(0, 64, 2, dtype=np.float32) / np.float32(64)))).astype(np.float32)
    ang = (np.arange(S, dtype=np.float32)[:, None] * inv[None, :]).astype(np.float32)
    cos = np.cos(ang).astype(np.float32)
    sin = np.sin(ang).astype(np.float32)
    cf = np.concatenate([cos, cos], axis=1)
    sf = np.concatenate([-sin, sin], axis=1)
    cT = np.ascontiguousarray(np.concatenate([cf, cf], axis=1).T)
    sT = np.ascontiguousarray(np.concatenate([sf, sf], axis=1).T)
    return cT, sT


def const_tables():
    cT, sT = rope_tables_np()
    t = np.arange(S)
    icnt = np.zeros((2, 128, S), np.float32)
    for gi, w in enumerate((2, 4, 8, 16)):
        lo = np.clip(t - w // 2, 0, S)
        hi = np.clip(t + w - w // 2, 0, S)
        icnt[gi // 2, (gi % 2) * 64:(gi % 2) * 64 + 64, :] = (1.0 / (hi - lo).astype(np.float32))[None, :]
    kl = np.arange(128)[:, None]
    ql = np.arange(128)[None, :]
    m1 = (ql <= kl).astype(np.float32)
    m2 = (kl <= ql).astype(np.float32)
    amask = np.stack([np.tile(m1, (1, 3)), np.tile(m2, (1, 3))], axis=1)
    s_ = np.arange(128)[:, None]
    l_ = np.arange(128)[None, :]
    tri = np.stack([(s_ <= l_), (s_ >= l_)], axis=1).astype(np.float32)
    selm = np.zeros((48, 4, 4, 128), np.float32)
    for k in range(16):
        selm[k, k // 4, k % 4, :] = 1.0
        selm[32 + k, k // 4, k % 4, :] = 1.0
    NEG = -30000.0
    mb = np.stack([np.tile(np.where(l_ >= s_, 0.0, NEG), (1, 4)), np.tile(np.where(s_ >= l_, 0.0, NEG), (1, 4))], axis=1)
    rmask = np.ones((48, 128), np.float32)
    rmask[:, 0] = 0.0
    return {"ropec": cT, "ropes": sT, "icnt": icnt, "amask": np.ascontiguousarray(amask),
            "tri": np.ascontiguousarray(tri), "selm": selm, "mbias": np.ascontiguousarray(mb.astype(np.float32)),
            "rmask": rmask}


def make_in_maps(inputs):
    nw = np.ascontiguousarray(
        inputs["norm_w"].reshape(DEPTH, 6, 8, 128).transpose(3, 0, 1, 2).reshape(128, DEPTH * 6 * 8))
    qk = np.arange(256, 1280)
    rel = (qk - 256) % 64
    partner = qk - rel + np.where(rel < 32, rel + 32, rel - 32)
    cols = []
    for i in range(8):
        cols.append(qk[i * 128:(i + 1) * 128])
        cols.append(partner[i * 128:(i + 1) * 128])
    cols.append(np.arange(0, 256))
    cols = np.concatenate(cols)
    shared = {
        "normw": nw,
        "wg": tileize(inputs["ffn_w_gate"].reshape(DEPTH * 2, D, DFF)),
        "wu": tileize(inputs["ffn_w_up"].reshape(DEPTH * 2, D, DFF)),
        "wd": tileize(inputs["ffn_w_down"].reshape(DEPTH * 2, DFF, D)),
        "ev_win": tileize(inputs["ev_w_in"][:, :, cols]),
        "ev_wv": tileize(inputs["ev_w_in"][:, :, 1280:1536]),
        "ev_pool_w": np.ascontiguousarray(inputs["ev_pool_w"]),
        "ev_pool_scale": np.ascontiguousarray(inputs["ev_pool_scale"].reshape(2, 2, 128).transpose(0, 2, 1)),
        "ev_sink": np.ascontiguousarray(inputs["ev_sink"]),
        "ev_wout": tileize(inputs["ev_w_out"]),
    }
    ocols = []
    for m in range(4):
        ocols.append(np.arange(m * 128, (m + 1) * 128))
        ocols.append(np.arange(512 + m * 128, 512 + (m + 1) * 128))
    ocols.append(np.arange(1024, 4128))
    ocols = np.concatenate(ocols)
    f32 = np.float32
    ab = np.zeros((2, 48, 2), f32)
    ab[:, 0:16, 0] = inputs["ssm_A_log"][:, 0]
    ab[:, 32:48, 0] = inputs["ssm_A_log"][:, 1]
    ab[:, 0:16, 1] = inputs["ssm_dt_bias"][:, 0]
    ab[:, 32:48, 1] = inputs["ssm_dt_bias"][:, 1]
    cw = np.concatenate([inputs["ssm_conv_w"].reshape(2, 4, 16, 128).transpose(0, 3, 2, 1),
                         inputs["ssm_conv_b"].reshape(2, 16, 128).transpose(0, 2, 1)[..., None]], axis=-1)
    shared.update({
        "od_win": tileize(inputs["od_w_in"][:, :, ocols[:4096]]),
        "od_wdt": np.ascontiguousarray(inputs["od_w_in"][:, :, 4096:4128]),
        "od_wout": tileize(inputs["od_w_out"]),
        "cv_dww": np.ascontiguousarray(inputs["cv_dw_w"].reshape(2, 31, 4, 128).transpose(0, 3, 2, 1)),
        "cv_vec": np.ascontiguousarray(np.stack([inputs["cv_dw_b"], inputs["cv_ln_g"], inputs["cv_ln_b"]], axis=1)
                                       .reshape(2, 3, 4, 128).transpose(0, 3, 1, 2)),
        "ssm_cw": np.ascontiguousarray(cw.astype(f32)),
        "ssm_dexp": np.ascontiguousarray(np.repeat(inputs["ssm_D"], 64, axis=1).reshape(2, 8, 128).transpose(0, 2, 1)),
        "ssm_nw": np.ascontiguousarray(inputs["ssm_norm_w"].reshape(2, 8, 128).transpose(0, 2, 1)),
        "ssm_ab": ab,
        "ssm_A_log": np.ascontiguousarray(inputs["ssm_A_log"].reshape(2, 32)),
        "ssm_dt_bias": np.ascontiguousarray(inputs["ssm_dt_bias"].reshape(2, 32)),
    })
    shared.update(const_tables())
    maps = []
    for b in range(NCORES):
        m = dict(shared)
        m["x"] = np.ascontiguousarray(inputs["x"][b])
        maps.append(m)
    return maps


def kernel(**inputs):
    inputs = {k: np.asarray(v) for k, v in inputs.items()}
    nc = build_program()
    in_maps = make_in_maps(inputs)
    res = run_bass_kernel_spmd(nc, in_maps, core_ids=list(range(NCORES)))
    return np.stack([np.asarray(r["out"]) for r in res.results], axis=0).astype(np.float32)
```

```python
import numpy as np
import concourse.bass as bass
import concourse.mybir as mybir
from concourse.bass_utils import run_bass_kernel_spmd

F32 = mybir.dt.float32
BF16 = mybir.dt.bfloat16
ALU = mybir.AluOpType
AF = mybir.ActivationFunctionType

D = 1024
S = 4096
DEPTH = 4
DFF = 2816
NCORES = 8
EPS = 1e-6
TG = 1024
NG = S // TG
NB = TG // 512


W_TILED = {"wg": (D, DFF), "wu": (D, DFF), "wd": (DFF, D), "ev_win": (D, 2304), "ev_wv": (D, 256),
           "ev_wout": (D, D), "od_win": (D, 4096), "od_wout": (1536, D)}


def tile_offset(K, M, kt, ct):
    nct = M // 256
    off = 0
    for k_ in range(kt):
        off += nct * 128 * min(8, (K - k_ * 1024) // 128) * 256
    return off + ct * 128 * min(8, (K - kt * 1024) // 128) * 256


def tileize(W):
    lead = W.shape[:-2]
    K, M = W.shape[-2:]
    W2 = W.reshape((-1, K, M))
    outs = []
    for kt in range((K + 1023) // 1024):
        nk = min(8, (K - kt * 1024) // 128)
        blk = W2[:, kt * 1024:kt * 1024 + nk * 128, :].reshape(-1, nk, 128, M // 256, 256)
        outs.append(np.ascontiguousarray(blk.transpose(0, 3, 2, 1, 4)).reshape(W2.shape[0], -1))
    return np.ascontiguousarray(np.concatenate(outs, axis=1)).reshape(lead + (K * M,))


class Op:
    __slots__ = ("eng", "fn", "dma", "deps", "idx", "sem", "semval", "sig", "presem")


class Sched:
    ENG = ("pe", "act", "dve", "pool", "sp")

    def __init__(self, nc):
        self.nc = nc
        self.ops = []
        self.last_w = {}
        self.readers = {}
        self.dma_since = []
        self.last_on = {}

    def barrier(self):
        frontier = set(self.dma_since)
        for e, idx in self.last_on.items():
            frontier.add(idx)
        self.dma_since = []
        for e in self.ENG:
            self.add(e, lambda eng: eng.nop(), extra=frontier)

    def add(self, eng, fn, reads=(), writes=(), dma=False, extra=()):
        op = Op()
        op.eng = eng
        op.fn = fn
        op.dma = dma
        op.idx = len(self.ops)
        op.sig = False
        op.sem = None
        op.semval = 0
        op.presem = None
        deps = {}
        for k in reads:
            w = self.last_w.get(k)
            if w is not None:
                deps[w] = True
            if isinstance(k, tuple) and k[0] == "ps":
                for r in self.readers.get(k, ()):
                    if self.ops[r].eng != eng:
                        deps[r] = True
        for k in writes:
            w = self.last_w.get(k)
            if w is not None:
                deps.setdefault(w, False)
            for r in self.readers.get(k, ()):
                deps.setdefault(r, False)
        for k in reads:
            self.readers.setdefault(k, []).append(op.idx)
        for k in writes:
            self.last_w[k] = op.idx
            self.readers[k] = []
        for x in extra:
            deps[x] = True
        deps.pop(op.idx, None)
        op.deps = deps
        self.ops.append(op)
        if dma:
            self.dma_since.append(op.idx)
        else:
            self.last_on[eng] = op.idx
        return op

    def _needs_sync(self, c, p, raw):
        if p.dma:
            return True
        if c.eng != p.eng:
            return True
        if c.dma:
            return True
        if c.eng == "pe":
            return False
        return raw

    def emit(self, stack):
        nc = self.nc
        ops = self.ops
        for c in ops:
            for pi, raw in c.deps.items():
                p = ops[pi]
                if not p.dma and self._needs_sync(c, p, raw):
                    p.sig = True
        NDMA = 40
        dma_sems = [stack.enter_context(nc.semaphore("dq%d" % i)) for i in range(NDMA)]
        dma_cum = [0] * NDMA
        dma_rr = 0
        EPOCH = 20000
        eng_sems = {e: [] for e in self.ENG}
        eng_cnt = {e: 0 for e in self.ENG}
        for op in ops:
            if op.dma:
                i = dma_rr % NDMA
                dma_rr += 1
                op.sem = dma_sems[i]
                op.presem = (dma_sems[i], dma_cum[i]) if dma_cum[i] > 0 else None
                dma_cum[i] += 16
                op.semval = dma_cum[i]
            elif op.sig:
                e = op.eng
                if eng_cnt[e] % EPOCH == 0:
                    eng_sems[e].append(stack.enter_context(
                        nc.semaphore("e_%s_%d" % (e, len(eng_sems[e])))))
                eng_cnt[e] += 1
                op.sem = eng_sems[e][-1]
                op.semval = (eng_cnt[e] - 1) % EPOCH + 1
        per = {e: [] for e in self.ENG}
        for op in ops:
            per[op.eng].append(op)

        def run(engname, eng):
            waited = {}
            for op in per[engname]:
                ws = []
                for pi, raw in op.deps.items():
                    p = ops[pi]
                    if self._needs_sync(op, p, raw):
                        ws.append((p.sem, p.semval))
                if op.presem is not None:
                    ws.append(op.presem)
                for sem, val in ws:
                    key = id(sem)
                    if waited.get(key, 0) >= val:
                        continue
                    waited[key] = val
                    eng.wait_ge(sem, val)
                ins = op.fn(eng)
                if op.dma:
                    ins.then_inc(op.sem, 16)
                elif op.sig:
                    ins.then_inc(op.sem, 1)

        with nc.Block() as block:
            @block.tensor
            def _(e):
                run("pe", e)

            @block.scalar
            def _(e):
                run("act", e)

            @block.vector
            def _(e):
                run("dve", e)

            @block.gpsimd
            def _(e):
                run("pool", e)

            @block.sync
            def _(e):
                run("sp", e)


class Builder:
    def __init__(self, nc, stop=None, aps=None, wplan=None):
        self.nc = nc
        self.aps = aps
        self.wplan = wplan
        self.wrec = []
        self.w_issued = 0
        self.PF = 2
        self.sc = Sched(nc)
        self.stop = stop
        self.uid = 0
        nc_ = nc
        A = nc_.alloc_sbuf_tensor
        self.bX = A("bX", [128, 8 * TG], F32)
        self.bXN = A("bXN", [128, 8 * TG], BF16)
        self.bSQ = A("bSQ", [128, 8 * TG], BF16)
        self.bH = A("bH", [128, 22 * TG], BF16)
        self.NWS = 4
        self.wsF = [A("wsF%d" % i, [128, 2048], F32) for i in range(self.NWS)]
        self.wsB = [A("wsB%d" % i, [128, 2048], BF16) for i in range(self.NWS)]
        self.ws_i = 0
        self.bR = A("bR", [128, 2 * TG], F32)
        self.bSG = [A("bSG%d" % i, [128, TG], F32) for i in range(2)]
        self.sg_i = 0
        self.bT = [A("bT%d" % i, [128, TG], F32) for i in range(4)]
        self.t_i = 0
        self.ones_bf = A("ones_bf", [128, 128], BF16)
        self.ident_f = A("ident_f", [128, 128], F32)
        self.normw = A("normw_sb", [128, DEPTH * 6 * 8], F32)
        self.poolW = A("poolW", [128, 2, 128], BF16)
        self.poolS = A("poolS", [128, 2], F32)
        self.esink = A("esink", [128, 12], F32)
        self.amask = A("amask_sb", [128, 2, 384], F32)
        self.stg_i = 0
        self.ones_f = A("ones_f", [128, 128], F32)
        self.ident_b = A("ident_b", [128, 128], BF16)
        self.dww = A("dww", [128, 4, 31], F32)
        self.cvvec = A("cvvec", [128, 3, 4], F32)
        self.scw = A("scw", [128, 16, 5], F32)
        self.dexp = A("dexp", [128, 8], F32)
        self.snw = A("snw", [128, 8], F32)
        self.abcol = A("abcol", [48, 2], F32)
        self.abrow = A("abrow", [128, 2, 32], F32)
        self.tri = A("tri_sb", [128, 2, 128], F32)
        self.selm = A("selm_sb", [48, 4, 4, 128], F32)
        self.mbias = A("mbias_sb", [128, 2, 512], BF16)
        self.ps = nc_.alloc_psum_tensor("ps", [128, 8, 512], F32)
        self.ps_i = 0
        self.NPG = 8 // NB

    def op(self, eng, fn, reads=(), writes=(), dma=False):
        return self.sc.add(eng, fn, reads, writes, dma)

    def psum_group(self):
        g = self.ps_i % self.NPG
        self.ps_i += 1
        key = ("ps", g)
        view = self.ps[:, g * NB:(g + 1) * NB, :]
        return key, view

    def dma(self, out, in_, reads, writes, eng="sp"):
        return self.op(eng, lambda e, o=out, i=in_: e.dma_start(out=o, in_=i), reads, writes, dma=True)

    def _issue_w(self, j, spec):
        name, idx, r0, nk, c0, ncols = spec
        W = self.aps[name][idx]
        i = j % self.NWS
        fv = self.wsF[i][:, 0:nk * ncols].rearrange("p (k m) -> p k m", k=nk)
        bv = self.wsB[i][:, 0:nk * ncols].rearrange("p (k m) -> p k m", k=nk)
        if name in W_TILED:
            K, M = W_TILED[name]
            off = tile_offset(K, M, r0 // 1024, c0 // 256)
            assert ncols == 256 and nk == min(8, (K - r0) // 128)
            src = W[off:off + 128 * nk * 256].rearrange("(p k m) -> p k m", p=128, k=nk)
        else:
            src = W[r0:r0 + nk * 128, c0:c0 + ncols].rearrange("(k p) m -> p k m", p=128)
        self.dma(fv, src, reads=[], writes=[("wsF", i)])
        if j % 2 == 0:
            self.op("act", lambda e, o=bv, s=fv: e.copy(out=o, in_=s),
                    reads=[("wsF", i)], writes=[("wsB", i)])
        else:
            self.op("dve", lambda e, o=bv, s=fv: e.tensor_copy(out=o, in_=s),
                    reads=[("wsF", i)], writes=[("wsB", i)])

    def load_w(self, Wd, r0, nk, c0, ncols):
        j = self.ws_i
        self.ws_i += 1
        spec = (Wd[0], Wd[1], r0, nk, c0, ncols)
        if self.wplan is None:
            self.wrec.append(spec)
            self._issue_w(j, spec)
        else:
            assert self.wplan[j] == spec, (j, self.wplan[j], spec)
            upto = min(j + self.PF, len(self.wplan) - 1)
            while self.w_issued <= upto:
                self._issue_w(self.w_issued, self.wplan[self.w_issued])
                self.w_issued += 1
        i = j % self.NWS
        bv = self.wsB[i][:, 0:nk * ncols].rearrange("p (k m) -> p k m", k=nk)
        return ("wsB", i), bv

    def init_consts(self, aps):
        nc = self.nc
        self.op("pool", lambda e: e.memset(self.ones_bf[:], 1.0), [], ["ones_bf"])
        self.op("pool", lambda e: e.memset(self.ident_f[:], 0.0), [], ["ident_f"])
        self.op("pool", lambda e: e.affine_select(
            out=self.ident_f[:], in_=self.ident_f[:], pattern=[[-1, 128]],
            compare_op=ALU.not_equal, fill=1.0, base=0, channel_multiplier=1),
            ["ident_f"], ["ident_f"])
        self.dma(self.normw[:], aps["normw"], [], ["normw"])
        self.dma(self.amask[:], aps["amask"], [], ["amask"])
        self.op("pool", lambda e: e.memset(self.ones_f[:], 1.0), [], ["ones_f"])
        self.op("pool", lambda e: e.tensor_copy(out=self.ident_b[:], in_=self.ident_f[:]), ["ident_f"], ["ident_b"])
        self.dma(self.tri[:], aps["tri"], [], ["tri"])
        self.dma(self.selm[:], aps["selm"], [], ["selm"])
        st_ = self.bX[:, 0:1024].rearrange("p (d t) -> p d t", d=2)
        self.dma(st_, aps["mbias"], [], [("bX", 0)])
        self.op("pool", lambda e: e.tensor_copy(out=self.mbias[:], in_=st_), [("bX", 0)], ["mbias"])

    def flat(self, pv):
        return pv.rearrange("p b t -> p (b t)")

    def stage_load(self, x, X):
        hf = self.bH[:].bitcast(F32)
        xv = self.bX[:].rearrange("p (c t) -> p c t", c=8)
        for g in range(NG):
            tv = hf[:, 0:8 * D].rearrange("p (tb f) -> p tb f", tb=8)
            self.dma(tv, x[g * TG:(g + 1) * TG, :].rearrange("(tb p) f -> p tb f", p=128),
                     [], ["bH"])
            n = 0
            for tb in range(8):
                for c0 in range(0, 8, 4):
                    key, pv = self.psum_group()
                    pf = self.flat(pv)
                    for cc in range(4):
                        c = c0 + cc
                        self.op("pe", lambda e, o=pf[:, cc * 128:(cc + 1) * 128], i=tv[:, tb, c * 128:(c + 1) * 128]:
                                e.transpose(out=o, in_=i, identity=self.ident_f[:]),
                                ["bH", "ident_f"], [key])
                    src = pf[:, 0:512].rearrange("p (c t) -> p c t", c=4)
                    dst = xv[:, c0:c0 + 4, tb * 128:(tb + 1) * 128]
                    if n % 2 == 0:
                        self.op("act", lambda e, o=dst, i=src: e.copy(out=o, in_=i), [key],
                                [("bX", c0 + k) for k in range(4)])
                    else:
                        self.op("dve", lambda e, o=dst, i=src: e.tensor_copy(out=o, in_=i), [key],
                                [("bX", c0 + k) for k in range(4)])
                    n += 1
            self.dma(X[:, :, g * TG:(g + 1) * TG].rearrange("c p t -> p c t"), xv,
                     [("bX", c) for c in range(8)], [("X", g, c) for c in range(8)])

    def stage_store(self, X, out):
        hf = self.bH[:].bitcast(F32)
        xv = self.bX[:].rearrange("p (c t) -> p c t", c=8)
        for g in range(NG):
            tv = hf[:, 0:8 * D].rearrange("p (tb f) -> p tb f", tb=8)
            self.dma(xv, X[:, :, g * TG:(g + 1) * TG].rearrange("c p t -> p c t"),
                     [("X", g, c) for c in range(8)], [("bX", c) for c in range(8)])
            n = 0
            for tb in range(8):
                for c0 in range(0, 8, 4):
                    key, pv = self.psum_group()
                    pf = self.flat(pv)
                    for cc in range(4):
                        c = c0 + cc
                        self.op("pe", lambda e, o=pf[:, cc * 128:(cc + 1) * 128], i=xv[:, c, tb * 128:(tb + 1) * 128]:
                                e.transpose(out=o, in_=i, identity=self.ident_f[:]),
                                [("bX", c), "ident_f"], [key])
                    dst = tv[:, tb, c0 * 128:(c0 + 4) * 128]
                    src = pf[:, 0:512]
                    if n % 2 == 0:
                        self.op("act", lambda e, o=dst, i=src: e.copy(out=o, in_=i), [key], ["bH"])
                    else:
                        self.op("dve", lambda e, o=dst, i=src: e.tensor_copy(out=o, in_=i), [key], ["bH"])
                    n += 1
            self.dma(out[g * TG:(g + 1) * TG, :].rearrange("(tb p) f -> p tb f", p=128), tv,
                     ["bH"], [("out", g)])

    def rms_rinv(self, sqv, nchunks, off, scale, bias, sqkeys):
        key, pv = self.psum_group()
        for b in range(NB):
            for c in range(nchunks):
                self.op("pe", lambda e, o=pv[:, b, :], r=sqv[:, c, b * 512:(b + 1) * 512], st=(c == 0), sp=(c == nchunks - 1):
                        e.matmul(o, lhsT=self.ones_bf[:], rhs=r, start=st, stop=sp),
                        ["ones_bf", sqkeys[c]], [key])
        rv = self.bR[:, off:off + TG]
        rk = ("bR", off)
        self.op("act", lambda e, o=rv, i=self.flat(pv): e.activation(out=o, in_=i, func=AF.Sqrt, scale=scale, bias=bias),
                [key], [rk])
        self.op("dve", lambda e, o=rv: e.reciprocal(out=o, in_=o), [rk], [rk])
        return rk, rv

    def views(self):
        xv = self.bX[:].rearrange("p (c t) -> p c t", c=8)
        xn = self.bXN[:].rearrange("p (c t) -> p c t", c=8)
        sq = self.bSQ[:].rearrange("p (c t) -> p c t", c=8)
        kX = [("bX", c) for c in range(8)]
        kXN = [("bXN", c) for c in range(8)]
        kSQ = [("bSQ", c) for c in range(8)]
        return xv, xn, sq, kX, kXN, kSQ

    def load_norm(self, X, g, npre):
        xv, xn, sq, kX, kXN, kSQ = self.views()
        t0 = g * TG
        self.dma(xv, X[:, :, t0:t0 + TG].rearrange("c p t -> p c t"),
                 [("X", g, c) for c in range(8)], kX)
        for c in range(8):
            self.op("act", lambda e, o=sq[:, c, :], i=xv[:, c, :]: e.activation(out=o, in_=i, func=AF.Square),
                    [kX[c]], [kSQ[c]])
        rk, rv = self.rms_rinv(sq, 8, 0, 1.0 / D, EPS, kSQ)
        for c in range(8):
            self.op("dve", lambda e, o=xn[:, c, :], i=xv[:, c, :], s=self.normw[:, npre + c:npre + c + 1], r=rv:
                    e.scalar_tensor_tensor(out=o, in0=i, scalar=s, in1=r, op0=ALU.mult, op1=ALU.mult),
                    [kX[c], "normw", rk], [kXN[c]])
        return xn, kXN

    def proj_post_residual(self, X, g, rhs, rkeys, KC, W, npost, half):
        xv, xn, sq, kX, kXN, kSQ = self.views()
        nkt = (KC + 7) // 8
        for ct in range(4):
            grp = [self.psum_group(), self.psum_group()]
            for kt in range(nkt):
                nk = min(8, KC - kt * 8)
                kw, w = self.load_w(W, kt * 1024, nk, ct * 256, 256)
                for mm in range(2):
                    kp, pv = grp[mm]
                    for kk in range(nk):
                        kc = kt * 8 + kk
                        for b in range(NB):
                            self.op("pe", lambda e, o=pv[:, b, :], w_=w[:, kk, mm * 128:(mm + 1) * 128], r=rhs[:, kc, b * 512:(b + 1) * 512], st=(kc == 0), sp=(kc == KC - 1):
                                    e.matmul(o, lhsT=w_, rhs=r, start=st, stop=sp), [kw, rkeys[kc]], [kp])
            for mm in range(2):
                oc = ct * 2 + mm
                kp, pv = grp[mm]
                self.op("dve", lambda e, o=xv[:, oc, :], i=self.flat(pv): e.tensor_copy(out=o, in_=i),
                        [kp], [kX[oc]])
                self.op("act", lambda e, o=sq[:, oc, :], i=xv[:, oc, :]: e.activation(out=o, in_=i, func=AF.Square),
                        [kX[oc]], [kSQ[oc]])
        if half:
            rk2, rv2 = self.rms_rinv(sq, 8, TG, 4.0 / D, 4.0 * EPS, kSQ)
        else:
            rk2, rv2 = self.rms_rinv(sq, 8, TG, 1.0 / D, EPS, kSQ)
        self.residual_add(X, g, xv, kX, rk2, rv2, npost)

    def stage_ffn(self, X, Wg, Wu, Wd, npre, npost):
        h = self.bH[:].rearrange("p (c t) -> p c t", c=22)
        xv, xn, sq, kX, kXN, kSQ = self.views()
        sqs = [self.bSG[i // 2][:].bitcast(BF16)[:, (i % 2) * TG:(i % 2 + 1) * TG] for i in range(4)]
        ksqs = [("sqs", i) for i in range(4)]
        self.load_norm(X, 0, npre)
        for g in range(NG):
            for ct in range(11):
                kg, wg = self.load_w(Wg, 0, 8, ct * 256, 256)
                ku, wu = self.load_w(Wu, 0, 8, ct * 256, 256)
                for mm in range(2):
                    m = ct * 2 + mm
                    kpg, pg = self.psum_group()
                    for k in range(8):
                        for b in range(NB):
                            self.op("pe", lambda e, o=pg[:, b, :], w=wg[:, k, mm * 128:(mm + 1) * 128], r=xn[:, k, b * 512:(b + 1) * 512], st=(k == 0), sp=(k == 7):
                                    e.matmul(o, lhsT=w, rhs=r, start=st, stop=sp), [kg, kXN[k]], [kpg])
                    kpu, pu = self.psum_group()
                    for k in range(8):
                        for b in range(NB):
                            self.op("pe", lambda e, o=pu[:, b, :], w=wu[:, k, mm * 128:(mm + 1) * 128], r=xn[:, k, b * 512:(b + 1) * 512], st=(k == 0), sp=(k == 7):
                                    e.matmul(o, lhsT=w, rhs=r, start=st, stop=sp), [ku, kXN[k]], [kpu])
                    si = self.sg_i % 2
                    self.sg_i += 1
                    sg = self.bSG[si]
                    self.op("act", lambda e, o=sg[:], i=self.flat(pg): e.activation(out=o, in_=i, func=AF.Silu),
                            [kpg], [("bSG", si), ("sqs", 2 * si), ("sqs", 2 * si + 1)])
                    self.op("dve", lambda e, o=h[:, m, :], a=sg[:], b_=self.flat(pu): e.tensor_tensor(out=o, in0=a, in1=b_, op=ALU.mult),
                            [("bSG", si), ("sqs", 2 * si), ("sqs", 2 * si + 1), kpu], [("bH", m)])
            nxt = g + 1 if g + 1 < NG else None
            t1 = (g + 1) * TG
            stk = None

            def sq_chunks(c0):
                for c in range(c0, c0 + 4):
                    ti = c % 4
                    self.dma(self.bT[ti][:], X[c, :, t1:t1 + TG], [("X", nxt, c)], [("bT", ti)])
                    self.op("act", lambda e, o=sqs[c % 4], i_=self.bT[ti][:]: e.activation(out=o, in_=i_, func=AF.Square),
                            [("bT", ti)], [ksqs[c % 4], ("bSG", (c % 4) // 2)])

            def stat_mms(c0, key, pv):
                for c in range(c0, c0 + 4):
                    for b in range(NB):
                        self.op("pe", lambda e, o=pv[:, b, :], r=sqs[c % 4][:, b * 512:(b + 1) * 512], st=(c == 0), sp=(c == 7):
                                e.matmul(o, lhsT=self.ones_bf[:], rhs=r, start=st, stop=sp),
                                ["ones_bf", ksqs[c % 4], ("bSG", (c % 4) // 2)], [key])

            for ct in range(4):
                if nxt is not None and ct == 0:
                    sq_chunks(0)
                grp = [self.psum_group(), self.psum_group()]
                for kt in range(3):
                    nk = 8 if kt < 2 else 6
                    kw, w = self.load_w(Wd, kt * 1024, nk, ct * 256, 256)
                    for mm in range(2):
                        kp, pv = grp[mm]
                        for kk in range(nk):
                            kc = kt * 8 + kk
                            for b in range(NB):
                                self.op("pe", lambda e, o=pv[:, b, :], w_=w[:, kk, mm * 128:(mm + 1) * 128], r=h[:, kc, b * 512:(b + 1) * 512], st=(kc == 0), sp=(kc == 21):
                                        e.matmul(o, lhsT=w_, rhs=r, start=st, stop=sp), [kw, ("bH", kc)], [kp])
                if nxt is not None and ct == 0:
                    stk = self.psum_group()
                    stat_mms(0, stk[0], stk[1])
                    sq_chunks(4)
                if nxt is not None and ct == 1:
                    stat_mms(4, stk[0], stk[1])
                for mm in range(2):
                    oc = ct * 2 + mm
                    kp, pv = grp[mm]
                    self.op("dve", lambda e, o=xv[:, oc, :], i=self.flat(pv): e.tensor_copy(out=o, in_=i),
                            [kp], [kX[oc]])
                    self.op("act", lambda e, o=sq[:, oc, :], i=xv[:, oc, :]: e.activation(out=o, in_=i, func=AF.Square),
                            [kX[oc]], [kSQ[oc]])
                if nxt is not None and ct == 1:
                    rv = self.bR[:, 0:TG]
                    rk = ("bR", 0)
                    self.op("act", lambda e, o=rv, i=self.flat(stk[1]): e.activation(out=o, in_=i, func=AF.Sqrt, scale=1.0 / D, bias=EPS),
                            [stk[0]], [rk])
                    self.op("dve", lambda e, o=rv: e.reciprocal(out=o, in_=o), [rk], [rk])
                    for c in range(8):
                        ti = c % 4
                        self.dma(self.bT[ti][:], X[c, :, t1:t1 + TG], [("X", nxt, c)], [("bT", ti)])
                        self.op("dve", lambda e, o=xn[:, c, :], i=self.bT[ti][:], s=self.normw[:, npre + c:npre + c + 1], r=rv:
                                e.scalar_tensor_tensor(out=o, in0=i, scalar=s, in1=r, op0=ALU.mult, op1=ALU.mult),
                                [("bT", ti), "normw", rk], [kXN[c]])
            rk2, rv2 = self.rms_rinv(sq, 8, TG, 4.0 / D, 4.0 * EPS, kSQ)
            self.residual_add(X, g, xv, kX, rk2, rv2, npost)

    def residual_add(self, X, g, yv, kY, rk, rv, npost):
        t0 = g * TG
        for oc in range(8):
            ti = self.t_i % 4
            self.t_i += 1
            xr = self.bT[ti]
            kt_ = ("bT", ti)
            self.dma(xr[:], X[oc, :, t0:t0 + TG], [("X", g, oc)], [kt_])
            self.op("dve", lambda e, o=yv[:, oc, :], s=self.normw[:, npost + oc:npost + oc + 1], r=rv:
                    e.scalar_tensor_tensor(out=o, in0=o, scalar=s, in1=r, op0=ALU.mult, op1=ALU.mult),
                    [kY[oc], "normw", rk], [kY[oc]])
            self.op("pool", lambda e, o=xr[:], b_=yv[:, oc, :]: e.tensor_tensor(out=o, in0=o, in1=b_, op=ALU.add),
                    [kt_, kY[oc]], [kt_])
            self.dma(X[oc, :, t0:t0 + TG], xr[:], [kt_], [("X", g, oc)])


    def even_consts(self, aps, li):
        stg = self.wsF[0][:, 0:256].rearrange("p (c m) -> p c m", c=2)
        stg = self.bT[0][:, 0:256].rearrange("p (c m) -> p c m", c=2)
        kt_ = ("bT", 0)
        self.op("pool", lambda e, o=stg: e.memset(o, 0.0), [], [kt_])
        for c in range(2):
            for gg in range(2):
                self.dma(stg[gg * 64:(gg + 1) * 64, c, gg * 64:(gg + 1) * 64], aps["ev_pool_w"][li, 2 * c + gg],
                         [kt_], [kt_])
        self.op("pool", lambda e, o=self.poolW[:], i=stg: e.tensor_copy(out=o, in_=i), [kt_], ["poolW"])
        self.dma(self.poolS[:], aps["ev_pool_scale"][li], ["poolS"], ["poolS"])
        self.dma(self.esink[:], aps["ev_sink"][li].partition_broadcast(128), ["esink"], ["esink"])
        self.op("act", lambda e, o=self.esink[:]: e.activation(out=o, in_=o, func=AF.Exp), ["esink"], ["esink"])

    def stage_even_in(self, X, aps, li, npre, QT, KT, V, U):
        xv, xn_, sq, kX, kXN_, kSQ = self.views()
        Win = ("ev_win", li)
        Wv = ("ev_wv", li)
        sqb = sq
        vst = self.bH[:, 0:8 * 256].rearrange("p (tb f) -> p tb f", tb=8)
        ust = self.bR[:].rearrange("p (c t) -> p c t", c=2)
        for g in range(NG):
            t0 = g * TG
            xn, kXN = self.load_norm(X, g, npre)
            cosT = self.bSG[0]
            sinT = self.bSG[1]
            self.dma(cosT[:], aps["ropec"][:, t0:t0 + TG], [], [("bSG", 0)])
            self.dma(sinT[:], aps["ropes"][:, t0:t0 + TG], [], [("bSG", 1)])
            for i in range(8):
                kw, w = self.load_w(Win, 0, 8, i * 256, 256)
                kq, pq = self.psum_group()
                for k in range(8):
                    for b in range(NB):
                        self.op("pe", lambda e, o=pq[:, b, :], w_=w[:, k, 0:128], r=xn[:, k, b * 512:(b + 1) * 512], st=(k == 0), sp=(k == 7):
                                e.matmul(o, lhsT=w_, rhs=r, start=st, stop=sp), [kw, kXN[k]], [kq])
                kp, pp = self.psum_group()
                for k in range(8):
                    for b in range(NB):
                        self.op("pe", lambda e, o=pp[:, b, :], w_=w[:, k, 128:256], r=xn[:, k, b * 512:(b + 1) * 512], st=(k == 0), sp=(k == 7):
                                e.matmul(o, lhsT=w_, rhs=r, start=st, stop=sp), [kw, kXN[k]], [kp])
                t1 = self.bT[0]
                t2 = self.bT[1]
                self.op("dve", lambda e, o=t1[:], a=self.flat(pq), c_=cosT[:]: e.tensor_tensor(out=o, in0=a, in1=c_, op=ALU.mult),
                        [kq, ("bSG", 0)], [("bT", 0)])
                self.op("dve", lambda e, o=t2[:], a=self.flat(pp), c_=sinT[:]: e.tensor_tensor(out=o, in0=a, in1=c_, op=ALU.mult),
                        [kp, ("bSG", 1)], [("bT", 1)])
                self.op("pool", lambda e, o=sqb[:, i, :], a=t1[:], b_=t2[:]: e.tensor_tensor(out=o, in0=a, in1=b_, op=ALU.add),
                        [("bT", 0), ("bT", 1)], [kSQ[i]])
                if i < 6:
                    self.dma(QT[i * 128:(i + 1) * 128, t0:t0 + TG], sqb[:, i, :], [kSQ[i]], [("QT", g)])
                else:
                    self.dma(KT[(i - 6) * 128:(i - 5) * 128, t0:t0 + TG], sqb[:, i, :], [kSQ[i]], [("KT", g)])
            kw, w = self.load_w(Win, 0, 8, 2048, 256)
            for mm in range(2):
                kp, pv = self.psum_group()
                for k in range(8):
                    for b in range(NB):
                        self.op("pe", lambda e, o=pv[:, b, :], w_=w[:, k, mm * 128:(mm + 1) * 128], r=xn[:, k, b * 512:(b + 1) * 512], st=(k == 0), sp=(k == 7):
                                e.matmul(o, lhsT=w_, rhs=r, start=st, stop=sp), [kw, kXN[k]], [kp])
                self.op("act", lambda e, o=ust[:, mm, :], i_=self.flat(pv): e.copy(out=o, in_=i_), [kp], [("bR", mm * TG)])
                self.dma(U[mm, :, t0:t0 + TG], ust[:, mm, :], [("bR", mm * TG)], [("U", g)])
            kw, w = self.load_w(Wv, 0, 8, 0, 256)
            for tb in range(8):
                kp, pv = self.psum_group()
                pf = self.flat(pv)
                for k in range(8):
                    self.op("pe", lambda e, o=pf[:, 0:256], l=xn[:, k, tb * 128:(tb + 1) * 128], r=w[:, k, :], st=(k == 0), sp=(k == 7):
                            e.matmul(o, lhsT=l, rhs=r, start=st, stop=sp), [kw, kXN[k]], [kp])
                if tb % 2 == 0:
                    self.op("act", lambda e, o=vst[:, tb, :], i_=pf[:, 0:256]: e.copy(out=o, in_=i_), [kp], [("bH", "v")])
                else:
                    self.op("dve", lambda e, o=vst[:, tb, :], i_=pf[:, 0:256]: e.tensor_copy(out=o, in_=i_), [kp], [("bH", "v")])
            self.dma(V[t0:t0 + TG, :].rearrange("(tb p) f -> p tb f", p=128), vst, [("bH", "v")], [("V", g)])

    def stage_even_pool(self, aps, U, AO):
        L = TG + 16
        bx = self.bX
        up = bx[:, 0:2 * L].rearrange("p (c t) -> p c t", c=2)
        a1 = bx[:, 2 * L:4 * L].rearrange("p (c t) -> p c t", c=2)
        a2 = bx[:, 4 * L:6 * L].rearrange("p (c t) -> p c t", c=2)
        xnf = self.bXN[:].bitcast(F32)
        a3 = xnf[:, 0:L]
        a4 = xnf[:, L:2 * L]
        icnt = self.bR[:].rearrange("p (c t) -> p c t", c=2)
        res = self.bSQ[:].bitcast(F32)[:, 0:2 * TG].rearrange("p (c t) -> p c t", c=2)
        pb = self.bH[:, 0:2 * TG].rearrange("p (c t) -> p c t", c=2)
        ob = self.bH[:, 2 * TG:4 * TG].rearrange("p (c t) -> p c t", c=2)
        for g in range(NG):
            t0 = g * TG
            lo = max(t0 - 8, 0)
            hi = min(t0 + TG + 8, S)
            j0 = lo - (t0 - 8)
            self.op("pool", lambda e, o=up: e.memset(o, 0.0), [], ["pl_u"])
            self.dma(up[:, :, j0:j0 + hi - lo], U[:, :, lo:hi].rearrange("c p t -> p c t"),
                     ["pl_u"] + [("U", gg) for gg in (g - 1, g, g + 1) if 0 <= gg < NG], ["pl_u"])
            self.dma(icnt, aps["icnt"][:, :, t0:t0 + TG].rearrange("c p t -> p c t"), [], [("bR", 0), ("bR", TG)])
            self.op("dve", lambda e: e.tensor_tensor(out=a1[:, :, 1:L], in0=up[:, :, 0:L - 1], in1=up[:, :, 1:L], op=ALU.add),
                    ["pl_u"], ["pl_a1"])
            self.op("dve", lambda e: e.tensor_tensor(out=a2[:, :, 2:L - 1], in0=a1[:, :, 1:L - 2], in1=a1[:, :, 3:L], op=ALU.add),
                    ["pl_a1"], ["pl_a2"])
            self.op("dve", lambda e: e.tensor_tensor(out=a3[:, 4:L - 3], in0=a2[:, 1, 2:L - 5], in1=a2[:, 1, 6:L - 1], op=ALU.add),
                    ["pl_a2"], ["pl_a3"])
            self.op("dve", lambda e: e.tensor_tensor(out=a4[:, 8:L - 8], in0=a3[:, 4:L - 12], in1=a3[:, 12:L - 4], op=ALU.add),
                    ["pl_a3"], ["pl_a4"])
            self.op("dve", lambda e: e.tensor_tensor(out=res[0:64, 0, :], in0=a1[0:64, 0, 8:TG + 8], in1=icnt[0:64, 0, :], op=ALU.mult),
                    ["pl_a1", ("bR", 0)], ["pl_res"])
            self.op("dve", lambda e: e.tensor_tensor(out=res[64:128, 0, :], in0=a2[64:128, 0, 8:TG + 8], in1=icnt[64:128, 0, :], op=ALU.mult),
                    ["pl_a2", ("bR", 0)], ["pl_res"])
            self.op("dve", lambda e: e.tensor_tensor(out=res[0:64, 1, :], in0=a3[0:64, 8:TG + 8], in1=icnt[0:64, 1, :], op=ALU.mult),
                    ["pl_a3", ("bR", TG)], ["pl_res"])
            self.op("dve", lambda e: e.tensor_tensor(out=res[64:128, 1, :], in0=a4[64:128, 8:TG + 8], in1=icnt[64:128, 1, :], op=ALU.mult),
                    ["pl_a4", ("bR", TG)], ["pl_res"])
            self.op("dve", lambda e: e.tensor_tensor(out=pb, in0=res, in1=up[:, :, 8:TG + 8], op=ALU.subtract),
                    ["pl_res", "pl_u"], ["pl_pb"])
            for c in range(2):
                kp, pv = self.psum_group()
                for b in range(NB):
                    self.op("pe", lambda e, o=pv[:, b, :], l=self.poolW[:, c, :], r=pb[:, c, b * 512:(b + 1) * 512]:
                            e.matmul(o, lhsT=l, rhs=r, start=True, stop=True), ["poolW", "pl_pb"], [kp])
                self.op("dve", lambda e, o=ob[:, c, :], i_=self.flat(pv), s=self.poolS[:, c:c + 1]:
                        e.tensor_scalar(out=o, in0=i_, scalar1=s, scalar2=None, op0=ALU.mult), [kp, "poolS"], [("pl_ob", c)])
                self.dma(AO[c, :, t0:t0 + TG], ob[:, c, :], [("pl_ob", c)], [("AOp", g)])

    def stage_even_attn(self, aps, li, QT, KT, V, AO):
        KW = TG + 256
        q_sb = self.bH[0:64, 0:12 * TG].rearrange("p (h t) -> p h t", h=12)
        k_sb = self.bXN[0:64, 0:4 * KW].rearrange("p (h t) -> p h t", h=4)
        v_sb = self.bSQ[:, 0:10 * 256].rearrange("p (kb f) -> p kb f", kb=10)
        o_sb = self.bX[:].bitcast(BF16)[0:64, 0:12 * TG].rearrange("p (h t) -> p h t", h=12)
        Pb = [self.bSG[0][:].bitcast(BF16)[:, 0:384], self.bSG[1][:].bitcast(BF16)[:, 0:384],
              self.bT[0][:].bitcast(BF16)[:, 0:384], self.bT[1][:].bitcast(BF16)[:, 0:384]]
        Pk = [("bSG", 0), ("bSG", 1), ("bT", 0), ("bT", 1)]
        recs = [self.bR[0:64, 0:384], self.bR[0:64, 512:896]]
        AOf = AO.rearrange("c p t -> (c p) t")
        pi = 0
        ri = 0
        for g in range(NG):
            t0 = g * TG
            lo = max(t0 - 128, 0)
            hi = min(t0 + TG + 128, S)
            j0 = lo - (t0 - 128)
            nbr = [gg for gg in (g - 1, g, g + 1) if 0 <= gg < NG]
            self.dma(q_sb, QT[:, t0:t0 + TG].rearrange("(h d) t -> d h t", d=64), [("QT", g)], ["at_q"])
            self.dma(k_sb[:, :, j0:j0 + hi - lo], KT[:, lo:hi].rearrange("(h d) t -> d h t", d=64),
                     [("KT", gg) for gg in nbr], ["at_k"])
            self.dma(v_sb[:, j0 // 128:(j0 + hi - lo) // 128, :], V[lo:hi, :].rearrange("(kb p) f -> p kb f", p=128),
                     [("V", gg) for gg in nbr], ["at_v"])
            for qi in range(8):
                n = g * 8 + qi
                for gk in range(4):
                    kbs = [kb for kb in (n - 1, n, n + 1) if 0 <= kb < S // 128]
                    Ps = []
                    for kb in kbs:
                        lk = kb - (g * 8 - 1)
                        ks, pv = self.psum_group()
                        pf = self.flat(pv)
                        self.op("pe", lambda e, o=pf[:, 0:384].rearrange("p (h q) -> p h q", h=3), l=k_sb[:, gk, lk * 128:(lk + 1) * 128],
                                r=q_sb[:, 3 * gk:3 * gk + 3, qi * 128:(qi + 1) * 128]:
                                e.matmul(o, lhsT=l, rhs=r, start=True, stop=True), ["at_k", "at_q"], [ks])
                        P = Pb[pi % 4]
                        kP = Pk[pi % 4]
                        pi += 1
                        self.op("act", lambda e, o=P, i_=pf[:, 0:384]: e.activation(out=o, in_=i_, func=AF.Exp, scale=0.125),
                                [ks], [kP])
                        if kb == n - 1:
                            self.op("dve", lambda e, o=P: e.tensor_tensor(out=o, in0=o, in1=self.amask[:, 0, :], op=ALU.mult),
                                    [kP, "amask"], [kP])
                        elif kb == n + 1:
                            self.op("dve", lambda e, o=P: e.tensor_tensor(out=o, in0=o, in1=self.amask[:, 1, :], op=ALU.mult),
                                    [kP, "amask"], [kP])
                        Ps.append((P, kP, lk))
                    kn, pn = self.psum_group()
                    num = pn[0:64, 0, 0:384]
                    den = pn[0:64, 1, 0:384]
                    for idx, (P, kP, lk) in enumerate(Ps):
                        st = (idx == 0)
                        sp = (idx == len(Ps) - 1)
                        self.op("pe", lambda e, o=num, l=v_sb[:, lk, gk * 64:(gk + 1) * 64], r=P, st=st, sp=sp:
                                e.matmul(o, lhsT=l, rhs=r, start=st, stop=sp), ["at_v", kP], [kn])
                        self.op("pe", lambda e, o=den, l=self.ones_bf[:, 0:64], r=P, st=st, sp=sp:
                                e.matmul(o, lhsT=l, rhs=r, start=st, stop=sp), ["ones_bf", kP], [kn])
                    rec = recs[ri % 2]
                    kr = ("rec", ri % 2)
                    ri += 1
                    for j in range(3):
                        hh = 3 * gk + j
                        self.op("dve", lambda e, o=rec[:, j * 128:(j + 1) * 128], i_=den[:, j * 128:(j + 1) * 128],
                                s=self.esink[0:64, hh:hh + 1]:
                                e.tensor_scalar(out=o, in0=i_, scalar1=s, scalar2=None, op0=ALU.add), [kn, "esink"], [kr])
                    self.op("dve", lambda e, o=rec: e.reciprocal(out=o, in_=o), [kr], [kr])
                    self.op("dve", lambda e, o=o_sb[:, 3 * gk:3 * gk + 3, qi * 128:(qi + 1) * 128],
                            a=num.rearrange("p (h q) -> p h q", h=3), b_=rec.rearrange("p (h q) -> p h q", h=3):
                            e.tensor_tensor(out=o, in0=a, in1=b_, op=ALU.mult), [kn, kr], ["at_o"])
            self.dma(AOf[256:1024, t0:t0 + TG].rearrange("(h d) t -> d h t", d=64), o_sb, ["at_o"], [("AOa", g)])

    def stage_even_out(self, X, aps, li, npost, AO):
        xv, xn, sq, kX, kXN, kSQ = self.views()
        for g in range(NG):
            t0 = g * TG
            self.dma(xn, AO[:, :, t0:t0 + TG].rearrange("c p t -> p c t"), [("AOp", g), ("AOa", g)], kXN)
            self.proj_post_residual(X, g, xn, kXN, 8, ("ev_wout", li), npost, False)

    def stg4(self):
        i = self.stg_i % 4
        self.stg_i += 1
        t = [self.bT[0], self.bT[1], self.bSG[0], self.bSG[1]][i]
        k = [("bT", 0), ("bT", 1), ("bSG", 0), ("bSG", 1)][i]
        return t, k

    def odd_consts(self, aps, li):
        self.dma(self.dww[:], aps["cv_dww"][li], ["dww"], ["dww"])
        self.dma(self.cvvec[:], aps["cv_vec"][li], ["cvvec"], ["cvvec"])
        self.dma(self.scw[:], aps["ssm_cw"][li], ["scw"], ["scw"])
        self.dma(self.dexp[:], aps["ssm_dexp"][li], ["dexp"], ["dexp"])
        self.dma(self.snw[:], aps["ssm_nw"][li], ["snw"], ["snw"])
        self.dma(self.abcol[:], aps["ssm_ab"][li], ["abcol"], ["abcol"])
        self.op("act", lambda e, o=self.abcol[:, 0:1]: e.activation(out=o, in_=o, func=AF.Exp), ["abcol"], ["abcol"])
        self.op("dve", lambda e, o=self.abcol[:, 0:1]: e.tensor_scalar(out=o, in0=o, scalar1=-1.0, scalar2=None, op0=ALU.mult),
                ["abcol"], ["abcol"])
        self.dma(self.abrow[:, 0, :], aps["ssm_A_log"][li].partition_broadcast(128), ["abrow"], ["abrow"])
        self.dma(self.abrow[:, 1, :], aps["ssm_dt_bias"][li].partition_broadcast(128), ["abrow"], ["abrow"])
        self.op("act", lambda e, o=self.abrow[:, 0, :]: e.activation(out=o, in_=o, func=AF.Exp), ["abrow"], ["abrow"])
        self.op("dve", lambda e, o=self.abrow[:, 0, :]: e.tensor_scalar(out=o, in0=o, scalar1=-1.0, scalar2=None, op0=ALU.mult),
                ["abrow"], ["abrow"])

    def stage_odd_in(self, X, aps, li, npre, G, ZS, XBCr, DTr, DTt):
        Win = ("od_win", li)
        for g in range(NG):
            t0 = g * TG
            xn, kXN = self.load_norm(X, g, npre)
            for i in range(16):
                kw, w = self.load_w(Win, 0, 8, i * 256, 256)
                pgs = []
                for mm in range(2):
                    kp, pv = self.psum_group()
                    for k in range(8):
                        for b in range(NB):
                            self.op("pe", lambda e, o=pv[:, b, :], w_=w[:, k, mm * 128:(mm + 1) * 128], r=xn[:, k, b * 512:(b + 1) * 512], st=(k == 0), sp=(k == 7):
                                    e.matmul(o, lhsT=w_, rhs=r, start=st, stop=sp), [kw, kXN[k]], [kp])
                    pgs.append((kp, pv))
                if i < 4:
                    s1, k1 = self.stg4()
                    s2, k2 = self.stg4()
                    self.op("act", lambda e, o=s1[:], i_=self.flat(pgs[1][1]): e.activation(out=o, in_=i_, func=AF.Sigmoid),
                            [pgs[1][0]], [k1])
                    self.op("dve", lambda e, o=s2[:], a=self.flat(pgs[0][1]), b_=s1[:]: e.tensor_tensor(out=o, in0=a, in1=b_, op=ALU.mult),
                            [pgs[0][0], k1], [k2])
                    self.dma(G[i, :, t0:t0 + TG], s2[:], [k2], [("G", g)])
                elif i < 8:
                    for mm in range(2):
                        s1, k1 = self.stg4()
                        self.op("act", lambda e, o=s1[:], i_=self.flat(pgs[mm][1]): e.activation(out=o, in_=i_, func=AF.Silu),
                                [pgs[mm][0]], [k1])
                        self.dma(ZS[(i - 4) * 2 + mm, :, t0:t0 + TG], s1[:], [k1], [("ZS", g)])
                else:
                    for mm in range(2):
                        s1, k1 = self.stg4()
                        if mm == 0:
                            self.op("act", lambda e, o=s1[:], i_=self.flat(pgs[mm][1]): e.copy(out=o, in_=i_), [pgs[mm][0]], [k1])
                        else:
                            self.op("dve", lambda e, o=s1[:], i_=self.flat(pgs[mm][1]): e.tensor_copy(out=o, in_=i_), [pgs[mm][0]], [k1])
                        self.dma(XBCr[(i - 8) * 2 + mm, :, t0:t0 + TG], s1[:], [k1], [("XBCr", g)])
            kw, w = self.load_w(("od_wdt", li), 0, 8, 0, 32)
            kp, pv = self.psum_group()
            for k in range(8):
                for b in range(NB):
                    self.op("pe", lambda e, o=pv[0:32, b, :], w_=w[:, k, :], r=xn[:, k, b * 512:(b + 1) * 512], st=(k == 0), sp=(k == 7):
                            e.matmul(o, lhsT=w_, rhs=r, start=st, stop=sp), [kw, kXN[k]], [kp])
            s1, k1 = self.stg4()
            self.op("act", lambda e, o=s1[0:32, :], i_=self.flat(pv)[0:32, :]: e.copy(out=o, in_=i_), [kp], [k1])
            self.dma(DTr[:, t0:t0 + TG], s1[0:32, :], [k1], [("DTr", g)])
            kp, pv = self.psum_group()
            pf = self.flat(pv)
            for tb in range(8):
                for k in range(8):
                    self.op("pe", lambda e, o=pf[:, tb * 32:(tb + 1) * 32], l=xn[:, k, tb * 128:(tb + 1) * 128], r=w[:, k, :], st=(k == 0), sp=(k == 7):
                            e.matmul(o, lhsT=l, rhs=r, start=st, stop=sp), [kw, kXN[k]], [kp])
            s1, k1 = self.stg4()
            self.op("dve", lambda e, o=s1[:, 0:256], i_=pf[:, 0:256]: e.tensor_copy(out=o, in_=i_), [kp], [k1])
            self.dma(DTt[:, g * 8:(g + 1) * 8, :],
                     s1[:, 0:256].rearrange("p (tb j) -> p tb j", tb=8), [k1], [("DTt", g)])

    def stage_odd_conv(self, aps, li, G, XBCr, XBC, MO):
        hf = self.bH[:].bitcast(F32)
        acc = hf[:, 0:4 * TG].rearrange("p (c t) -> p c t", c=4)
        sqr = hf[:, 4 * TG:8 * TG].rearrange("p (c t) -> p c t", c=4)
        mo = self.bXN[:, 0:4 * TG].rearrange("p (c t) -> p c t", c=4)
        mean = self.bR[:, 0:TG]
        rstd = self.bR[:, TG:2 * TG]
        LG = TG + 30
        gp = self.bX[:, 0:4 * LG].rearrange("p (c t) -> p c t", c=4)
        for g in range(NG):
            t0 = g * TG
            lo = max(t0 - 15, 0)
            hi = min(t0 + TG + 15, S)
            j0 = lo - (t0 - 15)
            self.op("pool", lambda e, o=gp: e.memset(o, 0.0), [], ["cv_gp"])
            self.dma(gp[:, :, j0:j0 + hi - lo], G[:, :, lo:hi].rearrange("c p t -> p c t"),
                     ["cv_gp"] + [("G", gg) for gg in (g - 1, g, g + 1) if 0 <= gg < NG], ["cv_gp"])
            for c in range(4):
                self.op("dve", lambda e, o=acc[:, c, :], i_=gp[:, c, 0:TG], s1=self.dww[:, c, 0:1], s2=self.cvvec[:, 0, c:c + 1]:
                        e.tensor_scalar(out=o, in0=i_, scalar1=s1, scalar2=s2, op0=ALU.mult, op1=ALU.add),
                        ["cv_gp", "dww", "cvvec"], [("cv_acc", c)])
            for k in range(1, 31):
                for c in range(4):
                    self.op("dve", lambda e, o=acc[:, c, :], i_=gp[:, c, k:k + TG], s=self.dww[:, c, k:k + 1]:
                            e.scalar_tensor_tensor(out=o, in0=i_, scalar=s, in1=o, op0=ALU.mult, op1=ALU.add),
                            ["cv_gp", "dww", ("cv_acc", c)], [("cv_acc", c)])
            for c in range(4):
                self.op("act", lambda e, o=sqr[:, c, :], i_=acc[:, c, :]: e.activation(out=o, in_=i_, func=AF.Square),
                        [("cv_acc", c)], [("cv_sq", c)])
            k1, p1 = self.psum_group()
            k2, p2 = self.psum_group()
            for b in range(NB):
                for c in range(4):
                    self.op("pe", lambda e, o=p1[:, b, :], r=acc[:, c, b * 512:(b + 1) * 512], st=(c == 0), sp=(c == 3):
                            e.matmul(o, lhsT=self.ones_f[:], rhs=r, start=st, stop=sp), ["ones_f", ("cv_acc", c)], [k1])
            for b in range(NB):
                for c in range(4):
                    self.op("pe", lambda e, o=p2[:, b, :], r=sqr[:, c, b * 512:(b + 1) * 512], st=(c == 0), sp=(c == 3):
                            e.matmul(o, lhsT=self.ones_f[:], rhs=r, start=st, stop=sp), ["ones_f", ("cv_sq", c)], [k2])
            self.op("act", lambda e, o=mean, i_=self.flat(p1): e.mul(out=o, in_=i_, mul=1.0 / 512), [k1], [("bR", 0)])
            msq = sqr[:, 0, :]
            self.op("act", lambda e, o=msq, i_=mean: e.activation(out=o, in_=i_, func=AF.Square), [("bR", 0), k2], [("cv_sq", 0)])
            self.op("dve", lambda e, o=rstd, i_=self.flat(p2), m=msq:
                    e.scalar_tensor_tensor(out=o, in0=i_, scalar=1.0 / 512, in1=m, op0=ALU.mult, op1=ALU.subtract),
                    [k2, ("cv_sq", 0)], [("bR", TG)])
            self.op("act", lambda e, o=rstd: e.activation(out=o, in_=o, func=AF.Sqrt, scale=1.0, bias=EPS), [("bR", TG)], [("bR", TG)])
            self.op("dve", lambda e, o=rstd: e.reciprocal(out=o, in_=o), [("bR", TG)], [("bR", TG)])
            for c in range(4):
                self.op("dve", lambda e, o=acc[:, c, :], m=mean: e.tensor_tensor(out=o, in0=o, in1=m, op=ALU.subtract),
                        [("cv_acc", c), ("bR", 0)], [("cv_acc", c)])
                self.op("dve", lambda e, o=acc[:, c, :], s=self.cvvec[:, 1, c:c + 1], r=rstd:
                        e.scalar_tensor_tensor(out=o, in0=o, scalar=s, in1=r, op0=ALU.mult, op1=ALU.mult),
                        [("cv_acc", c), ("bR", TG), "cvvec"], [("cv_acc", c)])
                self.op("act", lambda e, o=mo[:, c, :], i_=acc[:, c, :], bb=self.cvvec[:, 2, c:c + 1]:
                        e.activation(out=o, in_=i_, func=AF.Silu, bias=bb), [("cv_acc", c), "cvvec"], [("bXN", c)])
            self.dma(MO[:, :, t0:t0 + TG].rearrange("c p t -> p c t"), mo, [("bXN", c) for c in range(4)], [("MO", g)])
        LX = TG + 3
        xp = self.bX[:, 0:4 * LX].rearrange("p (c t) -> p c t", c=4)
        for g in range(NG):
            t0 = g * TG
            lo = max(t0 - 2, 0)
            hi = min(t0 + TG + 1, S)
            j0 = lo - (t0 - 2)
            for cb in range(4):
                self.op("pool", lambda e, o=xp: e.memset(o, 0.0), [], ["cv_gp"])
                self.dma(xp[:, :, j0:j0 + hi - lo], XBCr[cb * 4:cb * 4 + 4, :, lo:hi].rearrange("c p t -> p c t"),
                         ["cv_gp"] + [("XBCr", gg) for gg in (g - 1, g, g + 1) if 0 <= gg < NG], ["cv_gp"])
                for cc in range(4):
                    c = cb * 4 + cc
                    self.op("dve", lambda e, o=acc[:, cc, :], i_=xp[:, cc, 0:TG], s1=self.scw[:, c, 0:1], s2=self.scw[:, c, 4:5]:
                            e.tensor_scalar(out=o, in0=i_, scalar1=s1, scalar2=s2, op0=ALU.mult, op1=ALU.add),
                            ["cv_gp", "scw"], [("cv_acc", cc)])
                for k in range(1, 4):
                    for cc in range(4):
                        c = cb * 4 + cc
                        self.op("dve", lambda e, o=acc[:, cc, :], i_=xp[:, cc, k:k + TG], s=self.scw[:, c, k:k + 1]:
                                e.scalar_tensor_tensor(out=o, in0=i_, scalar=s, in1=o, op0=ALU.mult, op1=ALU.add),
                                ["cv_gp", "scw", ("cv_acc", cc)], [("cv_acc", cc)])
                for cc in range(4):
                    self.op("act", lambda e, o=acc[:, cc, :]: e.activation(out=o, in_=o, func=AF.Silu),
                            [("cv_acc", cc)], [("cv_acc", cc)])
                self.dma(XBC[cb * 4:cb * 4 + 4, :, t0:t0 + TG].rearrange("c p t -> p c t"), acc,
                         [("cv_acc", cc) for cc in range(4)], [("XBC", g)])

    def stage_odd_dt(self, aps, li, DTr, DTt):
        xf = self.bXN[:].bitcast(F32)
        dtk = xf[:, 0:1024]
        dtd = xf[:, 1024:2048]
        cdk = xf[:, 2048:3072]
        a_tok = xf[:, 3072:4096]
        sf = self.bSQ[:].bitcast(F32)
        acs_tok = sf[:, 0:1024]
        raw = sf[:, 1024:2048]
        hf = self.bH[:].bitcast(F32)
        AT = hf[0:48, 0:S]
        c1 = hf[0:48, S:2 * S]
        acsT = self.bX[0:48, 0:S]
        nacsT = self.bX[0:48, S:2 * S]
        dm = lambda t: t.rearrange("p (d c h) -> p c d h", d=2, h=16)
        self.dma(raw, DTt.rearrange("p c j -> p (c j)"), [("DTt", g) for g in range(NG)], ["dtraw"])
        for d in range(2):
            self.op("dve", lambda e, d=d: e.tensor_tensor(
                out=dtk[:, d * 512:(d + 1) * 512].rearrange("p (c h) -> p c h", h=16),
                in0=raw.rearrange("p (c j) -> p c j", j=32)[:, :, d * 16:(d + 1) * 16],
                in1=self.abrow[:, 1:2, d * 16:(d + 1) * 16].broadcast_to([128, 32, 16]), op=ALU.add),
                ["dtraw", "abrow"], ["dtk"])
        self.op("act", lambda e: e.activation(out=dtk, in_=dtk, func=AF.Exp), ["dtk"], ["dtk"])
        self.op("act", lambda e: e.activation(out=dtk, in_=dtk, func=AF.Ln, bias=1.0), ["dtk"], ["dtk"])
        for d in range(2):
            self.op("dve", lambda e, d=d: e.tensor_tensor(
                out=a_tok[:, d * 512:(d + 1) * 512].rearrange("p (c h) -> p c h", h=16),
                in0=dtk[:, d * 512:(d + 1) * 512].rearrange("p (c h) -> p c h", h=16),
                in1=self.abrow[:, 0:1, d * 16:(d + 1) * 16].broadcast_to([128, 32, 16]), op=ALU.mult),
                ["dtk", "abrow"], ["a_tok"])
        import os
        dcut = int(os.environ.get("DT_CUT", "99"))
        if dcut <= 1:
            return
        k1, p1 = self.psum_group()
        for d in range(2):
            self.op("pe", lambda e, d=d: e.matmul(p1[:, d, :], lhsT=self.tri[:, d, :], rhs=a_tok[:, d * 512:(d + 1) * 512],
                                                   start=True, stop=True), ["a_tok", "tri"], [k1])
        self.op("dve", lambda e: e.tensor_copy(out=acs_tok, in_=self.flat(p1)), [k1], ["acs_tok"])
        k2, p2 = self.psum_group()
        for d in range(2):
            self.op("pe", lambda e, d=d: e.matmul(p2[:, d, :], lhsT=self.ones_f[:], rhs=a_tok[:, d * 512:(d + 1) * 512],
                                                   start=True, stop=True), ["a_tok", "ones_f"], [k2])
        self.op("act", lambda e: e.activation(out=cdk, in_=self.flat(p2), func=AF.Exp), [k2], ["cdk"])
        self.op("dve", lambda e: e.tensor_tensor(out=dtd, in0=self.flat(p2), in1=acs_tok, op=ALU.subtract), [k2, "acs_tok"], ["dtd"])
        self.op("act", lambda e: e.activation(out=dtd, in_=dtd, func=AF.Exp), ["dtd"], ["dtd"])
        self.op("dve", lambda e: e.tensor_tensor(out=dtd, in0=dtd, in1=dtk, op=ALU.mult), ["dtd", "dtk"], ["dtd"])
        if dcut <= 2:
            return
        self.op("pool", lambda e: e.memset(AT, 0.0), [], ["AT"])
        self.dma(AT[0:16, :], DTr[0:16, :], ["AT"] + [("DTr", g) for g in range(NG)], ["AT"])
        self.dma(AT[32:48, :], DTr[16:32, :], ["AT"], ["AT"])
        self.op("act", lambda e: e.activation(out=AT, in_=AT, func=AF.Exp, bias=self.abcol[:, 1:2]), ["AT", "abcol"], ["AT"])
        self.op("act", lambda e: e.activation(out=AT, in_=AT, func=AF.Ln, bias=1.0), ["AT"], ["AT"])
        self.op("dve", lambda e: e.tensor_scalar(out=AT, in0=AT, scalar1=self.abcol[:, 0:1], scalar2=None, op0=ALU.mult),
                ["AT", "abcol"], ["AT"])
        if dcut <= 3:
            return
        c3 = lambda t: t.rearrange("p (c l) -> p c l", l=128)
        srcb, dstb = AT, c1
        names = {id(AT): "AT", id(c1): "c1", id(acsT): "acsT"}
        bufs = [c1, acsT]
        cur, curk = AT, "AT"
        step = 1
        n = 0
        while step < 128:
            dst = bufs[n % 2]
            dk = ["c1", "acsT"][n % 2]
            self.op("dve", lambda e, dst=dst, cur=cur, step=step: e.tensor_copy(out=c3(dst)[:, :, 0:step], in_=c3(cur)[:, :, 0:step]),
                    [curk], [dk])
            self.op("dve", lambda e, dst=dst, cur=cur, step=step: e.tensor_tensor(
                out=c3(dst)[:, :, step:128], in0=c3(cur)[:, :, step:128], in1=c3(cur)[:, :, 0:128 - step], op=ALU.add),
                [curk], [dk])
            cur, curk = dst, dk
            step *= 2
            n += 1
        cs = cur
        csk = curk
        self.op("dve", lambda e: e.tensor_copy(out=acsT[0:16, :], in_=cs[0:16, :]), [csk], ["acsT"])
        self.op("dve", lambda e: e.tensor_tensor(out=c3(acsT)[32:48], in0=c3(cs)[32:48, :, 127:128].broadcast_to([16, 32, 128]),
                                                 in1=c3(cs)[32:48], op=ALU.subtract), [csk, "acsT"], ["acsT"])
        self.op("dve", lambda e: e.tensor_tensor(out=acsT[32:48, :], in0=acsT[32:48, :], in1=AT[32:48, :], op=ALU.add),
                ["acsT", "AT"], ["acsT"])
        self.op("dve", lambda e: e.tensor_scalar(out=nacsT[0:16, :], in0=acsT[0:16, :], scalar1=-1.0, scalar2=None, op0=ALU.mult),
                ["acsT"], ["nacsT"])
        self.op("dve", lambda e: e.tensor_scalar(out=nacsT[32:48, :], in0=acsT[32:48, :], scalar1=-1.0, scalar2=None, op0=ALU.mult),
                ["acsT"], ["nacsT"])

    def stage_odd_ssd(self, aps, XBC, Y, d):
        xf = self.bXN[:].bitcast(F32)
        dtk = xf[:, 0:1024].rearrange("p (d c h) -> p d c h", d=2, h=16)
        dtd = xf[:, 1024:2048].rearrange("p (d c h) -> p d c h", d=2, h=16)
        cdk = xf[:, 2048:3072].rearrange("p (d c h) -> p d c h", d=2, h=16)
        acsT = self.bX[0:48, 0:S]
        nacsT = self.bX[0:48, S:2 * S]
        r0 = 0 if d == 0 else 32
        state = self.bR[:, 0:1024]
        state_bf = self.bR[:, 1024:1536].bitcast(BF16)
        hb = self.bH
        hf = hb[:].bitcast(F32)
        CB0 = [0, 16384]
        tokb_ = [hb[:, o:o + 1536] for o in CB0]
        Xdt_ = [hb[:, o + 1536:o + 2560].rearrange("p (h q) -> p h q", h=16) for o in CB0]
        Xdd_ = [hb[:, o + 2560:o + 3584].rearrange("p (h q) -> p h q", h=16) for o in CB0]
        MT = [hb[:, 3584 + i * 512:3584 + (i + 1) * 512].rearrange("p (h l) -> p h l", h=4) for i in range(2)]
        Cs = [hb[:, 4608 + i * 512:4608 + (i + 1) * 512].rearrange("p (h l) -> p h l", h=4) for i in range(2)]
        Lf = [hf[:, 3072 + i * 512:3072 + (i + 1) * 512] for i in range(2)]
        Ef = [hf[:, 4096 + i * 512:4096 + (i + 1) * 512] for i in range(2)]
        rhs2 = [hf[0:48, 5120 + i * 512:5120 + (i + 1) * 512] for i in range(2)]
        fbs = [hb[:, 12288 + i * 2048:12288 + (i + 1) * 2048].rearrange("p (c t) -> p c t", c=16) for i in range(2)]
        fxs = [self.bSQ[:].bitcast(F32)[:, i * 2048:(i + 1) * 2048].rearrange("p (c t) -> p c t", c=16) for i in range(2)]
        yst = [self.bT[0], self.bT[1], self.bSG[0], self.bSG[1]]
        ysk = [("bT", 0), ("bT", 1), ("bSG", 0), ("bSG", 1)]
        PS = self.ps
        self.op("pool", lambda e: e.memset(self.bR[:, 0:1536], 0.0), [], [("st", g) for g in range(4)] + [("stb", g) for g in range(4)])
        order = list(range(S // 128)) if d == 0 else list(range(S // 128 - 1, -1, -1))
        items = [(ci, c, g) for ci, c in enumerate(order) for g in range(4)]

        def prep(ci, c):
            fi = ci % 2
            sl = slice(c * 128, (c + 1) * 128)
            fx, fb = fxs[fi], fbs[fi]
            self.dma(fx, XBC[:, :, sl].rearrange("c p t -> p c t"), [("XBC", c // (TG // 128))], [("sfx", fi)])
            self.op("pool", lambda e, o=fb, i_=fx: e.tensor_copy(out=o, in_=i_), [("sfx", fi)], [("sfb", fi)])
            tk = tokb_[fi]
            for half in range(2):
                pt = PS[:, 6, :].bitcast(BF16)
                for i in range(6):
                    ii = half * 6 + i
                    self.op("pe", lambda e, o=pt[:, i * 128:(i + 1) * 128], i_=fb[:, ii, :]: e.transpose(out=o, in_=i_, identity=self.ident_b[:]),
                            [("sfb", fi), "ident_b"], [("psb", 6)])
                self.op("act", lambda e, o=tk[:, half * 768:(half + 1) * 768], i_=pt[:, 0:768]: e.copy(out=o, in_=i_),
                        [("psb", 6)], [("tokb", fi, half)])
            xs3 = tk[:, 0:1024].rearrange("p (h q) -> p h q", h=16)
            self.op("dve", lambda e, o=Xdt_[fi], c=c: e.tensor_tensor(out=o, in0=xs3, in1=dtk[:, d, c, :].unsqueeze(2).broadcast_to([128, 16, 64]), op=ALU.mult),
                    [("tokb", fi, 0), ("tokb", fi, 1), "dtk"], [("Xdt", fi)])
            self.op("pool", lambda e, o=Xdd_[fi], c=c: e.tensor_tensor(out=o, in0=xs3, in1=dtd[:, d, c, :].unsqueeze(2).broadcast_to([128, 16, 64]), op=ALU.mult),
                    [("tokb", fi, 0), ("tokb", fi, 1), "dtd"], [("Xdd", fi)])

        def front(idx):
            ci, c, g = items[idx]
            if g == 0:
                prep(ci, c)
            fi = ci % 2
            bi = idx % 2
            sl = slice(c * 128, (c + 1) * 128)
            fb = fbs[fi]
            slot = idx % 4
            cbt = PS[:, 0, slot * 128:(slot + 1) * 128]
            kcb = ("psb", 0, slot)
            self.op("pe", lambda e, o=cbt, l=fb[:, 8 + g, :], r=fb[:, 12 + g, :]: e.matmul(o, lhsT=l, rhs=r, start=True, stop=True),
                    [("sfb", fi)], [kcb])
            r2 = rhs2[bi][r0:r0 + 16, :]
            kr2 = ("rhs2", bi)
            self.op("dve", lambda e, o=r2.rearrange("p (h l) -> p h l", h=4), sl=sl, g=g:
                    e.tensor_tensor(out=o, in0=acsT[r0:r0 + 16, sl].unsqueeze(1).broadcast_to([16, 4, 128]),
                                    in1=self.selm[r0:r0 + 16, g, :, :], op=ALU.mult), ["acsT", "selm"], [kr2])
            pdiff = PS[:, 1 + bi, :]
            kdf = ("psb", 1 + bi)
            self.op("pe", lambda e, o=pdiff, sl=sl, g=g: e.matmul(o, lhsT=nacsT[r0:r0 + 16, sl], rhs=self.selm[r0:r0 + 16, g, :, :].rearrange("p h l -> p (h l)"),
                                                       start=True, stop=False), ["nacsT", "selm"], [kdf])
            self.op("pe", lambda e, o=pdiff, r=r2: e.matmul(o, lhsT=self.ones_f[r0:r0 + 16, :], rhs=r, start=False, stop=False),
                    ["ones_f", kr2], [kdf])
            self.op("pe", lambda e, o=pdiff: e.matmul(o, lhsT=self.ident_b[:], rhs=self.mbias[:, d, :], start=False, stop=True),
                    ["ident_b", "mbias"], [kdf])
            pE = PS[:, 3 + bi, :]
            kpe = ("psb", 3 + bi)
            self.op("pe", lambda e, o=pE, r=r2: e.matmul(o, lhsT=self.ones_f[r0:r0 + 16, :], rhs=r, start=True, stop=True),
                    ["ones_f", kr2], [kpe])
            L = Lf[bi]
            kL = ("Lf", bi)
            self.op("act", lambda e, o=L, i_=pdiff: e.activation(out=o, in_=i_, func=AF.Exp), [kdf], [kL])
            E = Ef[bi]
            kE = ("Ef", bi)
            self.op("act", lambda e, o=E, i_=pE: e.activation(out=o, in_=i_, func=AF.Exp), [kpe], [kE])
            mt = MT[bi]
            self.op("dve", lambda e, o=mt, a=L.rearrange("p (h l) -> p h l", h=4), b_=cbt.unsqueeze(1).broadcast_to([128, 4, 128]):
                    e.tensor_tensor(out=o, in0=a, in1=b_, op=ALU.mult), [kL, kcb], [("MT", bi)])
            self.op("pool", lambda e, o=Cs[bi], a=E.rearrange("p (h l) -> p h l", h=4), b_=fb[:, 12 + g, :].unsqueeze(1).broadcast_to([128, 4, 128]):
                    e.tensor_tensor(out=o, in0=a, in1=b_, op=ALU.mult), [kE, ("sfb", fi)], [("Cs", bi)])

        def back(idx):
            ci, c, g = items[idx]
            fi = ci % 2
            bi = idx % 2
            sl = slice(c * 128, (c + 1) * 128)
            mt, cs_ = MT[bi], Cs[bi]
            Xdt, Xdd, tk = Xdt_[fi], Xdd_[fi], tokb_[fi]
            py = PS[0:64, 5, :]
            ky = ("psb", 5)
            for hh in range(4):
                head = 4 * g + hh
                self.op("pe", lambda e, o=py[:, hh * 128:(hh + 1) * 128], l=Xdt[:, head, :], r=mt[:, hh, :]:
                        e.matmul(o, lhsT=l, rhs=r, start=True, stop=False), [("Xdt", fi), ("MT", bi)], [ky])
                self.op("pe", lambda e, o=py[:, hh * 128:(hh + 1) * 128], l=state_bf[:, head * 64:(head + 1) * 64], r=cs_[:, hh, :]:
                        e.matmul(o, lhsT=l, rhs=r, start=False, stop=True), [("stb", g), ("Cs", bi)], [ky])
            ys = yst[g][0:64, 0:512]
            self.op("act", lambda e, o=ys, i_=py: e.copy(out=o, in_=i_), [ky], [ysk[g]])
            self.dma(Y[g * 256:(g + 1) * 256, sl].rearrange("(h p) l -> p h l", p=64), ys.rearrange("p (h l) -> p h l", h=4),
                     [ysk[g]], [("Y", d, c // (TG // 128))])
            pst = PS[:, 7, bi * 256:(bi + 1) * 256]
            kst = ("psb", 7, bi)
            self.op("pe", lambda e, o=pst, l=tk[:, (8 + g) * 128:(9 + g) * 128], r=Xdd[:, 4 * g:4 * g + 4, :]:
                    e.matmul(o, lhsT=l, rhs=r, start=True, stop=True), [("tokb", fi, 1), ("Xdd", fi)], [kst])
            stg = state[:, g * 256:(g + 1) * 256]
            self.op("dve", lambda e, o=stg.rearrange("p (h q) -> p h q", h=4), c=c, g=g:
                    e.tensor_tensor(out=o, in0=o, in1=cdk[:, d, c, 4 * g:4 * g + 4].unsqueeze(2).broadcast_to([128, 4, 64]), op=ALU.mult),
                    [("st", g), "cdk"], [("st", g)])
            self.op("dve", lambda e, o=stg, i_=pst: e.tensor_tensor(out=o, in0=o, in1=i_, op=ALU.add),
                    [("st", g), kst], [("st", g)])
            self.op("act", lambda e, o=state_bf[:, g * 256:(g + 1) * 256], i_=stg: e.copy(out=o, in_=i_), [("st", g)], [("stb", g)])

        n = len(items)
        for idx in range(n):
            front(idx)
            if idx >= 1:
                back(idx - 1)
        back(n - 1)

    def stage_odd_out(self, X, aps, li, npost, Yf, Yb, XBC, ZS, MO):
        xv, xn_, sq, kX, kXN_, kSQ = self.views()
        rhs = self.bH[:, 0:12 * TG].rearrange("p (c t) -> p c t", c=12)
        rkeys = [("bH", m) for m in range(12)]
        for g in range(NG):
            t0 = g * TG
            self.dma(rhs[:, 0:4, :], MO[:, :, t0:t0 + TG].rearrange("c p t -> p c t"), [("MO", g)], rkeys[0:4])
            for c in range(8):
                a, ka = self.stg4()
                b, kb = self.stg4()
                xs, kx = self.stg4()
                z, kz = self.stg4()
                self.dma(a[:], Yf[c * 128:(c + 1) * 128, t0:t0 + TG], [("Y", 0, g)], [ka])
                self.dma(b[:], Yb[c * 128:(c + 1) * 128, t0:t0 + TG], [("Y", 1, g)], [kb])
                self.dma(xs[:], XBC[c, :, t0:t0 + TG], [("XBC", g)], [kx])
                self.dma(z[:], ZS[c, :, t0:t0 + TG], [("ZS", g)], [kz])
                self.op("pool", lambda e, o=a[:], b_=b[:]: e.tensor_tensor(out=o, in0=o, in1=b_, op=ALU.add), [ka, kb], [ka])
                self.op("dve", lambda e, o=a[:], x_=xs[:], s=self.dexp[:, c:c + 1]:
                        e.scalar_tensor_tensor(out=o, in0=x_, scalar=s, in1=o, op0=ALU.mult, op1=ALU.add), [ka, kx, "dexp"], [ka])
                self.op("dve", lambda e, o=xv[:, c, :], a_=a[:], z_=z[:]: e.tensor_tensor(out=o, in0=a_, in1=z_, op=ALU.mult),
                        [ka, kz], [kX[c]])
                self.op("act", lambda e, o=sq[:, c, :], i_=xv[:, c, :]: e.activation(out=o, in_=i_, func=AF.Square),
                        [kX[c]], [kSQ[c]])
            rk, rv = self.rms_rinv(sq, 8, 0, 1.0 / D, EPS, kSQ)
            for c in range(8):
                self.op("dve", lambda e, o=rhs[:, 4 + c, :], i_=xv[:, c, :], s=self.snw[:, c:c + 1], r=rv:
                        e.scalar_tensor_tensor(out=o, in0=i_, scalar=s, in1=r, op0=ALU.mult, op1=ALU.mult),
                        [kX[c], "snw", rk], [rkeys[4 + c]])
            self.proj_post_residual(X, g, rhs, rkeys, 12, ("od_wout", li), npost, False)

def build_program(stop=None, only=None):
    _, plan = _build(stop, only, None)
    nc, _ = _build(stop, only, plan)
    return nc


def _build(stop, only, wplan):
    from contextlib import ExitStack
    nc = bass.Bass("TRN2", target_bir_lowering=False)
    dt = nc.dram_tensor
    aps = {}

    def inp(name, shape):
        aps[name] = dt(name, shape, F32, kind="ExternalInput").ap()

    inp("x", [S, D])
    inp("normw", [128, DEPTH * 6 * 8])
    import os
    nff = 1 if os.environ.get("DEV_SMALL") else DEPTH * 2
    inp("wg", [nff, D * DFF])
    inp("wu", [nff, D * DFF])
    inp("wd", [nff, DFF * D])
    inp("ev_win", [2, D * 2304])
    inp("ev_wv", [2, D * 256])
    inp("ev_pool_w", [2, 4, 64, 64])
    inp("ev_pool_scale", [2, 128, 2])
    inp("ev_sink", [2, 12])
    inp("ev_wout", [2, D * D])
    inp("ropec", [128, S])
    inp("ropes", [128, S])
    inp("icnt", [2, 128, S])
    inp("amask", [128, 2, 384])
    inp("od_win", [2, D * 4096])
    inp("od_wdt", [2, D, 32])
    inp("od_wout", [2, 1536 * D])
    inp("cv_dww", [2, 128, 4, 31])
    inp("cv_vec", [2, 128, 3, 4])
    inp("ssm_cw", [2, 128, 16, 5])
    inp("ssm_dexp", [2, 128, 8])
    inp("ssm_nw", [2, 128, 8])
    inp("ssm_ab", [2, 48, 2])
    inp("ssm_A_log", [2, 32])
    inp("ssm_dt_bias", [2, 32])
    inp("tri", [128, 2, 128])
    inp("selm", [48, 4, 4, 128])
    inp("mbias", [128, 2, 512])
    inp("rmask", [48, 128])
    out = dt("out", [S, D], F32, kind="ExternalOutput").ap()
    X = dt("Xs", [8, 128, S], F32, kind="Internal").ap()
    QT = dt("QTs", [768, S], BF16, kind="Internal").ap()
    KT = dt("KTs", [256, S], BF16, kind="Internal").ap()
    V = dt("Vs", [S, 256], BF16, kind="Internal").ap()
    U = dt("Us", [2, 128, S], F32, kind="Internal").ap()
    AO = dt("AOs", [8, 128, S], BF16, kind="Internal").ap()
    Gs = dt("Gs", [4, 128, S], F32, kind="Internal").ap()
    ZS = dt("ZSs", [8, 128, S], F32, kind="Internal").ap()
    XBCr = dt("XBCrs", [16, 128, S], F32, kind="Internal").ap()
    XBC = dt("XBCs", [16, 128, S], F32, kind="Internal").ap()
    DTr = dt("DTrs", [32, S], F32, kind="Internal").ap()
    DTt = dt("DTts", [128, 32, 32], F32, kind="Internal").ap()
    Yf = dt("Yfs", [1024, S], F32, kind="Internal").ap()
    Yb = dt("Ybs", [1024, S], F32, kind="Internal").ap()
    MO = dt("MOs", [4, 128, S], BF16, kind="Internal").ap()
    B = Builder(nc, stop, aps, wplan)
    B.init_consts(aps)
    B.stage_load(aps["x"], X)
    nst = 0
    seq = [(l, sub) for l in range(DEPTH) for sub in range(3)]
    if stop is not None:
        seq = seq[:stop]
    if only is not None:
        seq = only
    for (l, sub) in seq:
        base = l * 6 * 8
        if True:
            if sub == 0:
                B.sc.barrier(); B.stage_ffn(X, ("wg", l * 2), ("wu", l * 2), ("wd", l * 2), base + 0, base + 8)
            elif sub == 2:
                B.sc.barrier(); B.stage_ffn(X, ("wg", l * 2 + 1), ("wu", l * 2 + 1), ("wd", l * 2 + 1), base + 32, base + 40)
            elif l % 2 == 0:
                li = l // 2
                B.even_consts(aps, li)
                B.sc.barrier(); B.stage_even_in(X, aps, li, base + 16, QT, KT, V, U)
                B.sc.barrier(); B.stage_even_pool(aps, U, AO)
                B.sc.barrier(); B.stage_even_attn(aps, li, QT, KT, V, AO)
                B.sc.barrier(); B.stage_even_out(X, aps, li, base + 24, AO)
            else:
                li = l // 2
                import os
                ocut = int(os.environ.get("ODD_CUT", "99"))
                B.odd_consts(aps, li)
                oskip = int(os.environ.get("ODD_SKIP", "0"))
                if ocut >= 1 and oskip < 1:
                    B.sc.barrier(); B.stage_odd_in(X, aps, li, base + 16, Gs, ZS, XBCr, DTr, DTt)
                if ocut >= 2 and oskip < 2:
                    B.sc.barrier(); B.stage_odd_conv(aps, li, Gs, XBCr, XBC, MO)
                if ocut >= 3:
                    B.sc.barrier(); B.stage_odd_dt(aps, li, DTr, DTt)
                if ocut >= 4:
                    B.sc.barrier(); B.stage_odd_ssd(aps, XBC, Yf, 0)
                if ocut >= 5:
                    B.sc.barrier(); B.stage_odd_ssd(aps, XBC, Yb, 1)
                if ocut >= 6:
                    B.sc.barrier(); B.stage_odd_out(X, aps, li, base + 24, Yf, Yb, XBC, ZS, MO)
    B.sc.barrier(); B.stage_store(X, out)
    fin = [("out", g) for g in range(NG)]
    if os.environ.get("DBG"):
        def dbg(name, ap, keys):
            o = dt("dbg_" + name, list(ap.shape), ap.dtype, kind="ExternalOutput").ap()
            B.dma(o, ap, keys, [("dbg", name)])
            fin.append(("dbg", name))
        G4 = range(NG)
        dbg("G", Gs, [("G", g) for g in G4])
        dbg("XBC", XBC, [("XBC", g) for g in G4])
        dbg("MO", MO, [("MO", g) for g in G4])
        dbg("ZS", ZS, [("ZS", g) for g in G4])
        dbg("Yf", Yf, [("Y", 0, g) for g in G4])
        dbg("Yb", Yb, [("Y", 1, g) for g in G4])
        dbg("DTr", DTr, [("DTr", g) for g in G4])
        dbg("tok", B.bXN[:].bitcast(F32)[:, 0:3072], ["dtk", "dtd", "cdk"])
        dbg("acsT", B.bX[0:48, 0:S], ["acsT"])
    B.op("sp", lambda e: e.nop(), fin, [])
    if wplan is None:
        return None, B.wrec
    with ExitStack() as st:
        B.sc.emit(st)
    return nc, B.wrec


def rope_tables_np():
    inv = (1.0 / (np.float32(10000.0) ** (np.arange(0, 64, 2, dtype=np.float32) / np.float32(64)))).astype(np.float32)
    ang = (np.arange(S, dtype=np.float32)[:, None] * inv[None, :]).astype(np.float32)
    cos = np.cos(ang).astype(np.float32)
    sin = np.sin(ang).astype(np.float32)
    cf = np.concatenate([cos, cos], axis=1)
    sf = np.concatenate([-sin, sin], axis=1)
    cT = np.ascontiguousarray(np.concatenate([cf, cf], axis=1).T)
    sT = np.ascontiguousarray(np.concatenate([sf, sf], axis=1).T)
    return cT, sT


def const_tables():
    cT, sT = rope_tables_np()
    t = np.arange(S)
    icnt = np.zeros((2, 128, S), np.float32)
    for gi, w in enumerate((2, 4, 8, 16)):
        lo = np.clip(t - w // 2, 0, S)
        hi = np.clip(t + w - w // 2, 0, S)
        icnt[gi // 2, (gi % 2) * 64:(gi % 2) * 64 + 64, :] = (1.0 / (hi - lo).astype(np.float32))[None, :]
    kl = np.arange(128)[:, None]
    ql = np.arange(128)[None, :]
    m1 = (ql <= kl).astype(np.float32)
    m2 = (kl <= ql).astype(np.float32)
    amask = np.stack([np.tile(m1, (1, 3)), np.tile(m2, (1, 3))], axis=1)
    s_ = np.arange(128)[:, None]
    l_ = np.arange(128)[None, :]
    tri = np.stack([(s_ <= l_), (s_ >= l_)], axis=1).astype(np.float32)
    selm = np.zeros((48, 4, 4, 128), np.float32)
    for k in range(16):
        selm[k, k // 4, k % 4, :] = 1.0
        selm[32 + k, k // 4, k % 4, :] = 1.0
    NEG = -30000.0
    mb = np.stack([np.tile(np.where(l_ >= s_, 0.0, NEG), (1, 4)), np.tile(np.where(s_ >= l_, 0.0, NEG), (1, 4))], axis=1)
    rmask = np.ones((48, 128), np.float32)
    rmask[:, 0] = 0.0
    return {"ropec": cT, "ropes": sT, "icnt": icnt, "amask": np.ascontiguousarray(amask),
            "tri": np.ascontiguousarray(tri), "selm": selm, "mbias": np.ascontiguousarray(mb.astype(np.float32)),
            "rmask": rmask}


def make_in_maps(inputs):
    nw = np.ascontiguousarray(
        inputs["norm_w"].reshape(DEPTH, 6, 8, 128).transpose(3, 0, 1, 2).reshape(128, DEPTH * 6 * 8))
    qk = np.arange(256, 1280)
    rel = (qk - 256) % 64
    partner = qk - rel + np.where(rel < 32, rel + 32, rel - 32)
    cols = []
    for i in range(8):
        cols.append(qk[i * 128:(i + 1) * 128])
        cols.append(partner[i * 128:(i + 1) * 128])
    cols.append(np.arange(0, 256))
    cols = np.concatenate(cols)
    shared = {
        "normw": nw,
        "wg": tileize(inputs["ffn_w_gate"].reshape(DEPTH * 2, D, DFF)),
        "wu": tileize(inputs["ffn_w_up"].reshape(DEPTH * 2, D, DFF)),
        "wd": tileize(inputs["ffn_w_down"].reshape(DEPTH * 2, DFF, D)),
        "ev_win": tileize(inputs["ev_w_in"][:, :, cols]),
        "ev_wv": tileize(inputs["ev_w_in"][:, :, 1280:1536]),
        "ev_pool_w": np.ascontiguousarray(inputs["ev_pool_w"]),
        "ev_pool_scale": np.ascontiguousarray(inputs["ev_pool_scale"].reshape(2, 2, 128).transpose(0, 2, 1)),
        "ev_sink": np.ascontiguousarray(inputs["ev_sink"]),
        "ev_wout": tileize(inputs["ev_w_out"]),
    }
    ocols = []
    for m in range(4):
        ocols.append(np.arange(m * 128, (m + 1) * 128))
        ocols.append(np.arange(512 + m * 128, 512 + (m + 1) * 128))
    ocols.append(np.arange(1024, 4128))
    ocols = np.concatenate(ocols)
    f32 = np.float32
    ab = np.zeros((2, 48, 2), f32)
    ab[:, 0:16, 0] = inputs["ssm_A_log"][:, 0]
    ab[:, 32:48, 0] = inputs["ssm_A_log"][:, 1]
    ab[:, 0:16, 1] = inputs["ssm_dt_bias"][:, 0]
    ab[:, 32:48, 1] = inputs["ssm_dt_bias"][:, 1]
    cw = np.concatenate([inputs["ssm_conv_w"].reshape(2, 4, 16, 128).transpose(0, 3, 2, 1),
                         inputs["ssm_conv_b"].reshape(2, 16, 128).transpose(0, 2, 1)[..., None]], axis=-1)
    shared.update({
        "od_win": tileize(inputs["od_w_in"][:, :, ocols[:4096]]),
        "od_wdt": np.ascontiguousarray(inputs["od_w_in"][:, :, 4096:4128]),
        "od_wout": tileize(inputs["od_w_out"]),
        "cv_dww": np.ascontiguousarray(inputs["cv_dw_w"].reshape(2, 31, 4, 128).transpose(0, 3, 2, 1)),
        "cv_vec": np.ascontiguousarray(np.stack([inputs["cv_dw_b"], inputs["cv_ln_g"], inputs["cv_ln_b"]], axis=1)
                                       .reshape(2, 3, 4, 128).transpose(0, 3, 1, 2)),
        "ssm_cw": np.ascontiguousarray(cw.astype(f32)),
        "ssm_dexp": np.ascontiguousarray(np.repeat(inputs["ssm_D"], 64, axis=1).reshape(2, 8, 128).transpose(0, 2, 1)),
        "ssm_nw": np.ascontiguousarray(inputs["ssm_norm_w"].reshape(2, 8, 128).transpose(0, 2, 1)),
        "ssm_ab": ab,
        "ssm_A_log": np.ascontiguousarray(inputs["ssm_A_log"].reshape(2, 32)),
        "ssm_dt_bias": np.ascontiguousarray(inputs["ssm_dt_bias"].reshape(2, 32)),
    })
    shared.update(const_tables())
    maps = []
    for b in range(NCORES):
        m = dict(shared)
        m["x"] = np.ascontiguousarray(inputs["x"][b])
        maps.append(m)
    return maps


def kernel(**inputs):
    inputs = {k: np.asarray(v) for k, v in inputs.items()}
    nc = build_program()
    in_maps = make_in_maps(inputs)
    res = run_bass_kernel_spmd(nc, in_maps, core_ids=list(range(NCORES)))
    return np.stack([np.asarray(r["out"]) for r in res.results], axis=0).astype(np.float32)
```

```python
import numpy as np
import concourse.bass as bass
import concourse.mybir as mybir
from concourse.bass_utils import run_bass_kernel_spmd

F32 = mybir.dt.float32
BF16 = mybir.dt.bfloat16
ALU = mybir.AluOpType
AF = mybir.ActivationFunctionType

D = 1024
S = 4096
DEPTH = 4
DFF = 2816
NCORES = 8
EPS = 1e-6
TG = 1024
NG = S // TG
NB = TG // 512


W_TILED = {"wg": (D, DFF), "wu": (D, DFF), "wd": (DFF, D), "ev_win": (D, 2304), "ev_wv": (D, 256),
           "ev_wout": (D, D), "od_win": (D, 4096), "od_wout": (1536, D)}


def tile_offset(K, M, kt, ct):
    nct = M // 256
    off = 0
    for k_ in range(kt):
        off += nct * 128 * min(8, (K - k_ * 1024) // 128) * 256
    return off + ct * 128 * min(8, (K - kt * 1024) // 128) * 256


def tileize(W):
    lead = W.shape[:-2]
    K, M = W.shape[-2:]
    W2 = W.reshape((-1, K, M))
    outs = []
    for kt in range((K + 1023) // 1024):
        nk = min(8, (K - kt * 1024) // 128)
        blk = W2[:, kt * 1024:kt * 1024 + nk * 128, :].reshape(-1, nk, 128, M // 256, 256)
        outs.append(np.ascontiguousarray(blk.transpose(0, 3, 2, 1, 4)).reshape(W2.shape[0], -1))
    return np.ascontiguousarray(np.concatenate(outs, axis=1)).reshape(lead + (K * M,))


class Op:
    __slots__ = ("eng", "fn", "dma", "deps", "idx", "sem", "semval", "sig", "presem")


class Sched:
    ENG = ("pe", "act", "dve", "pool", "sp")

    def __init__(self, nc):
        self.nc = nc
        self.ops = []
        self.last_w = {}
        self.readers = {}
        self.dma_since = []
        self.last_on = {}

    def barrier(self):
        frontier = set(self.dma_since)
        for e, idx in self.last_on.items():
            frontier.add(idx)
        self.dma_since = []
        for e in self.ENG:
            self.add(e, lambda eng: eng.nop(), extra=frontier)

    def add(self, eng, fn, reads=(), writes=(), dma=False, extra=()):
        op = Op()
        op.eng = eng
        op.fn = fn
        op.dma = dma
        op.idx = len(self.ops)
        op.sig = False
        op.sem = None
        op.semval = 0
        op.presem = None
        deps = {}
        for k in reads:
            w = self.last_w.get(k)
            if w is not None:
                deps[w] = True
            if isinstance(k, tuple) and k[0] == "ps":
                for r in self.readers.get(k, ()):
                    if self.ops[r].eng != eng:
                        deps[r] = True
        for k in writes:
            w = self.last_w.get(k)
            if w is not None:
                deps.setdefault(w, False)
            for r in self.readers.get(k, ()):
                deps.setdefault(r, False)
        for k in reads:
            self.readers.setdefault(k, []).append(op.idx)
        for k in writes:
            self.last_w[k] = op.idx
            self.readers[k] = []
        for x in extra:
            deps[x] = True
        deps.pop(op.idx, None)
        op.deps = deps
        self.ops.append(op)
        if dma:
            self.dma_since.append(op.idx)
        else:
            self.last_on[eng] = op.idx
        return op

    def _needs_sync(self, c, p, raw):
        if p.dma:
            return True
        if c.eng != p.eng:
            return True
        if c.dma:
            return True
        if c.eng == "pe":
            return False
        return raw

    def emit(self, stack):
        nc = self.nc
        ops = self.ops
        for c in ops:
            for pi, raw in c.deps.items():
                p = ops[pi]
                if not p.dma and self._needs_sync(c, p, raw):
                    p.sig = True
        NDMA = 40
        dma_sems = [stack.enter_context(nc.semaphore("dq%d" % i)) for i in range(NDMA)]
        dma_cum = [0] * NDMA
        dma_rr = 0
        EPOCH = 20000
        eng_sems = {e: [] for e in self.ENG}
        eng_cnt = {e: 0 for e in self.ENG}
        for op in ops:
            if op.dma:
                i = dma_rr % NDMA
                dma_rr += 1
                op.sem = dma_sems[i]
                op.presem = (dma_sems[i], dma_cum[i]) if dma_cum[i] > 0 else None
                dma_cum[i] += 16
                op.semval = dma_cum[i]
            elif op.sig:
                e = op.eng
                if eng_cnt[e] % EPOCH == 0:
                    eng_sems[e].append(stack.enter_context(
                        nc.semaphore("e_%s_%d" % (e, len(eng_sems[e])))))
                eng_cnt[e] += 1
                op.sem = eng_sems[e][-1]
                op.semval = (eng_cnt[e] - 1) % EPOCH + 1
        per = {e: [] for e in self.ENG}
        for op in ops:
            per[op.eng].append(op)

        def run(engname, eng):
            waited = {}
            for op in per[engname]:
                ws = []
                for pi, raw in op.deps.items():
                    p = ops[pi]
                    if self._needs_sync(op, p, raw):
                        ws.append((p.sem, p.semval))
                if op.presem is not None:
                    ws.append(op.presem)
                for sem, val in ws:
                    key = id(sem)
                    if waited.get(key, 0) >= val:
                        continue
                    waited[key] = val
                    eng.wait_ge(sem, val)
                ins = op.fn(eng)
                if op.dma:
                    ins.then_inc(op.sem, 16)
                elif op.sig:
                    ins.then_inc(op.sem, 1)

        with nc.Block() as block:
            @block.tensor
            def _(e):
                run("pe", e)

            @block.scalar
            def _(e):
                run("act", e)

            @block.vector
            def _(e):
                run("dve", e)

            @block.gpsimd
            def _(e):
                run("pool", e)

            @block.sync
            def _(e):
                run("sp", e)


class Builder:
    def __init__(self, nc, stop=None, aps=None, wplan=None):
        self.nc = nc
        self.aps = aps
        self.wplan = wplan
        self.wrec = []
        self.w_issued = 0
        self.PF = 2
        self.sc = Sched(nc)
        self.stop = stop
        self.uid = 0
        nc_ = nc
        A = nc_.alloc_sbuf_tensor
        self.bX = A("bX", [128, 8 * TG], F32)
        self.bXN = A("bXN", [128, 8 * TG], BF16)
        self.bSQ = A("bSQ", [128, 8 * TG], BF16)
        self.bH = A("bH", [128, 22 * TG], BF16)
        self.NWS = 4
        self.wsF = [A("wsF%d" % i, [128, 2048], F32) for i in range(self.NWS)]
        self.wsB = [A("wsB%d" % i, [128, 2048], BF16) for i in range(self.NWS)]
        self.ws_i = 0
        self.bR = A("bR", [128, 2 * TG], F32)
        self.bSG = [A("bSG%d" % i, [128, TG], F32) for i in range(2)]
        self.sg_i = 0
        self.bT = [A("bT%d" % i, [128, TG], F32) for i in range(4)]
        self.t_i = 0
        self.ones_bf = A("ones_bf", [128, 128], BF16)
        self.ident_f = A("ident_f", [128, 128], F32)
        self.normw = A("normw_sb", [128, DEPTH * 6 * 8], F32)
        self.poolW = A("poolW", [128, 2, 128], BF16)
        self.poolS = A("poolS", [128, 2], F32)
        self.esink = A("esink", [128, 12], F32)
        self.amask = A("amask_sb", [128, 2, 384], F32)
        self.stg_i = 0
        self.ones_f = A("ones_f", [128, 128], F32)
        self.ident_b = A("ident_b", [128, 128], BF16)
        self.dww = A("dww", [128, 4, 31], F32)
        self.cvvec = A("cvvec", [128, 3, 4], F32)
        self.scw = A("scw", [128, 16, 5], F32)
        self.dexp = A("dexp", [128, 8], F32)
        self.snw = A("snw", [128, 8], F32)
        self.abcol = A("abcol", [48, 2], F32)
        self.abrow = A("abrow", [128, 2, 32], F32)
        self.tri = A("tri_sb", [128, 2, 128], F32)
        self.selm = A("selm_sb", [48, 4, 4, 128], F32)
        self.mbias = A("mbias_sb", [128, 2, 512], BF16)
        self.ps = nc_.alloc_psum_tensor("ps", [128, 8, 512], F32)
        self.ps_i = 0
        self.NPG = 8 // NB

    def op(self, eng, fn, reads=(), writes=(), dma=False):
        return self.sc.add(eng, fn, reads, writes, dma)

    def psum_group(self):
        g = self.ps_i % self.NPG
        self.ps_i += 1
        key = ("ps", g)
        view = self.ps[:, g * NB:(g + 1) * NB, :]
        return key, view

    def dma(self, out, in_, reads, writes, eng="sp"):
        return self.op(eng, lambda e, o=out, i=in_: e.dma_start(out=o, in_=i), reads, writes, dma=True)

    def _issue_w(self, j, spec):
        name, idx, r0, nk, c0, ncols = spec
        W = self.aps[name][idx]
        i = j % self.NWS
        fv = self.wsF[i][:, 0:nk * ncols].rearrange("p (k m) -> p k m", k=nk)
        bv = self.wsB[i][:, 0:nk * ncols].rearrange("p (k m) -> p k m", k=nk)
        if name in W_TILED:
            K, M = W_TILED[name]
            off = tile_offset(K, M, r0 // 1024, c0 // 256)
            assert ncols == 256 and nk == min(8, (K - r0) // 128)
            src = W[off:off + 128 * nk * 256].rearrange("(p k m) -> p k m", p=128, k=nk)
        else:
            src = W[r0:r0 + nk * 128, c0:c0 + ncols].rearrange("(k p) m -> p k m", p=128)
        self.dma(fv, src, reads=[], writes=[("wsF", i)])
        if j % 2 == 0:
            self.op("act", lambda e, o=bv, s=fv: e.copy(out=o, in_=s),
                    reads=[("wsF", i)], writes=[("wsB", i)])
        else:
            self.op("dve", lambda e, o=bv, s=fv: e.tensor_copy(out=o, in_=s),
                    reads=[("wsF", i)], writes=[("wsB", i)])

    def load_w(self, Wd, r0, nk, c0, ncols):
        j = self.ws_i
        self.ws_i += 1
        spec = (Wd[0], Wd[1], r0, nk, c0, ncols)
        if self.wplan is None:
            self.wrec.append(spec)
            self._issue_w(j, spec)
        else:
            assert self.wplan[j] == spec, (j, self.wplan[j], spec)
            upto = min(j + self.PF, len(self.wplan) - 1)
            while self.w_issued <= upto:
                self._issue_w(self.w_issued, self.wplan[self.w_issued])
                self.w_issued += 1
        i = j % self.NWS
        bv = self.wsB[i][:, 0:nk * ncols].rearrange("p (k m) -> p k m", k=nk)
        return ("wsB", i), bv

    def init_consts(self, aps):
        nc = self.nc
        self.op("pool", lambda e: e.memset(self.ones_bf[:], 1.0), [], ["ones_bf"])
        self.op("pool", lambda e: e.memset(self.ident_f[:], 0.0), [], ["ident_f"])
        self.op("pool", lambda e: e.affine_select(
            out=self.ident_f[:], in_=self.ident_f[:], pattern=[[-1, 128]],
            compare_op=ALU.not_equal, fill=1.0, base=0, channel_multiplier=1),
            ["ident_f"], ["ident_f"])
        self.dma(self.normw[:], aps["normw"], [], ["normw"])
        self.dma(self.amask[:], aps["amask"], [], ["amask"])
        self.op("pool", lambda e: e.memset(self.ones_f[:], 1.0), [], ["ones_f"])
        self.op("pool", lambda e: e.tensor_copy(out=self.ident_b[:], in_=self.ident_f[:]), ["ident_f"], ["ident_b"])
        self.dma(self.tri[:], aps["tri"], [], ["tri"])
        self.dma(self.selm[:], aps["selm"], [], ["selm"])
        st_ = self.bX[:, 0:1024].rearrange("p (d t) -> p d t", d=2)
        self.dma(st_, aps["mbias"], [], [("bX", 0)])
        self.op("pool", lambda e: e.tensor_copy(out=self.mbias[:], in_=st_), [("bX", 0)], ["mbias"])

    def flat(self, pv):
        return pv.rearrange("p b t -> p (b t)")

    def stage_load(self, x, X):
        hf = self.bH[:].bitcast(F32)
        xv = self.bX[:].rearrange("p (c t) -> p c t", c=8)
        for g in range(NG):
            tv = hf[:, 0:8 * D].rearrange("p (tb f) -> p tb f", tb=8)
            self.dma(tv, x[g * TG:(g + 1) * TG, :].rearrange("(tb p) f -> p tb f", p=128),
                     [], ["bH"])
            n = 0
            for tb in range(8):
                for c0 in range(0, 8, 4):
                    key, pv = self.psum_group()
                    pf = self.flat(pv)
                    for cc in range(4):
                        c = c0 + cc
                        self.op("pe", lambda e, o=pf[:, cc * 128:(cc + 1) * 128], i=tv[:, tb, c * 128:(c + 1) * 128]:
                                e.transpose(out=o, in_=i, identity=self.ident_f[:]),
                                ["bH", "ident_f"], [key])
                    src = pf[:, 0:512].rearrange("p (c t) -> p c t", c=4)
                    dst = xv[:, c0:c0 + 4, tb * 128:(tb + 1) * 128]
                    if n % 2 == 0:
                        self.op("act", lambda e, o=dst, i=src: e.copy(out=o, in_=i), [key],
                                [("bX", c0 + k) for k in range(4)])
                    else:
                        self.op("dve", lambda e, o=dst, i=src: e.tensor_copy(out=o, in_=i), [key],
                                [("bX", c0 + k) for k in range(4)])
                    n += 1
            self.dma(X[:, :, g * TG:(g + 1) * TG].rearrange("c p t -> p c t"), xv,
                     [("bX", c) for c in range(8)], [("X", g, c) for c in range(8)])

    def stage_store(self, X, out):
        hf = self.bH[:].bitcast(F32)
        xv = self.bX[:].rearrange("p (c t) -> p c t", c=8)
        for g in range(NG):
            tv = hf[:, 0:8 * D].rearrange("p (tb f) -> p tb f", tb=8)
            self.dma(xv, X[:, :, g * TG:(g + 1) * TG].rearrange("c p t -> p c t"),
                     [("X", g, c) for c in range(8)], [("bX", c) for c in range(8)])
            n = 0
            for tb in range(8):
                for c0 in range(0, 8, 4):
                    key, pv = self.psum_group()
                    pf = self.flat(pv)
                    for cc in range(4):
                        c = c0 + cc
                        self.op("pe", lambda e, o=pf[:, cc * 128:(cc + 1) * 128], i=xv[:, c, tb * 128:(tb + 1) * 128]:
                                e.transpose(out=o, in_=i, identity=self.ident_f[:]),
                                [("bX", c), "ident_f"], [key])
                    dst = tv[:, tb, c0 * 128:(c0 + 4) * 128]
                    src = pf[:, 0:512]
                    if n % 2 == 0:
                        self.op("act", lambda e, o=dst, i=src: e.copy(out=o, in_=i), [key], ["bH"])
                    else:
                        self.op("dve", lambda e, o=dst, i=src: e.tensor_copy(out=o, in_=i), [key], ["bH"])
                    n += 1
            self.dma(out[g * TG:(g + 1) * TG, :].rearrange("(tb p) f -> p tb f", p=128), tv,
                     ["bH"], [("out", g)])

    def rms_rinv(self, sqv, nchunks, off, scale, bias, sqkeys):
        key, pv = self.psum_group()
        for b in range(NB):
            for c in range(nchunks):
                self.op("pe", lambda e, o=pv[:, b, :], r=sqv[:, c, b * 512:(b + 1) * 512], st=(c == 0), sp=(c == nchunks - 1):
                        e.matmul(o, lhsT=self.ones_bf[:], rhs=r, start=st, stop=sp),
                        ["ones_bf", sqkeys[c]], [key])
        rv = self.bR[:, off:off + TG]
        rk = ("bR", off)
        self.op("act", lambda e, o=rv, i=self.flat(pv): e.activation(out=o, in_=i, func=AF.Sqrt, scale=scale, bias=bias),
                [key], [rk])
        self.op("dve", lambda e, o=rv: e.reciprocal(out=o, in_=o), [rk], [rk])
        return rk, rv

    def views(self):
        xv = self.bX[:].rearrange("p (c t) -> p c t", c=8)
        xn = self.bXN[:].rearrange("p (c t) -> p c t", c=8)
        sq = self.bSQ[:].rearrange("p (c t) -> p c t", c=8)
        kX = [("bX", c) for c in range(8)]
        kXN = [("bXN", c) for c in range(8)]
        kSQ = [("bSQ", c) for c in range(8)]
        return xv, xn, sq, kX, kXN, kSQ

    def load_norm(self, X, g, npre):
        xv, xn, sq, kX, kXN, kSQ = self.views()
        t0 = g * TG
        self.dma(xv, X[:, :, t0:t0 + TG].rearrange("c p t -> p c t"),
                 [("X", g, c) for c in range(8)], kX)
        for c in range(8):
            self.op("act", lambda e, o=sq[:, c, :], i=xv[:, c, :]: e.activation(out=o, in_=i, func=AF.Square),
                    [kX[c]], [kSQ[c]])
        rk, rv = self.rms_rinv(sq, 8, 0, 1.0 / D, EPS, kSQ)
        for c in range(8):
            self.op("dve", lambda e, o=xn[:, c, :], i=xv[:, c, :], s=self.normw[:, npre + c:npre + c + 1], r=rv:
                    e.scalar_tensor_tensor(out=o, in0=i, scalar=s, in1=r, op0=ALU.mult, op1=ALU.mult),
                    [kX[c], "normw", rk], [kXN[c]])
        return xn, kXN

    def proj_post_residual(self, X, g, rhs, rkeys, KC, W, npost, half):
        xv, xn, sq, kX, kXN, kSQ = self.views()
        nkt = (KC + 7) // 8
        for ct in range(4):
            grp = [self.psum_group(), self.psum_group()]
            for kt in range(nkt):
                nk = min(8, KC - kt * 8)
                kw, w = self.load_w(W, kt * 1024, nk, ct * 256, 256)
                for mm in range(2):
                    kp, pv = grp[mm]
                    for kk in range(nk):
                        kc = kt * 8 + kk
                        for b in range(NB):
                            self.op("pe", lambda e, o=pv[:, b, :], w_=w[:, kk, mm * 128:(mm + 1) * 128], r=rhs[:, kc, b * 512:(b + 1) * 512], st=(kc == 0), sp=(kc == KC - 1):
                                    e.matmul(o, lhsT=w_, rhs=r, start=st, stop=sp), [kw, rkeys[kc]], [kp])
            for mm in range(2):
                oc = ct * 2 + mm
                kp, pv = grp[mm]
                self.op("dve", lambda e, o=xv[:, oc, :], i=self.flat(pv): e.tensor_copy(out=o, in_=i),
                        [kp], [kX[oc]])
                self.op("act", lambda e, o=sq[:, oc, :], i=xv[:, oc, :]: e.activation(out=o, in_=i, func=AF.Square),
                        [kX[oc]], [kSQ[oc]])
        if half:
            rk2, rv2 = self.rms_rinv(sq, 8, TG, 4.0 / D, 4.0 * EPS, kSQ)
        else:
            rk2, rv2 = self.rms_rinv(sq, 8, TG, 1.0 / D, EPS, kSQ)
        self.residual_add(X, g, xv, kX, rk2, rv2, npost)

    def stage_ffn(self, X, Wg, Wu, Wd, npre, npost):
        h = self.bH[:].rearrange("p (c t) -> p c t", c=22)
        xv, xn, sq, kX, kXN, kSQ = self.views()
        sqs = [self.bSG[i // 2][:].bitcast(BF16)[:, (i % 2) * TG:(i % 2 + 1) * TG] for i in range(4)]
        ksqs = [("sqs", i) for i in range(4)]
        self.load_norm(X, 0, npre)
        pending = None
        for g in range(NG):
            for ct in range(11):
                if pending is not None and 1 <= ct <= 8:
                    self.residual_add(X, pending[0], xv, kX, pending[1], pending[2], npost, ocs=[ct - 1])
                    if ct == 8:
                        pending = None
                kg, wg = self.load_w(Wg, 0, 8, ct * 256, 256)
                ku, wu = self.load_w(Wu, 0, 8, ct * 256, 256)
                for mm in range(2):
                    m = ct * 2 + mm
                    kpg, pg = self.psum_group()
                    for k in range(8):
                        for b in range(NB):
                            self.op("pe", lambda e, o=pg[:, b, :], w=wg[:, k, mm * 128:(mm + 1) * 128], r=xn[:, k, b * 512:(b + 1) * 512], st=(k == 0), sp=(k == 7):
                                    e.matmul(o, lhsT=w, rhs=r, start=st, stop=sp), [kg, kXN[k]], [kpg])
                    kpu, pu = self.psum_group()
                    for k in range(8):
                        for b in range(NB):
                            self.op("pe", lambda e, o=pu[:, b, :], w=wu[:, k, mm * 128:(mm + 1) * 128], r=xn[:, k, b * 512:(b + 1) * 512], st=(k == 0), sp=(k == 7):
                                    e.matmul(o, lhsT=w, rhs=r, start=st, stop=sp), [ku, kXN[k]], [kpu])
                    si = self.sg_i % 2
                    self.sg_i += 1
                    sg = self.bSG[si]
                    self.op("act", lambda e, o=sg[:], i=self.flat(pg): e.activation(out=o, in_=i, func=AF.Silu),
                            [kpg], [("bSG", si), ("sqs", 2 * si), ("sqs", 2 * si + 1)])
                    self.op("dve", lambda e, o=h[:, m, :], a=sg[:], b_=self.flat(pu): e.tensor_tensor(out=o, in0=a, in1=b_, op=ALU.mult),
                            [("bSG", si), ("sqs", 2 * si), ("sqs", 2 * si + 1), kpu], [("bH", m)])
            nxt = g + 1 if g + 1 < NG else None
            t1 = (g + 1) * TG
            stk = None

            def sq_chunks(c0):
                for c in range(c0, c0 + 4):
                    ti = c % 4
                    self.dma(self.bT[ti][:], X[c, :, t1:t1 + TG], [("X", nxt, c)], [("bT", ti)])
                    self.op("act", lambda e, o=sqs[c % 4], i_=self.bT[ti][:]: e.activation(out=o, in_=i_, func=AF.Square),
                            [("bT", ti)], [ksqs[c % 4], ("bSG", (c % 4) // 2)])

            def stat_mms(c0, key, pv):
                for c in range(c0, c0 + 4):
                    for b in range(NB):
                        self.op("pe", lambda e, o=pv[:, b, :], r=sqs[c % 4][:, b * 512:(b + 1) * 512], st=(c == 0), sp=(c == 7):
                                e.matmul(o, lhsT=self.ones_bf[:], rhs=r, start=st, stop=sp),
                                ["ones_bf", ksqs[c % 4], ("bSG", (c % 4) // 2)], [key])

            for ct in range(4):
                if nxt is not None and ct == 0:
                    sq_chunks(0)
                grp = [self.psum_group(), self.psum_group()]
                for kt in range(3):
                    nk = 8 if kt < 2 else 6
                    kw, w = self.load_w(Wd, kt * 1024, nk, ct * 256, 256)
                    for mm in range(2):
                        kp, pv = grp[mm]
                        for kk in range(nk):
                            kc = kt * 8 + kk
                            for b in range(NB):
                                self.op("pe", lambda e, o=pv[:, b, :], w_=w[:, kk, mm * 128:(mm + 1) * 128], r=h[:, kc, b * 512:(b + 1) * 512], st=(kc == 0), sp=(kc == 21):
                                        e.matmul(o, lhsT=w_, rhs=r, start=st, stop=sp), [kw, ("bH", kc)], [kp])
                if nxt is not None and ct == 0:
                    stk = self.psum_group()
                    stat_mms(0, stk[0], stk[1])
                    sq_chunks(4)
                if nxt is not None and ct == 1:
                    stat_mms(4, stk[0], stk[1])
                for mm in range(2):
                    oc = ct * 2 + mm
                    kp, pv = grp[mm]
                    self.op("dve", lambda e, o=xv[:, oc, :], i=self.flat(pv): e.tensor_copy(out=o, in_=i),
                            [kp], [kX[oc]])
                    self.op("act", lambda e, o=sq[:, oc, :], i=xv[:, oc, :]: e.activation(out=o, in_=i, func=AF.Square),
                            [kX[oc]], [kSQ[oc]])
                if nxt is not None and ct == 1:
                    rv = self.bR[:, 0:TG]
                    rk = ("bR", 0)
                    self.op("act", lambda e, o=rv, i=self.flat(stk[1]): e.activation(out=o, in_=i, func=AF.Sqrt, scale=1.0 / D, bias=EPS),
                            [stk[0]], [rk])
                    self.op("dve", lambda e, o=rv: e.reciprocal(out=o, in_=o), [rk], [rk])
                    for c in range(8):
                        ti = c % 4
                        self.dma(self.bT[ti][:], X[c, :, t1:t1 + TG], [("X", nxt, c)], [("bT", ti)])
                        self.op("dve", lambda e, o=xn[:, c, :], i=self.bT[ti][:], s=self.normw[:, npre + c:npre + c + 1], r=rv:
                                e.scalar_tensor_tensor(out=o, in0=i, scalar=s, in1=r, op0=ALU.mult, op1=ALU.mult),
                                [("bT", ti), "normw", rk], [kXN[c]])
            rk2, rv2 = self.rms_rinv(sq, 8, TG, 4.0 / D, 4.0 * EPS, kSQ)
            if g + 1 < NG:
                pending = (g, rk2, rv2)
            else:
                self.residual_add(X, g, xv, kX, rk2, rv2, npost)

    def residual_add(self, X, g, yv, kY, rk, rv, npost, ocs=range(8)):
        t0 = g * TG
        for oc in ocs:
            ti = self.t_i % 4
            self.t_i += 1
            xr = self.bT[ti]
            kt_ = ("bT", ti)
            self.dma(xr[:], X[oc, :, t0:t0 + TG], [("X", g, oc)], [kt_])
            self.op("dve", lambda e, o=yv[:, oc, :], s=self.normw[:, npost + oc:npost + oc + 1], r=rv:
                    e.scalar_tensor_tensor(out=o, in0=o, scalar=s, in1=r, op0=ALU.mult, op1=ALU.mult),
                    [kY[oc], "normw", rk], [kY[oc]])
            self.op("pool", lambda e, o=xr[:], b_=yv[:, oc, :]: e.tensor_tensor(out=o, in0=o, in1=b_, op=ALU.add),
                    [kt_, kY[oc]], [kt_])
            self.dma(X[oc, :, t0:t0 + TG], xr[:], [kt_], [("X", g, oc)])


    def even_consts(self, aps, li):
        stg = self.wsF[0][:, 0:256].rearrange("p (c m) -> p c m", c=2)
        stg = self.bT[0][:, 0:256].rearrange("p (c m) -> p c m", c=2)
        kt_ = ("bT", 0)
        self.op("pool", lambda e, o=stg: e.memset(o, 0.0), [], [kt_])
        for c in range(2):
            for gg in range(2):
                self.dma(stg[gg * 64:(gg + 1) * 64, c, gg * 64:(gg + 1) * 64], aps["ev_pool_w"][li, 2 * c + gg],
                         [kt_], [kt_])
        self.op("pool", lambda e, o=self.poolW[:], i=stg: e.tensor_copy(out=o, in_=i), [kt_], ["poolW"])
        self.dma(self.poolS[:], aps["ev_pool_scale"][li], ["poolS"], ["poolS"])
        self.dma(self.esink[:], aps["ev_sink"][li].partition_broadcast(128), ["esink"], ["esink"])
        self.op("act", lambda e, o=self.esink[:]: e.activation(out=o, in_=o, func=AF.Exp), ["esink"], ["esink"])

    def stage_even_in(self, X, aps, li, npre, QT, KT, V, U):
        xv, xn_, sq, kX, kXN_, kSQ = self.views()
        Win = ("ev_win", li)
        Wv = ("ev_wv", li)
        sqb = sq
        vst = self.bH[:, 0:8 * 256].rearrange("p (tb f) -> p tb f", tb=8)
        ust = self.bR[:].rearrange("p (c t) -> p c t", c=2)
        for g in range(NG):
            t0 = g * TG
            xn, kXN = self.load_norm(X, g, npre)
            cosT = self.bSG[0]
            sinT = self.bSG[1]
            self.dma(cosT[:], aps["ropec"][:, t0:t0 + TG], [], [("bSG", 0)])
            self.dma(sinT[:], aps["ropes"][:, t0:t0 + TG], [], [("bSG", 1)])
            for i in range(8):
                kw, w = self.load_w(Win, 0, 8, i * 256, 256)
                kq, pq = self.psum_group()
                for k in range(8):
                    for b in range(NB):
                        self.op("pe", lambda e, o=pq[:, b, :], w_=w[:, k, 0:128], r=xn[:, k, b * 512:(b + 1) * 512], st=(k == 0), sp=(k == 7):
                                e.matmul(o, lhsT=w_, rhs=r, start=st, stop=sp), [kw, kXN[k]], [kq])
                kp, pp = self.psum_group()
                for k in range(8):
                    for b in range(NB):
                        self.op("pe", lambda e, o=pp[:, b, :], w_=w[:, k, 128:256], r=xn[:, k, b * 512:(b + 1) * 512], st=(k == 0), sp=(k == 7):
                                e.matmul(o, lhsT=w_, rhs=r, start=st, stop=sp), [kw, kXN[k]], [kp])
                t1 = self.bT[0]
                t2 = self.bT[1]
                self.op("dve", lambda e, o=t1[:], a=self.flat(pq), c_=cosT[:]: e.tensor_tensor(out=o, in0=a, in1=c_, op=ALU.mult),
                        [kq, ("bSG", 0)], [("bT", 0)])
                self.op("dve", lambda e, o=t2[:], a=self.flat(pp), c_=sinT[:]: e.tensor_tensor(out=o, in0=a, in1=c_, op=ALU.mult),
                        [kp, ("bSG", 1)], [("bT", 1)])
                self.op("pool", lambda e, o=sqb[:, i, :], a=t1[:], b_=t2[:]: e.tensor_tensor(out=o, in0=a, in1=b_, op=ALU.add),
                        [("bT", 0), ("bT", 1)], [kSQ[i]])
                if i < 6:
                    self.dma(QT[i * 128:(i + 1) * 128, t0:t0 + TG], sqb[:, i, :], [kSQ[i]], [("QT", g)])
                else:
                    self.dma(KT[(i - 6) * 128:(i - 5) * 128, t0:t0 + TG], sqb[:, i, :], [kSQ[i]], [("KT", g)])
            kw, w = self.load_w(Win, 0, 8, 2048, 256)
            for mm in range(2):
                kp, pv = self.psum_group()
                for k in range(8):
                    for b in range(NB):
                        self.op("pe", lambda e, o=pv[:, b, :], w_=w[:, k, mm * 128:(mm + 1) * 128], r=xn[:, k, b * 512:(b + 1) * 512], st=(k == 0), sp=(k == 7):
                                e.matmul(o, lhsT=w_, rhs=r, start=st, stop=sp), [kw, kXN[k]], [kp])
                self.op("act", lambda e, o=ust[:, mm, :], i_=self.flat(pv): e.copy(out=o, in_=i_), [kp], [("bR", mm * TG)])
                self.dma(U[mm, :, t0:t0 + TG], ust[:, mm, :], [("bR", mm * TG)], [("U", g)])
            kw, w = self.load_w(Wv, 0, 8, 0, 256)
            for tb in range(8):
                kp, pv = self.psum_group()
                pf = self.flat(pv)
                for k in range(8):
                    self.op("pe", lambda e, o=pf[:, 0:256], l=xn[:, k, tb * 128:(tb + 1) * 128], r=w[:, k, :], st=(k == 0), sp=(k == 7):
                            e.matmul(o, lhsT=l, rhs=r, start=st, stop=sp), [kw, kXN[k]], [kp])
                if tb % 2 == 0:
                    self.op("act", lambda e, o=vst[:, tb, :], i_=pf[:, 0:256]: e.copy(out=o, in_=i_), [kp], [("bH", "v")])
                else:
                    self.op("dve", lambda e, o=vst[:, tb, :], i_=pf[:, 0:256]: e.tensor_copy(out=o, in_=i_), [kp], [("bH", "v")])
            self.dma(V[t0:t0 + TG, :].rearrange("(tb p) f -> p tb f", p=128), vst, [("bH", "v")], [("V", g)])

    def stage_even_pool(self, aps, U, AO):
        L = TG + 16
        bx = self.bX
        up = bx[:, 0:2 * L].rearrange("p (c t) -> p c t", c=2)
        a1 = bx[:, 2 * L:4 * L].rearrange("p (c t) -> p c t", c=2)
        a2 = bx[:, 4 * L:6 * L].rearrange("p (c t) -> p c t", c=2)
        xnf = self.bXN[:].bitcast(F32)
        a3 = xnf[:, 0:L]
        a4 = xnf[:, L:2 * L]
        icnt = self.bR[:].rearrange("p (c t) -> p c t", c=2)
        res = self.bSQ[:].bitcast(F32)[:, 0:2 * TG].rearrange("p (c t) -> p c t", c=2)
        pb = self.bH[:, 0:2 * TG].rearrange("p (c t) -> p c t", c=2)
        ob = self.bH[:, 2 * TG:4 * TG].rearrange("p (c t) -> p c t", c=2)
        for g in range(NG):
            t0 = g * TG
            lo = max(t0 - 8, 0)
            hi = min(t0 + TG + 8, S)
            j0 = lo - (t0 - 8)
            self.op("pool", lambda e, o=up: e.memset(o, 0.0), [], ["pl_u"])
            self.dma(up[:, :, j0:j0 + hi - lo], U[:, :, lo:hi].rearrange("c p t -> p c t"),
                     ["pl_u"] + [("U", gg) for gg in (g - 1, g, g + 1) if 0 <= gg < NG], ["pl_u"])
            self.dma(icnt, aps["icnt"][:, :, t0:t0 + TG].rearrange("c p t -> p c t"), [], [("bR", 0), ("bR", TG)])
            self.op("dve", lambda e: e.tensor_tensor(out=a1[:, :, 1:L], in0=up[:, :, 0:L - 1], in1=up[:, :, 1:L], op=ALU.add),
                    ["pl_u"], ["pl_a1"])
            self.op("dve", lambda e: e.tensor_tensor(out=a2[:, :, 2:L - 1], in0=a1[:, :, 1:L - 2], in1=a1[:, :, 3:L], op=ALU.add),
                    ["pl_a1"], ["pl_a2"])
            self.op("dve", lambda e: e.tensor_tensor(out=a3[:, 4:L - 3], in0=a2[:, 1, 2:L - 5], in1=a2[:, 1, 6:L - 1], op=ALU.add),
                    ["pl_a2"], ["pl_a3"])
            self.op("dve", lambda e: e.tensor_tensor(out=a4[:, 8:L - 8], in0=a3[:, 4:L - 12], in1=a3[:, 12:L - 4], op=ALU.add),
                    ["pl_a3"], ["pl_a4"])
            self.op("dve", lambda e: e.tensor_tensor(out=res[0:64, 0, :], in0=a1[0:64, 0, 8:TG + 8], in1=icnt[0:64, 0, :], op=ALU.mult),
                    ["pl_a1", ("bR", 0)], ["pl_res"])
            self.op("dve", lambda e: e.tensor_tensor(out=res[64:128, 0, :], in0=a2[64:128, 0, 8:TG + 8], in1=icnt[64:128, 0, :], op=ALU.mult),
                    ["pl_a2", ("bR", 0)], ["pl_res"])
            self.op("dve", lambda e: e.tensor_tensor(out=res[0:64, 1, :], in0=a3[0:64, 8:TG + 8], in1=icnt[0:64, 1, :], op=ALU.mult),
                    ["pl_a3", ("bR", TG)], ["pl_res"])
            self.op("dve", lambda e: e.tensor_tensor(out=res[64:128, 1, :], in0=a4[64:128, 8:TG + 8], in1=icnt[64:128, 1, :], op=ALU.mult),
                    ["pl_a4", ("bR", TG)], ["pl_res"])
            self.op("dve", lambda e: e.tensor_tensor(out=pb, in0=res, in1=up[:, :, 8:TG + 8], op=ALU.subtract),
                    ["pl_res", "pl_u"], ["pl_pb"])
            for c in range(2):
                kp, pv = self.psum_group()
                for b in range(NB):
                    self.op("pe", lambda e, o=pv[:, b, :], l=self.poolW[:, c, :], r=pb[:, c, b * 512:(b + 1) * 512]:
                            e.matmul(o, lhsT=l, rhs=r, start=True, stop=True), ["poolW", "pl_pb"], [kp])
                self.op("dve", lambda e, o=ob[:, c, :], i_=self.flat(pv), s=self.poolS[:, c:c + 1]:
                        e.tensor_scalar(out=o, in0=i_, scalar1=s, scalar2=None, op0=ALU.mult), [kp, "poolS"], [("pl_ob", c)])
                self.dma(AO[c, :, t0:t0 + TG], ob[:, c, :], [("pl_ob", c)], [("AOp", g)])

    def stage_even_attn(self, aps, li, QT, KT, V, AO):
        KW = TG + 256
        q_sb = self.bH[0:64, 0:12 * TG].rearrange("p (h t) -> p h t", h=12)
        k_sb = self.bXN[0:64, 0:4 * KW].rearrange("p (h t) -> p h t", h=4)
        v_sb = self.bSQ[:, 0:10 * 256].rearrange("p (kb f) -> p kb f", kb=10)
        o_sb = self.bX[:].bitcast(BF16)[0:64, 0:12 * TG].rearrange("p (h t) -> p h t", h=12)
        Pb = [self.bSG[0][:].bitcast(BF16)[:, 0:384], self.bSG[1][:].bitcast(BF16)[:, 0:384],
              self.bT[0][:].bitcast(BF16)[:, 0:384], self.bT[1][:].bitcast(BF16)[:, 0:384]]
        Pk = [("bSG", 0), ("bSG", 1), ("bT", 0), ("bT", 1)]
        recs = [self.bR[0:64, 0:384], self.bR[0:64, 512:896]]
        AOf = AO.rearrange("c p t -> (c p) t")
        pi = 0
        ri = 0
        for g in range(NG):
            t0 = g * TG
            lo = max(t0 - 128, 0)
            hi = min(t0 + TG + 128, S)
            j0 = lo - (t0 - 128)
            nbr = [gg for gg in (g - 1, g, g + 1) if 0 <= gg < NG]
            self.dma(q_sb, QT[:, t0:t0 + TG].rearrange("(h d) t -> d h t", d=64), [("QT", g)], ["at_q"])
            self.dma(k_sb[:, :, j0:j0 + hi - lo], KT[:, lo:hi].rearrange("(h d) t -> d h t", d=64),
                     [("KT", gg) for gg in nbr], ["at_k"])
            self.dma(v_sb[:, j0 // 128:(j0 + hi - lo) // 128, :], V[lo:hi, :].rearrange("(kb p) f -> p kb f", p=128),
                     [("V", gg) for gg in nbr], ["at_v"])
            for qi in range(8):
                n = g * 8 + qi
                for gk in range(4):
                    kbs = [kb for kb in (n - 1, n, n + 1) if 0 <= kb < S // 128]
                    Ps = []
                    for kb in kbs:
                        lk = kb - (g * 8 - 1)
                        ks, pv = self.psum_group()
                        pf = self.flat(pv)
                        self.op("pe", lambda e, o=pf[:, 0:384].rearrange("p (h q) -> p h q", h=3), l=k_sb[:, gk, lk * 128:(lk + 1) * 128],
                                r=q_sb[:, 3 * gk:3 * gk + 3, qi * 128:(qi + 1) * 128]:
                                e.matmul(o, lhsT=l, rhs=r, start=True, stop=True), ["at_k", "at_q"], [ks])
                        P = Pb[pi % 4]
                        kP = Pk[pi % 4]
                        pi += 1
                        self.op("act", lambda e, o=P, i_=pf[:, 0:384]: e.activation(out=o, in_=i_, func=AF.Exp, scale=0.125),
                                [ks], [kP])
                        if kb == n - 1:
                            self.op("dve", lambda e, o=P: e.tensor_tensor(out=o, in0=o, in1=self.amask[:, 0, :], op=ALU.mult),
                                    [kP, "amask"], [kP])
                        elif kb == n + 1:
                            self.op("dve", lambda e, o=P: e.tensor_tensor(out=o, in0=o, in1=self.amask[:, 1, :], op=ALU.mult),
                                    [kP, "amask"], [kP])
                        Ps.append((P, kP, lk))
                    kn, pn = self.psum_group()
                    num = pn[0:64, 0, 0:384]
                    den = pn[0:64, 1, 0:384]
                    for idx, (P, kP, lk) in enumerate(Ps):
                        st = (idx == 0)
                        sp = (idx == len(Ps) - 1)
                        self.op("pe", lambda e, o=num, l=v_sb[:, lk, gk * 64:(gk + 1) * 64], r=P, st=st, sp=sp:
                                e.matmul(o, lhsT=l, rhs=r, start=st, stop=sp), ["at_v", kP], [kn])
                        self.op("pe", lambda e, o=den, l=self.ones_bf[:, 0:64], r=P, st=st, sp=sp:
                                e.matmul(o, lhsT=l, rhs=r, start=st, stop=sp), ["ones_bf", kP], [kn])
                    rec = recs[ri % 2]
                    kr = ("rec", ri % 2)
                    ri += 1
                    for j in range(3):
                        hh = 3 * gk + j
                        self.op("dve", lambda e, o=rec[:, j * 128:(j + 1) * 128], i_=den[:, j * 128:(j + 1) * 128],
                                s=self.esink[0:64, hh:hh + 1]:
                                e.tensor_scalar(out=o, in0=i_, scalar1=s, scalar2=None, op0=ALU.add), [kn, "esink"], [kr])
                    self.op("dve", lambda e, o=rec: e.reciprocal(out=o, in_=o), [kr], [kr])
                    self.op("dve", lambda e, o=o_sb[:, 3 * gk:3 * gk + 3, qi * 128:(qi + 1) * 128],
                            a=num.rearrange("p (h q) -> p h q", h=3), b_=rec.rearrange("p (h q) -> p h q", h=3):
                            e.tensor_tensor(out=o, in0=a, in1=b_, op=ALU.mult), [kn, kr], ["at_o"])
            self.dma(AOf[256:1024, t0:t0 + TG].rearrange("(h d) t -> d h t", d=64), o_sb, ["at_o"], [("AOa", g)])

    def stage_even_out(self, X, aps, li, npost, AO):
        xv, xn, sq, kX, kXN, kSQ = self.views()
        for g in range(NG):
            t0 = g * TG
            self.dma(xn, AO[:, :, t0:t0 + TG].rearrange("c p t -> p c t"), [("AOp", g), ("AOa", g)], kXN)
            self.proj_post_residual(X, g, xn, kXN, 8, ("ev_wout", li), npost, False)

    def stg4(self):
        i = self.stg_i % 4
        self.stg_i += 1
        t = [self.bT[0], self.bT[1], self.bSG[0], self.bSG[1]][i]
        k = [("bT", 0), ("bT", 1), ("bSG", 0), ("bSG", 1)][i]
        return t, k

    def odd_consts(self, aps, li):
        self.dma(self.dww[:], aps["cv_dww"][li], ["dww"], ["dww"])
        self.dma(self.cvvec[:], aps["cv_vec"][li], ["cvvec"], ["cvvec"])
        self.dma(self.scw[:], aps["ssm_cw"][li], ["scw"], ["scw"])
        self.dma(self.dexp[:], aps["ssm_dexp"][li], ["dexp"], ["dexp"])
        self.dma(self.snw[:], aps["ssm_nw"][li], ["snw"], ["snw"])
        self.dma(self.abcol[:], aps["ssm_ab"][li], ["abcol"], ["abcol"])
        self.op("act", lambda e, o=self.abcol[:, 0:1]: e.activation(out=o, in_=o, func=AF.Exp), ["abcol"], ["abcol"])
        self.op("dve", lambda e, o=self.abcol[:, 0:1]: e.tensor_scalar(out=o, in0=o, scalar1=-1.0, scalar2=None, op0=ALU.mult),
                ["abcol"], ["abcol"])
        self.dma(self.abrow[:, 0, :], aps["ssm_A_log"][li].partition_broadcast(128), ["abrow"], ["abrow"])
        self.dma(self.abrow[:, 1, :], aps["ssm_dt_bias"][li].partition_broadcast(128), ["abrow"], ["abrow"])
        self.op("act", lambda e, o=self.abrow[:, 0, :]: e.activation(out=o, in_=o, func=AF.Exp), ["abrow"], ["abrow"])
        self.op("dve", lambda e, o=self.abrow[:, 0, :]: e.tensor_scalar(out=o, in0=o, scalar1=-1.0, scalar2=None, op0=ALU.mult),
                ["abrow"], ["abrow"])

    def stage_odd_in(self, X, aps, li, npre, G, ZS, XBCr, DTr, DTt):
        Win = ("od_win", li)
        for g in range(NG):
            t0 = g * TG
            xn, kXN = self.load_norm(X, g, npre)
            for i in range(16):
                kw, w = self.load_w(Win, 0, 8, i * 256, 256)
                pgs = []
                for mm in range(2):
                    kp, pv = self.psum_group()
                    for k in range(8):
                        for b in range(NB):
                            self.op("pe", lambda e, o=pv[:, b, :], w_=w[:, k, mm * 128:(mm + 1) * 128], r=xn[:, k, b * 512:(b + 1) * 512], st=(k == 0), sp=(k == 7):
                                    e.matmul(o, lhsT=w_, rhs=r, start=st, stop=sp), [kw, kXN[k]], [kp])
                    pgs.append((kp, pv))
                if i < 4:
                    s1, k1 = self.stg4()
                    s2, k2 = self.stg4()
                    self.op("act", lambda e, o=s1[:], i_=self.flat(pgs[1][1]): e.activation(out=o, in_=i_, func=AF.Sigmoid),
                            [pgs[1][0]], [k1])
                    self.op("dve", lambda e, o=s2[:], a=self.flat(pgs[0][1]), b_=s1[:]: e.tensor_tensor(out=o, in0=a, in1=b_, op=ALU.mult),
                            [pgs[0][0], k1], [k2])
                    self.dma(G[i, :, t0:t0 + TG], s2[:], [k2], [("G", g)])
                elif i < 8:
                    for mm in range(2):
                        s1, k1 = self.stg4()
                        self.op("act", lambda e, o=s1[:], i_=self.flat(pgs[mm][1]): e.activation(out=o, in_=i_, func=AF.Silu),
                                [pgs[mm][0]], [k1])
                        self.dma(ZS[(i - 4) * 2 + mm, :, t0:t0 + TG], s1[:], [k1], [("ZS", g)])
                else:
                    for mm in range(2):
                        s1, k1 = self.stg4()
                        if mm == 0:
                            self.op("act", lambda e, o=s1[:], i_=self.flat(pgs[mm][1]): e.copy(out=o, in_=i_), [pgs[mm][0]], [k1])
                        else:
                            self.op("dve", lambda e, o=s1[:], i_=self.flat(pgs[mm][1]): e.tensor_copy(out=o, in_=i_), [pgs[mm][0]], [k1])
                        self.dma(XBCr[(i - 8) * 2 + mm, :, t0:t0 + TG], s1[:], [k1], [("XBCr", g)])
            kw, w = self.load_w(("od_wdt", li), 0, 8, 0, 32)
            kp, pv = self.psum_group()
            for k in range(8):
                for b in range(NB):
                    self.op("pe", lambda e, o=pv[0:32, b, :], w_=w[:, k, :], r=xn[:, k, b * 512:(b + 1) * 512], st=(k == 0), sp=(k == 7):
                            e.matmul(o, lhsT=w_, rhs=r, start=st, stop=sp), [kw, kXN[k]], [kp])
            s1, k1 = self.stg4()
            self.op("act", lambda e, o=s1[0:32, :], i_=self.flat(pv)[0:32, :]: e.copy(out=o, in_=i_), [kp], [k1])
            self.dma(DTr[:, t0:t0 + TG], s1[0:32, :], [k1], [("DTr", g)])
            kp, pv = self.psum_group()
            pf = self.flat(pv)
            for tb in range(8):
                for k in range(8):
                    self.op("pe", lambda e, o=pf[:, tb * 32:(tb + 1) * 32], l=xn[:, k, tb * 128:(tb + 1) * 128], r=w[:, k, :], st=(k == 0), sp=(k == 7):
                            e.matmul(o, lhsT=l, rhs=r, start=st, stop=sp), [kw, kXN[k]], [kp])
            s1, k1 = self.stg4()
            self.op("dve", lambda e, o=s1[:, 0:256], i_=pf[:, 0:256]: e.tensor_copy(out=o, in_=i_), [kp], [k1])
            self.dma(DTt[:, g * 8:(g + 1) * 8, :],
                     s1[:, 0:256].rearrange("p (tb j) -> p tb j", tb=8), [k1], [("DTt", g)])

    def stage_odd_conv(self, aps, li, G, XBCr, XBC, MO):
        hf = self.bH[:].bitcast(F32)
        acc = hf[:, 0:4 * TG].rearrange("p (c t) -> p c t", c=4)
        sqr = hf[:, 4 * TG:8 * TG].rearrange("p (c t) -> p c t", c=4)
        mo = self.bXN[:, 0:4 * TG].rearrange("p (c t) -> p c t", c=4)
        mean = self.bR[:, 0:TG]
        rstd = self.bR[:, TG:2 * TG]
        LG = TG + 30
        gp = self.bX[:, 0:4 * LG].rearrange("p (c t) -> p c t", c=4)
        for g in range(NG):
            t0 = g * TG
            lo = max(t0 - 15, 0)
            hi = min(t0 + TG + 15, S)
            j0 = lo - (t0 - 15)
            self.op("pool", lambda e, o=gp: e.memset(o, 0.0), [], ["cv_gp"])
            self.dma(gp[:, :, j0:j0 + hi - lo], G[:, :, lo:hi].rearrange("c p t -> p c t"),
                     ["cv_gp"] + [("G", gg) for gg in (g - 1, g, g + 1) if 0 <= gg < NG], ["cv_gp"])
            for c in range(4):
                self.op("dve", lambda e, o=acc[:, c, :], i_=gp[:, c, 0:TG], s1=self.dww[:, c, 0:1], s2=self.cvvec[:, 0, c:c + 1]:
                        e.tensor_scalar(out=o, in0=i_, scalar1=s1, scalar2=s2, op0=ALU.mult, op1=ALU.add),
                        ["cv_gp", "dww", "cvvec"], [("cv_acc", c)])
            for k in range(1, 31):
                for c in range(4):
                    self.op("dve", lambda e, o=acc[:, c, :], i_=gp[:, c, k:k + TG], s=self.dww[:, c, k:k + 1]:
                            e.scalar_tensor_tensor(out=o, in0=i_, scalar=s, in1=o, op0=ALU.mult, op1=ALU.add),
                            ["cv_gp", "dww", ("cv_acc", c)], [("cv_acc", c)])
            for c in range(4):
                self.op("act", lambda e, o=sqr[:, c, :], i_=acc[:, c, :]: e.activation(out=o, in_=i_, func=AF.Square),
                        [("cv_acc", c)], [("cv_sq", c)])
            k1, p1 = self.psum_group()
            k2, p2 = self.psum_group()
            for b in range(NB):
                for c in range(4):
                    self.op("pe", lambda e, o=p1[:, b, :], r=acc[:, c, b * 512:(b + 1) * 512], st=(c == 0), sp=(c == 3):
                            e.matmul(o, lhsT=self.ones_f[:], rhs=r, start=st, stop=sp), ["ones_f", ("cv_acc", c)], [k1])
            for b in range(NB):
                for c in range(4):
                    self.op("pe", lambda e, o=p2[:, b, :], r=sqr[:, c, b * 512:(b + 1) * 512], st=(c == 0), sp=(c == 3):
                            e.matmul(o, lhsT=self.ones_f[:], rhs=r, start=st, stop=sp), ["ones_f", ("cv_sq", c)], [k2])
            self.op("act", lambda e, o=mean, i_=self.flat(p1): e.mul(out=o, in_=i_, mul=1.0 / 512), [k1], [("bR", 0)])
            msq = sqr[:, 0, :]
            self.op("act", lambda e, o=msq, i_=mean: e.activation(out=o, in_=i_, func=AF.Square), [("bR", 0), k2], [("cv_sq", 0)])
            self.op("dve", lambda e, o=rstd, i_=self.flat(p2), m=msq:
                    e.scalar_tensor_tensor(out=o, in0=i_, scalar=1.0 / 512, in1=m, op0=ALU.mult, op1=ALU.subtract),
                    [k2, ("cv_sq", 0)], [("bR", TG)])
            self.op("act", lambda e, o=rstd: e.activation(out=o, in_=o, func=AF.Sqrt, scale=1.0, bias=EPS), [("bR", TG)], [("bR", TG)])
            self.op("dve", lambda e, o=rstd: e.reciprocal(out=o, in_=o), [("bR", TG)], [("bR", TG)])
            for c in range(4):
                self.op("dve", lambda e, o=acc[:, c, :], m=mean: e.tensor_tensor(out=o, in0=o, in1=m, op=ALU.subtract),
                        [("cv_acc", c), ("bR", 0)], [("cv_acc", c)])
                self.op("dve", lambda e, o=acc[:, c, :], s=self.cvvec[:, 1, c:c + 1], r=rstd:
                        e.scalar_tensor_tensor(out=o, in0=o, scalar=s, in1=r, op0=ALU.mult, op1=ALU.mult),
                        [("cv_acc", c), ("bR", TG), "cvvec"], [("cv_acc", c)])
                self.op("act", lambda e, o=mo[:, c, :], i_=acc[:, c, :], bb=self.cvvec[:, 2, c:c + 1]:
                        e.activation(out=o, in_=i_, func=AF.Silu, bias=bb), [("cv_acc", c), "cvvec"], [("bXN", c)])
            self.dma(MO[:, :, t0:t0 + TG].rearrange("c p t -> p c t"), mo, [("bXN", c) for c in range(4)], [("MO", g)])
        LX = TG + 3
        xp = self.bX[:, 0:4 * LX].rearrange("p (c t) -> p c t", c=4)
        for g in range(NG):
            t0 = g * TG
            lo = max(t0 - 2, 0)
            hi = min(t0 + TG + 1, S)
            j0 = lo - (t0 - 2)
            for cb in range(4):
                self.op("pool", lambda e, o=xp: e.memset(o, 0.0), [], ["cv_gp"])
                self.dma(xp[:, :, j0:j0 + hi - lo], XBCr[cb * 4:cb * 4 + 4, :, lo:hi].rearrange("c p t -> p c t"),
                         ["cv_gp"] + [("XBCr", gg) for gg in (g - 1, g, g + 1) if 0 <= gg < NG], ["cv_gp"])
                for cc in range(4):
                    c = cb * 4 + cc
                    self.op("dve", lambda e, o=acc[:, cc, :], i_=xp[:, cc, 0:TG], s1=self.scw[:, c, 0:1], s2=self.scw[:, c, 4:5]:
                            e.tensor_scalar(out=o, in0=i_, scalar1=s1, scalar2=s2, op0=ALU.mult, op1=ALU.add),
                            ["cv_gp", "scw"], [("cv_acc", cc)])
                for k in range(1, 4):
                    for cc in range(4):
                        c = cb * 4 + cc
                        self.op("dve", lambda e, o=acc[:, cc, :], i_=xp[:, cc, k:k + TG], s=self.scw[:, c, k:k + 1]:
                                e.scalar_tensor_tensor(out=o, in0=i_, scalar=s, in1=o, op0=ALU.mult, op1=ALU.add),
                                ["cv_gp", "scw", ("cv_acc", cc)], [("cv_acc", cc)])
                for cc in range(4):
                    self.op("act", lambda e, o=acc[:, cc, :]: e.activation(out=o, in_=o, func=AF.Silu),
                            [("cv_acc", cc)], [("cv_acc", cc)])
                self.dma(XBC[cb * 4:cb * 4 + 4, :, t0:t0 + TG].rearrange("c p t -> p c t"), acc,
                         [("cv_acc", cc) for cc in range(4)], [("XBC", g)])

    def stage_odd_dt(self, aps, li, DTr, DTt):
        xf = self.bXN[:].bitcast(F32)
        dtk = xf[:, 0:1024]
        dtd = xf[:, 1024:2048]
        cdk = xf[:, 2048:3072]
        a_tok = xf[:, 3072:4096]
        sf = self.bSQ[:].bitcast(F32)
        acs_tok = sf[:, 0:1024]
        raw = sf[:, 1024:2048]
        hf = self.bH[:].bitcast(F32)
        AT = hf[0:48, 0:S]
        c1 = hf[0:48, S:2 * S]
        acsT = self.bX[0:48, 0:S]
        nacsT = self.bX[0:48, S:2 * S]
        dm = lambda t: t.rearrange("p (d c h) -> p c d h", d=2, h=16)
        self.dma(raw, DTt.rearrange("p c j -> p (c j)"), [("DTt", g) for g in range(NG)], ["dtraw"])
        for d in range(2):
            self.op("dve", lambda e, d=d: e.tensor_tensor(
                out=dtk[:, d * 512:(d + 1) * 512].rearrange("p (c h) -> p c h", h=16),
                in0=raw.rearrange("p (c j) -> p c j", j=32)[:, :, d * 16:(d + 1) * 16],
                in1=self.abrow[:, 1:2, d * 16:(d + 1) * 16].broadcast_to([128, 32, 16]), op=ALU.add),
                ["dtraw", "abrow"], ["dtk"])
        self.op("act", lambda e: e.activation(out=dtk, in_=dtk, func=AF.Exp), ["dtk"], ["dtk"])
        self.op("act", lambda e: e.activation(out=dtk, in_=dtk, func=AF.Ln, bias=1.0), ["dtk"], ["dtk"])
        for d in range(2):
            self.op("dve", lambda e, d=d: e.tensor_tensor(
                out=a_tok[:, d * 512:(d + 1) * 512].rearrange("p (c h) -> p c h", h=16),
                in0=dtk[:, d * 512:(d + 1) * 512].rearrange("p (c h) -> p c h", h=16),
                in1=self.abrow[:, 0:1, d * 16:(d + 1) * 16].broadcast_to([128, 32, 16]), op=ALU.mult),
                ["dtk", "abrow"], ["a_tok"])
        import os
        dcut = int(os.environ.get("DT_CUT", "99"))
        if dcut <= 1:
            return
        k1, p1 = self.psum_group()
        for d in range(2):
            self.op("pe", lambda e, d=d: e.matmul(p1[:, d, :], lhsT=self.tri[:, d, :], rhs=a_tok[:, d * 512:(d + 1) * 512],
                                                   start=True, stop=True), ["a_tok", "tri"], [k1])
        self.op("dve", lambda e: e.tensor_copy(out=acs_tok, in_=self.flat(p1)), [k1], ["acs_tok"])
        k2, p2 = self.psum_group()
        for d in range(2):
            self.op("pe", lambda e, d=d: e.matmul(p2[:, d, :], lhsT=self.ones_f[:], rhs=a_tok[:, d * 512:(d + 1) * 512],
                                                   start=True, stop=True), ["a_tok", "ones_f"], [k2])
        self.op("act", lambda e: e.activation(out=cdk, in_=self.flat(p2), func=AF.Exp), [k2], ["cdk"])
        self.op("dve", lambda e: e.tensor_tensor(out=dtd, in0=self.flat(p2), in1=acs_tok, op=ALU.subtract), [k2, "acs_tok"], ["dtd"])
        self.op("act", lambda e: e.activation(out=dtd, in_=dtd, func=AF.Exp), ["dtd"], ["dtd"])
        self.op("dve", lambda e: e.tensor_tensor(out=dtd, in0=dtd, in1=dtk, op=ALU.mult), ["dtd", "dtk"], ["dtd"])
        if dcut <= 2:
            return
        self.op("pool", lambda e: e.memset(AT, 0.0), [], ["AT"])
        self.dma(AT[0:16, :], DTr[0:16, :], ["AT"] + [("DTr", g) for g in range(NG)], ["AT"])
        self.dma(AT[32:48, :], DTr[16:32, :], ["AT"], ["AT"])
        self.op("act", lambda e: e.activation(out=AT, in_=AT, func=AF.Exp, bias=self.abcol[:, 1:2]), ["AT", "abcol"], ["AT"])
        self.op("act", lambda e: e.activation(out=AT, in_=AT, func=AF.Ln, bias=1.0), ["AT"], ["AT"])
        self.op("dve", lambda e: e.tensor_scalar(out=AT, in0=AT, scalar1=self.abcol[:, 0:1], scalar2=None, op0=ALU.mult),
                ["AT", "abcol"], ["AT"])
        if dcut <= 3:
            return
        c3 = lambda t: t.rearrange("p (c l) -> p c l", l=128)
        srcb, dstb = AT, c1
        names = {id(AT): "AT", id(c1): "c1", id(acsT): "acsT"}
        bufs = [c1, acsT]
        cur, curk = AT, "AT"
        step = 1
        n = 0
        while step < 128:
            dst = bufs[n % 2]
            dk = ["c1", "acsT"][n % 2]
            self.op("dve", lambda e, dst=dst, cur=cur, step=step: e.tensor_copy(out=c3(dst)[:, :, 0:step], in_=c3(cur)[:, :, 0:step]),
                    [curk], [dk])
            self.op("dve", lambda e, dst=dst, cur=cur, step=step: e.tensor_tensor(
                out=c3(dst)[:, :, step:128], in0=c3(cur)[:, :, step:128], in1=c3(cur)[:, :, 0:128 - step], op=ALU.add),
                [curk], [dk])
            cur, curk = dst, dk
            step *= 2
            n += 1
        cs = cur
        csk = curk
        self.op("dve", lambda e: e.tensor_copy(out=acsT[0:16, :], in_=cs[0:16, :]), [csk], ["acsT"])
        self.op("dve", lambda e: e.tensor_tensor(out=c3(acsT)[32:48], in0=c3(cs)[32:48, :, 127:128].broadcast_to([16, 32, 128]),
                                                 in1=c3(cs)[32:48], op=ALU.subtract), [csk, "acsT"], ["acsT"])
        self.op("dve", lambda e: e.tensor_tensor(out=acsT[32:48, :], in0=acsT[32:48, :], in1=AT[32:48, :], op=ALU.add),
                ["acsT", "AT"], ["acsT"])
        self.op("dve", lambda e: e.tensor_scalar(out=nacsT[0:16, :], in0=acsT[0:16, :], scalar1=-1.0, scalar2=None, op0=ALU.mult),
                ["acsT"], ["nacsT"])
        self.op("dve", lambda e: e.tensor_scalar(out=nacsT[32:48, :], in0=acsT[32:48, :], scalar1=-1.0, scalar2=None, op0=ALU.mult),
                ["acsT"], ["nacsT"])

    def stage_odd_ssd(self, aps, XBC, Y, d):
        xf = self.bXN[:].bitcast(F32)
        dtk = xf[:, 0:1024].rearrange("p (d c h) -> p d c h", d=2, h=16)
        dtd = xf[:, 1024:2048].rearrange("p (d c h) -> p d c h", d=2, h=16)
        cdk = xf[:, 2048:3072].rearrange("p (d c h) -> p d c h", d=2, h=16)
        acsT = self.bX[0:48, 0:S]
        nacsT = self.bX[0:48, S:2 * S]
        r0 = 0 if d == 0 else 32
        state = self.bR[:, 0:1024]
        state_bf = self.bR[:, 1024:1536].bitcast(BF16)
        hb = self.bH
        hf = hb[:].bitcast(F32)
        CB0 = [0, 16384]
        tokb_ = [hb[:, o:o + 1536] for o in CB0]
        Xdt_ = [hb[:, o + 1536:o + 2560].rearrange("p (h q) -> p h q", h=16) for o in CB0]
        Xdd_ = [hb[:, o + 2560:o + 3584].rearrange("p (h q) -> p h q", h=16) for o in CB0]
        MT = [hb[:, 3584 + i * 512:3584 + (i + 1) * 512].rearrange("p (h l) -> p h l", h=4) for i in range(2)]
        Cs = [hb[:, 4608 + i * 512:4608 + (i + 1) * 512].rearrange("p (h l) -> p h l", h=4) for i in range(2)]
        Lf = [hf[:, 3072 + i * 512:3072 + (i + 1) * 512] for i in range(2)]
        Ef = [hf[:, 4096 + i * 512:4096 + (i + 1) * 512] for i in range(2)]
        rhs2 = [hf[0:48, 5120 + i * 512:5120 + (i + 1) * 512] for i in range(2)]
        fbs = [hb[:, 12288 + i * 2048:12288 + (i + 1) * 2048].rearrange("p (c t) -> p c t", c=16) for i in range(2)]
        fxs = [self.bSQ[:].bitcast(F32)[:, i * 2048:(i + 1) * 2048].rearrange("p (c t) -> p c t", c=16) for i in range(2)]
        yst = [self.bT[0], self.bT[1], self.bSG[0], self.bSG[1]]
        ysk = [("bT", 0), ("bT", 1), ("bSG", 0), ("bSG", 1)]
        PS = self.ps
        self.op("pool", lambda e: e.memset(self.bR[:, 0:1536], 0.0), [], [("st", g) for g in range(4)] + [("stb", g) for g in range(4)])
        order = list(range(S // 128)) if d == 0 else list(range(S // 128 - 1, -1, -1))
        items = [(ci, c, g) for ci, c in enumerate(order) for g in range(4)]

        def prep(ci, c):
            fi = ci % 2
            sl = slice(c * 128, (c + 1) * 128)
            fx, fb = fxs[fi], fbs[fi]
            self.dma(fx, XBC[:, :, sl].rearrange("c p t -> p c t"), [("XBC", c // (TG // 128))], [("sfx", fi)])
            self.op("pool", lambda e, o=fb, i_=fx: e.tensor_copy(out=o, in_=i_), [("sfx", fi)], [("sfb", fi)])
            tk = tokb_[fi]
            for half in range(2):
                pt = PS[:, 6, :].bitcast(BF16)
                for i in range(6):
                    ii = half * 6 + i
                    self.op("pe", lambda e, o=pt[:, i * 128:(i + 1) * 128], i_=fb[:, ii, :]: e.transpose(out=o, in_=i_, identity=self.ident_b[:]),
                            [("sfb", fi), "ident_b"], [("psb", 6)])
                self.op("act", lambda e, o=tk[:, half * 768:(half + 1) * 768], i_=pt[:, 0:768]: e.copy(out=o, in_=i_),
                        [("psb", 6)], [("tokb", fi, half)])
            xs3 = tk[:, 0:1024].rearrange("p (h q) -> p h q", h=16)
            self.op("dve", lambda e, o=Xdt_[fi], c=c: e.tensor_tensor(out=o, in0=xs3, in1=dtk[:, d, c, :].unsqueeze(2).broadcast_to([128, 16, 64]), op=ALU.mult),
                    [("tokb", fi, 0), ("tokb", fi, 1), "dtk"], [("Xdt", fi)])
            self.op("pool", lambda e, o=Xdd_[fi], c=c: e.tensor_tensor(out=o, in0=xs3, in1=dtd[:, d, c, :].unsqueeze(2).broadcast_to([128, 16, 64]), op=ALU.mult),
                    [("tokb", fi, 0), ("tokb", fi, 1), "dtd"], [("Xdd", fi)])

        def front(idx):
            ci, c, g = items[idx]
            if g == 0:
                prep(ci, c)
            fi = ci % 2
            bi = idx % 2
            sl = slice(c * 128, (c + 1) * 128)
            fb = fbs[fi]
            slot = idx % 4
            cbt = PS[:, 0, slot * 128:(slot + 1) * 128]
            kcb = ("psb", 0, slot)
            self.op("pe", lambda e, o=cbt, l=fb[:, 8 + g, :], r=fb[:, 12 + g, :]: e.matmul(o, lhsT=l, rhs=r, start=True, stop=True),
                    [("sfb", fi)], [kcb])
            r2 = rhs2[bi][r0:r0 + 16, :]
            kr2 = ("rhs2", bi)
            self.op("dve", lambda e, o=r2.rearrange("p (h l) -> p h l", h=4), sl=sl, g=g:
                    e.tensor_tensor(out=o, in0=acsT[r0:r0 + 16, sl].unsqueeze(1).broadcast_to([16, 4, 128]),
                                    in1=self.selm[r0:r0 + 16, g, :, :], op=ALU.mult), ["acsT", "selm"], [kr2])
            pdiff = PS[:, 1 + bi, :]
            kdf = ("psb", 1 + bi)
            self.op("pe", lambda e, o=pdiff, sl=sl, g=g: e.matmul(o, lhsT=nacsT[r0:r0 + 16, sl], rhs=self.selm[r0:r0 + 16, g, :, :].rearrange("p h l -> p (h l)"),
                                                       start=True, stop=False), ["nacsT", "selm"], [kdf])
            self.op("pe", lambda e, o=pdiff, r=r2: e.matmul(o, lhsT=self.ones_f[r0:r0 + 16, :], rhs=r, start=False, stop=False),
                    ["ones_f", kr2], [kdf])
            self.op("pe", lambda e, o=pdiff: e.matmul(o, lhsT=self.ident_b[:], rhs=self.mbias[:, d, :], start=False, stop=True),
                    ["ident_b", "mbias"], [kdf])
            pE = PS[:, 3 + bi, :]
            kpe = ("psb", 3 + bi)
            self.op("pe", lambda e, o=pE, r=r2: e.matmul(o, lhsT=self.ones_f[r0:r0 + 16, :], rhs=r, start=True, stop=True),
                    ["ones_f", kr2], [kpe])
            L = Lf[bi]
            kL = ("Lf", bi)
            self.op("act", lambda e, o=L, i_=pdiff: e.activation(out=o, in_=i_, func=AF.Exp), [kdf], [kL])
            E = Ef[bi]
            kE = ("Ef", bi)
            self.op("act", lambda e, o=E, i_=pE: e.activation(out=o, in_=i_, func=AF.Exp), [kpe], [kE])
            mt = MT[bi]
            self.op("dve", lambda e, o=mt, a=L.rearrange("p (h l) -> p h l", h=4), b_=cbt.unsqueeze(1).broadcast_to([128, 4, 128]):
                    e.tensor_tensor(out=o, in0=a, in1=b_, op=ALU.mult), [kL, kcb], [("MT", bi)])
            self.op("pool", lambda e, o=Cs[bi], a=E.rearrange("p (h l) -> p h l", h=4), b_=fb[:, 12 + g, :].unsqueeze(1).broadcast_to([128, 4, 128]):
                    e.tensor_tensor(out=o, in0=a, in1=b_, op=ALU.mult), [kE, ("sfb", fi)], [("Cs", bi)])

        def back(idx):
            ci, c, g = items[idx]
            fi = ci % 2
            bi = idx % 2
            sl = slice(c * 128, (c + 1) * 128)
            mt, cs_ = MT[bi], Cs[bi]
            Xdt, Xdd, tk = Xdt_[fi], Xdd_[fi], tokb_[fi]
            py = PS[0:64, 5, :]
            ky = ("psb", 5)
            for hh in range(4):
                head = 4 * g + hh
                self.op("pe", lambda e, o=py[:, hh * 128:(hh + 1) * 128], l=Xdt[:, head, :], r=mt[:, hh, :]:
                        e.matmul(o, lhsT=l, rhs=r, start=True, stop=False), [("Xdt", fi), ("MT", bi)], [ky])
                self.op("pe", lambda e, o=py[:, hh * 128:(hh + 1) * 128], l=state_bf[:, head * 64:(head + 1) * 64], r=cs_[:, hh, :]:
                        e.matmul(o, lhsT=l, rhs=r, start=False, stop=True), [("stb", g), ("Cs", bi)], [ky])
            ys = yst[g][0:64, 0:512]
            self.op("act", lambda e, o=ys, i_=py: e.copy(out=o, in_=i_), [ky], [ysk[g]])
            self.dma(Y[g * 256:(g + 1) * 256, sl].rearrange("(h p) l -> p h l", p=64), ys.rearrange("p (h l) -> p h l", h=4),
                     [ysk[g]], [("Y", d, c // (TG // 128))])
            pst = PS[:, 7, bi * 256:(bi + 1) * 256]
            kst = ("psb", 7, bi)
            self.op("pe", lambda e, o=pst, l=tk[:, (8 + g) * 128:(9 + g) * 128], r=Xdd[:, 4 * g:4 * g + 4, :]:
                    e.matmul(o, lhsT=l, rhs=r, start=True, stop=True), [("tokb", fi, 1), ("Xdd", fi)], [kst])
            stg = state[:, g * 256:(g + 1) * 256]
            self.op("dve", lambda e, o=stg.rearrange("p (h q) -> p h q", h=4), c=c, g=g:
                    e.tensor_tensor(out=o, in0=o, in1=cdk[:, d, c, 4 * g:4 * g + 4].unsqueeze(2).broadcast_to([128, 4, 64]), op=ALU.mult),
                    [("st", g), "cdk"], [("st", g)])
            self.op("dve", lambda e, o=stg, i_=pst: e.tensor_tensor(out=o, in0=o, in1=i_, op=ALU.add),
                    [("st", g), kst], [("st", g)])
            self.op("act", lambda e, o=state_bf[:, g * 256:(g + 1) * 256], i_=stg: e.copy(out=o, in_=i_), [("st", g)], [("stb", g)])

        n = len(items)
        for idx in range(n):
            front(idx)
            if idx >= 1:
                back(idx - 1)
        back(n - 1)

    def stage_odd_out(self, X, aps, li, npost, Yf, Yb, XBC, ZS, MO):
        xv, xn_, sq, kX, kXN_, kSQ = self.views()
        rhs = self.bH[:, 0:12 * TG].rearrange("p (c t) -> p c t", c=12)
        rkeys = [("bH", m) for m in range(12)]
        for g in range(NG):
            t0 = g * TG
            self.dma(rhs[:, 0:4, :], MO[:, :, t0:t0 + TG].rearrange("c p t -> p c t"), [("MO", g)], rkeys[0:4])
            for c in range(8):
                a, ka = self.stg4()
                b, kb = self.stg4()
                xs, kx = self.stg4()
                z, kz = self.stg4()
                self.dma(a[:], Yf[c * 128:(c + 1) * 128, t0:t0 + TG], [("Y", 0, g)], [ka])
                self.dma(b[:], Yb[c * 128:(c + 1) * 128, t0:t0 + TG], [("Y", 1, g)], [kb])
                self.dma(xs[:], XBC[c, :, t0:t0 + TG], [("XBC", g)], [kx])
                self.dma(z[:], ZS[c, :, t0:t0 + TG], [("ZS", g)], [kz])
                self.op("pool", lambda e, o=a[:], b_=b[:]: e.tensor_tensor(out=o, in0=o, in1=b_, op=ALU.add), [ka, kb], [ka])
                self.op("dve", lambda e, o=a[:], x_=xs[:], s=self.dexp[:, c:c + 1]:
                        e.scalar_tensor_tensor(out=o, in0=x_, scalar=s, in1=o, op0=ALU.mult, op1=ALU.add), [ka, kx, "dexp"], [ka])
                self.op("dve", lambda e, o=xv[:, c, :], a_=a[:], z_=z[:]: e.tensor_tensor(out=o, in0=a_, in1=z_, op=ALU.mult),
                        [ka, kz], [kX[c]])
                self.op("act", lambda e, o=sq[:, c, :], i_=xv[:, c, :]: e.activation(out=o, in_=i_, func=AF.Square),
                        [kX[c]], [kSQ[c]])
            rk, rv = self.rms_rinv(sq, 8, 0, 1.0 / D, EPS, kSQ)
            for c in range(8):
                self.op("dve", lambda e, o=rhs[:, 4 + c, :], i_=xv[:, c, :], s=self.snw[:, c:c + 1], r=rv:
                        e.scalar_tensor_tensor(out=o, in0=i_, scalar=s, in1=r, op0=ALU.mult, op1=ALU.mult),
                        [kX[c], "snw", rk], [rkeys[4 + c]])
            self.proj_post_residual(X, g, rhs, rkeys, 12, ("od_wout", li), npost, False)

def build_program(stop=None, only=None):
    _, plan = _build(stop, only, None)
    nc, _ = _build(stop, only, plan)
    return nc


def _build(stop, only, wplan):
    from contextlib import ExitStack
    nc = bass.Bass("TRN2", target_bir_lowering=False)
    dt = nc.dram_tensor
    aps = {}

    def inp(name, shape):
        aps[name] = dt(name, shape, F32, kind="ExternalInput").ap()

    inp("x", [S, D])
    inp("normw", [128, DEPTH * 6 * 8])
    import os
    nff = 1 if os.environ.get("DEV_SMALL") else DEPTH * 2
    inp("wg", [nff, D * DFF])
    inp("wu", [nff, D * DFF])
    inp("wd", [nff, DFF * D])
    inp("ev_win", [2, D * 2304])
    inp("ev_wv", [2, D * 256])
    inp("ev_pool_w", [2, 4, 64, 64])
    inp("ev_pool_scale", [2, 128, 2])
    inp("ev_sink", [2, 12])
    inp("ev_wout", [2, D * D])
    inp("ropec", [128, S])
    inp("ropes", [128, S])
    inp("icnt", [2, 128, S])
    inp("amask", [128, 2, 384])
    inp("od_win", [2, D * 4096])
    inp("od_wdt", [2, D, 32])
    inp("od_wout", [2, 1536 * D])
    inp("cv_dww", [2, 128, 4, 31])
    inp("cv_vec", [2, 128, 3, 4])
    inp("ssm_cw", [2, 128, 16, 5])
    inp("ssm_dexp", [2, 128, 8])
    inp("ssm_nw", [2, 128, 8])
    inp("ssm_ab", [2, 48, 2])
    inp("ssm_A_log", [2, 32])
    inp("ssm_dt_bias", [2, 32])
    inp("tri", [128, 2, 128])
    inp("selm", [48, 4, 4, 128])
    inp("mbias", [128, 2, 512])
    inp("rmask", [48, 128])
    out = dt("out", [S, D], F32, kind="ExternalOutput").ap()
    X = dt("Xs", [8, 128, S], F32, kind="Internal").ap()
    QT = dt("QTs", [768, S], BF16, kind="Internal").ap()
    KT = dt("KTs", [256, S], BF16, kind="Internal").ap()
    V = dt("Vs", [S, 256], BF16, kind="Internal").ap()
    U = dt("Us", [2, 128, S], F32, kind="Internal").ap()
    AO = dt("AOs", [8, 128, S], BF16, kind="Internal").ap()
    Gs = dt("Gs", [4, 128, S], F32, kind="Internal").ap()
    ZS = dt("ZSs", [8, 128, S], F32, kind="Internal").ap()
    XBCr = dt("XBCrs", [16, 128, S], F32, kind="Internal").ap()
    XBC = dt("XBCs", [16, 128, S], F32, kind="Internal").ap()
    DTr = dt("DTrs", [32, S], F32, kind="Internal").ap()
    DTt = dt("DTts", [128, 32, 32], F32, kind="Internal").ap()
    Yf = dt("Yfs", [1024, S], F32, kind="Internal").ap()
    Yb = dt("Ybs", [1024, S], F32, kind="Internal").ap()
    MO = dt("MOs", [4, 128, S], BF16, kind="Internal").ap()
    B = Builder(nc, stop, aps, wplan)
    B.init_consts(aps)
    B.stage_load(aps["x"], X)
    nst = 0
    seq = [(l, sub) for l in range(DEPTH) for sub in range(3)]
    if stop is not None:
        seq = seq[:stop]
    if only is not None:
        seq = only
    for (l, sub) in seq:
        base = l * 6 * 8
        if True:
            if sub == 0:
                B.sc.barrier(); B.stage_ffn(X, ("wg", l * 2), ("wu", l * 2), ("wd", l * 2), base + 0, base + 8)
            elif sub == 2:
                B.sc.barrier(); B.stage_ffn(X, ("wg", l * 2 + 1), ("wu", l * 2 + 1), ("wd", l * 2 + 1), base + 32, base + 40)
            elif l % 2 == 0:
                li = l // 2
                B.even_consts(aps, li)
                B.sc.barrier(); B.stage_even_in(X, aps, li, base + 16, QT, KT, V, U)
                B.sc.barrier(); B.stage_even_pool(aps, U, AO)
                B.sc.barrier(); B.stage_even_attn(aps, li, QT, KT, V, AO)
                B.sc.barrier(); B.stage_even_out(X, aps, li, base + 24, AO)
            else:
                li = l // 2
                import os
                ocut = int(os.environ.get("ODD_CUT", "99"))
                B.odd_consts(aps, li)
                oskip = int(os.environ.get("ODD_SKIP", "0"))
                if ocut >= 1 and oskip < 1:
                    B.sc.barrier(); B.stage_odd_in(X, aps, li, base + 16, Gs, ZS, XBCr, DTr, DTt)
                if ocut >= 2 and oskip < 2:
                    B.sc.barrier(); B.stage_odd_conv(aps, li, Gs, XBCr, XBC, MO)
                if ocut >= 3:
                    B.sc.barrier(); B.stage_odd_dt(aps, li, DTr, DTt)
                if ocut >= 4:
                    B.sc.barrier(); B.stage_odd_ssd(aps, XBC, Yf, 0)
                if ocut >= 5:
                    B.sc.barrier(); B.stage_odd_ssd(aps, XBC, Yb, 1)
                if ocut >= 6:
                    B.sc.barrier(); B.stage_odd_out(X, aps, li, base + 24, Yf, Yb, XBC, ZS, MO)
    B.sc.barrier(); B.stage_store(X, out)
    fin = [("out", g) for g in range(NG)]
    if os.environ.get("DBG"):
        def dbg(name, ap, keys):
            o = dt("dbg_" + name, list(ap.shape), ap.dtype, kind="ExternalOutput").ap()
            B.dma(o, ap, keys, [("dbg", name)])
            fin.append(("dbg", name))
        G4 = range(NG)
        dbg("G", Gs, [("G", g) for g in G4])
        dbg("XBC", XBC, [("XBC", g) for g in G4])
        dbg("MO", MO, [("MO", g) for g in G4])
        dbg("ZS", ZS, [("ZS", g) for g in G4])
        dbg("Yf", Yf, [("Y", 0, g) for g in G4])
        dbg("Yb", Yb, [("Y", 1, g) for g in G4])
        dbg("DTr", DTr, [("DTr", g) for g in G4])
        dbg("tok", B.bXN[:].bitcast(F32)[:, 0:3072], ["dtk", "dtd", "cdk"])
        dbg("acsT", B.bX[0:48, 0:S], ["acsT"])
    B.op("sp", lambda e: e.nop(), fin, [])
    if wplan is None:
        return None, B.wrec
    with ExitStack() as st:
        B.sc.emit(st)
    return nc, B.wrec


def rope_tables_np():
    inv = (1.0 / (np.float32(10000.0) ** (np.arange(0, 64, 2, dtype=np.float32) / np.float32(64)))).astype(np.float32)
    ang = (np.arange(S, dtype=np.float32)[:, None] * inv[None, :]).astype(np.float32)
    cos = np.cos(ang).astype(np.float32)
    sin = np.sin(ang).astype(np.float32)
    cf = np.concatenate([cos, cos], axis=1)
    sf = np.concatenate([-sin, sin], axis=1)
    cT = np.ascontiguousarray(np.concatenate([cf, cf], axis=1).T)
    sT = np.ascontiguousarray(np.concatenate([sf, sf], axis=1).T)
    return cT, sT


def const_tables():
    cT, sT = rope_tables_np()
    t = np.arange(S)
    icnt = np.zeros((2, 128, S), np.float32)
    for gi, w in enumerate((2, 4, 8, 16)):
        lo = np.clip(t - w // 2, 0, S)
        hi = np.clip(t + w - w // 2, 0, S)
        icnt[gi // 2, (gi % 2) * 64:(gi % 2) * 64 + 64, :] = (1.0 / (hi - lo).astype(np.float32))[None, :]
    kl = np.arange(128)[:, None]
    ql = np.arange(128)[None, :]
    m1 = (ql <= kl).astype(np.float32)
    m2 = (kl <= ql).astype(np.float32)
    amask = np.stack([np.tile(m1, (1, 3)), np.tile(m2, (1, 3))], axis=1)
    s_ = np.arange(128)[:, None]
    l_ = np.arange(128)[None, :]
    tri = np.stack([(s_ <= l_), (s_ >= l_)], axis=1).astype(np.float32)
    selm = np.zeros((48, 4, 4, 128), np.float32)
    for k in range(16):
        selm[k, k // 4, k % 4, :] = 1.0
        selm[32 + k, k // 4, k % 4, :] = 1.0
    NEG = -30000.0
    mb = np.stack([np.tile(np.where(l_ >= s_, 0.0, NEG), (1, 4)), np.tile(np.where(s_ >= l_, 0.0, NEG), (1, 4))], axis=1)
    rmask = np.ones((48, 128), np.float32)
    rmask[:, 0] = 0.0
    return {"ropec": cT, "ropes": sT, "icnt": icnt, "amask": np.ascontiguousarray(amask),
            "tri": np.ascontiguousarray(tri), "selm": selm, "mbias": np.ascontiguousarray(mb.astype(np.float32)),
            "rmask": rmask}


def make_in_maps(inputs):
    nw = np.ascontiguousarray(
        inputs["norm_w"].reshape(DEPTH, 6, 8, 128).transpose(3, 0, 1, 2).reshape(128, DEPTH * 6 * 8))
    qk = np.arange(256, 1280)
    rel = (qk - 256) % 64
    partner = qk - rel + np.where(rel < 32, rel + 32, rel - 32)
    cols = []
    for i in range(8):
        cols.append(qk[i * 128:(i + 1) * 128])
        cols.append(partner[i * 128:(i + 1) * 128])
    cols.append(np.arange(0, 256))
    cols = np.concatenate(cols)
    shared = {
        "normw": nw,
        "wg": tileize(inputs["ffn_w_gate"].reshape(DEPTH * 2, D, DFF)),
        "wu": tileize(inputs["ffn_w_up"].reshape(DEPTH * 2, D, DFF)),
        "wd": tileize(inputs["ffn_w_down"].reshape(DEPTH * 2, DFF, D)),
        "ev_win": tileize(inputs["ev_w_in"][:, :, cols]),
        "ev_wv": tileize(inputs["ev_w_in"][:, :, 1280:1536]),
        "ev_pool_w": np.ascontiguousarray(inputs["ev_pool_w"]),
        "ev_pool_scale": np.ascontiguousarray(inputs["ev_pool_scale"].reshape(2, 2, 128).transpose(0, 2, 1)),
        "ev_sink": np.ascontiguousarray(inputs["ev_sink"]),
        "ev_wout": tileize(inputs["ev_w_out"]),
    }
    ocols = []
    for m in range(4):
        ocols.append(np.arange(m * 128, (m + 1) * 128))
        ocols.append(np.arange(512 + m * 128, 512 + (m + 1) * 128))
    ocols.append(np.arange(1024, 4128))
    ocols = np.concatenate(ocols)
    f32 = np.float32
    ab = np.zeros((2, 48, 2), f32)
    ab[:, 0:16, 0] = inputs["ssm_A_log"][:, 0]
    ab[:, 32:48, 0] = inputs["ssm_A_log"][:, 1]
    ab[:, 0:16, 1] = inputs["ssm_dt_bias"][:, 0]
    ab[:, 32:48, 1] = inputs["ssm_dt_bias"][:, 1]
    cw = np.concatenate([inputs["ssm_conv_w"].reshape(2, 4, 16, 128).transpose(0, 3, 2, 1),
                         inputs["ssm_conv_b"].reshape(2, 16, 128).transpose(0, 2, 1)[..., None]], axis=-1)
    shared.update({
        "od_win": tileize(inputs["od_w_in"][:, :, ocols[:4096]]),
        "od_wdt": np.ascontiguousarray(inputs["od_w_in"][:, :, 4096:4128]),
        "od_wout": tileize(inputs["od_w_out"]),
        "cv_dww": np.ascontiguousarray(inputs["cv_dw_w"].reshape(2, 31, 4, 128).transpose(0, 3, 2, 1)),
        "cv_vec": np.ascontiguousarray(np.stack([inputs["cv_dw_b"], inputs["cv_ln_g"], inputs["cv_ln_b"]], axis=1)
                                       .reshape(2, 3, 4, 128).transpose(0, 3, 1, 2)),
        "ssm_cw": np.ascontiguousarray(cw.astype(f32)),
        "ssm_dexp": np.ascontiguousarray(np.repeat(inputs["ssm_D"], 64, axis=1).reshape(2, 8, 128).transpose(0, 2, 1)),
        "ssm_nw": np.ascontiguousarray(inputs["ssm_norm_w"].reshape(2, 8, 128).transpose(0, 2, 1)),
        "ssm_ab": ab,
        "ssm_A_log": np.ascontiguousarray(inputs["ssm_A_log"].reshape(2, 32)),
        "ssm_dt_bias": np.ascontiguousarray(inputs["ssm_dt_bias"].reshape(2, 32)),
    })
    shared.update(const_tables())
    maps = []
    for b in range(NCORES):
        m = dict(shared)
        m["x"] = np.ascontiguousarray(inputs["x"][b])
        maps.append(m)
    return maps


def kernel(**inputs):
    inputs = {k: np.asarray(v) for k, v in inputs.items()}
    nc = build_program()
    in_maps = make_in_maps(inputs)
    res = run_bass_kernel_spmd(nc, in_maps, core_ids=list(range(NCORES)))
    return np.stack([np.asarray(r["out"]) for r in res.results], axis=0).astype(np.float32)
```

```python
import numpy as np
import concourse.bass as bass
import concourse.mybir as mybir
from concourse.bass_utils import run_bass_kernel_spmd

F32 = mybir.dt.float32
BF16 = mybir.dt.bfloat16
ALU = mybir.AluOpType
AF = mybir.ActivationFunctionType

D = 1024
S = 4096
DEPTH = 4
DFF = 2816
NCORES = 8
EPS = 1e-6
TG = 1024
NG = S // TG
NB = TG // 512


W_TILED = {"wg": (D, DFF), "wu": (D, DFF), "wd": (DFF, D), "ev_win": (D, 2304), "ev_wv": (D, 256),
           "ev_wout": (D, D), "od_win": (D, 4096), "od_wout": (1536, D)}


def tile_offset(K, M, kt, ct):
    nct = M // 256
    off = 0
    for k_ in range(kt):
        off += nct * 128 * min(8, (K - k_ * 1024) // 128) * 256
    return off + ct * 128 * min(8, (K - kt * 1024) // 128) * 256


def tileize(W):
    lead = W.shape[:-2]
    K, M = W.shape[-2:]
    W2 = W.reshape((-1, K, M))
    outs = []
    for kt in range((K + 1023) // 1024):
        nk = min(8, (K - kt * 1024) // 128)
        blk = W2[:, kt * 1024:kt * 1024 + nk * 128, :].reshape(-1, nk, 128, M // 256, 256)
        outs.append(np.ascontiguousarray(blk.transpose(0, 3, 2, 1, 4)).reshape(W2.shape[0], -1))
    return np.ascontiguousarray(np.concatenate(outs, axis=1)).reshape(lead + (K * M,))


class Op:
    __slots__ = ("eng", "fn", "dma", "deps", "idx", "sem", "semval", "sig", "presem")


class Sched:
    ENG = ("pe", "act", "dve", "pool", "sp")

    def __init__(self, nc):
        self.nc = nc
        self.ops = []
        self.last_w = {}
        self.readers = {}
        self.dma_since = []
        self.last_on = {}

    def barrier(self):
        frontier = set(self.dma_since)
        for e, idx in self.last_on.items():
            frontier.add(idx)
        self.dma_since = []
        for e in self.ENG:
            self.add(e, lambda eng: eng.nop(), extra=frontier)

    def add(self, eng, fn, reads=(), writes=(), dma=False, extra=()):
        op = Op()
        op.eng = eng
        op.fn = fn
        op.dma = dma
        op.idx = len(self.ops)
        op.sig = False
        op.sem = None
        op.semval = 0
        op.presem = None
        deps = {}
        for k in reads:
            w = self.last_w.get(k)
            if w is not None:
                deps[w] = True
            if isinstance(k, tuple) and k[0] == "ps":
                for r in self.readers.get(k, ()):
                    if self.ops[r].eng != eng:
                        deps[r] = True
        for k in writes:
            w = self.last_w.get(k)
            if w is not None:
                deps.setdefault(w, False)
            for r in self.readers.get(k, ()):
                deps.setdefault(r, False)
        for k in reads:
            self.readers.setdefault(k, []).append(op.idx)
        for k in writes:
            self.last_w[k] = op.idx
            self.readers[k] = []
        for x in extra:
            deps[x] = True
        deps.pop(op.idx, None)
        op.deps = deps
        self.ops.append(op)
        if dma:
            self.dma_since.append(op.idx)
        else:
            self.last_on[eng] = op.idx
        return op

    def _needs_sync(self, c, p, raw):
        if p.dma:
            return True
        if c.eng != p.eng:
            return True
        if c.dma:
            return True
        if c.eng == "pe":
            return False
        return raw

    def emit(self, stack):
        nc = self.nc
        ops = self.ops
        for c in ops:
            for pi, raw in c.deps.items():
                p = ops[pi]
                if not p.dma and self._needs_sync(c, p, raw):
                    p.sig = True
        NDMA = 40
        dma_sems = [stack.enter_context(nc.semaphore("dq%d" % i)) for i in range(NDMA)]
        dma_cum = [0] * NDMA
        dma_rr = 0
        EPOCH = 20000
        eng_sems = {e: [] for e in self.ENG}
        eng_cnt = {e: 0 for e in self.ENG}
        for op in ops:
            if op.dma:
                i = dma_rr % NDMA
                dma_rr += 1
                op.sem = dma_sems[i]
                op.presem = (dma_sems[i], dma_cum[i]) if dma_cum[i] > 0 else None
                dma_cum[i] += 16
                op.semval = dma_cum[i]
            elif op.sig:
                e = op.eng
                if eng_cnt[e] % EPOCH == 0:
                    eng_sems[e].append(stack.enter_context(
                        nc.semaphore("e_%s_%d" % (e, len(eng_sems[e])))))
                eng_cnt[e] += 1
                op.sem = eng_sems[e][-1]
                op.semval = (eng_cnt[e] - 1) % EPOCH + 1
        per = {e: [] for e in self.ENG}
        for op in ops:
            per[op.eng].append(op)

        def run(engname, eng):
            waited = {}
            for op in per[engname]:
                ws = []
                for pi, raw in op.deps.items():
                    p = ops[pi]
                    if self._needs_sync(op, p, raw):
                        ws.append((p.sem, p.semval))
                if op.presem is not None:
                    ws.append(op.presem)
                for sem, val in ws:
                    key = id(sem)
                    if waited.get(key, 0) >= val:
                        continue
                    waited[key] = val
                    eng.wait_ge(sem, val)
                ins = op.fn(eng)
                if op.dma:
                    ins.then_inc(op.sem, 16)
                elif op.sig:
                    ins.then_inc(op.sem, 1)

        with nc.Block() as block:
            @block.tensor
            def _(e):
                run("pe", e)

            @block.scalar
            def _(e):
                run("act", e)

            @block.vector
            def _(e):
                run("dve", e)

            @block.gpsimd
            def _(e):
                run("pool", e)

            @block.sync
            def _(e):
                run("sp", e)


class Builder:
    def __init__(self, nc, stop=None, aps=None, wplan=None):
        self.nc = nc
        self.aps = aps
        self.wplan = wplan
        self.wrec = []
        self.w_issued = 0
        self.PF = 2
        self.sc = Sched(nc)
        self.stop = stop
        self.uid = 0
        nc_ = nc
        A = nc_.alloc_sbuf_tensor
        self.bX = A("bX", [128, 8 * TG], F32)
        self.bXN = A("bXN", [128, 8 * TG], BF16)
        self.bSQ = A("bSQ", [128, 8 * TG], BF16)
        self.bH = A("bH", [128, 22 * TG], BF16)
        self.NWS = 4
        self.wsF = [A("wsF%d" % i, [128, 2048], F32) for i in range(self.NWS)]
        self.wsB = [A("wsB%d" % i, [128, 2048], BF16) for i in range(self.NWS)]
        self.ws_i = 0
        self.bR = A("bR", [128, 2 * TG], F32)
        self.bSG = [A("bSG%d" % i, [128, TG], F32) for i in range(2)]
        self.sg_i = 0
        self.bT = [A("bT%d" % i, [128, TG], F32) for i in range(4)]
        self.t_i = 0
        self.ones_bf = A("ones_bf", [128, 128], BF16)
        self.ident_f = A("ident_f", [128, 128], F32)
        self.normw = A("normw_sb", [128, DEPTH * 6 * 8], F32)
        self.poolW = A("poolW", [128, 2, 128], BF16)
        self.poolS = A("poolS", [128, 2], F32)
        self.esink = A("esink", [128, 12], F32)
        self.amask = A("amask_sb", [128, 2, 384], F32)
        self.stg_i = 0
        self.ones_f = A("ones_f", [128, 128], F32)
        self.ident_b = A("ident_b", [128, 128], BF16)
        self.dww = A("dww", [128, 4, 31], F32)
        self.cvvec = A("cvvec", [128, 3, 4], F32)
        self.scw = A("scw", [128, 16, 5], F32)
        self.dexp = A("dexp", [128, 8], F32)
        self.snw = A("snw", [128, 8], F32)
        self.abcol = A("abcol", [48, 2], F32)
        self.abrow = A("abrow", [128, 2, 32], F32)
        self.tri = A("tri_sb", [128, 2, 128], F32)
        self.selm = A("selm_sb", [48, 4, 4, 128], F32)
        self.mbias = A("mbias_sb", [128, 2, 512], BF16)
        self.ps = nc_.alloc_psum_tensor("ps", [128, 8, 512], F32)
        self.ps_i = 0
        self.NPG = 8 // NB

    def op(self, eng, fn, reads=(), writes=(), dma=False):
        return self.sc.add(eng, fn, reads, writes, dma)

    def psum_group(self):
        g = self.ps_i % self.NPG
        self.ps_i += 1
        key = ("ps", g)
        view = self.ps[:, g * NB:(g + 1) * NB, :]
        return key, view

    def dma(self, out, in_, reads, writes, eng="sp"):
        return self.op(eng, lambda e, o=out, i=in_: e.dma_start(out=o, in_=i), reads, writes, dma=True)

    def _issue_w(self, j, spec):
        name, idx, r0, nk, c0, ncols = spec
        W = self.aps[name][idx]
        i = j % self.NWS
        fv = self.wsF[i][:, 0:nk * ncols].rearrange("p (k m) -> p k m", k=nk)
        bv = self.wsB[i][:, 0:nk * ncols].rearrange("p (k m) -> p k m", k=nk)
        if name in W_TILED:
            K, M = W_TILED[name]
            off = tile_offset(K, M, r0 // 1024, c0 // 256)
            assert ncols == 256 and nk == min(8, (K - r0) // 128)
            src = W[off:off + 128 * nk * 256].rearrange("(p k m) -> p k m", p=128, k=nk)
        else:
            src = W[r0:r0 + nk * 128, c0:c0 + ncols].rearrange("(k p) m -> p k m", p=128)
        self.dma(fv, src, reads=[], writes=[("wsF", i)])
        if j % 2 == 0:
            self.op("act", lambda e, o=bv, s=fv: e.copy(out=o, in_=s),
                    reads=[("wsF", i)], writes=[("wsB", i)])
        else:
            self.op("dve", lambda e, o=bv, s=fv: e.tensor_copy(out=o, in_=s),
                    reads=[("wsF", i)], writes=[("wsB", i)])

    def load_w(self, Wd, r0, nk, c0, ncols):
        j = self.ws_i
        self.ws_i += 1
        spec = (Wd[0], Wd[1], r0, nk, c0, ncols)
        if self.wplan is None:
            self.wrec.append(spec)
            self._issue_w(j, spec)
        else:
            assert self.wplan[j] == spec, (j, self.wplan[j], spec)
            upto = min(j + self.PF, len(self.wplan) - 1)
            while self.w_issued <= upto:
                self._issue_w(self.w_issued, self.wplan[self.w_issued])
                self.w_issued += 1
        i = j % self.NWS
        bv = self.wsB[i][:, 0:nk * ncols].rearrange("p (k m) -> p k m", k=nk)
        return ("wsB", i), bv

    def init_consts(self, aps):
        nc = self.nc
        self.op("pool", lambda e: e.memset(self.ones_bf[:], 1.0), [], ["ones_bf"])
        self.op("pool", lambda e: e.memset(self.ident_f[:], 0.0), [], ["ident_f"])
        self.op("pool", lambda e: e.affine_select(
            out=self.ident_f[:], in_=self.ident_f[:], pattern=[[-1, 128]],
            compare_op=ALU.not_equal, fill=1.0, base=0, channel_multiplier=1),
            ["ident_f"], ["ident_f"])
        self.dma(self.normw[:], aps["normw"], [], ["normw"])
        self.dma(self.amask[:], aps["amask"], [], ["amask"])
        self.op("pool", lambda e: e.memset(self.ones_f[:], 1.0), [], ["ones_f"])
        self.op("pool", lambda e: e.tensor_copy(out=self.ident_b[:], in_=self.ident_f[:]), ["ident_f"], ["ident_b"])
        self.dma(self.tri[:], aps["tri"], [], ["tri"])
        self.dma(self.selm[:], aps["selm"], [], ["selm"])
        st_ = self.bX[:, 0:1024].rearrange("p (d t) -> p d t", d=2)
        self.dma(st_, aps["mbias"], [], [("bX", 0)])
        self.op("pool", lambda e: e.tensor_copy(out=self.mbias[:], in_=st_), [("bX", 0)], ["mbias"])

    def flat(self, pv):
        return pv.rearrange("p b t -> p (b t)")

    def stage_load(self, x, X):
        hf = self.bH[:].bitcast(F32)
        xv = self.bX[:].rearrange("p (c t) -> p c t", c=8)
        for g in range(NG):
            tv = hf[:, 0:8 * D].rearrange("p (tb f) -> p tb f", tb=8)
            self.dma(tv, x[g * TG:(g + 1) * TG, :].rearrange("(tb p) f -> p tb f", p=128),
                     [], ["bH"])
            n = 0
            for tb in range(8):
                for c0 in range(0, 8, 4):
                    key, pv = self.psum_group()
                    pf = self.flat(pv)
                    for cc in range(4):
                        c = c0 + cc
                        self.op("pe", lambda e, o=pf[:, cc * 128:(cc + 1) * 128], i=tv[:, tb, c * 128:(c + 1) * 128]:
                                e.transpose(out=o, in_=i, identity=self.ident_f[:]),
                                ["bH", "ident_f"], [key])
                    src = pf[:, 0:512].rearrange("p (c t) -> p c t", c=4)
                    dst = xv[:, c0:c0 + 4, tb * 128:(tb + 1) * 128]
                    if n % 2 == 0:
                        self.op("act", lambda e, o=dst, i=src: e.copy(out=o, in_=i), [key],
                                [("bX", c0 + k) for k in range(4)])
                    else:
                        self.op("dve", lambda e, o=dst, i=src: e.tensor_copy(out=o, in_=i), [key],
                                [("bX", c0 + k) for k in range(4)])
                    n += 1
            self.dma(X[:, :, g * TG:(g + 1) * TG].rearrange("c p t -> p c t"), xv,
                     [("bX", c) for c in range(8)], [("X", g, c) for c in range(8)])

    def stage_store(self, X, out):
        hf = self.bH[:].bitcast(F32)
        xv = self.bX[:].rearrange("p (c t) -> p c t", c=8)
        for g in range(NG):
            tv = hf[:, 0:8 * D].rearrange("p (tb f) -> p tb f", tb=8)
            self.dma(xv, X[:, :, g * TG:(g + 1) * TG].rearrange("c p t -> p c t"),
                     [("X", g, c) for c in range(8)], [("bX", c) for c in range(8)])
            n = 0
            for tb in range(8):
                for c0 in range(0, 8, 4):
                    key, pv = self.psum_group()
                    pf = self.flat(pv)
                    for cc in range(4):
                        c = c0 + cc
                        self.op("pe", lambda e, o=pf[:, cc * 128:(cc + 1) * 128], i=xv[:, c, tb * 128:(tb + 1) * 128]:
                                e.transpose(out=o, in_=i, identity=self.ident_f[:]),
                                [("bX", c), "ident_f"], [key])
                    dst = tv[:, tb, c0 * 128:(c0 + 4) * 128]
                    src = pf[:, 0:512]
                    if n % 2 == 0:
                        self.op("act", lambda e, o=dst, i=src: e.copy(out=o, in_=i), [key], ["bH"])
                    else:
                        self.op("dve", lambda e, o=dst, i=src: e.tensor_copy(out=o, in_=i), [key], ["bH"])
                    n += 1
            self.dma(out[g * TG:(g + 1) * TG, :].rearrange("(tb p) f -> p tb f", p=128), tv,
                     ["bH"], [("out", g)])

    def rms_rinv(self, sqv, nchunks, off, scale, bias, sqkeys):
        key, pv = self.psum_group()
        for b in range(NB):
            for c in range(nchunks):
                self.op("pe", lambda e, o=pv[:, b, :], r=sqv[:, c, b * 512:(b + 1) * 512], st=(c == 0), sp=(c == nchunks - 1):
                        e.matmul(o, lhsT=self.ones_bf[:], rhs=r, start=st, stop=sp),
                        ["ones_bf", sqkeys[c]], [key])
        rv = self.bR[:, off:off + TG]
        rk = ("bR", off)
        self.op("act", lambda e, o=rv, i=self.flat(pv): e.activation(out=o, in_=i, func=AF.Sqrt, scale=scale, bias=bias),
                [key], [rk])
        self.op("dve", lambda e, o=rv: e.reciprocal(out=o, in_=o), [rk], [rk])
        return rk, rv

    def views(self):
        xv = self.bX[:].rearrange("p (c t) -> p c t", c=8)
        xn = self.bXN[:].rearrange("p (c t) -> p c t", c=8)
        sq = self.bSQ[:].rearrange("p (c t) -> p c t", c=8)
        kX = [("bX", c) for c in range(8)]
        kXN = [("bXN", c) for c in range(8)]
        kSQ = [("bSQ", c) for c in range(8)]
        return xv, xn, sq, kX, kXN, kSQ

    def load_norm(self, X, g, npre):
        xv, xn, sq, kX, kXN, kSQ = self.views()
        t0 = g * TG
        self.dma(xv, X[:, :, t0:t0 + TG].rearrange("c p t -> p c t"),
                 [("X", g, c) for c in range(8)], kX)
        for c in range(8):
            self.op("act", lambda e, o=sq[:, c, :], i=xv[:, c, :]: e.activation(out=o, in_=i, func=AF.Square),
                    [kX[c]], [kSQ[c]])
        rk, rv = self.rms_rinv(sq, 8, 0, 1.0 / D, EPS, kSQ)
        for c in range(8):
            self.op("dve", lambda e, o=xn[:, c, :], i=xv[:, c, :], s=self.normw[:, npre + c:npre + c + 1], r=rv:
                    e.scalar_tensor_tensor(out=o, in0=i, scalar=s, in1=r, op0=ALU.mult, op1=ALU.mult),
                    [kX[c], "normw", rk], [kXN[c]])
        return xn, kXN

    def proj_post_residual(self, X, g, rhs, rkeys, KC, W, npost, half):
        xv, xn, sq, kX, kXN, kSQ = self.views()
        nkt = (KC + 7) // 8
        for ct in range(4):
            grp = [self.psum_group(), self.psum_group()]
            for kt in range(nkt):
                nk = min(8, KC - kt * 8)
                kw, w = self.load_w(W, kt * 1024, nk, ct * 256, 256)
                for mm in range(2):
                    kp, pv = grp[mm]
                    for kk in range(nk):
                        kc = kt * 8 + kk
                        for b in range(NB):
                            self.op("pe", lambda e, o=pv[:, b, :], w_=w[:, kk, mm * 128:(mm + 1) * 128], r=rhs[:, kc, b * 512:(b + 1) * 512], st=(kc == 0), sp=(kc == KC - 1):
                                    e.matmul(o, lhsT=w_, rhs=r, start=st, stop=sp), [kw, rkeys[kc]], [kp])
            for mm in range(2):
                oc = ct * 2 + mm
                kp, pv = grp[mm]
                self.op("dve", lambda e, o=xv[:, oc, :], i=self.flat(pv): e.tensor_copy(out=o, in_=i),
                        [kp], [kX[oc]])
                self.op("act", lambda e, o=sq[:, oc, :], i=xv[:, oc, :]: e.activation(out=o, in_=i, func=AF.Square),
                        [kX[oc]], [kSQ[oc]])
        if half:
            rk2, rv2 = self.rms_rinv(sq, 8, TG, 4.0 / D, 4.0 * EPS, kSQ)
        else:
            rk2, rv2 = self.rms_rinv(sq, 8, TG, 1.0 / D, EPS, kSQ)
        self.residual_add(X, g, xv, kX, rk2, rv2, npost)

    def stage_ffn(self, X, Wg, Wu, Wd, npre, npost):
        h = self.bH[:].rearrange("p (c t) -> p c t", c=22)
        xv, xn, sq, kX, kXN, kSQ = self.views()
        sqs = [self.bSG[i // 2][:].bitcast(BF16)[:, (i % 2) * TG:(i % 2 + 1) * TG] for i in range(4)]
        ksqs = [("sqs", i) for i in range(4)]
        self.load_norm(X, 0, npre)
        pending = None
        for g in range(NG):
            for ct in range(11):
                if pending is not None and 1 <= ct <= 8:
                    self.residual_add(X, pending[0], xv, kX, pending[1], pending[2], npost, ocs=[ct - 1])
                    if ct == 8:
                        pending = None
                kg, wg = self.load_w(Wg, 0, 8, ct * 256, 256)
                ku, wu = self.load_w(Wu, 0, 8, ct * 256, 256)
                for mm in range(2):
                    m = ct * 2 + mm
                    kpg, pg = self.psum_group()
                    for k in range(8):
                        for b in range(NB):
                            self.op("pe", lambda e, o=pg[:, b, :], w=wg[:, k, mm * 128:(mm + 1) * 128], r=xn[:, k, b * 512:(b + 1) * 512], st=(k == 0), sp=(k == 7):
                                    e.matmul(o, lhsT=w, rhs=r, start=st, stop=sp), [kg, kXN[k]], [kpg])
                    kpu, pu = self.psum_group()
                    for k in range(8):
                        for b in range(NB):
                            self.op("pe", lambda e, o=pu[:, b, :], w=wu[:, k, mm * 128:(mm + 1) * 128], r=xn[:, k, b * 512:(b + 1) * 512], st=(k == 0), sp=(k == 7):
                                    e.matmul(o, lhsT=w, rhs=r, start=st, stop=sp), [ku, kXN[k]], [kpu])
                    si = self.sg_i % 2
                    self.sg_i += 1
                    sg = self.bSG[si]
                    self.op("act", lambda e, o=sg[:], i=self.flat(pg): e.activation(out=o, in_=i, func=AF.Silu),
                            [kpg], [("bSG", si), ("sqs", 2 * si), ("sqs", 2 * si + 1)])
                    self.op("dve", lambda e, o=h[:, m, :], a=sg[:], b_=self.flat(pu): e.tensor_tensor(out=o, in0=a, in1=b_, op=ALU.mult),
                            [("bSG", si), ("sqs", 2 * si), ("sqs", 2 * si + 1), kpu], [("bH", m)])
            nxt = g + 1 if g + 1 < NG else None
            t1 = (g + 1) * TG
            stk = None

            def sq_chunks(c0):
                for c in range(c0, c0 + 4):
                    ti = c % 4
                    self.dma(self.bT[ti][:], X[c, :, t1:t1 + TG], [("X", nxt, c)], [("bT", ti)])
                    self.op("act", lambda e, o=sqs[c % 4], i_=self.bT[ti][:]: e.activation(out=o, in_=i_, func=AF.Square),
                            [("bT", ti)], [ksqs[c % 4], ("bSG", (c % 4) // 2)])

            def stat_mms(c0, key, pv):
                for c in range(c0, c0 + 4):
                    for b in range(NB):
                        self.op("pe", lambda e, o=pv[:, b, :], r=sqs[c % 4][:, b * 512:(b + 1) * 512], st=(c == 0), sp=(c == 7):
                                e.matmul(o, lhsT=self.ones_bf[:], rhs=r, start=st, stop=sp),
                                ["ones_bf", ksqs[c % 4], ("bSG", (c % 4) // 2)], [key])

            for ct in range(4):
                if nxt is not None and ct == 0:
                    sq_chunks(0)
                grp = [self.psum_group(), self.psum_group()]
                for kt in range(3):
                    nk = 8 if kt < 2 else 6
                    kw, w = self.load_w(Wd, kt * 1024, nk, ct * 256, 256)
                    for mm in range(2):
                        kp, pv = grp[mm]
                        for kk in range(nk):
                            kc = kt * 8 + kk
                            for b in range(NB):
                                self.op("pe", lambda e, o=pv[:, b, :], w_=w[:, kk, mm * 128:(mm + 1) * 128], r=h[:, kc, b * 512:(b + 1) * 512], st=(kc == 0), sp=(kc == 21):
                                        e.matmul(o, lhsT=w_, rhs=r, start=st, stop=sp), [kw, ("bH", kc)], [kp])
                if nxt is not None and ct == 0:
                    stk = self.psum_group()
                    stat_mms(0, stk[0], stk[1])
                    sq_chunks(4)
                if nxt is not None and ct == 1:
                    stat_mms(4, stk[0], stk[1])
                for mm in range(2):
                    oc = ct * 2 + mm
                    kp, pv = grp[mm]
                    self.op("dve", lambda e, o=xv[:, oc, :], i=self.flat(pv): e.tensor_copy(out=o, in_=i),
                            [kp], [kX[oc]])
                    self.op("act", lambda e, o=sq[:, oc, :], i=xv[:, oc, :]: e.activation(out=o, in_=i, func=AF.Square),
                            [kX[oc]], [kSQ[oc]])
                if nxt is not None and ct == 1:
                    rv = self.bR[:, 0:TG]
                    rk = ("bR", 0)
                    self.op("act", lambda e, o=rv, i=self.flat(stk[1]): e.activation(out=o, in_=i, func=AF.Sqrt, scale=1.0 / D, bias=EPS),
                            [stk[0]], [rk])
                    self.op("dve", lambda e, o=rv: e.reciprocal(out=o, in_=o), [rk], [rk])
                    for c in range(8):
                        ti = c % 4
                        self.dma(self.bT[ti][:], X[c, :, t1:t1 + TG], [("X", nxt, c)], [("bT", ti)])
                        self.op("dve", lambda e, o=xn[:, c, :], i=self.bT[ti][:], s=self.normw[:, npre + c:npre + c + 1], r=rv:
                                e.scalar_tensor_tensor(out=o, in0=i, scalar=s, in1=r, op0=ALU.mult, op1=ALU.mult),
                                [("bT", ti), "normw", rk], [kXN[c]])
            rk2, rv2 = self.rms_rinv(sq, 8, TG, 4.0 / D, 4.0 * EPS, kSQ)
            if g + 1 < NG:
                pending = (g, rk2, rv2)
            else:
                self.residual_add(X, g, xv, kX, rk2, rv2, npost)

    def residual_add(self, X, g, yv, kY, rk, rv, npost, ocs=range(8)):
        t0 = g * TG
        for oc in ocs:
            ti = self.t_i % 4
            self.t_i += 1
            xr = self.bT[ti]
            kt_ = ("bT", ti)
            self.dma(xr[:], X[oc, :, t0:t0 + TG], [("X", g, oc)], [kt_], eng="pool")
            self.op("dve", lambda e, o=yv[:, oc, :], s=self.normw[:, npost + oc:npost + oc + 1], r=rv:
                    e.scalar_tensor_tensor(out=o, in0=o, scalar=s, in1=r, op0=ALU.mult, op1=ALU.mult),
                    [kY[oc], "normw", rk], [kY[oc]])
            self.op("pool", lambda e, o=xr[:], b_=yv[:, oc, :]: e.tensor_tensor(out=o, in0=o, in1=b_, op=ALU.add),
                    [kt_, kY[oc]], [kt_])
            self.dma(X[oc, :, t0:t0 + TG], xr[:], [kt_], [("X", g, oc)], eng="pool")


    def even_consts(self, aps, li):
        stg = self.wsF[0][:, 0:256].rearrange("p (c m) -> p c m", c=2)
        stg = self.bT[0][:, 0:256].rearrange("p (c m) -> p c m", c=2)
        kt_ = ("bT", 0)
        self.op("pool", lambda e, o=stg: e.memset(o, 0.0), [], [kt_])
        for c in range(2):
            for gg in range(2):
                self.dma(stg[gg * 64:(gg + 1) * 64, c, gg * 64:(gg + 1) * 64], aps["ev_pool_w"][li, 2 * c + gg],
                         [kt_], [kt_])
        self.op("pool", lambda e, o=self.poolW[:], i=stg: e.tensor_copy(out=o, in_=i), [kt_], ["poolW"])
        self.dma(self.poolS[:], aps["ev_pool_scale"][li], ["poolS"], ["poolS"])
        self.dma(self.esink[:], aps["ev_sink"][li].partition_broadcast(128), ["esink"], ["esink"])
        self.op("act", lambda e, o=self.esink[:]: e.activation(out=o, in_=o, func=AF.Exp), ["esink"], ["esink"])

    def stage_even_in(self, X, aps, li, npre, QT, KT, V, U):
        xv, xn_, sq, kX, kXN_, kSQ = self.views()
        Win = ("ev_win", li)
        Wv = ("ev_wv", li)
        sqb = sq
        vst = self.bH[:, 0:8 * 256].rearrange("p (tb f) -> p tb f", tb=8)
        ust = self.bR[:].rearrange("p (c t) -> p c t", c=2)
        for g in range(NG):
            t0 = g * TG
            xn, kXN = self.load_norm(X, g, npre)
            cosT = self.bSG[0]
            sinT = self.bSG[1]
            self.dma(cosT[:], aps["ropec"][:, t0:t0 + TG], [], [("bSG", 0)])
            self.dma(sinT[:], aps["ropes"][:, t0:t0 + TG], [], [("bSG", 1)])
            for i in range(8):
                kw, w = self.load_w(Win, 0, 8, i * 256, 256)
                kq, pq = self.psum_group()
                for k in range(8):
                    for b in range(NB):
                        self.op("pe", lambda e, o=pq[:, b, :], w_=w[:, k, 0:128], r=xn[:, k, b * 512:(b + 1) * 512], st=(k == 0), sp=(k == 7):
                                e.matmul(o, lhsT=w_, rhs=r, start=st, stop=sp), [kw, kXN[k]], [kq])
                kp, pp = self.psum_group()
                for k in range(8):
                    for b in range(NB):
                        self.op("pe", lambda e, o=pp[:, b, :], w_=w[:, k, 128:256], r=xn[:, k, b * 512:(b + 1) * 512], st=(k == 0), sp=(k == 7):
                                e.matmul(o, lhsT=w_, rhs=r, start=st, stop=sp), [kw, kXN[k]], [kp])
                t1 = self.bT[0]
                t2 = self.bT[1]
                self.op("dve", lambda e, o=t1[:], a=self.flat(pq), c_=cosT[:]: e.tensor_tensor(out=o, in0=a, in1=c_, op=ALU.mult),
                        [kq, ("bSG", 0)], [("bT", 0)])
                self.op("dve", lambda e, o=t2[:], a=self.flat(pp), c_=sinT[:]: e.tensor_tensor(out=o, in0=a, in1=c_, op=ALU.mult),
                        [kp, ("bSG", 1)], [("bT", 1)])
                self.op("pool", lambda e, o=sqb[:, i, :], a=t1[:], b_=t2[:]: e.tensor_tensor(out=o, in0=a, in1=b_, op=ALU.add),
                        [("bT", 0), ("bT", 1)], [kSQ[i]])
                if i < 6:
                    self.dma(QT[i * 128:(i + 1) * 128, t0:t0 + TG], sqb[:, i, :], [kSQ[i]], [("QT", g)])
                else:
                    self.dma(KT[(i - 6) * 128:(i - 5) * 128, t0:t0 + TG], sqb[:, i, :], [kSQ[i]], [("KT", g)])
            kw, w = self.load_w(Win, 0, 8, 2048, 256)
            for mm in range(2):
                kp, pv = self.psum_group()
                for k in range(8):
                    for b in range(NB):
                        self.op("pe", lambda e, o=pv[:, b, :], w_=w[:, k, mm * 128:(mm + 1) * 128], r=xn[:, k, b * 512:(b + 1) * 512], st=(k == 0), sp=(k == 7):
                                e.matmul(o, lhsT=w_, rhs=r, start=st, stop=sp), [kw, kXN[k]], [kp])
                self.op("act", lambda e, o=ust[:, mm, :], i_=self.flat(pv): e.copy(out=o, in_=i_), [kp], [("bR", mm * TG)])
                self.dma(U[mm, :, t0:t0 + TG], ust[:, mm, :], [("bR", mm * TG)], [("U", g)])
            kw, w = self.load_w(Wv, 0, 8, 0, 256)
            for tb in range(8):
                kp, pv = self.psum_group()
                pf = self.flat(pv)
                for k in range(8):
                    self.op("pe", lambda e, o=pf[:, 0:256], l=xn[:, k, tb * 128:(tb + 1) * 128], r=w[:, k, :], st=(k == 0), sp=(k == 7):
                            e.matmul(o, lhsT=l, rhs=r, start=st, stop=sp), [kw, kXN[k]], [kp])
                if tb % 2 == 0:
                    self.op("act", lambda e, o=vst[:, tb, :], i_=pf[:, 0:256]: e.copy(out=o, in_=i_), [kp], [("bH", "v")])
                else:
                    self.op("dve", lambda e, o=vst[:, tb, :], i_=pf[:, 0:256]: e.tensor_copy(out=o, in_=i_), [kp], [("bH", "v")])
            self.dma(V[t0:t0 + TG, :].rearrange("(tb p) f -> p tb f", p=128), vst, [("bH", "v")], [("V", g)])

    def stage_even_pool(self, aps, U, AO):
        L = TG + 16
        bx = self.bX
        up = bx[:, 0:2 * L].rearrange("p (c t) -> p c t", c=2)
        a1 = bx[:, 2 * L:4 * L].rearrange("p (c t) -> p c t", c=2)
        a2 = bx[:, 4 * L:6 * L].rearrange("p (c t) -> p c t", c=2)
        xnf = self.bXN[:].bitcast(F32)
        a3 = xnf[:, 0:L]
        a4 = xnf[:, L:2 * L]
        icnt = self.bR[:].rearrange("p (c t) -> p c t", c=2)
        res = self.bSQ[:].bitcast(F32)[:, 0:2 * TG].rearrange("p (c t) -> p c t", c=2)
        pb = self.bH[:, 0:2 * TG].rearrange("p (c t) -> p c t", c=2)
        ob = self.bH[:, 2 * TG:4 * TG].rearrange("p (c t) -> p c t", c=2)
        for g in range(NG):
            t0 = g * TG
            lo = max(t0 - 8, 0)
            hi = min(t0 + TG + 8, S)
            j0 = lo - (t0 - 8)
            self.op("pool", lambda e, o=up: e.memset(o, 0.0), [], ["pl_u"])
            self.dma(up[:, :, j0:j0 + hi - lo], U[:, :, lo:hi].rearrange("c p t -> p c t"),
                     ["pl_u"] + [("U", gg) for gg in (g - 1, g, g + 1) if 0 <= gg < NG], ["pl_u"])
            self.dma(icnt, aps["icnt"][:, :, t0:t0 + TG].rearrange("c p t -> p c t"), [], [("bR", 0), ("bR", TG)])
            self.op("dve", lambda e: e.tensor_tensor(out=a1[:, :, 1:L], in0=up[:, :, 0:L - 1], in1=up[:, :, 1:L], op=ALU.add),
                    ["pl_u"], ["pl_a1"])
            self.op("dve", lambda e: e.tensor_tensor(out=a2[:, :, 2:L - 1], in0=a1[:, :, 1:L - 2], in1=a1[:, :, 3:L], op=ALU.add),
                    ["pl_a1"], ["pl_a2"])
            self.op("dve", lambda e: e.tensor_tensor(out=a3[:, 4:L - 3], in0=a2[:, 1, 2:L - 5], in1=a2[:, 1, 6:L - 1], op=ALU.add),
                    ["pl_a2"], ["pl_a3"])
            self.op("dve", lambda e: e.tensor_tensor(out=a4[:, 8:L - 8], in0=a3[:, 4:L - 12], in1=a3[:, 12:L - 4], op=ALU.add),
                    ["pl_a3"], ["pl_a4"])
            self.op("dve", lambda e: e.tensor_tensor(out=res[0:64, 0, :], in0=a1[0:64, 0, 8:TG + 8], in1=icnt[0:64, 0, :], op=ALU.mult),
                    ["pl_a1", ("bR", 0)], ["pl_res"])
            self.op("dve", lambda e: e.tensor_tensor(out=res[64:128, 0, :], in0=a2[64:128, 0, 8:TG + 8], in1=icnt[64:128, 0, :], op=ALU.mult),
                    ["pl_a2", ("bR", 0)], ["pl_res"])
            self.op("dve", lambda e: e.tensor_tensor(out=res[0:64, 1, :], in0=a3[0:64, 8:TG + 8], in1=icnt[0:64, 1, :], op=ALU.mult),
                    ["pl_a3", ("bR", TG)], ["pl_res"])
            self.op("dve", lambda e: e.tensor_tensor(out=res[64:128, 1, :], in0=a4[64:128, 8:TG + 8], in1=icnt[64:128, 1, :], op=ALU.mult),
                    ["pl_a4", ("bR", TG)], ["pl_res"])
            self.op("dve", lambda e: e.tensor_tensor(out=pb, in0=res, in1=up[:, :, 8:TG + 8], op=ALU.subtract),
                    ["pl_res", "pl_u"], ["pl_pb"])
            for c in range(2):
                kp, pv = self.psum_group()
                for b in range(NB):
                    self.op("pe", lambda e, o=pv[:, b, :], l=self.poolW[:, c, :], r=pb[:, c, b * 512:(b + 1) * 512]:
                            e.matmul(o, lhsT=l, rhs=r, start=True, stop=True), ["poolW", "pl_pb"], [kp])
                self.op("dve", lambda e, o=ob[:, c, :], i_=self.flat(pv), s=self.poolS[:, c:c + 1]:
                        e.tensor_scalar(out=o, in0=i_, scalar1=s, scalar2=None, op0=ALU.mult), [kp, "poolS"], [("pl_ob", c)])
                self.dma(AO[c, :, t0:t0 + TG], ob[:, c, :], [("pl_ob", c)], [("AOp", g)])

    def stage_even_attn(self, aps, li, QT, KT, V, AO):
        KW = TG + 256
        q_sb = self.bH[0:64, 0:12 * TG].rearrange("p (h t) -> p h t", h=12)
        k_sb = self.bXN[0:64, 0:4 * KW].rearrange("p (h t) -> p h t", h=4)
        v_sb = self.bSQ[:, 0:10 * 256].rearrange("p (kb f) -> p kb f", kb=10)
        o_sb = self.bX[:].bitcast(BF16)[0:64, 0:12 * TG].rearrange("p (h t) -> p h t", h=12)
        Pb = [self.bSG[0][:].bitcast(BF16)[:, 0:384], self.bSG[1][:].bitcast(BF16)[:, 0:384],
              self.bT[0][:].bitcast(BF16)[:, 0:384], self.bT[1][:].bitcast(BF16)[:, 0:384]]
        Pk = [("bSG", 0), ("bSG", 1), ("bT", 0), ("bT", 1)]
        recs = [self.bR[0:64, 0:384], self.bR[0:64, 512:896]]
        AOf = AO.rearrange("c p t -> (c p) t")
        pi = 0
        ri = 0
        for g in range(NG):
            t0 = g * TG
            lo = max(t0 - 128, 0)
            hi = min(t0 + TG + 128, S)
            j0 = lo - (t0 - 128)
            nbr = [gg for gg in (g - 1, g, g + 1) if 0 <= gg < NG]
            self.dma(q_sb, QT[:, t0:t0 + TG].rearrange("(h d) t -> d h t", d=64), [("QT", g)], ["at_q"])
            self.dma(k_sb[:, :, j0:j0 + hi - lo], KT[:, lo:hi].rearrange("(h d) t -> d h t", d=64),
                     [("KT", gg) for gg in nbr], ["at_k"])
            self.dma(v_sb[:, j0 // 128:(j0 + hi - lo) // 128, :], V[lo:hi, :].rearrange("(kb p) f -> p kb f", p=128),
                     [("V", gg) for gg in nbr], ["at_v"])
            for qi in range(8):
                n = g * 8 + qi
                for gk in range(4):
                    kbs = [kb for kb in (n - 1, n, n + 1) if 0 <= kb < S // 128]
                    Ps = []
                    for kb in kbs:
                        lk = kb - (g * 8 - 1)
                        ks, pv = self.psum_group()
                        pf = self.flat(pv)
                        self.op("pe", lambda e, o=pf[:, 0:384].rearrange("p (h q) -> p h q", h=3), l=k_sb[:, gk, lk * 128:(lk + 1) * 128],
                                r=q_sb[:, 3 * gk:3 * gk + 3, qi * 128:(qi + 1) * 128]:
                                e.matmul(o, lhsT=l, rhs=r, start=True, stop=True), ["at_k", "at_q"], [ks])
                        P = Pb[pi % 4]
                        kP = Pk[pi % 4]
                        pi += 1
                        self.op("act", lambda e, o=P, i_=pf[:, 0:384]: e.activation(out=o, in_=i_, func=AF.Exp, scale=0.125),
                                [ks], [kP])
                        if kb == n - 1:
                            self.op("dve", lambda e, o=P: e.tensor_tensor(out=o, in0=o, in1=self.amask[:, 0, :], op=ALU.mult),
                                    [kP, "amask"], [kP])
                        elif kb == n + 1:
                            self.op("dve", lambda e, o=P: e.tensor_tensor(out=o, in0=o, in1=self.amask[:, 1, :], op=ALU.mult),
                                    [kP, "amask"], [kP])
                        Ps.append((P, kP, lk))
                    kn, pn = self.psum_group()
                    num = pn[0:64, 0, 0:384]
                    den = pn[0:64, 1, 0:384]
                    for idx, (P, kP, lk) in enumerate(Ps):
                        st = (idx == 0)
                        sp = (idx == len(Ps) - 1)
                        self.op("pe", lambda e, o=num, l=v_sb[:, lk, gk * 64:(gk + 1) * 64], r=P, st=st, sp=sp:
                                e.matmul(o, lhsT=l, rhs=r, start=st, stop=sp), ["at_v", kP], [kn])
                        self.op("pe", lambda e, o=den, l=self.ones_bf[:, 0:64], r=P, st=st, sp=sp:
                                e.matmul(o, lhsT=l, rhs=r, start=st, stop=sp), ["ones_bf", kP], [kn])
                    rec = recs[ri % 2]
                    kr = ("rec", ri % 2)
                    ri += 1
                    for j in range(3):
                        hh = 3 * gk + j
                        self.op("dve", lambda e, o=rec[:, j * 128:(j + 1) * 128], i_=den[:, j * 128:(j + 1) * 128],
                                s=self.esink[0:64, hh:hh + 1]:
                                e.tensor_scalar(out=o, in0=i_, scalar1=s, scalar2=None, op0=ALU.add), [kn, "esink"], [kr])
                    self.op("dve", lambda e, o=rec: e.reciprocal(out=o, in_=o), [kr], [kr])
                    self.op("dve", lambda e, o=o_sb[:, 3 * gk:3 * gk + 3, qi * 128:(qi + 1) * 128],
                            a=num.rearrange("p (h q) -> p h q", h=3), b_=rec.rearrange("p (h q) -> p h q", h=3):
                            e.tensor_tensor(out=o, in0=a, in1=b_, op=ALU.mult), [kn, kr], ["at_o"])
            self.dma(AOf[256:1024, t0:t0 + TG].rearrange("(h d) t -> d h t", d=64), o_sb, ["at_o"], [("AOa", g)])

    def stage_even_out(self, X, aps, li, npost, AO):
        xv, xn, sq, kX, kXN, kSQ = self.views()
        for g in range(NG):
            t0 = g * TG
            self.dma(xn, AO[:, :, t0:t0 + TG].rearrange("c p t -> p c t"), [("AOp", g), ("AOa", g)], kXN)
            self.proj_post_residual(X, g, xn, kXN, 8, ("ev_wout", li), npost, False)

    def stg4(self):
        i = self.stg_i % 4
        self.stg_i += 1
        t = [self.bT[0], self.bT[1], self.bSG[0], self.bSG[1]][i]
        k = [("bT", 0), ("bT", 1), ("bSG", 0), ("bSG", 1)][i]
        return t, k

    def odd_consts(self, aps, li):
        self.dma(self.dww[:], aps["cv_dww"][li], ["dww"], ["dww"])
        self.dma(self.cvvec[:], aps["cv_vec"][li], ["cvvec"], ["cvvec"])
        self.dma(self.scw[:], aps["ssm_cw"][li], ["scw"], ["scw"])
        self.dma(self.dexp[:], aps["ssm_dexp"][li], ["dexp"], ["dexp"])
        self.dma(self.snw[:], aps["ssm_nw"][li], ["snw"], ["snw"])
        self.dma(self.abcol[:], aps["ssm_ab"][li], ["abcol"], ["abcol"])
        self.op("act", lambda e, o=self.abcol[:, 0:1]: e.activation(out=o, in_=o, func=AF.Exp), ["abcol"], ["abcol"])
        self.op("dve", lambda e, o=self.abcol[:, 0:1]: e.tensor_scalar(out=o, in0=o, scalar1=-1.0, scalar2=None, op0=ALU.mult),
                ["abcol"], ["abcol"])
        self.dma(self.abrow[:, 0, :], aps["ssm_A_log"][li].partition_broadcast(128), ["abrow"], ["abrow"])
        self.dma(self.abrow[:, 1, :], aps["ssm_dt_bias"][li].partition_broadcast(128), ["abrow"], ["abrow"])
        self.op("act", lambda e, o=self.abrow[:, 0, :]: e.activation(out=o, in_=o, func=AF.Exp), ["abrow"], ["abrow"])
        self.op("dve", lambda e, o=self.abrow[:, 0, :]: e.tensor_scalar(out=o, in0=o, scalar1=-1.0, scalar2=None, op0=ALU.mult),
                ["abrow"], ["abrow"])

    def stage_odd_in(self, X, aps, li, npre, G, ZS, XBCr, DTr, DTt):
        Win = ("od_win", li)
        for g in range(NG):
            t0 = g * TG
            xn, kXN = self.load_norm(X, g, npre)
            for i in range(16):
                kw, w = self.load_w(Win, 0, 8, i * 256, 256)
                pgs = []
                for mm in range(2):
                    kp, pv = self.psum_group()
                    for k in range(8):
                        for b in range(NB):
                            self.op("pe", lambda e, o=pv[:, b, :], w_=w[:, k, mm * 128:(mm + 1) * 128], r=xn[:, k, b * 512:(b + 1) * 512], st=(k == 0), sp=(k == 7):
                                    e.matmul(o, lhsT=w_, rhs=r, start=st, stop=sp), [kw, kXN[k]], [kp])
                    pgs.append((kp, pv))
                if i < 4:
                    s1, k1 = self.stg4()
                    s2, k2 = self.stg4()
                    self.op("act", lambda e, o=s1[:], i_=self.flat(pgs[1][1]): e.activation(out=o, in_=i_, func=AF.Sigmoid),
                            [pgs[1][0]], [k1])
                    self.op("dve", lambda e, o=s2[:], a=self.flat(pgs[0][1]), b_=s1[:]: e.tensor_tensor(out=o, in0=a, in1=b_, op=ALU.mult),
                            [pgs[0][0], k1], [k2])
                    self.dma(G[i, :, t0:t0 + TG], s2[:], [k2], [("G", g)])
                elif i < 8:
                    for mm in range(2):
                        s1, k1 = self.stg4()
                        self.op("act", lambda e, o=s1[:], i_=self.flat(pgs[mm][1]): e.activation(out=o, in_=i_, func=AF.Silu),
                                [pgs[mm][0]], [k1])
                        self.dma(ZS[(i - 4) * 2 + mm, :, t0:t0 + TG], s1[:], [k1], [("ZS", g)])
                else:
                    for mm in range(2):
                        s1, k1 = self.stg4()
                        if mm == 0:
                            self.op("act", lambda e, o=s1[:], i_=self.flat(pgs[mm][1]): e.copy(out=o, in_=i_), [pgs[mm][0]], [k1])
                        else:
                            self.op("dve", lambda e, o=s1[:], i_=self.flat(pgs[mm][1]): e.tensor_copy(out=o, in_=i_), [pgs[mm][0]], [k1])
                        self.dma(XBCr[(i - 8) * 2 + mm, :, t0:t0 + TG], s1[:], [k1], [("XBCr", g)])
            kw, w = self.load_w(("od_wdt", li), 0, 8, 0, 32)
            kp, pv = self.psum_group()
            for k in range(8):
                for b in range(NB):
                    self.op("pe", lambda e, o=pv[0:32, b, :], w_=w[:, k, :], r=xn[:, k, b * 512:(b + 1) * 512], st=(k == 0), sp=(k == 7):
                            e.matmul(o, lhsT=w_, rhs=r, start=st, stop=sp), [kw, kXN[k]], [kp])
            s1, k1 = self.stg4()
            self.op("act", lambda e, o=s1[0:32, :], i_=self.flat(pv)[0:32, :]: e.copy(out=o, in_=i_), [kp], [k1])
            self.dma(DTr[:, t0:t0 + TG], s1[0:32, :], [k1], [("DTr", g)])
            kp, pv = self.psum_group()
            pf = self.flat(pv)
            for tb in range(8):
                for k in range(8):
                    self.op("pe", lambda e, o=pf[:, tb * 32:(tb + 1) * 32], l=xn[:, k, tb * 128:(tb + 1) * 128], r=w[:, k, :], st=(k == 0), sp=(k == 7):
                            e.matmul(o, lhsT=l, rhs=r, start=st, stop=sp), [kw, kXN[k]], [kp])
            s1, k1 = self.stg4()
            self.op("dve", lambda e, o=s1[:, 0:256], i_=pf[:, 0:256]: e.tensor_copy(out=o, in_=i_), [kp], [k1])
            self.dma(DTt[:, g * 8:(g + 1) * 8, :],
                     s1[:, 0:256].rearrange("p (tb j) -> p tb j", tb=8), [k1], [("DTt", g)])

    def stage_odd_conv(self, aps, li, G, XBCr, XBC, MO):
        hf = self.bH[:].bitcast(F32)
        acc = hf[:, 0:4 * TG].rearrange("p (c t) -> p c t", c=4)
        sqr = hf[:, 4 * TG:8 * TG].rearrange("p (c t) -> p c t", c=4)
        mo = self.bXN[:, 0:4 * TG].rearrange("p (c t) -> p c t", c=4)
        mean = self.bR[:, 0:TG]
        rstd = self.bR[:, TG:2 * TG]
        LG = TG + 30
        gp = self.bX[:, 0:4 * LG].rearrange("p (c t) -> p c t", c=4)
        for g in range(NG):
            t0 = g * TG
            lo = max(t0 - 15, 0)
            hi = min(t0 + TG + 15, S)
            j0 = lo - (t0 - 15)
            self.op("pool", lambda e, o=gp: e.memset(o, 0.0), [], ["cv_gp"])
            self.dma(gp[:, :, j0:j0 + hi - lo], G[:, :, lo:hi].rearrange("c p t -> p c t"),
                     ["cv_gp"] + [("G", gg) for gg in (g - 1, g, g + 1) if 0 <= gg < NG], ["cv_gp"])
            for c in range(4):
                self.op("dve", lambda e, o=acc[:, c, :], i_=gp[:, c, 0:TG], s1=self.dww[:, c, 0:1], s2=self.cvvec[:, 0, c:c + 1]:
                        e.tensor_scalar(out=o, in0=i_, scalar1=s1, scalar2=s2, op0=ALU.mult, op1=ALU.add),
                        ["cv_gp", "dww", "cvvec"], [("cv_acc", c)])
            for k in range(1, 31):
                for c in range(4):
                    self.op("dve", lambda e, o=acc[:, c, :], i_=gp[:, c, k:k + TG], s=self.dww[:, c, k:k + 1]:
                            e.scalar_tensor_tensor(out=o, in0=i_, scalar=s, in1=o, op0=ALU.mult, op1=ALU.add),
                            ["cv_gp", "dww", ("cv_acc", c)], [("cv_acc", c)])
            for c in range(4):
                self.op("act", lambda e, o=sqr[:, c, :], i_=acc[:, c, :]: e.activation(out=o, in_=i_, func=AF.Square),
                        [("cv_acc", c)], [("cv_sq", c)])
            k1, p1 = self.psum_group()
            k2, p2 = self.psum_group()
            for b in range(NB):
                for c in range(4):
                    self.op("pe", lambda e, o=p1[:, b, :], r=acc[:, c, b * 512:(b + 1) * 512], st=(c == 0), sp=(c == 3):
                            e.matmul(o, lhsT=self.ones_f[:], rhs=r, start=st, stop=sp), ["ones_f", ("cv_acc", c)], [k1])
            for b in range(NB):
                for c in range(4):
                    self.op("pe", lambda e, o=p2[:, b, :], r=sqr[:, c, b * 512:(b + 1) * 512], st=(c == 0), sp=(c == 3):
                            e.matmul(o, lhsT=self.ones_f[:], rhs=r, start=st, stop=sp), ["ones_f", ("cv_sq", c)], [k2])
            self.op("act", lambda e, o=mean, i_=self.flat(p1): e.mul(out=o, in_=i_, mul=1.0 / 512), [k1], [("bR", 0)])
            msq = sqr[:, 0, :]
            self.op("act", lambda e, o=msq, i_=mean: e.activation(out=o, in_=i_, func=AF.Square), [("bR", 0), k2], [("cv_sq", 0)])
            self.op("dve", lambda e, o=rstd, i_=self.flat(p2), m=msq:
                    e.scalar_tensor_tensor(out=o, in0=i_, scalar=1.0 / 512, in1=m, op0=ALU.mult, op1=ALU.subtract),
                    [k2, ("cv_sq", 0)], [("bR", TG)])
            self.op("act", lambda e, o=rstd: e.activation(out=o, in_=o, func=AF.Sqrt, scale=1.0, bias=EPS), [("bR", TG)], [("bR", TG)])
            self.op("dve", lambda e, o=rstd: e.reciprocal(out=o, in_=o), [("bR", TG)], [("bR", TG)])
            for c in range(4):
                self.op("dve", lambda e, o=acc[:, c, :], m=mean: e.tensor_tensor(out=o, in0=o, in1=m, op=ALU.subtract),
                        [("cv_acc", c), ("bR", 0)], [("cv_acc", c)])
                self.op("dve", lambda e, o=acc[:, c, :], s=self.cvvec[:, 1, c:c + 1], r=rstd:
                        e.scalar_tensor_tensor(out=o, in0=o, scalar=s, in1=r, op0=ALU.mult, op1=ALU.mult),
                        [("cv_acc", c), ("bR", TG), "cvvec"], [("cv_acc", c)])
                self.op("act", lambda e, o=mo[:, c, :], i_=acc[:, c, :], bb=self.cvvec[:, 2, c:c + 1]:
                        e.activation(out=o, in_=i_, func=AF.Silu, bias=bb), [("cv_acc", c), "cvvec"], [("bXN", c)])
            self.dma(MO[:, :, t0:t0 + TG].rearrange("c p t -> p c t"), mo, [("bXN", c) for c in range(4)], [("MO", g)])
        LX = TG + 3
        xp = self.bX[:, 0:4 * LX].rearrange("p (c t) -> p c t", c=4)
        for g in range(NG):
            t0 = g * TG
            lo = max(t0 - 2, 0)
            hi = min(t0 + TG + 1, S)
            j0 = lo - (t0 - 2)
            for cb in range(4):
                self.op("pool", lambda e, o=xp: e.memset(o, 0.0), [], ["cv_gp"])
                self.dma(xp[:, :, j0:j0 + hi - lo], XBCr[cb * 4:cb * 4 + 4, :, lo:hi].rearrange("c p t -> p c t"),
                         ["cv_gp"] + [("XBCr", gg) for gg in (g - 1, g, g + 1) if 0 <= gg < NG], ["cv_gp"])
                for cc in range(4):
                    c = cb * 4 + cc
                    self.op("dve", lambda e, o=acc[:, cc, :], i_=xp[:, cc, 0:TG], s1=self.scw[:, c, 0:1], s2=self.scw[:, c, 4:5]:
                            e.tensor_scalar(out=o, in0=i_, scalar1=s1, scalar2=s2, op0=ALU.mult, op1=ALU.add),
                            ["cv_gp", "scw"], [("cv_acc", cc)])
                for k in range(1, 4):
                    for cc in range(4):
                        c = cb * 4 + cc
                        self.op("dve", lambda e, o=acc[:, cc, :], i_=xp[:, cc, k:k + TG], s=self.scw[:, c, k:k + 1]:
                                e.scalar_tensor_tensor(out=o, in0=i_, scalar=s, in1=o, op0=ALU.mult, op1=ALU.add),
                                ["cv_gp", "scw", ("cv_acc", cc)], [("cv_acc", cc)])
                for cc in range(4):
                    self.op("act", lambda e, o=acc[:, cc, :]: e.activation(out=o, in_=o, func=AF.Silu),
                            [("cv_acc", cc)], [("cv_acc", cc)])
                self.dma(XBC[cb * 4:cb * 4 + 4, :, t0:t0 + TG].rearrange("c p t -> p c t"), acc,
                         [("cv_acc", cc) for cc in range(4)], [("XBC", g)])

    def stage_odd_dt(self, aps, li, DTr, DTt):
        xf = self.bXN[:].bitcast(F32)
        dtk = xf[:, 0:1024]
        dtd = xf[:, 1024:2048]
        cdk = xf[:, 2048:3072]
        a_tok = xf[:, 3072:4096]
        sf = self.bSQ[:].bitcast(F32)
        acs_tok = sf[:, 0:1024]
        raw = sf[:, 1024:2048]
        hf = self.bH[:].bitcast(F32)
        AT = hf[0:48, 0:S]
        c1 = hf[0:48, S:2 * S]
        acsT = self.bX[0:48, 0:S]
        nacsT = self.bX[0:48, S:2 * S]
        dm = lambda t: t.rearrange("p (d c h) -> p c d h", d=2, h=16)
        self.dma(raw, DTt.rearrange("p c j -> p (c j)"), [("DTt", g) for g in range(NG)], ["dtraw"])
        for d in range(2):
            self.op("dve", lambda e, d=d: e.tensor_tensor(
                out=dtk[:, d * 512:(d + 1) * 512].rearrange("p (c h) -> p c h", h=16),
                in0=raw.rearrange("p (c j) -> p c j", j=32)[:, :, d * 16:(d + 1) * 16],
                in1=self.abrow[:, 1:2, d * 16:(d + 1) * 16].broadcast_to([128, 32, 16]), op=ALU.add),
                ["dtraw", "abrow"], ["dtk"])
        self.op("act", lambda e: e.activation(out=dtk, in_=dtk, func=AF.Exp), ["dtk"], ["dtk"])
        self.op("act", lambda e: e.activation(out=dtk, in_=dtk, func=AF.Ln, bias=1.0), ["dtk"], ["dtk"])
        for d in range(2):
            self.op("dve", lambda e, d=d: e.tensor_tensor(
                out=a_tok[:, d * 512:(d + 1) * 512].rearrange("p (c h) -> p c h", h=16),
                in0=dtk[:, d * 512:(d + 1) * 512].rearrange("p (c h) -> p c h", h=16),
                in1=self.abrow[:, 0:1, d * 16:(d + 1) * 16].broadcast_to([128, 32, 16]), op=ALU.mult),
                ["dtk", "abrow"], ["a_tok"])
        import os
        dcut = int(os.environ.get("DT_CUT", "99"))
        if dcut <= 1:
            return
        k1, p1 = self.psum_group()
        for d in range(2):
            self.op("pe", lambda e, d=d: e.matmul(p1[:, d, :], lhsT=self.tri[:, d, :], rhs=a_tok[:, d * 512:(d + 1) * 512],
                                                   start=True, stop=True), ["a_tok", "tri"], [k1])
        self.op("dve", lambda e: e.tensor_copy(out=acs_tok, in_=self.flat(p1)), [k1], ["acs_tok"])
        k2, p2 = self.psum_group()
        for d in range(2):
            self.op("pe", lambda e, d=d: e.matmul(p2[:, d, :], lhsT=self.ones_f[:], rhs=a_tok[:, d * 512:(d + 1) * 512],
                                                   start=True, stop=True), ["a_tok", "ones_f"], [k2])
        self.op("act", lambda e: e.activation(out=cdk, in_=self.flat(p2), func=AF.Exp), [k2], ["cdk"])
        self.op("dve", lambda e: e.tensor_tensor(out=dtd, in0=self.flat(p2), in1=acs_tok, op=ALU.subtract), [k2, "acs_tok"], ["dtd"])
        self.op("act", lambda e: e.activation(out=dtd, in_=dtd, func=AF.Exp), ["dtd"], ["dtd"])
        self.op("dve", lambda e: e.tensor_tensor(out=dtd, in0=dtd, in1=dtk, op=ALU.mult), ["dtd", "dtk"], ["dtd"])
        if dcut <= 2:
            return
        self.op("pool", lambda e: e.memset(AT, 0.0), [], ["AT"])
        self.dma(AT[0:16, :], DTr[0:16, :], ["AT"] + [("DTr", g) for g in range(NG)], ["AT"])
        self.dma(AT[32:48, :], DTr[16:32, :], ["AT"], ["AT"])
        self.op("act", lambda e: e.activation(out=AT, in_=AT, func=AF.Exp, bias=self.abcol[:, 1:2]), ["AT", "abcol"], ["AT"])
        self.op("act", lambda e: e.activation(out=AT, in_=AT, func=AF.Ln, bias=1.0), ["AT"], ["AT"])
        self.op("dve", lambda e: e.tensor_scalar(out=AT, in0=AT, scalar1=self.abcol[:, 0:1], scalar2=None, op0=ALU.mult),
                ["AT", "abcol"], ["AT"])
        if dcut <= 3:
            return
        c3 = lambda t: t.rearrange("p (c l) -> p c l", l=128)
        srcb, dstb = AT, c1
        names = {id(AT): "AT", id(c1): "c1", id(acsT): "acsT"}
        bufs = [c1, acsT]
        cur, curk = AT, "AT"
        step = 1
        n = 0
        while step < 128:
            dst = bufs[n % 2]
            dk = ["c1", "acsT"][n % 2]
            self.op("dve", lambda e, dst=dst, cur=cur, step=step: e.tensor_copy(out=c3(dst)[:, :, 0:step], in_=c3(cur)[:, :, 0:step]),
                    [curk], [dk])
            self.op("dve", lambda e, dst=dst, cur=cur, step=step: e.tensor_tensor(
                out=c3(dst)[:, :, step:128], in0=c3(cur)[:, :, step:128], in1=c3(cur)[:, :, 0:128 - step], op=ALU.add),
                [curk], [dk])
            cur, curk = dst, dk
            step *= 2
            n += 1
        cs = cur
        csk = curk
        self.op("dve", lambda e: e.tensor_copy(out=acsT[0:16, :], in_=cs[0:16, :]), [csk], ["acsT"])
        self.op("dve", lambda e: e.tensor_tensor(out=c3(acsT)[32:48], in0=c3(cs)[32:48, :, 127:128].broadcast_to([16, 32, 128]),
                                                 in1=c3(cs)[32:48], op=ALU.subtract), [csk, "acsT"], ["acsT"])
        self.op("dve", lambda e: e.tensor_tensor(out=acsT[32:48, :], in0=acsT[32:48, :], in1=AT[32:48, :], op=ALU.add),
                ["acsT", "AT"], ["acsT"])
        self.op("dve", lambda e: e.tensor_scalar(out=nacsT[0:16, :], in0=acsT[0:16, :], scalar1=-1.0, scalar2=None, op0=ALU.mult),
                ["acsT"], ["nacsT"])
        self.op("dve", lambda e: e.tensor_scalar(out=nacsT[32:48, :], in0=acsT[32:48, :], scalar1=-1.0, scalar2=None, op0=ALU.mult),
                ["acsT"], ["nacsT"])

    def stage_odd_ssd(self, aps, XBC, Y, d):
        xf = self.bXN[:].bitcast(F32)
        dtk = xf[:, 0:1024].rearrange("p (d c h) -> p d c h", d=2, h=16)
        dtd = xf[:, 1024:2048].rearrange("p (d c h) -> p d c h", d=2, h=16)
        cdk = xf[:, 2048:3072].rearrange("p (d c h) -> p d c h", d=2, h=16)
        acsT = self.bX[0:48, 0:S]
        nacsT = self.bX[0:48, S:2 * S]
        r0 = 0 if d == 0 else 32
        state = self.bR[:, 0:1024]
        state_bf = self.bR[:, 1024:1536].bitcast(BF16)
        hb = self.bH
        hf = hb[:].bitcast(F32)
        CB0 = [0, 16384]
        tokb_ = [hb[:, o:o + 1536] for o in CB0]
        Xdt_ = [hb[:, o + 1536:o + 2560].rearrange("p (h q) -> p h q", h=16) for o in CB0]
        Xdd_ = [hb[:, o + 2560:o + 3584].rearrange("p (h q) -> p h q", h=16) for o in CB0]
        MT = [hb[:, 3584 + i * 512:3584 + (i + 1) * 512].rearrange("p (h l) -> p h l", h=4) for i in range(2)]
        Cs = [hb[:, 4608 + i * 512:4608 + (i + 1) * 512].rearrange("p (h l) -> p h l", h=4) for i in range(2)]
        Lf = [hf[:, 3072 + i * 512:3072 + (i + 1) * 512] for i in range(2)]
        Ef = [hf[:, 4096 + i * 512:4096 + (i + 1) * 512] for i in range(2)]
        rhs2 = [hf[0:48, 5120 + i * 512:5120 + (i + 1) * 512] for i in range(2)]
        fbs = [hb[:, 12288 + i * 2048:12288 + (i + 1) * 2048].rearrange("p (c t) -> p c t", c=16) for i in range(2)]
        fxs = [self.bSQ[:].bitcast(F32)[:, i * 2048:(i + 1) * 2048].rearrange("p (c t) -> p c t", c=16) for i in range(2)]
        yst = [self.bT[0], self.bT[1], self.bSG[0], self.bSG[1]]
        ysk = [("bT", 0), ("bT", 1), ("bSG", 0), ("bSG", 1)]
        PS = self.ps
        self.op("pool", lambda e: e.memset(self.bR[:, 0:1536], 0.0), [], [("st", g) for g in range(4)] + [("stb", g) for g in range(4)])
        order = list(range(S // 128)) if d == 0 else list(range(S // 128 - 1, -1, -1))
        items = [(ci, c, g) for ci, c in enumerate(order) for g in range(4)]

        def prep(ci, c):
            fi = ci % 2
            sl = slice(c * 128, (c + 1) * 128)
            fx, fb = fxs[fi], fbs[fi]
            self.dma(fx, XBC[:, :, sl].rearrange("c p t -> p c t"), [("XBC", c // (TG // 128))], [("sfx", fi)])
            self.op("pool", lambda e, o=fb, i_=fx: e.tensor_copy(out=o, in_=i_), [("sfx", fi)], [("sfb", fi)])
            tk = tokb_[fi]
            for half in range(2):
                pt = PS[:, 6, :].bitcast(BF16)
                for i in range(6):
                    ii = half * 6 + i
                    self.op("pe", lambda e, o=pt[:, i * 128:(i + 1) * 128], i_=fb[:, ii, :]: e.transpose(out=o, in_=i_, identity=self.ident_b[:]),
                            [("sfb", fi), "ident_b"], [("psb", 6)])
                self.op("act", lambda e, o=tk[:, half * 768:(half + 1) * 768], i_=pt[:, 0:768]: e.copy(out=o, in_=i_),
                        [("psb", 6)], [("tokb", fi, half)])
            xs3 = tk[:, 0:1024].rearrange("p (h q) -> p h q", h=16)
            self.op("dve", lambda e, o=Xdt_[fi], c=c: e.tensor_tensor(out=o, in0=xs3, in1=dtk[:, d, c, :].unsqueeze(2).broadcast_to([128, 16, 64]), op=ALU.mult),
                    [("tokb", fi, 0), ("tokb", fi, 1), "dtk"], [("Xdt", fi)])
            self.op("pool", lambda e, o=Xdd_[fi], c=c: e.tensor_tensor(out=o, in0=xs3, in1=dtd[:, d, c, :].unsqueeze(2).broadcast_to([128, 16, 64]), op=ALU.mult),
                    [("tokb", fi, 0), ("tokb", fi, 1), "dtd"], [("Xdd", fi)])

        def front(idx):
            ci, c, g = items[idx]
            if g == 0:
                prep(ci, c)
            fi = ci % 2
            bi = idx % 2
            sl = slice(c * 128, (c + 1) * 128)
            fb = fbs[fi]
            slot = idx % 4
            cbt = PS[:, 0, slot * 128:(slot + 1) * 128]
            kcb = ("psb", 0, slot)
            self.op("pe", lambda e, o=cbt, l=fb[:, 8 + g, :], r=fb[:, 12 + g, :]: e.matmul(o, lhsT=l, rhs=r, start=True, stop=True),
                    [("sfb", fi)], [kcb])
            r2 = rhs2[bi][r0:r0 + 16, :]
            kr2 = ("rhs2", bi)
            self.op("dve", lambda e, o=r2.rearrange("p (h l) -> p h l", h=4), sl=sl, g=g:
                    e.tensor_tensor(out=o, in0=acsT[r0:r0 + 16, sl].unsqueeze(1).broadcast_to([16, 4, 128]),
                                    in1=self.selm[r0:r0 + 16, g, :, :], op=ALU.mult), ["acsT", "selm"], [kr2])
            pdiff = PS[:, 1 + bi, :]
            kdf = ("psb", 1 + bi)
            self.op("pe", lambda e, o=pdiff, sl=sl, g=g: e.matmul(o, lhsT=nacsT[r0:r0 + 16, sl], rhs=self.selm[r0:r0 + 16, g, :, :].rearrange("p h l -> p (h l)"),
                                                       start=True, stop=False), ["nacsT", "selm"], [kdf])
            self.op("pe", lambda e, o=pdiff, r=r2: e.matmul(o, lhsT=self.ones_f[r0:r0 + 16, :], rhs=r, start=False, stop=False),
                    ["ones_f", kr2], [kdf])
            self.op("pe", lambda e, o=pdiff: e.matmul(o, lhsT=self.ident_b[:], rhs=self.mbias[:, d, :], start=False, stop=True),
                    ["ident_b", "mbias"], [kdf])
            pE = PS[:, 3 + bi, :]
            kpe = ("psb", 3 + bi)
            self.op("pe", lambda e, o=pE, r=r2: e.matmul(o, lhsT=self.ones_f[r0:r0 + 16, :], rhs=r, start=True, stop=True),
                    ["ones_f", kr2], [kpe])
            L = Lf[bi]
            kL = ("Lf", bi)
            self.op("act", lambda e, o=L, i_=pdiff: e.activation(out=o, in_=i_, func=AF.Exp), [kdf], [kL])
            E = Ef[bi]
            kE = ("Ef", bi)
            self.op("act", lambda e, o=E, i_=pE: e.activation(out=o, in_=i_, func=AF.Exp), [kpe], [kE])
            mt = MT[bi]
            self.op("dve", lambda e, o=mt, a=L.rearrange("p (h l) -> p h l", h=4), b_=cbt.unsqueeze(1).broadcast_to([128, 4, 128]):
                    e.tensor_tensor(out=o, in0=a, in1=b_, op=ALU.mult), [kL, kcb], [("MT", bi)])
            self.op("pool", lambda e, o=Cs[bi], a=E.rearrange("p (h l) -> p h l", h=4), b_=fb[:, 12 + g, :].unsqueeze(1).broadcast_to([128, 4, 128]):
                    e.tensor_tensor(out=o, in0=a, in1=b_, op=ALU.mult), [kE, ("sfb", fi)], [("Cs", bi)])

        def back(idx):
            ci, c, g = items[idx]
            fi = ci % 2
            bi = idx % 2
            sl = slice(c * 128, (c + 1) * 128)
            mt, cs_ = MT[bi], Cs[bi]
            Xdt, Xdd, tk = Xdt_[fi], Xdd_[fi], tokb_[fi]
            py = PS[0:64, 5, :]
            ky = ("psb", 5)
            for hh in range(4):
                head = 4 * g + hh
                self.op("pe", lambda e, o=py[:, hh * 128:(hh + 1) * 128], l=Xdt[:, head, :], r=mt[:, hh, :]:
                        e.matmul(o, lhsT=l, rhs=r, start=True, stop=False), [("Xdt", fi), ("MT", bi)], [ky])
                self.op("pe", lambda e, o=py[:, hh * 128:(hh + 1) * 128], l=state_bf[:, head * 64:(head + 1) * 64], r=cs_[:, hh, :]:
                        e.matmul(o, lhsT=l, rhs=r, start=False, stop=True), [("stb", g), ("Cs", bi)], [ky])
            ys = yst[g][0:64, 0:512]
            self.op("act", lambda e, o=ys, i_=py: e.copy(out=o, in_=i_), [ky], [ysk[g]])
            self.dma(Y[g * 256:(g + 1) * 256, sl].rearrange("(h p) l -> p h l", p=64), ys.rearrange("p (h l) -> p h l", h=4),
                     [ysk[g]], [("Y", d, c // (TG // 128))])
            pst = PS[:, 7, bi * 256:(bi + 1) * 256]
            kst = ("psb", 7, bi)
            self.op("pe", lambda e, o=pst, l=tk[:, (8 + g) * 128:(9 + g) * 128], r=Xdd[:, 4 * g:4 * g + 4, :]:
                    e.matmul(o, lhsT=l, rhs=r, start=True, stop=True), [("tokb", fi, 1), ("Xdd", fi)], [kst])
            stg = state[:, g * 256:(g + 1) * 256]
            self.op("dve", lambda e, o=stg.rearrange("p (h q) -> p h q", h=4), c=c, g=g:
                    e.tensor_tensor(out=o, in0=o, in1=cdk[:, d, c, 4 * g:4 * g + 4].unsqueeze(2).broadcast_to([128, 4, 64]), op=ALU.mult),
                    [("st", g), "cdk"], [("st", g)])
            self.op("dve", lambda e, o=stg, i_=pst: e.tensor_tensor(out=o, in0=o, in1=i_, op=ALU.add),
                    [("st", g), kst], [("st", g)])
            self.op("act", lambda e, o=state_bf[:, g * 256:(g + 1) * 256], i_=stg: e.copy(out=o, in_=i_), [("st", g)], [("stb", g)])

        n = len(items)
        for idx in range(n):
            front(idx)
            if idx >= 1:
                back(idx - 1)
        back(n - 1)

    def stage_odd_out(self, X, aps, li, npost, Yf, Yb, XBC, ZS, MO):
        xv, xn_, sq, kX, kXN_, kSQ = self.views()
        rhs = self.bH[:, 0:12 * TG].rearrange("p (c t) -> p c t", c=12)
        rkeys = [("bH", m) for m in range(12)]
        for g in range(NG):
            t0 = g * TG
            self.dma(rhs[:, 0:4, :], MO[:, :, t0:t0 + TG].rearrange("c p t -> p c t"), [("MO", g)], rkeys[0:4])
            for c in range(8):
                a, ka = self.stg4()
                b, kb = self.stg4()
                xs, kx = self.stg4()
                z, kz = self.stg4()
                self.dma(a[:], Yf[c * 128:(c + 1) * 128, t0:t0 + TG], [("Y", 0, g)], [ka])
                self.dma(b[:], Yb[c * 128:(c + 1) * 128, t0:t0 + TG], [("Y", 1, g)], [kb])
                self.dma(xs[:], XBC[c, :, t0:t0 + TG], [("XBC", g)], [kx])
                self.dma(z[:], ZS[c, :, t0:t0 + TG], [("ZS", g)], [kz])
                self.op("pool", lambda e, o=a[:], b_=b[:]: e.tensor_tensor(out=o, in0=o, in1=b_, op=ALU.add), [ka, kb], [ka])
                self.op("dve", lambda e, o=a[:], x_=xs[:], s=self.dexp[:, c:c + 1]:
                        e.scalar_tensor_tensor(out=o, in0=x_, scalar=s, in1=o, op0=ALU.mult, op1=ALU.add), [ka, kx, "dexp"], [ka])
                self.op("dve", lambda e, o=xv[:, c, :], a_=a[:], z_=z[:]: e.tensor_tensor(out=o, in0=a_, in1=z_, op=ALU.mult),
                        [ka, kz], [kX[c]])
                self.op("act", lambda e, o=sq[:, c, :], i_=xv[:, c, :]: e.activation(out=o, in_=i_, func=AF.Square),
                        [kX[c]], [kSQ[c]])
            rk, rv = self.rms_rinv(sq, 8, 0, 1.0 / D, EPS, kSQ)
            for c in range(8):
                self.op("dve", lambda e, o=rhs[:, 4 + c, :], i_=xv[:, c, :], s=self.snw[:, c:c + 1], r=rv:
                        e.scalar_tensor_tensor(out=o, in0=i_, scalar=s, in1=r, op0=ALU.mult, op1=ALU.mult),
                        [kX[c], "snw", rk], [rkeys[4 + c]])
            self.proj_post_residual(X, g, rhs, rkeys, 12, ("od_wout", li), npost, False)

def build_program(stop=None, only=None):
    _, plan = _build(stop, only, None)
    nc, _ = _build(stop, only, plan)
    return nc


def _build(stop, only, wplan):
    from contextlib import ExitStack
    nc = bass.Bass("TRN2", target_bir_lowering=False)
    dt = nc.dram_tensor
    aps = {}

    def inp(name, shape):
        aps[name] = dt(name, shape, F32, kind="ExternalInput").ap()

    inp("x", [S, D])
    inp("normw", [128, DEPTH * 6 * 8])
    import os
    nff = 1 if os.environ.get("DEV_SMALL") else DEPTH * 2
    inp("wg", [nff, D * DFF])
    inp("wu", [nff, D * DFF])
    inp("wd", [nff, DFF * D])
    inp("ev_win", [2, D * 2304])
    inp("ev_wv", [2, D * 256])
    inp("ev_pool_w", [2, 4, 64, 64])
    inp("ev_pool_scale", [2, 128, 2])
    inp("ev_sink", [2, 12])
    inp("ev_wout", [2, D * D])
    inp("ropec", [128, S])
    inp("ropes", [128, S])
    inp("icnt", [2, 128, S])
    inp("amask", [128, 2, 384])
    inp("od_win", [2, D * 4096])
    inp("od_wdt", [2, D, 32])
    inp("od_wout", [2, 1536 * D])
    inp("cv_dww", [2, 128, 4, 31])
    inp("cv_vec", [2, 128, 3, 4])
    inp("ssm_cw", [2, 128, 16, 5])
    inp("ssm_dexp", [2, 128, 8])
    inp("ssm_nw", [2, 128, 8])
    inp("ssm_ab", [2, 48, 2])
    inp("ssm_A_log", [2, 32])
    inp("ssm_dt_bias", [2, 32])
    inp("tri", [128, 2, 128])
    inp("selm", [48, 4, 4, 128])
    inp("mbias", [128, 2, 512])
    inp("rmask", [48, 128])
    out = dt("out", [S, D], F32, kind="ExternalOutput").ap()
    X = dt("Xs", [8, 128, S], F32, kind="Internal").ap()
    QT = dt("QTs", [768, S], BF16, kind="Internal").ap()
    KT = dt("KTs", [256, S], BF16, kind="Internal").ap()
    V = dt("Vs", [S, 256], BF16, kind="Internal").ap()
    U = dt("Us", [2, 128, S], F32, kind="Internal").ap()
    AO = dt("AOs", [8, 128, S], BF16, kind="Internal").ap()
    Gs = dt("Gs", [4, 128, S], F32, kind="Internal").ap()
    ZS = dt("ZSs", [8, 128, S], F32, kind="Internal").ap()
    XBCr = dt("XBCrs", [16, 128, S], F32, kind="Internal").ap()
    XBC = dt("XBCs", [16, 128, S], F32, kind="Internal").ap()
    DTr = dt("DTrs", [32, S], F32, kind="Internal").ap()
    DTt = dt("DTts", [128, 32, 32], F32, kind="Internal").ap()
    Yf = dt("Yfs", [1024, S], F32, kind="Internal").ap()
    Yb = dt("Ybs", [1024, S], F32, kind="Internal").ap()
    MO = dt("MOs", [4, 128, S], BF16, kind="Internal").ap()
    B = Builder(nc, stop, aps, wplan)
    B.init_consts(aps)
    B.stage_load(aps["x"], X)
    nst = 0
    seq = [(l, sub) for l in range(DEPTH) for sub in range(3)]
    if stop is not None:
        seq = seq[:stop]
    if only is not None:
        seq = only
    for (l, sub) in seq:
        base = l * 6 * 8
        if True:
            if sub == 0:
                B.sc.barrier(); B.stage_ffn(X, ("wg", l * 2), ("wu", l * 2), ("wd", l * 2), base + 0, base + 8)
            elif sub == 2:
                B.sc.barrier(); B.stage_ffn(X, ("wg", l * 2 + 1), ("wu", l * 2 + 1), ("wd", l * 2 + 1), base + 32, base + 40)
            elif l % 2 == 0:
                li = l // 2
                B.even_consts(aps, li)
                B.sc.barrier(); B.stage_even_in(X, aps, li, base + 16, QT, KT, V, U)
                B.sc.barrier(); B.stage_even_pool(aps, U, AO)
                B.sc.barrier(); B.stage_even_attn(aps, li, QT, KT, V, AO)
                B.sc.barrier(); B.stage_even_out(X, aps, li, base + 24, AO)
            else:
                li = l // 2
                import os
                ocut = int(os.environ.get("ODD_CUT", "99"))
                B.odd_consts(aps, li)
                oskip = int(os.environ.get("ODD_SKIP", "0"))
                if ocut >= 1 and oskip < 1:
                    B.sc.barrier(); B.stage_odd_in(X, aps, li, base + 16, Gs, ZS, XBCr, DTr, DTt)
                if ocut >= 2 and oskip < 2:
                    B.sc.barrier(); B.stage_odd_conv(aps, li, Gs, XBCr, XBC, MO)
                if ocut >= 3:
                    B.sc.barrier(); B.stage_odd_dt(aps, li, DTr, DTt)
                if ocut >= 4:
                    B.sc.barrier(); B.stage_odd_ssd(aps, XBC, Yf, 0)
                if ocut >= 5:
                    B.sc.barrier(); B.stage_odd_ssd(aps, XBC, Yb, 1)
                if ocut >= 6:
                    B.sc.barrier(); B.stage_odd_out(X, aps, li, base + 24, Yf, Yb, XBC, ZS, MO)
    B.sc.barrier(); B.stage_store(X, out)
    fin = [("out", g) for g in range(NG)]
    if os.environ.get("DBG"):
        def dbg(name, ap, keys):
            o = dt("dbg_" + name, list(ap.shape), ap.dtype, kind="ExternalOutput").ap()
            B.dma(o, ap, keys, [("dbg", name)])
            fin.append(("dbg", name))
        G4 = range(NG)
        dbg("G", Gs, [("G", g) for g in G4])
        dbg("XBC", XBC, [("XBC", g) for g in G4])
        dbg("MO", MO, [("MO", g) for g in G4])
        dbg("ZS", ZS, [("ZS", g) for g in G4])
        dbg("Yf", Yf, [("Y", 0, g) for g in G4])
        dbg("Yb", Yb, [("Y", 1, g) for g in G4])
        dbg("DTr", DTr, [("DTr", g) for g in G4])
        dbg("tok", B.bXN[:].bitcast(F32)[:, 0:3072], ["dtk", "dtd", "cdk"])
        dbg("acsT", B.bX[0:48, 0:S], ["acsT"])
    B.op("sp", lambda e: e.nop(), fin, [])
    if wplan is None:
        return None, B.wrec
    with ExitStack() as st:
        B.sc.emit(st)
    return nc, B.wrec


def rope_tables_np():
    inv = (1.0 / (np.float32(10000.0) ** (np.arange(0, 64, 2, dtype=np.float32) / np.float32(64)))).astype(np.float32)
    ang = (np.arange(S, dtype=np.float32)[:, None] * inv[None, :]).astype(np.float32)
    cos = np.cos(ang).astype(np.float32)
    sin = np.sin(ang).astype(np.float32)
    cf = np.concatenate([cos, cos], axis=1)
    sf = np.concatenate([-sin, sin], axis=1)
    cT = np.ascontiguousarray(np.concatenate([cf, cf], axis=1).T)
    sT = np.ascontiguousarray(np.concatenate([sf, sf], axis=1).T)
    return cT, sT


def const_tables():
    cT, sT = rope_tables_np()
    t = np.arange(S)
    icnt = np.zeros((2, 128, S), np.float32)
    for gi, w in enumerate((2, 4, 8, 16)):
        lo = np.clip(t - w // 2, 0, S)
        hi = np.clip(t + w - w // 2, 0, S)
        icnt[gi // 2, (gi % 2) * 64:(gi % 2) * 64 + 64, :] = (1.0 / (hi - lo).astype(np.float32))[None, :]
    kl = np.arange(128)[:, None]
    ql = np.arange(128)[None, :]
    m1 = (ql <= kl).astype(np.float32)
    m2 = (kl <= ql).astype(np.float32)
    amask = np.stack([np.tile(m1, (1, 3)), np.tile(m2, (1, 3))], axis=1)
    s_ = np.arange(128)[:, None]
    l_ = np.arange(128)[None, :]
    tri = np.stack([(s_ <= l_), (s_ >= l_)], axis=1).astype(np.float32)
    selm = np.zeros((48, 4, 4, 128), np.float32)
    for k in range(16):
        selm[k, k // 4, k % 4, :] = 1.0
        selm[32 + k, k // 4, k % 4, :] = 1.0
    NEG = -30000.0
    mb = np.stack([np.tile(np.where(l_ >= s_, 0.0, NEG), (1, 4)), np.tile(np.where(s_ >= l_, 0.0, NEG), (1, 4))], axis=1)
    rmask = np.ones((48, 128), np.float32)
    rmask[:, 0] = 0.0
    return {"ropec": cT, "ropes": sT, "icnt": icnt, "amask": np.ascontiguousarray(amask),
            "tri": np.ascontiguousarray(tri), "selm": selm, "mbias": np.ascontiguousarray(mb.astype(np.float32)),
            "rmask": rmask}


def make_in_maps(inputs):
    nw = np.ascontiguousarray(
        inputs["norm_w"].reshape(DEPTH, 6, 8, 128).transpose(3, 0, 1, 2).reshape(128, DEPTH * 6 * 8))
    qk = np.arange(256, 1280)
    rel = (qk - 256) % 64
    partner = qk - rel + np.where(rel < 32, rel + 32, rel - 32)
    cols = []
    for i in range(8):
        cols.append(qk[i * 128:(i + 1) * 128])
        cols.append(partner[i * 128:(i + 1) * 128])
    cols.append(np.arange(0, 256))
    cols = np.concatenate(cols)
    shared = {
        "normw": nw,
        "wg": tileize(inputs["ffn_w_gate"].reshape(DEPTH * 2, D, DFF)),
        "wu": tileize(inputs["ffn_w_up"].reshape(DEPTH * 2, D, DFF)),
        "wd": tileize(inputs["ffn_w_down"].reshape(DEPTH * 2, DFF, D)),
        "ev_win": tileize(inputs["ev_w_in"][:, :, cols]),
        "ev_wv": tileize(inputs["ev_w_in"][:, :, 1280:1536]),
        "ev_pool_w": np.ascontiguousarray(inputs["ev_pool_w"]),
        "ev_pool_scale": np.ascontiguousarray(inputs["ev_pool_scale"].reshape(2, 2, 128).transpose(0, 2, 1)),
        "ev_sink": np.ascontiguousarray(inputs["ev_sink"]),
        "ev_wout": tileize(inputs["ev_w_out"]),
    }
    ocols = []
    for m in range(4):
        ocols.append(np.arange(m * 128, (m + 1) * 128))
        ocols.append(np.arange(512 + m * 128, 512 + (m + 1) * 128))
    ocols.append(np.arange(1024, 4128))
    ocols = np.concatenate(ocols)
    f32 = np.float32
    ab = np.zeros((2, 48, 2), f32)
    ab[:, 0:16, 0] = inputs["ssm_A_log"][:, 0]
    ab[:, 32:48, 0] = inputs["ssm_A_log"][:, 1]
    ab[:, 0:16, 1] = inputs["ssm_dt_bias"][:, 0]
    ab[:, 32:48, 1] = inputs["ssm_dt_bias"][:, 1]
    cw = np.concatenate([inputs["ssm_conv_w"].reshape(2, 4, 16, 128).transpose(0, 3, 2, 1),
                         inputs["ssm_conv_b"].reshape(2, 16, 128).transpose(0, 2, 1)[..., None]], axis=-1)
    shared.update({
        "od_win": tileize(inputs["od_w_in"][:, :, ocols[:4096]]),
        "od_wdt": np.ascontiguousarray(inputs["od_w_in"][:, :, 4096:4128]),
        "od_wout": tileize(inputs["od_w_out"]),
        "cv_dww": np.ascontiguousarray(inputs["cv_dw_w"].reshape(2, 31, 4, 128).transpose(0, 3, 2, 1)),
        "cv_vec": np.ascontiguousarray(np.stack([inputs["cv_dw_b"], inputs["cv_ln_g"], inputs["cv_ln_b"]], axis=1)
                                       .reshape(2, 3, 4, 128).transpose(0, 3, 1, 2)),
        "ssm_cw": np.ascontiguousarray(cw.astype(f32)),
        "ssm_dexp": np.ascontiguousarray(np.repeat(inputs["ssm_D"], 64, axis=1).reshape(2, 8, 128).transpose(0, 2, 1)),
        "ssm_nw": np.ascontiguousarray(inputs["ssm_norm_w"].reshape(2, 8, 128).transpose(0, 2, 1)),
        "ssm_ab": ab,
        "ssm_A_log": np.ascontiguousarray(inputs["ssm_A_log"].reshape(2, 32)),
        "ssm_dt_bias": np.ascontiguousarray(inputs["ssm_dt_bias"].reshape(2, 32)),
    })
    shared.update(const_tables())
    maps = []
    for b in range(NCORES):
        m = dict(shared)
        m["x"] = np.ascontiguousarray(inputs["x"][b])
        maps.append(m)
    return maps


def kernel(**inputs):
    inputs = {k: np.asarray(v) for k, v in inputs.items()}
    nc = build_program()
    in_maps = make_in_maps(inputs)
    res = run_bass_kernel_spmd(nc, in_maps, core_ids=list(range(NCORES)))
    return np.stack([np.asarray(r["out"]) for r in res.results], axis=0).astype(np.float32)
```
